# Optimizing a Trainium2 kernel written in Bass

```python
import math
import jax
import jax.numpy as jnp
from jax import lax
import numpy as np

D_MODEL = 2048
BATCH = 1
SEQ = 8192
DEPTH = 2

N_A_LAYERS = DEPTH // 2
N_B_LAYERS = DEPTH - N_A_LAYERS

GDN_QK_HEADS = 16
GDN_V_HEADS = 32
GDN_HEAD_DIM = 128
GDN_QK_DIM = GDN_QK_HEADS * GDN_HEAD_DIM
GDN_V_DIM = GDN_V_HEADS * GDN_HEAD_DIM
CONV_DIM = 2 * GDN_QK_DIM + GDN_V_DIM
GDN_PROJ = CONV_DIM + GDN_V_DIM + 2 * GDN_V_HEADS
CONV_K = 4
GDN_CHUNK = 64

FOX_HEADS = 16
FOX_KV_HEADS = 2
FOX_GROUP = FOX_HEADS // FOX_KV_HEADS
FOX_HEAD_DIM = 256
FOX_Q_DIM = FOX_HEADS * FOX_HEAD_DIM
FOX_KV_DIM = FOX_KV_HEADS * FOX_HEAD_DIM
KV_PROJ = 2 * FOX_KV_DIM + FOX_HEADS
Q_BLOCK = 128

FFN_HIDDEN = ((8 * D_MODEL // 3 + 255) // 256) * 256

NORM_EPS = 1e-6

kernel_name = "yoco_gdn_fox_adaln_trunk"


def rms_norm(x, w):
    xf = x.astype(jnp.float32)
    y = xf * lax.rsqrt(jnp.mean(xf * xf, axis=-1, keepdims=True) + NORM_EPS)
    return (y * w.astype(jnp.float32)).astype(x.dtype)


def modulate(h, shift, scale):
    return h * (1 + scale) + shift


def l2_normalize(x):
    xf = x.astype(jnp.float32)
    return xf * lax.rsqrt(jnp.sum(xf * xf, axis=-1, keepdims=True) + NORM_EPS)


def swiglu(h, w_in, w_out):
    gate, up = jnp.split(h @ w_in, 2, axis=-1)
    return (jax.nn.silu(gate) * up) @ w_out


def causal_conv(x, w):
    width = w.shape[0]
    length = x.shape[1]
    xp = jnp.pad(x, ((0, 0), (width - 1, 0), (0, 0)))
    return sum(xp[:, i:i + length] * w[i] for i in range(width))


def gated_delta_rule_chunked(q, k, v, g, beta):
    b, h, length, dk = q.shape
    dv = v.shape[-1]
    n = length // GDN_CHUNK
    blk = lambda t: t.reshape(b, h, n, GDN_CHUNK, *t.shape[3:])
    q = blk(q) * dk ** -0.5
    k = blk(k)
    v = blk(v)
    beta = blk(beta)
    g = jnp.cumsum(blk(g), axis=-1)
    causal = jnp.tril(jnp.ones((GDN_CHUNK, GDN_CHUNK), dtype=bool))
    strict = jnp.tril(jnp.ones((GDN_CHUNK, GDN_CHUNK), dtype=bool), k=-1)
    decay = jnp.exp(jnp.where(causal, g[..., :, None] - g[..., None, :], -jnp.inf))
    k_beta = k * beta[..., None]
    a_strict = jnp.where(strict, jnp.einsum('bhnid,bhnjd->bhnij', k_beta, k) * decay, 0.0)
    eye = jnp.eye(GDN_CHUNK, dtype=q.dtype)
    rhs = jnp.concatenate([v * beta[..., None], k_beta * jnp.exp(g)[..., None]], axis=-1)
    sol = lax.linalg.triangular_solve(a_strict + eye, rhs, left_side=True, lower=True,
                                      unit_diagonal=True)
    u, w = sol[..., :dv], sol[..., dv:]
    attn = jnp.where(causal, jnp.einsum('bhnid,bhnjd->bhnij', q, k) * decay, 0.0)
    g_last = g[..., -1]
    q_dec = q * jnp.exp(g)[..., None]
    k_dec = k * jnp.exp(g_last[..., None] - g)[..., None]

    def step(state, xs):
        q_c, k_c, u_c, w_c, attn_c, gl_c = xs
        v_new = u_c - jnp.einsum('bhik,bhkv->bhiv', w_c, state)
        o = jnp.einsum('bhik,bhkv->bhiv', q_c, state) + jnp.einsum('bhij,bhjv->bhiv', attn_c, v_new)
        state = state * jnp.exp(gl_c)[..., None, None] + jnp.einsum('bhik,bhiv->bhkv', k_c, v_new)
        return state, o

    xs = tuple(jnp.moveaxis(t, 2, 0) for t in (q_dec, k_dec, u, w, attn, g_last))
    s0 = jnp.zeros((b, h, dk, dv), jnp.float32)
    _, o = lax.scan(step, s0, xs)
    return jnp.moveaxis(o, 0, 2).reshape(b, h, length, dv)


def gated_deltanet(h, w_in, conv_w, a_log, dt_bias, norm_w, w_out):
    b, length, _ = h.shape
    qkv, z, beta_logit, a = jnp.split(
        h @ w_in, [CONV_DIM, CONV_DIM + GDN_V_DIM, CONV_DIM + GDN_V_DIM + GDN_V_HEADS], axis=-1)
    qkv = jax.nn.silu(causal_conv(qkv, conv_w))
    q, k, v = jnp.split(qkv, [GDN_QK_DIM, 2 * GDN_QK_DIM], axis=-1)
    rep = GDN_V_HEADS // GDN_QK_HEADS
    heads = lambda t, nh: t.reshape(b, length, nh, GDN_HEAD_DIM)
    q = jnp.repeat(l2_normalize(heads(q, GDN_QK_HEADS)), rep, axis=2)
    k = jnp.repeat(l2_normalize(heads(k, GDN_QK_HEADS)), rep, axis=2)
    v = heads(v, GDN_V_HEADS).astype(jnp.float32)
    beta = jax.nn.sigmoid(beta_logit.astype(jnp.float32))
    g = -jnp.exp(a_log.astype(jnp.float32)) * jax.nn.softplus(
        a.astype(jnp.float32) + dt_bias.astype(jnp.float32))
    tr = lambda t: jnp.swapaxes(t, 1, 2)
    o = gated_delta_rule_chunked(tr(q), tr(k), tr(v), tr(g), tr(beta))
    o = rms_norm(tr(o), norm_w) * jax.nn.silu(heads(z, GDN_V_HEADS).astype(jnp.float32))
    return o.reshape(b, length, GDN_V_DIM).astype(h.dtype) @ w_out


def shared_kv(x, cond, ada_w, ada_b, norm_w, w_kv, k_norm_w, forget_b):
    b, length, _ = x.shape
    shift, scale = (m[:, None, :] for m in jnp.split(cond @ ada_w + ada_b, 2, axis=-1))
    h = modulate(rms_norm(x, norm_w), shift, scale)
    k, v, f_logit = jnp.split(h @ w_kv, [FOX_KV_DIM, 2 * FOX_KV_DIM], axis=-1)
    k = rms_norm(k.reshape(b, length, FOX_KV_HEADS, FOX_HEAD_DIM), k_norm_w)
    v = v.reshape(b, length, FOX_KV_HEADS, FOX_HEAD_DIM)
    log_f = jax.nn.log_sigmoid(f_logit.astype(jnp.float32) + forget_b.astype(jnp.float32))
    f_cum = jnp.cumsum(log_f, axis=1).reshape(b, length, FOX_KV_HEADS, FOX_GROUP)
    return k, v, f_cum


def blocked_forgetting_softmax(q, k, v, f_cum):
    b, length, kvh, grp, hd = q.shape
    nb = length // Q_BLOCK
    qb = jnp.swapaxes(q.reshape(b, nb, Q_BLOCK, kvh, grp, hd), 0, 1)
    fb = jnp.swapaxes(f_cum.reshape(b, nb, Q_BLOCK, kvh, grp), 0, 1)
    f_k = jnp.transpose(f_cum, (0, 2, 3, 1))
    k_pos = jnp.arange(length)
    scale = hd ** -0.5

    def one_block(args):
        i, q_i, f_i = args
        s = jnp.einsum('bqhgd,bkhd->bhgqk', q_i, k, preferred_element_type=jnp.float32) * scale
        s = s + jnp.transpose(f_i, (0, 2, 3, 1))[..., None] - f_k[..., None, :]
        q_pos = i * Q_BLOCK + jnp.arange(Q_BLOCK)
        s = jnp.where(k_pos[None, :] <= q_pos[:, None], s, -jnp.inf)
        p = jax.nn.softmax(s, axis=-1).astype(v.dtype)
        return jnp.einsum('bhgqk,bkhd->bqhgd', p, v)

    o = lax.map(one_block, (jnp.arange(nb), qb, fb))
    return jnp.swapaxes(o, 0, 1).reshape(b, length, kvh, grp, hd)


def forgetting_attention(h, k, v, f_cum, w_in, q_norm_w, w_out):
    b, length, _ = h.shape
    q, gate = jnp.split(h @ w_in, 2, axis=-1)
    q = rms_norm(q.reshape(b, length, FOX_HEADS, FOX_HEAD_DIM), q_norm_w)
    q = q.reshape(b, length, FOX_KV_HEADS, FOX_GROUP, FOX_HEAD_DIM)
    o = blocked_forgetting_softmax(q, k, v, f_cum).reshape(b, length, FOX_Q_DIM)
    return (o * jax.nn.sigmoid(gate)) @ w_out


def setup_inputs(seed: int = 0) -> dict:
    key = jax.random.key(seed)
    ks = iter(jax.random.split(key, 40))
    f32 = jnp.float32

    def dense(shape, fan_in, s=1.0):
        return (s * fan_in ** -0.5) * jax.random.normal(next(ks), shape, f32)

    def gain(shape):
        return 1.0 + 0.02 * jax.random.normal(next(ks), shape, f32)

    def small(shape, s):
        return s * jax.random.normal(next(ks), shape, f32)

    d = D_MODEL
    x = jax.random.normal(next(ks), (BATCH, SEQ, d), f32)
    c = jax.random.normal(next(ks), (BATCH, d), f32)
    ada_w = dense((DEPTH, d, 6 * d), d, 0.5)
    ada_b = small((DEPTH, 6 * d), 0.02)
    norm_mix = gain((DEPTH, d))
    norm_ffn = gain((DEPTH, d))
    ffn_w_in = dense((DEPTH, d, 2 * FFN_HIDDEN), d)
    ffn_w_out = dense((DEPTH, FFN_HIDDEN, d), FFN_HIDDEN)
    gdn_w_in = dense((N_A_LAYERS, d, GDN_PROJ), d)
    gdn_conv = dense((N_A_LAYERS, CONV_K, CONV_DIM), CONV_K)
    gdn_a_log = jnp.log(jax.random.uniform(next(ks), (N_A_LAYERS, GDN_V_HEADS), f32, 1.0, 16.0))
    dt = jnp.exp(jax.random.uniform(next(ks), (N_A_LAYERS, GDN_V_HEADS), f32,
                                    math.log(1e-3), math.log(1e-1)))
    gdn_dt_bias = dt + jnp.log(-jnp.expm1(-dt))
    gdn_norm = gain((N_A_LAYERS, GDN_HEAD_DIM))
    gdn_w_out = dense((N_A_LAYERS, GDN_V_DIM, d), GDN_V_DIM)
    kv_ada_w = dense((d, 2 * d), d, 0.5)
    kv_ada_b = small((2 * d,), 0.02)
    kv_norm = gain((d,))
    kv_w = dense((d, KV_PROJ), d)
    k_norm = gain((FOX_HEAD_DIM,))
    forget_b = jax.random.uniform(next(ks), (FOX_HEADS,), f32, 1.0, 6.0)
    fox_w_in = dense((N_B_LAYERS, d, 2 * FOX_Q_DIM), d)
    q_norm = gain((N_B_LAYERS, FOX_HEAD_DIM))
    fox_w_out = dense((N_B_LAYERS, FOX_Q_DIM, d), FOX_Q_DIM)
    out_ada_w = dense((d, 2 * d), d, 0.5)
    out_ada_b = small((2 * d,), 0.02)
    out_norm = gain((d,))
    return {"x": x, "c": c, "ada_w": ada_w, "ada_b": ada_b, "norm_mix": norm_mix,
            "norm_ffn": norm_ffn, "ffn_w_in": ffn_w_in, "ffn_w_out": ffn_w_out,
            "gdn_w_in": gdn_w_in, "gdn_conv": gdn_conv, "gdn_a_log": gdn_a_log,
            "gdn_dt_bias": gdn_dt_bias, "gdn_norm": gdn_norm, "gdn_w_out": gdn_w_out,
            "kv_ada_w": kv_ada_w, "kv_ada_b": kv_ada_b, "kv_norm": kv_norm, "kv_w": kv_w,
            "k_norm": k_norm, "forget_b": forget_b, "fox_w_in": fox_w_in, "q_norm": q_norm,
            "fox_w_out": fox_w_out, "out_ada_w": out_ada_w, "out_ada_b": out_ada_b,
            "out_norm": out_norm}


def reference(x, c, ada_w, ada_b, norm_mix, norm_ffn, ffn_w_in, ffn_w_out, gdn_w_in, gdn_conv,
              gdn_a_log, gdn_dt_bias, gdn_norm, gdn_w_out, kv_ada_w, kv_ada_b, kv_norm, kv_w,
              k_norm, forget_b, fox_w_in, q_norm, fox_w_out, out_ada_w, out_ada_b, out_norm):
    cond = jax.nn.silu(c)
    k_sh = v_sh = f_sh = None
    for layer in range(DEPTH):
        sh_m, sc_m, g_m, sh_f, sc_f, g_f = (
            m[:, None, :] for m in jnp.split(cond @ ada_w[layer] + ada_b[layer], 6, axis=-1))
        h = modulate(rms_norm(x, norm_mix[layer]), sh_m, sc_m)
        if layer < N_A_LAYERS:
            y = gated_deltanet(h, gdn_w_in[layer], gdn_conv[layer], gdn_a_log[layer],
                               gdn_dt_bias[layer], gdn_norm[layer], gdn_w_out[layer])
        else:
            if layer == N_A_LAYERS:
                k_sh, v_sh, f_sh = shared_kv(x, cond, kv_ada_w, kv_ada_b, kv_norm, kv_w,
                                             k_norm, forget_b)
            j = layer - N_A_LAYERS
            y = forgetting_attention(h, k_sh, v_sh, f_sh, fox_w_in[j], q_norm[j], fox_w_out[j])
        x = x + g_m * y
        h = modulate(rms_norm(x, norm_ffn[layer]), sh_f, sc_f)
        x = x + g_f * swiglu(h, ffn_w_in[layer], ffn_w_out[layer])
    sh_o, sc_o = (m[:, None, :] for m in jnp.split(cond @ out_ada_w + out_ada_b, 2, axis=-1))
    return modulate(rms_norm(x, out_norm), sh_o, sc_o)
```

```python
import contextlib
import numpy as np
import concourse.bass as bass
import concourse.mybir as mybir
from concourse.bass_utils import run_bass_kernel_spmd

F32 = mybir.dt.float32
BF16 = mybir.dt.bfloat16
AF = mybir.ActivationFunctionType
ALU = mybir.AluOpType
AX = mybir.AxisListType

NCORES = 8
SEM_LIMIT = 30000


class Prog:
    ENGS = ("pe", "act", "dve", "pool", "sp")

    def __init__(self, nc, n_dma_sems=40, self_sync=True):
        self.nc = nc
        self.stack = contextlib.ExitStack()
        self.streams = {e: [] for e in self.ENGS}
        self.cnt = {e: 0 for e in self.ENGS}
        self.esem = {e: nc.alloc_semaphore("s_" + e + "0") for e in self.ENGS}
        self.egen = {e: 0 for e in self.ENGS}
        self.seen = {e: {} for e in self.ENGS}
        self.lastw = {}
        self.readers = {}
        self.self_sync = self_sync
        self.dsem = [nc.alloc_semaphore("s_dma%d" % i) for i in range(n_dma_sems)]
        self.dcnt = [0] * n_dma_sems
        self.drr = 0
        self.sems = {}
        for e in self.ENGS:
            self.sems[id(self.esem[e])] = self.esem[e]
        for s in self.dsem:
            self.sems[id(s)] = s
        self.n_ins = 0
        self.uid = 0

    def sb(self, shape, dtype, name=None):
        self.uid += 1
        return self.stack.enter_context(self.nc.sbuf_tensor(name or "sb%d" % self.uid, list(shape), dtype))

    def ps(self, shape, dtype, name=None):
        self.uid += 1
        return self.stack.enter_context(self.nc.psum_tensor(name or "ps%d" % self.uid, list(shape), dtype))

    @staticmethod
    def _key(k):
        if isinstance(k, tuple):
            return tuple(Prog._key(x) for x in k)
        if isinstance(k, (str, int)):
            return k
        return k.name

    def _deps(self, reads, writes):
        need = {}
        raw = {}
        reads = [self._key(k) for k in reads]
        writes = [self._key(k) for k in writes]

        def add(d, sk, v):
            if d.get(sk, 0) < v:
                d[sk] = v
        for k in reads:
            lw = self.lastw.get(k)
            if lw is not None:
                add(need, *lw)
                add(raw, *lw)
        for k in writes:
            lw = self.lastw.get(k)
            if lw is not None:
                add(need, *lw)
            for sk, v in self.readers.get(k, {}).items():
                add(need, sk, v)
        return need, raw

    def _emit_waits(self, eng, deps):
        need, raw = deps
        own = id(self.esem[eng])
        for sk, v in need.items():
            if sk == own:
                if eng == "pe" or not self.self_sync:
                    continue
            if self.seen[eng].get(sk, 0) >= v:
                continue
            self.seen[eng][sk] = v
            sem = self.sems[sk]
            self.streams[eng].append(lambda e, sem=sem, v=v: e.wait_ge(sem, v))
            self.n_ins += 1

    def _record(self, reads, writes, sk, v):
        reads = [self._key(k) for k in reads]
        writes = [self._key(k) for k in writes]
        for k in writes:
            self.lastw[k] = (sk, v)
            self.readers[k] = {}
        for k in reads:
            r = self.readers.setdefault(k, {})
            if r.get(sk, 0) < v:
                r[sk] = v

    def _excl(self, eng, reads, writes):
        extra = {}
        for k in reads:
            k = self._key(k)
            if isinstance(k, str) and k.startswith("psb"):
                for sk, v in self.readers.get(k, {}).items():
                    if sk != id(self.esem[eng]) and extra.get(sk, 0) < v:
                        extra[sk] = v
        return extra

    def op(self, eng, fn, reads=(), writes=()):
        extra = self._excl(eng, reads, writes)
        need, raw = self._deps(reads, writes)
        for sk, v in extra.items():
            if need.get(sk, 0) < v:
                need[sk] = v
        self._emit_waits(eng, (need, raw))
        if self.cnt[eng] >= SEM_LIMIT:
            self.egen[eng] += 1
            s = self.nc.alloc_semaphore("s_%s%d" % (eng, self.egen[eng]))
            self.esem[eng] = s
            self.sems[id(s)] = s
            self.cnt[eng] = 0
        self.cnt[eng] += 1
        sem = self.esem[eng]
        v = self.cnt[eng]
        self.streams[eng].append(lambda e, fn=fn, sem=sem: fn(e).then_inc(sem, 1))
        self.n_ins += 1
        self._record(reads, writes, id(sem), v)

    def dma(self, eng, out, in_, reads=(), writes=(), **kw):
        half = len(self.dsem) // 2
        if eng == "pool":
            self.drr_sw = (getattr(self, "drr_sw", -1) + 1) % half
            i = half + self.drr_sw
        else:
            self.drr = (self.drr + 1) % half
            i = self.drr
        sem = self.dsem[i]
        need, raw = self._deps(reads, writes)
        if self.dcnt[i] > 0:
            sk = id(sem)
            if need.get(sk, 0) < self.dcnt[i]:
                need[sk] = self.dcnt[i]
        self._emit_waits(eng, (need, raw))
        self.dcnt[i] += 16
        v = self.dcnt[i]
        self.streams[eng].append(
            lambda e, out=out, in_=in_, sem=sem, kw=kw: e.dma_start(out=out, in_=in_, **kw).then_inc(sem, 16))
        self.n_ins += 1
        self._record(reads, writes, id(sem), v)

    def wait_all(self, eng, keys):
        need, _ = self._deps((), keys)
        for sk, v in need.items():
            if self.seen[eng].get(sk, 0) >= v:
                continue
            self.seen[eng][sk] = v
            sem = self.sems[sk]
            self.streams[eng].append(lambda e, sem=sem, v=v: e.wait_ge(sem, v))

    def end_barrier(self, eng="sp"):
        for e2 in self.ENGS:
            if e2 == eng or self.cnt[e2] == 0:
                continue
            sem, v = self.esem[e2], self.cnt[e2]
            self.streams[eng].append(lambda e, sem=sem, v=v: e.wait_ge(sem, v))
        for i, sem in enumerate(self.dsem):
            if self.dcnt[i] > 0:
                v = self.dcnt[i]
                self.streams[eng].append(lambda e, sem=sem, v=v: e.wait_ge(sem, v))

    def emit(self):
        self.end_barrier("sp")
        with self.nc.Block() as block:
            @block.tensor
            def _(e):
                for f in self.streams["pe"]:
                    f(e)

            @block.scalar
            def _(e):
                for f in self.streams["act"]:
                    f(e)

            @block.vector
            def _(e):
                for f in self.streams["dve"]:
                    f(e)

            @block.gpsimd
            def _(e):
                for f in self.streams["pool"]:
                    f(e)

            @block.sync
            def _(e):
                for f in self.streams["sp"]:
                    f(e)
        self.stack.close()


D = 2048
KC = D // 128
EPS = 1e-6
FFN_H = 5632


def col_layout(v):
    v = np.ascontiguousarray(v, dtype=np.float32)
    return np.ascontiguousarray(v.reshape(-1, 128).T)


class Ctx:
    def __init__(self, P, mvblk=None):
        self.P = P
        self.ps = [P.ps([128, 512], F32, name="psb%d" % i) for i in range(8)]
        self.ones = P.sb([128, 128], F32, name="ones_f")
        P.op("pool", lambda e: e.memset(self.ones[:], 1.0), writes=[self.ones])
        self.sq = [P.sb([128, 512], F32, name="sq%d" % i) for i in range(2)]
        self.sqi = 0
        self.rstd = P.sb([128, 512], F32, name="rstd")
        self.tmp = [P.sb([128, 512], F32, name="tmpf%d" % i) for i in range(3)]
        self.tmpi = 0
        if mvblk is None:
            mv = [P.sb([128, KC, 128], F32, name="mvblk%d" % i) for i in range(2)]
            mvblk = [(t[:], t) for t in mv]
        self.mvblk = mvblk
        self.mvi = 0
        self.epsb = P.sb([128, 1], F32, name="epsb")
        P.op("pool", lambda e: e.memset(self.epsb[:], EPS), writes=[self.epsb])
        self.oneb = P.sb([128, 1], F32, name="oneb")
        P.op("pool", lambda e: e.memset(self.oneb[:], 1.0), writes=[self.oneb])

    def next_sq(self):
        self.sqi = (self.sqi + 1) % len(self.sq)
        return self.sq[self.sqi]

    def next_tmp(self):
        self.tmpi = (self.tmpi + 1) % len(self.tmp)
        return self.tmp[self.tmpi]


def load_cond(P, C, c_dram):
    craw = P.sb([128, KC], F32, name="craw")
    cond = P.sb([128, KC], F32, name="cond")
    P.dma("sp", craw[:], c_dram, writes=[craw])
    P.op("act", lambda e: e.activation(out=cond[:], in_=craw[:], func=AF.Silu), reads=[craw], writes=[cond])
    return cond


def matvec(P, C, w_dram, ncols, cond, bias_tile, out_tile, ps):
    noc = ncols // 128
    wv = w_dram.rearrange("(kc p) n -> p kc n", p=128)
    for oc in range(noc):
        C.mvi ^= 1
        blk, bkey = C.mvblk[C.mvi]
        P.dma("sp", blk, wv[:, :, oc * 128:(oc + 1) * 128], writes=[bkey])
        for kc in range(KC):
            P.op("pe", lambda e, blk=blk, kc=kc, oc=oc: e.matmul(
                ps[:, oc:oc + 1], lhsT=blk[:, kc, :], rhs=cond[:, kc:kc + 1], start=(kc == 0), stop=(kc == KC - 1)),
                reads=[bkey, cond], writes=[ps])
    P.op("dve", lambda e: e.tensor_tensor(out=out_tile[:, 0:noc], in0=ps[:, 0:noc], in1=bias_tile[:, 0:noc], op=ALU.add),
         reads=[ps, bias_tile], writes=[out_tile])


def mod_coeffs(P, normw, scale_ap, name, adakey):
    a = P.sb([128, KC], F32, name=name)
    P.op("dve", lambda e: e.scalar_tensor_tensor(out=a[:], in0=scale_ap, scalar=1.0, in1=normw[:], op0=ALU.add, op1=ALU.mult),
         reads=[normw, adakey], writes=[a])
    return a


def rms_rstd(P, C, ps, n_feat, T, rkeys):
    P.op("act", lambda e: e.activation(out=C.rstd[:, 0:T], in_=ps[:, 0:T], func=AF.Ln, scale=1.0 / n_feat, bias=C.epsb[:]),
         reads=[ps, C.epsb] + list(rkeys), writes=[C.rstd])
    P.op("act", lambda e: e.activation(out=C.rstd[:, 0:T], in_=C.rstd[:, 0:T], func=AF.Exp, scale=-0.5),
         reads=[C.rstd], writes=[C.rstd])


def rmsnorm_mod(P, C, xT, xkey, hT, hkey, acol, bcol, ntt, T=512, inplace=False, tts=None, xoff=None):
    ps = C.ps[7]
    for tt in (tts if tts is not None else range(ntt)):
        ts = slice(tt * T, (tt + 1) * T)
        hs = ts
        if xoff is not None:
            ts = slice(xoff, xoff + T)
        for kc in range(KC):
            sq = C.next_sq()
            P.op("act", lambda e, sq=sq, kc=kc, ts=ts: e.activation(out=sq[:, 0:T], in_=xT[:, kc, ts], func=AF.Square),
                 reads=[(xkey, kc, tt)], writes=[sq])
            P.op("pe", lambda e, sq=sq, kc=kc: e.matmul(ps[:, 0:T], lhsT=C.ones[:], rhs=sq[:, 0:T], start=(kc == 0), stop=(kc == KC - 1)),
                 reads=[sq, C.ones], writes=[ps])
        rms_rstd(P, C, ps, D, T, [])
        for kc in range(KC):
            tmp = C.next_tmp()
            P.op("dve", lambda e, tmp=tmp, kc=kc, ts=ts: e.scalar_tensor_tensor(
                out=tmp[:, 0:T], in0=xT[:, kc, ts], scalar=acol[:, kc:kc + 1], in1=C.rstd[:, 0:T], op0=ALU.mult, op1=ALU.mult),
                reads=[(xkey, kc, tt), acol, C.rstd], writes=[tmp])
            P.op("act", lambda e, tmp=tmp, kc=kc, hs=hs: e.activation(
                out=hT[:, kc, hs], in_=tmp[:, 0:T], func=AF.Identity, bias=bcol[:, kc:kc + 1], scale=1.0),
                reads=[tmp, bcol], writes=[(xkey, kc, tt) if inplace else (hkey, tt)])


def ffn(P, C, xT, xkey, hT, hkey, gcol, gkey, w_in_dram, w_out_dram, wA, wB, hid, ntt, T=512):
    nblk = FFN_H // 256
    wiv = w_in_dram.rearrange("(kc p) n -> p kc n", p=128)
    wov = w_out_dram.rearrange("(c p) n -> p c n", p=128)
    for j in range(nblk):
        wa = wA[j % 2]
        wb = wB[j % 2]
        hd = hid[j % 2]
        wa3 = wa[:].rearrange("p (kc n) -> p kc n", kc=KC)
        wb3 = wb[:].rearrange("p (c n) -> p c n", c=2)
        P.dma("pool", wa3[:, :, 0:256], wiv[:, :, j * 256:(j + 1) * 256], writes=[(wa, 0)])
        P.dma("pool", wa3[:, :, 256:512], wiv[:, :, FFN_H + j * 256:FFN_H + (j + 1) * 256], writes=[(wa, 1)])
        P.dma("pool", wb3, wov[:, 2 * j:2 * j + 2, :], writes=[wb])
        for tt in range(ntt):
            ts = slice(tt * T, (tt + 1) * T)
            for c2 in range(2):
                pg = C.ps[(2 * tt + c2) % 2]
                pu = C.ps[2 + (2 * tt + c2) % 2]
                for kc in range(KC):
                    P.op("pe", lambda e, pg=pg, kc=kc, c2=c2, ts=ts, wa3=wa3: e.matmul(
                        pg[:, 0:T], lhsT=wa3[:, kc, c2 * 128:(c2 + 1) * 128], rhs=hT[:, kc, ts], start=(kc == 0), stop=(kc == KC - 1)),
                        reads=[(wa, 0), (hkey, tt)], writes=[pg])
                for kc in range(KC):
                    P.op("pe", lambda e, pu=pu, kc=kc, c2=c2, ts=ts, wa3=wa3: e.matmul(
                        pu[:, 0:T], lhsT=wa3[:, kc, 256 + c2 * 128:256 + (c2 + 1) * 128], rhs=hT[:, kc, ts], start=(kc == 0), stop=(kc == KC - 1)),
                        reads=[(wa, 1), (hkey, tt)], writes=[pu])
                tmp = C.next_tmp()
                P.op("act", lambda e, tmp=tmp, pg=pg: e.activation(out=tmp[:, 0:T], in_=pg[:, 0:T], func=AF.Silu),
                     reads=[pg], writes=[tmp])
                P.op("dve", lambda e, tmp=tmp, pu=pu, c2=c2, ts=ts, hd=hd: e.tensor_tensor(
                    out=hd[:, c2, ts], in0=pu[:, 0:T], in1=tmp[:, 0:T], op=ALU.mult),
                    reads=[pu, tmp], writes=[(hd, tt)])
            for oc in range(KC):
                po = C.ps[4 + oc % 3]
                for c2 in range(2):
                    P.op("pe", lambda e, po=po, c2=c2, oc=oc, ts=ts, wb3=wb3, hd=hd: e.matmul(
                        po[:, 0:T], lhsT=wb3[:, c2, oc * 128:(oc + 1) * 128], rhs=hd[:, c2, ts], start=(c2 == 0), stop=(c2 == 1)),
                        reads=[wb, (hd, tt)], writes=[po])
                P.op("dve", lambda e, po=po, oc=oc, ts=ts: e.scalar_tensor_tensor(
                    out=xT[:, oc, ts], in0=po[:, 0:T], scalar=gcol[:, oc:oc + 1], in1=xT[:, oc, ts], op0=ALU.mult, op1=ALU.add),
                    reads=[po, (xkey, oc, tt), gkey], writes=[(xkey, oc, tt)])


TOK = 1024


def build_stage2(stop=99, final=False):
    nc = bass.Bass("TRN2", target_bir_lowering=False)
    dt = nc.dram_tensor
    xT_d = dt("xT", [D, TOK], F32, kind="ExternalInput").ap()
    oT_d = dt("oT", [4096, TOK], BF16, kind="ExternalInput").ap()
    c_d = dt("c", [128, KC], F32, kind="ExternalInput").ap()
    adaW_d = dt("adaW", [D, 4 * D], F32, kind="ExternalInput").ap()
    adaB_d = dt("adaB", [128, 64], F32, kind="ExternalInput").ap()
    kvaW_d = dt("kvaW", [D, 2 * D], F32, kind="ExternalInput").ap()
    kvaB_d = dt("kvaB", [128, 32], F32, kind="ExternalInput").ap()
    nffn_d = dt("nffn", [128, KC], F32, kind="ExternalInput").ap()
    nkv_d = dt("nkv", [128, KC], F32, kind="ExternalInput").ap()
    wout_d = dt("wout", [4096, D], F32, kind="ExternalInput").ap()
    win_d = dt("win", [D, 2 * FFN_H], F32, kind="ExternalInput").ap()
    wo2_d = dt("wo2", [FFN_H, D], F32, kind="ExternalInput").ap()
    if not final:
        kvw_d = dt("kvw", [D, 1040], F32, kind="ExternalInput").ap()
        knorm_d = dt("knorm", [128, 2], F32, kind="ExternalInput").ap()
        fb_d = dt("fb", [128, 16], F32, kind="ExternalInput").ap()
    x1T_d = dt("x1T", [D, TOK], F32, kind="ExternalOutput").ap()
    if not final:
        kT_d = dt("kT", [512, TOK], BF16, kind="ExternalOutput").ap()
        v_d = dt("v", [TOK, 512], BF16, kind="ExternalOutput").ap()
        lf_d = dt("lf", [TOK, 16], F32, kind="ExternalOutput").ap()

    P = Prog(nc)
    C = Ctx(P)
    xT = P.sb([128, KC, TOK], F32, name="xT_sb")
    hbuf = P.sb([128, KC * TOK], BF16, name="hbuf")
    hT = hbuf[:].rearrange("p (c t) -> p c t", c=KC)
    wA = [P.sb([128, 8192], BF16, name="wA%d" % i) for i in range(2)]
    wB = [P.sb([128, 4096], BF16, name="wB%d" % i) for i in range(2)]
    hid = [P.sb([128, 2, TOK], BF16, name="hid%d" % i) for i in range(2)]
    small = {}
    smalls = [("adaB", adaB_d, [128, 64]), ("kvaB", kvaB_d, [128, 32]), ("nffn", nffn_d, [128, KC]), ("nkv", nkv_d, [128, KC])]
    if not final:
        smalls += [("knorm", knorm_d, [128, 2]), ("fb", fb_d, [128, 16])]
    for nm, d_, shp in smalls:
        t = P.sb(shp, F32, name="sm_" + nm)
        P.dma("sp", t[:], d_, writes=[t])
        small[nm] = t
    xkeys = [("x", kc, tt) for kc in range(KC) for tt in range(2)]
    P.dma("sp", xT[:], xT_d.rearrange("(c p) t -> p c t", p=128), writes=xkeys)
    cond = load_cond(P, C, c_d)
    ada = P.sb([128, 64], F32, name="ada")
    matvec(P, C, adaW_d, 4 * D, cond, small["adaB"], ada, C.ps[6])
    kva = P.sb([128, 32], F32, name="kva")
    matvec(P, C, kvaW_d, 2 * D, cond, small["kvaB"], kva, C.ps[6])

    def finish():
        P.dma("sp", x1T_d.rearrange("(c p) t -> p c t", p=128), xT[:], reads=xkeys + [ada, kva], writes=["out_x1"])
        P.wait_all("sp", ["out_x1"])
        P.emit()
        return nc
    if stop == 1:
        return finish()
    ov = oT_d.rearrange("(c p) t -> p c t", p=128)
    wov = wout_d.rearrange("(c p) n -> p c n", p=128)
    o3 = hbuf[:].rearrange("p (c t) -> p c t", c=32)
    for tt in range(2):
        ts = slice(tt * 512, (tt + 1) * 512)
        P.dma("sp", o3, ov[:, :, ts], writes=[("h", 0), ("h", 1)])
        for ob in range(8):
            w = wA[ob % 2]
            w3 = w[:].rearrange("p (c n) -> p c n", c=32)
            P.dma("pool", w3, wov[:, :, ob * 256:(ob + 1) * 256], writes=[(w, 0), (w, 1)])
            for o2 in range(2):
                oc = ob * 2 + o2
                po = C.ps[4 + oc % 3]
                for kc in range(32):
                    P.op("pe", lambda e, po=po, kc=kc, o2=o2, w3=w3: e.matmul(
                        po[:], lhsT=w3[:, kc, o2 * 128:(o2 + 1) * 128], rhs=o3[:, kc, :], start=(kc == 0), stop=(kc == 31)),
                        reads=[(w, 0), (w, 1), ("h", 0), ("h", 1)], writes=[po])
                P.op("dve", lambda e, po=po, oc=oc, ts=ts: e.scalar_tensor_tensor(
                    out=xT[:, oc, ts], in0=po[:], scalar=ada[:, oc:oc + 1], in1=xT[:, oc, ts], op0=ALU.mult, op1=ALU.add),
                    reads=[po, ("x", oc, tt), ada], writes=[("x", oc, tt)])

    if stop == 2:
        return finish()
    a_f = mod_coeffs(P, small["nffn"], ada[:, 32:48], "a_f", ada)
    rmsnorm_mod(P, C, xT, "x", hT, "h", a_f, ada[:, 16:32], 2)
    if stop == 3:
        return finish()
    ffn(P, C, xT, "x", hT, "h", ada[:, 48:64], ada, win_d, wo2_d, wA, wB, hid, 2)
    if stop == 4:
        return finish()
    a_kv = mod_coeffs(P, small["nkv"], kva[:, 16:32], "a_kv", kva)
    if final:
        rmsnorm_mod(P, C, xT, "x", xT, "x", a_kv, kva[:, 0:16], 2, inplace=True)
        return finish()
    P.dma("sp", x1T_d.rearrange("(c p) t -> p c t", p=128), xT[:], reads=xkeys, writes=["out_x1"])

    rmsnorm_mod(P, C, xT, "x", hT, "h", a_kv, kva[:, 0:16], 2)
    kvv = kvw_d.rearrange("(kc p) n -> p kc n", p=128)
    wk3 = wA[0][:].rearrange("p (kc n) -> p kc n", kc=KC)
    wv3 = wA[1][:].rearrange("p (kc n) -> p kc n", kc=KC)
    wf = P.sb([128, KC, 16], BF16, name="wf")
    P.dma("pool", wk3, kvv[:, :, 0:512], writes=[(wA[0], 0), (wA[0], 1)])
    P.dma("pool", wv3, kvv[:, :, 512:1024], writes=[(wA[1], 0), (wA[1], 1)])
    P.dma("pool", wf[:], kvv[:, :, 1024:1040], writes=[wf])
    kraw = P.sb([128, 2, 512], F32, name="kraw")
    kout = P.sb([128, 4, TOK], BF16, name="kout")
    for tt in range(2):
        ts = slice(tt * 512, (tt + 1) * 512)
        for kh in range(2):
            for cc in range(2):
                c = kh * 2 + cc
                pk = C.ps[cc]
                for kc in range(KC):
                    P.op("pe", lambda e, pk=pk, kc=kc, c=c, ts=ts: e.matmul(
                        pk[:], lhsT=wk3[:, kc, c * 128:(c + 1) * 128], rhs=hT[:, kc, ts], start=(kc == 0), stop=(kc == KC - 1)),
                        reads=[(wA[0], 0), (wA[0], 1), ("h", tt)], writes=[pk])
                P.op("dve", lambda e, pk=pk, cc=cc: e.tensor_copy(out=kraw[:, cc, :], in_=pk[:]), reads=[pk], writes=[(kraw, cc)])
                sq = C.next_sq()
                P.op("act", lambda e, sq=sq, pk=pk: e.activation(out=sq[:], in_=pk[:], func=AF.Square), reads=[pk], writes=[sq])
                P.op("pe", lambda e, sq=sq, cc=cc: e.matmul(C.ps[7][:], lhsT=C.ones[:], rhs=sq[:], start=(cc == 0), stop=(cc == 1)),
                     reads=[sq, C.ones], writes=[C.ps[7]])
            rms_rstd(P, C, C.ps[7], 256, 512, [])
            for cc in range(2):
                c = kh * 2 + cc
                P.op("dve", lambda e, cc=cc, c=c, ts=ts: e.scalar_tensor_tensor(
                    out=kout[:, c, ts], in0=kraw[:, cc, :], scalar=small["knorm"][:, cc:cc + 1], in1=C.rstd[:], op0=ALU.mult, op1=ALU.mult),
                    reads=[(kraw, cc), small["knorm"], C.rstd], writes=[(kout, c)])
    P.dma("sp", kT_d.rearrange("(c p) t -> p c t", p=128), kout[:], reads=[(kout, c) for c in range(4)], writes=["out_k"])
    vout = P.sb([128, 8, 512], BF16, name="vout")
    lfo = P.sb([128, 8, 16], F32, name="lfo")
    lft = P.sb([128, 8, 16], F32, name="lft")
    for tb in range(8):
        tbs = slice(tb * 128, (tb + 1) * 128)
        pv = C.ps[tb % 2]
        for kc in range(KC):
            P.op("pe", lambda e, pv=pv, kc=kc, tbs=tbs: e.matmul(
                pv[:], lhsT=hT[:, kc, tbs], rhs=wv3[:, kc, :], start=(kc == 0), stop=(kc == KC - 1)),
                reads=[(wA[1], 0), (wA[1], 1), ("h", tb // 4)], writes=[pv])
        P.op("act", lambda e, pv=pv, tb=tb: e.activation(out=vout[:, tb, :], in_=pv[:], func=AF.Copy), reads=[pv], writes=[(vout, tb)])
        pf = C.ps[2 + tb % 2]
        for kc in range(KC):
            P.op("pe", lambda e, pf=pf, kc=kc, tbs=tbs: e.matmul(
                pf[:, 0:16], lhsT=hT[:, kc, tbs], rhs=wf[:, kc, :], start=(kc == 0), stop=(kc == KC - 1)),
                reads=[wf, ("h", tb // 4)], writes=[pf])
        P.op("dve", lambda e, pf=pf, tb=tb: e.tensor_tensor(out=lft[:, tb, :], in0=pf[:, 0:16], in1=small["fb"][:], op=ALU.add),
             reads=[pf, small["fb"]], writes=[(lft, tb)])
    P.op("act", lambda e: e.activation(out=lft[:], in_=lft[:], func=AF.Exp, scale=-1.0), reads=[(lft, tb) for tb in range(8)], writes=[lft])
    P.op("act", lambda e: e.activation(out=lft[:], in_=lft[:], func=AF.Ln, bias=C.oneb[:], scale=1.0), reads=[lft, C.oneb], writes=[lft])
    P.op("dve", lambda e: e.tensor_scalar(out=lfo[:], in0=lft[:], scalar1=-1.0, scalar2=None, op0=ALU.mult), reads=[lft], writes=[lfo])
    P.dma("sp", v_d.rearrange("(b p) n -> p b n", p=128), vout[:], reads=[(vout, tb) for tb in range(8)], writes=["out_v"])
    P.dma("sp", lf_d.rearrange("(b p) n -> p b n", p=128), lfo[:], reads=[lfo], writes=["out_lf"])
    P.wait_all("sp", ["out_x1", "out_k", "out_v", "out_lf"])
    P.emit()
    return nc


def core_blocks(i):
    return [i, 15 - i, 16 + i, 31 - i, 32 + i, 47 - i, 48 + i, 63 - i]


SLOT_EXT = [8, 16, 24, 32, 40, 48, 56, 64]


def stage3a_host_consts(i):
    import ml_dtypes
    blks = core_blocks(i)
    tri = (np.arange(128)[:, None] <= np.arange(128)[None, :]).astype(np.float32)
    masks = np.zeros((128, 8, 8, 128), np.float32)
    selx = np.zeros((8, 128, 64, 16), np.float32)
    for s_, gb in enumerate(blks):
        for m in range(8):
            jb = 8 * s_ + m
            if jb < gb:
                masks[:, s_, m, :] = 1.0
            elif jb == gb:
                masks[:, s_, m, :] = tri
        selx[s_, :, gb, :] = 1.0
    return masks.astype(ml_dtypes.bfloat16), selx


def build_stage3a():
    nc = bass.Bass("TRN2", target_bir_lowering=False)
    dt = nc.dram_tensor
    xT_d = dt("xT", [D, TOK], F32, kind="ExternalInput").ap()
    c_d = dt("c", [128, KC], F32, kind="ExternalInput").ap()
    adaW_d = dt("adaW", [D, 2 * D], F32, kind="ExternalInput").ap()
    adaB_d = dt("adaB", [128, 32], F32, kind="ExternalInput").ap()
    nmix_d = dt("nmix", [128, KC], F32, kind="ExternalInput").ap()
    win_d = dt("fwin", [D, 8192], F32, kind="ExternalInput").ap()
    qn_d = dt("qnorm", [128, 2], F32, kind="ExternalInput").ap()
    K_d = dt("KT", [2, 256, 8192], BF16, kind="ExternalInput").ap()
    V_d = dt("V", [8192, 512], BF16, kind="ExternalInput").ap()
    lf_d = dt("lf", [8192, 16], F32, kind="ExternalInput").ap()
    U_d = dt("U", [128, 128], F32, kind="ExternalInput").ap()
    mask_d = dt("masks", [128, 8, 8, 128], BF16, kind="ExternalInput").ap()
    selx_d = dt("selx", [8, 128, 64, 16], F32, kind="ExternalInput").ap()
    og_d = dt("ogT", [4096, TOK], BF16, kind="ExternalOutput").ap()
    Qs_d = dt("Qs", [4096, TOK], BF16, kind="Internal").ap()
    Gs_d = dt("Gs", [4096, TOK], BF16, kind="Internal").ap()

    P = Prog(nc)
    C = Ctx(P)
    bufK = P.sb([128, 8192], F32, name="bufK")
    bufV = P.sb([128, 16384], BF16, name="bufV")
    xt3 = bufK[:].rearrange("p (c t) -> p c t", c=KC)
    K3 = bufK[:].bitcast(BF16).rearrange("p (c t) -> p c t", c=2)
    hT = bufV[:].rearrange("p (c t) -> p c t", c=KC)
    V3 = bufV[:].rearrange("p (b d) -> p b d", b=64)
    small = {}
    for nm, d_, shp in (("adaB", adaB_d, [128, 32]), ("nmix", nmix_d, [128, KC]), ("qnorm", qn_d, [128, 2]), ("U", U_d, [128, 128])):
        t = P.sb(shp, F32, name="sm_" + nm)
        P.dma("sp", t[:], d_, writes=[t])
        small[nm] = t
    cond = load_cond(P, C, c_d)
    ada = P.sb([128, 32], F32, name="ada")
    matvec(P, C, adaW_d, 2 * D, cond, small["adaB"], ada, C.ps[6])
    a_m = mod_coeffs(P, small["nmix"], ada[:, 16:32], "a_m", ada)
    xv = xT_d.rearrange("(c p) t -> p c t", p=128)
    for tt in range(2):
        P.dma("sp", xt3, xv[:, :, tt * 512:(tt + 1) * 512], writes=[("xt", kc, t2) for kc in range(KC) for t2 in range(2)] + ["bufK"])
        rmsnorm_mod(P, C, xt3, "xt", hT, "h", a_m, ada[:, 0:16], 2, tts=[tt], xoff=0)

    wq = [P.sb([128, KC, 256], BF16, name="wq%d" % i) for i in range(2)]
    qraw = P.sb([128, 2, TOK], F32, name="qraw")
    qo = [P.sb([128, 2, TOK], BF16, name="qo%d" % i) for i in range(2)]
    wiv = win_d.rearrange("(kc p) n -> p kc n", p=128)
    Qsv = Qs_d.rearrange("(h c p) t -> h p c t", p=128, c=2)
    Gsv = Gs_d.rearrange("(h c p) t -> h p c t", p=128, c=2)
    for h in range(16):
        for isg in range(2):
            w = wq[isg]
            P.dma("pool", w[:], wiv[:, :, isg * 4096 + h * 256: isg * 4096 + (h + 1) * 256], writes=[w])
            out = qo[isg]
            for tt in range(2):
                ts = slice(tt * 512, (tt + 1) * 512)
                for cc in range(2):
                    pq = C.ps[(2 * tt + cc) % 4]
                    for kc in range(KC):
                        P.op("pe", lambda e, pq=pq, kc=kc, cc=cc, ts=ts, w=w: e.matmul(
                            pq[:], lhsT=w[:, kc, cc * 128:(cc + 1) * 128], rhs=hT[:, kc, ts], start=(kc == 0), stop=(kc == KC - 1)),
                            reads=[w, ("h", tt)], writes=[pq])
                    if isg:
                        P.op("act", lambda e, pq=pq, cc=cc, ts=ts, out=out: e.activation(out=out[:, cc, ts], in_=pq[:], func=AF.Sigmoid),
                             reads=[pq], writes=[(out, cc, tt)])
                    else:
                        P.op("dve", lambda e, pq=pq, cc=cc, ts=ts: e.tensor_copy(out=qraw[:, cc, ts], in_=pq[:]), reads=[pq], writes=[(qraw, cc, tt)])
                        sq = C.next_sq()
                        P.op("act", lambda e, sq=sq, pq=pq: e.activation(out=sq[:], in_=pq[:], func=AF.Square), reads=[pq], writes=[sq])
                        P.op("pe", lambda e, sq=sq, cc=cc: e.matmul(C.ps[7][:], lhsT=C.ones[:], rhs=sq[:], start=(cc == 0), stop=(cc == 1)),
                             reads=[sq, C.ones], writes=[C.ps[7]])
                if not isg:
                    rms_rstd(P, C, C.ps[7], 256, 512, [])
                    for cc in range(2):
                        P.op("dve", lambda e, cc=cc, ts=ts, out=out: e.scalar_tensor_tensor(
                            out=out[:, cc, ts], in0=qraw[:, cc, ts], scalar=small["qnorm"][:, cc:cc + 1], in1=C.rstd[:], op0=ALU.mult, op1=ALU.mult),
                            reads=[(qraw, cc, tt), small["qnorm"], C.rstd], writes=[(out, cc, tt)])
            P.dma("sp", (Gsv if isg else Qsv)[h], out[:], reads=[(out, cc, tt) for cc in range(2) for tt in range(2)],
                  writes=[("QG", isg, h)])

    lft = P.sb([128, 64, 16], F32, name="lft")
    csl = P.sb([128, 64, 16], F32, name="csl")
    tot = P.sb([128, 64, 16], F32, name="tot")
    incl = P.sb([128, 64, 16], F32, name="incl")
    P.dma("sp", lft[:], lf_d.rearrange("(b p) h -> p b h", p=128), writes=[lft])
    lf2 = lft[:].rearrange("p b h -> p (b h)")
    for half in range(2):
        hs = slice(half * 512, (half + 1) * 512)
        pa = C.ps[half]
        pb = C.ps[2 + half]
        P.op("pe", lambda e, pa=pa, hs=hs: e.matmul(pa[:], lhsT=small["U"][:], rhs=lf2[:, hs], start=True, stop=True),
             reads=[small["U"], lft], writes=[pa])
        P.op("pe", lambda e, pb=pb, hs=hs: e.matmul(pb[:], lhsT=C.ones[:], rhs=lf2[:, hs], start=True, stop=True),
             reads=[C.ones, lft], writes=[pb])
        P.op("dve", lambda e, pa=pa, hs=hs: e.tensor_copy(out=csl[:].rearrange("p b h -> p (b h)")[:, hs], in_=pa[:]), reads=[pa], writes=[csl])
        P.op("dve", lambda e, pb=pb, hs=hs: e.tensor_copy(out=tot[:].rearrange("p b h -> p (b h)")[:, hs], in_=pb[:]), reads=[pb], writes=[tot])
    P.op("dve", lambda e: e.tensor_copy(out=incl[:, 0, :], in_=tot[:, 0, :]), reads=[tot], writes=[incl])
    for b in range(1, 64):
        P.op("dve", lambda e, b=b: e.tensor_tensor(out=incl[:, b, :], in0=incl[:, b - 1, :], in1=tot[:, b, :], op=ALU.add),
             reads=[incl, tot], writes=[incl])
    P.op("dve", lambda e: e.tensor_tensor(out=csl[:], in0=csl[:], in1=incl[:], op=ALU.add), reads=[csl, incl], writes=[csl])
    P.op("dve", lambda e: e.tensor_tensor(out=lft[:], in0=tot[:], in1=csl[:], op=ALU.subtract), reads=[csl, tot], writes=[lft])
    Fneg = lft
    fref = P.sb([128, 8, 16], F32, name="fref")
    for s_ in range(8):
        P.dma("sp", tot[:], selx_d[s_], writes=[tot])
        P.op("dve", lambda e: e.tensor_tensor(out=tot[:], in0=tot[:], in1=incl[:], op=ALU.mult), reads=[tot, incl], writes=[tot])
        P.op("dve", lambda e, s_=s_: e.tensor_reduce(out=fref[:, s_, :], in_=tot[:].rearrange("p b h -> p h b"), axis=AX.X, op=ALU.add),
             reads=[tot], writes=[fref])

    masks = P.sb([128, 8, 8, 128], BF16, name="masks_sb")
    P.dma("sp", masks[:], mask_d, writes=[masks])
    onesb = P.sb([128, 128], BF16, name="onesb")
    P.op("pool", lambda e: e.memset(onesb[:], 1.0), writes=[onesb])
    Qt = P.sb([128, 4, 2, TOK], BF16, name="Qt")
    Gt = P.sb([128, 4, 2, TOK], BF16, name="Gt")
    bias = P.sb([128, 64, 4], F32, name="bias")
    pT = [P.sb([128, 4, 128], BF16, name="pT%d" % i) for i in range(3)]
    rec = P.sb([128, 512], F32, name="rec")
    pti = 0
    setsel = 0
    Kv = K_d.rearrange("k (c p) t -> k p c t", p=128)
    Vv = V_d.rearrange("(b p) n -> p b n", p=128)
    ogv = og_d.rearrange("(h c p) t -> p h c t", p=128, c=2)
    for kvh in range(2):
        P.dma("sp", K3, Kv[kvh], writes=["bufK"] + [("xt", kc, tt) for kc in range(KC) for tt in range(2)])
        P.dma("sp", V3, Vv[:, :, kvh * 256:(kvh + 1) * 256], writes=[("h", 0), ("h", 1)])
        for hg in range(2):
            h0 = kvh * 8 + hg * 4
            for j in range(4):
                P.dma("sp", Qt[:, j], Qsv[h0 + j], reads=[("QG", 0, h0 + j)], writes=[Qt])
                P.dma("sp", Gt[:, j], Gsv[h0 + j], reads=[("QG", 1, h0 + j)], writes=[Gt])
            for s_ in range(8):
                qs = slice(s_ * 128, (s_ + 1) * 128)
                ext = SLOT_EXT[s_]
                for j in range(4):
                    P.op("dve", lambda e, j=j, s_=s_, ext=ext, h0=h0: e.tensor_scalar(
                        out=bias[:, 0:ext, j], in0=Fneg[:, 0:ext, h0 + j], scalar1=fref[:, s_, h0 + j:h0 + j + 1], scalar2=0.0,
                        op0=ALU.add, op1=ALU.min), reads=[Fneg, fref], writes=[bias])
                setsel ^= 1
                acc = [C.ps[2 + 3 * setsel + k] for k in range(3)]
                for jb in range(ext):
                    ks = slice(jb * 128, (jb + 1) * 128)
                    pS = C.ps[jb % 2]
                    for cc in range(2):
                        P.op("pe", lambda e, pS=pS, cc=cc, ks=ks, qs=qs: e.matmul(
                            pS[:].rearrange("p (j t) -> p j t", j=4), lhsT=K3[:, cc, ks], rhs=Qt[:, :, cc, qs], start=(cc == 0), stop=(cc == 1)),
                            reads=["bufK", Qt], writes=[pS])
                    pti = (pti + 1) % 3
                    pt = pT[pti]
                    for j in range(4):
                        P.op("act", lambda e, pS=pS, pt=pt, j=j, jb=jb: e.activation(
                            out=pt[:, j, :], in_=pS[:, j * 128:(j + 1) * 128], func=AF.Exp, scale=1.0 / 16.0, bias=bias[:, jb, j:j + 1]),
                            reads=[pS, bias], writes=[(pt, j)])
                    m = jb - 8 * s_
                    if m >= 0:
                        for j in range(4):
                            P.op("pool", lambda e, pt=pt, j=j, s_=s_, m=m: e.tensor_tensor(
                                out=pt[:, j, :], in0=pt[:, j, :], in1=masks[:, s_, m, :], op=ALU.mult),
                                reads=[(pt, j), masks], writes=[(pt, j)])
                    pt2 = pt[:].rearrange("p j t -> p (j t)")
                    for k in range(3):
                        lhs = onesb[:] if k == 2 else V3[:, jb, k * 128:(k + 1) * 128]
                        P.op("pe", lambda e, k=k, lhs=lhs, pt2=pt2, jb=jb, ext=ext, acc=acc: e.matmul(
                            acc[k][:], lhsT=lhs, rhs=pt2, start=(jb == 0), stop=(jb == ext - 1)),
                            reads=[(pt, 0), (pt, 1), (pt, 2), (pt, 3), ("h", 0), ("h", 1), onesb], writes=[acc[k]])
                P.op("dve", lambda e, acc=acc: e.reciprocal(out=rec[:], in_=acc[2][:]), reads=[acc[2]], writes=[rec])
                for k in range(2):
                    tmp = C.next_tmp()
                    P.op("dve", lambda e, tmp=tmp, k=k, acc=acc: e.tensor_tensor(out=tmp[:], in0=acc[k][:], in1=rec[:], op=ALU.mult),
                         reads=[acc[k], rec], writes=[tmp])
                    P.op("pool", lambda e, tmp=tmp, k=k, qs=qs: e.tensor_tensor(
                        out=Gt[:, :, k, qs], in0=tmp[:].rearrange("p (j t) -> p j t", j=4), in1=Gt[:, :, k, qs], op=ALU.mult),
                        reads=[tmp, Gt], writes=[Gt])
            P.dma("sp", ogv[:, h0:h0 + 4], Gt[:], reads=[Gt], writes=[("og", h0)])
    P.wait_all("sp", [("og", h0) for h0 in (0, 4, 8, 12)])
    P.emit()
    return nc


SEQ = 8192


def stage1_host_consts():
    import ml_dtypes
    I64 = np.eye(64, dtype=np.float32)
    U64 = (np.arange(64)[:, None] <= np.arange(64)[None, :]).astype(np.float32)
    LS = (np.arange(64)[None, :] < np.arange(64)[:, None]).astype(np.float32)
    return dict(I4=np.ascontiguousarray(np.tile(I64, (1, 4))), U64=U64,
                UM4=np.ascontiguousarray(np.tile(U64, (1, 4))), LS4=np.ascontiguousarray(np.tile(LS, (1, 4))),
                identb=np.eye(128, dtype=np.float32).astype(ml_dtypes.bfloat16))


def build_stage1(NT=16, stop=99):
    nc = bass.Bass("TRN2", target_bir_lowering=False)
    dt = nc.dram_tensor
    T = NT * 512
    xT_d = dt("xT", [D, T], F32, kind="ExternalInput").ap()
    c_d = dt("c", [128, KC], F32, kind="ExternalInput").ap()
    adaW_d = dt("adaW", [D, 2 * D], F32, kind="ExternalInput").ap()
    adaB_d = dt("adaB", [128, 32], F32, kind="ExternalInput").ap()
    nmix_d = dt("nmix", [128, KC], F32, kind="ExternalInput").ap()
    w_d = dt("wqkvz", [D, 1536], F32, kind="ExternalInput").ap()
    wba_d = dt("wba", [D, 8], F32, kind="ExternalInput").ap()
    convw_d = dt("convw", [128, 8, 4], F32, kind="ExternalInput").ap()
    alog_d = dt("alog", [64, 8, 4], F32, kind="ExternalInput").ap()
    dtb_d = dt("dtb", [64, 8, 4], F32, kind="ExternalInput").ap()
    gn_d = dt("gnorm", [64, 512], F32, kind="ExternalInput").ap()
    I4_d = dt("I4", [64, 256], F32, kind="ExternalInput").ap()
    U64_d = dt("U64", [64, 64], F32, kind="ExternalInput").ap()
    UM4_d = dt("UM4", [64, 256], F32, kind="ExternalInput").ap()
    LS4_d = dt("LS4", [64, 256], F32, kind="ExternalInput").ap()
    idb_d = dt("identb", [128, 128], BF16, kind="ExternalInput").ap()
    o_d = dt("o", [T, 512], BF16, kind="ExternalOutput").ap()

    P = Prog(nc)
    xbuf = P.sb([128, KC * 512], F32, name="xbuf")
    xt3 = xbuf[:].rearrange("p (c t) -> p c t", c=KC)
    mv = [(xbuf[:, i * 2048:(i + 1) * 2048].rearrange("p (c n) -> p c n", c=KC), ("mvb", i)) for i in range(2)]
    C = Ctx(P, mvblk=mv)
    ps = C.ps
    ptb = ps[7][:].bitcast(BF16)

    def ld(name, d_, shp, dtype=F32, eng="sp"):
        t = P.sb(shp, dtype, name="c_" + name)
        P.dma(eng, t[:], d_, writes=[t])
        return t
    adaB = ld("adaB", adaB_d, [128, 32])
    nmix = ld("nmix", nmix_d, [128, KC])
    convw = ld("convw", convw_d, [128, 8, 4])
    alog = ld("alog", alog_d, [64, 8, 4])
    dtb = ld("dtb", dtb_d, [64, 8, 4])
    gn = ld("gn", gn_d, [64, 512])
    I4 = ld("I4", I4_d, [64, 256])
    U64 = ld("U64", U64_d, [64, 64])
    UM4 = ld("UM4", UM4_d, [64, 256])
    LS4 = ld("LS4", LS4_d, [64, 256])
    identb = ld("identb", idb_d, [128, 128], BF16)
    I64 = I4[:, 0:64]
    wqk = P.sb([128, KC, 1536], BF16, name="wqkvz_sb")
    wba = P.sb([128, KC, 8], BF16, name="wba_sb")
    wv_ = w_d.rearrange("(kc p) n -> p kc n", p=128)
    for j in range(3):
        P.dma("pool", wqk[:, :, j * 512:(j + 1) * 512], wv_[:, :, j * 512:(j + 1) * 512], writes=[(wqk, j)])
    P.dma("pool", wba[:], wba_d.rearrange("(kc p) n -> p kc n", p=128), writes=[wba])
    wkeys = [(wqk, j) for j in range(3)]

    cond = load_cond(P, C, c_d)
    ada = P.sb([128, 32], F32, name="ada")
    matvec(P, C, adaW_d, 2 * D, cond, adaB, ada, ps[6])
    a_m = mod_coeffs(P, nmix, ada[:, 16:32], "a_m", ada)
    eA = P.sb([64, 8, 4], F32, name="eA")
    P.op("act", lambda e: e.activation(out=eA[:], in_=alog[:], func=AF.Exp), reads=[alog], writes=[eA])

    hT = P.sb([128, KC, 512], BF16, name="hT")
    pre = P.sb([128, 8, 515], F32, name="pre")
    P.op("pool", lambda e: e.memset(pre[:], 0.0), writes=[(pre, n) for n in range(8)])
    qkT = [P.sb([128, 512], BF16, name="qkT%d" % n) for n in range(4)]
    vT = [P.sb([128, 512], BF16, name="vT%d" % n) for n in range(4)]
    actf = P.sb([128, 512], F32, name="actf")
    gz = P.sb([64, 8, 512], F32, name="gz")
    Sf = P.sb([128, 4, 128], F32, name="Sf")
    Sb = P.sb([128, 4, 128], BF16, name="Sb")
    P.op("pool", lambda e: e.memset(Sf[:], 0.0), writes=[Sf])
    P.op("pool", lambda e: e.memset(Sb[:], 0.0), writes=[Sb])
    sm = lambda name, dtype=F32: P.sb([64, 8, 4], dtype, name=name)
    beta, nbeta, xg, graw, gcum, eg, kds, bw = [sm(n) for n in ("beta", "nbeta", "xg", "graw", "gcum", "eg", "kds", "bw")]
    egl = P.sb([128, 32], F32, name="egl")
    t64 = lambda name, dtype=F32: P.sb([64, 4, 64], dtype, name=name)
    Dg, d1, d2, decT, dec, Y0 = [t64(n) for n in ("Dg", "d1", "d2", "decT", "dec", "Y0")]
    XX = [t64("XX%d" % i) for i in range(2)]
    YY = [t64("YY%d" % i) for i in range(2)]
    RR = [t64("RR%d" % i) for i in range(2)]
    TT = t64("TT", BF16)
    attnT = t64("attnT", BF16)
    kdec = P.sb([64, 4, 128], BF16, name="kdec")
    kbw = P.sb([64, 4, 128], BF16, name="kbw")
    bv = P.sb([64, 4, 128], BF16, name="bv")
    nwT = P.sb([128, 4, 64], BF16, name="nwT")
    vn = P.sb([64, 512], BF16, name="vn")
    ostmp = [P.sb([64, 128], F32, name="ostmp%d" % i) for i in range(2)]
    Oall = P.sb([64, 32, 128], F32, name="Oall")
    Osq = P.sb([64, 4, 128], F32, name="Osq")
    ss = P.sb([64, 32], F32, name="ss")
    ot = P.sb([64, 8, 512], BF16, name="ot")
    xv = xT_d.rearrange("(c p) t -> p c t", p=128)
    ov = o_d.rearrange("(c p) n -> p c n", p=64)
    pba = ps[2][0:64, 0:64]
    pgc = ps[2][0:64, 64:96]
    pgl = ps[2][:, 96:128]
    pX0 = ps[2][0:64, 128:384]
    pKQ = ps[3][0:64, 0:256]
    pG = ps[3][0:64, 256:512]
    pX = ps[4][0:64, 0:256]
    pY = ps[4][0:64, 256:512]
    pR = ps[5][0:64, 0:256]
    pwT = ps[5][:, 256:512]
    pVN = ps[6][0:64, :]
    pO = ps[0][0:64, :]
    pSU = ps[1]
    out_keys = []

    def finish_now(extra):
        P.dma("sp", ov[:, 0:8, :], ot[:], reads=list(extra) + [(ot, c) for c in range(8)], writes=[("out", 0)])
        P.wait_all("sp", [("out", 0)])
        P.emit()
        return nc
    if stop == 1:
        return finish_now([ada, a_m, eA, wba, Sf, Sb] + wkeys)

    for ti in range(NT):
        tsl = slice(ti * 512, (ti + 1) * 512)
        P.dma("sp", xt3, xv[:, :, tsl], writes=[("xt", kc, 0) for kc in range(KC)] + [("mvb", 0), ("mvb", 1)])
        rmsnorm_mod(P, C, xt3, "xt", hT, "h", a_m, ada[:, 0:16], 1, tts=[0])
        if stop == 2:
            return finish_now([("h", 0)])
        for c in range(8):
            for kc in range(KC):
                P.op("pe", lambda e, c=c, kc=kc: e.matmul(pba[:, c * 8:(c + 1) * 8], lhsT=hT[:, kc, c * 64:(c + 1) * 64], rhs=wba[:, kc, :],
                                                      start=(kc == 0), stop=(kc == KC - 1)), reads=[("h", 0), wba], writes=[ps[2]])
        pba3 = pba.rearrange("p (c n) -> p c n", n=8)
        P.op("act", lambda e: e.activation(out=beta[:], in_=pba3[:, :, 0:4], func=AF.Sigmoid), reads=[ps[2]], writes=[beta])
        P.op("dve", lambda e: e.tensor_tensor(out=xg[:], in0=pba3[:, :, 4:8], in1=dtb[:], op=ALU.add), reads=[ps[2], dtb], writes=[xg])
        if stop == 30:
            return finish_now([beta, xg])
        P.op("dve", lambda e: e.tensor_scalar(out=nbeta[:], in0=beta[:], scalar1=-1.0, scalar2=None, op0=ALU.mult), reads=[beta], writes=[nbeta])
        P.op("act", lambda e: e.activation(out=xg[:], in_=xg[:], func=AF.Exp), reads=[xg], writes=[xg])
        P.op("act", lambda e: e.activation(out=xg[:], in_=xg[:], func=AF.Ln, bias=C.oneb[0:64, :], scale=1.0), reads=[xg, C.oneb], writes=[xg])
        P.op("dve", lambda e: e.scalar_tensor_tensor(out=graw[:], in0=xg[:], scalar=-1.0, in1=eA[:], op0=ALU.mult, op1=ALU.mult),
             reads=[xg, eA], writes=[graw])
        if stop == 31:
            return finish_now([beta, nbeta, graw])
        g2 = graw[:].rearrange("p c h -> p (c h)")
        P.op("pe", lambda e: e.matmul(pgc, lhsT=U64[:], rhs=g2, start=True, stop=True), reads=[U64, graw], writes=[ps[2]])
        P.op("pe", lambda e: e.matmul(pgl, lhsT=C.ones[0:64, :], rhs=g2, start=True, stop=True), reads=[C.ones, graw], writes=[ps[2]])
        gc2 = gcum[:].rearrange("p c h -> p (c h)")
        P.op("dve", lambda e: e.tensor_copy(out=gc2, in_=pgc), reads=[ps[2]], writes=[gcum])
        P.op("act", lambda e: e.activation(out=eg[:].rearrange("p c h -> p (c h)"), in_=pgc, func=AF.Exp), reads=[ps[2]], writes=[eg])
        P.op("act", lambda e: e.activation(out=egl[:], in_=pgl, func=AF.Exp), reads=[ps[2]], writes=[egl])
        if stop == 32:
            return finish_now([beta, nbeta, graw, gcum, eg, egl])
        kd2 = kds[:].rearrange("p c h -> p (c h)")
        P.op("dve", lambda e: e.tensor_tensor(out=kd2, in0=pgl[0:64, :], in1=gc2, op=ALU.subtract), reads=[ps[2], gcum], writes=[kds])
        if stop == 33:
            return finish_now([beta, nbeta, graw, gcum, eg, egl, kds])
        P.op("act", lambda e: e.activation(out=kd2, in_=kd2, func=AF.Exp), reads=[kds], writes=[kds])
        if stop == 34:
            return finish_now([beta, nbeta, graw, gcum, eg, egl, kds])
        P.op("dve", lambda e: e.tensor_tensor(out=bw[:], in0=beta[:], in1=eg[:], op=ALU.mult), reads=[beta, eg], writes=[bw])
        if stop == 3:
            return finish_now([beta, nbeta, gcum, eg, egl, kds, bw])
        for n in range(8):
            pp = ps[n % 2]
            for kc in range(KC):
                P.op("pe", lambda e, pp=pp, kc=kc, n=n: e.matmul(pp[:], lhsT=wqk[:, kc, n * 128:(n + 1) * 128], rhs=hT[:, kc, :],
                                                            start=(kc == 0), stop=(kc == KC - 1)), reads=[("h", 0)] + wkeys, writes=[pp])
            P.op("pool", lambda e, n=n: e.tensor_copy(out=pre[:, n, 0:3], in_=pre[:, n, 512:515]), reads=[(pre, n)], writes=[(pre, n)])
            P.op("act", lambda e, pp=pp, n=n: e.activation(out=pre[:, n, 3:515], in_=pp[:], func=AF.Copy), reads=[pp], writes=[(pre, n)])
            acc = C.next_tmp()
            P.op("dve", lambda e, acc=acc, n=n: e.tensor_scalar(out=acc[:], in0=pre[:, n, 0:512], scalar1=convw[:, n, 0:1], scalar2=None, op0=ALU.mult),
                 reads=[(pre, n), convw], writes=[acc])
            for i in range(1, 4):
                P.op("dve", lambda e, acc=acc, n=n, i=i: e.scalar_tensor_tensor(out=acc[:], in0=pre[:, n, i:i + 512], scalar=convw[:, n, i:i + 1],
                                                                             in1=acc[:], op0=ALU.mult, op1=ALU.add),
                     reads=[(pre, n), convw, acc], writes=[acc])
            if n >= 4:
                P.op("act", lambda e, acc=acc, n=n: e.activation(out=vT[n - 4][:], in_=acc[:], func=AF.Silu), reads=[acc], writes=[vT[n - 4]])
            else:
                P.op("act", lambda e, acc=acc: e.activation(out=actf[:], in_=acc[:], func=AF.Silu), reads=[acc], writes=[actf])
                sq = C.next_sq()
                P.op("act", lambda e, sq=sq: e.activation(out=sq[:], in_=actf[:], func=AF.Square), reads=[actf], writes=[sq])
                P.op("pe", lambda e, sq=sq: e.matmul(ps[6][:], lhsT=C.ones[:], rhs=sq[:], start=True, stop=True), reads=[sq, C.ones], writes=[ps[6]])
                P.op("act", lambda e: e.activation(out=C.rstd[:], in_=ps[6][:], func=AF.Ln, scale=1.0, bias=C.epsb[:]),
                     reads=[ps[6], C.epsb], writes=[C.rstd])
                P.op("act", lambda e: e.activation(out=C.rstd[:], in_=C.rstd[:], func=AF.Exp, scale=-0.5), reads=[C.rstd], writes=[C.rstd])
                sc = (128.0 ** -0.5) if n < 2 else 1.0
                P.op("dve", lambda e, n=n, sc=sc: e.scalar_tensor_tensor(out=qkT[n][:], in0=actf[:], scalar=sc, in1=C.rstd[:], op0=ALU.mult, op1=ALU.mult),
                     reads=[actf, C.rstd], writes=[qkT[n]])
        if stop == 4:
            return finish_now(qkT + vT)
        for c in range(8):
            pz = ps[c % 2]
            for kc in range(KC):
                P.op("pe", lambda e, pz=pz, kc=kc, c=c: e.matmul(pz[0:64, :], lhsT=hT[:, kc, c * 64:(c + 1) * 64], rhs=wqk[:, kc, 1024:1536],
                                                            start=(kc == 0), stop=(kc == KC - 1)), reads=[("h", 0)] + wkeys, writes=[pz])
            zt = C.next_tmp()
            P.op("act", lambda e, pz=pz, zt=zt: e.activation(out=zt[0:64, :], in_=pz[0:64, :], func=AF.Silu), reads=[pz], writes=[zt])
            P.op("pool", lambda e, zt=zt, c=c: e.tensor_tensor(out=gz[:, c, :], in0=zt[0:64, :], in1=gn[:], op=ALU.mult), reads=[zt, gn], writes=[(gz, c)])

        if stop == 5:
            return finish_now([(gz, c) for c in range(8)])
        for c in range(8):
            if stop == 6 + c:
                return finish_now([Oall, Sb, Sf])
            cs = slice(c * 64, (c + 1) * 64)
            for h in range(4):
                P.op("dve", lambda e, h=h, c=c: e.tensor_scalar(out=Dg[:, h, :], in0=I64, scalar1=gcum[:, c, h:h + 1], scalar2=None, op0=ALU.mult),
                     reads=[I4, gcum], writes=[Dg])
            P.op("pe", lambda e: e.matmul(pG, lhsT=C.ones[0:64, 0:64], rhs=Dg[:].rearrange("p h i -> p (h i)"), start=True, stop=True),
                 reads=[C.ones, Dg], writes=[ps[3]])
            pG3 = pG.rearrange("p (h i) -> p h i", h=4)
            for h in range(4):
                P.op("dve", lambda e, h=h, c=c: e.tensor_scalar(out=d1[:, h, :], in0=pG3[:, h, :], scalar1=gcum[:, c, h:h + 1], scalar2=0.0,
                                                           op0=ALU.subtract, op1=ALU.min), reads=[ps[3], gcum], writes=[d1])
                P.op("dve", lambda e, h=h, c=c: e.tensor_scalar(out=d2[:, h, :], in0=pG3[:, h, :], scalar1=gcum[:, c, h:h + 1], scalar2=0.0,
                                                           op0=ALU.subtract, op1=ALU.max), reads=[ps[3], gcum], writes=[d2])
            P.op("act", lambda e: e.activation(out=decT[:], in_=d1[:], func=AF.Exp), reads=[d1], writes=[decT])
            P.op("act", lambda e: e.activation(out=dec[:], in_=d2[:], func=AF.Exp, scale=-1.0), reads=[d2], writes=[dec])
            P.op("pool", lambda e: e.tensor_tensor(out=decT[:].rearrange("p h i -> p (h i)"), in0=decT[:].rearrange("p h i -> p (h i)"), in1=UM4[:], op=ALU.mult),
                 reads=[decT, UM4], writes=[decT])
            P.op("pool", lambda e: e.tensor_tensor(out=dec[:].rearrange("p h i -> p (h i)"), in0=dec[:].rearrange("p h i -> p (h i)"), in1=LS4[:], op=ALU.mult),
                 reads=[dec, LS4], writes=[dec])
            for hq in range(2):
                P.op("pe", lambda e, hq=hq, cs=cs: e.transpose(ptb[0:64, hq * 128:(hq + 1) * 128], qkT[2 + hq][:, cs], identb[:]),
                     reads=[qkT[2 + hq], identb], writes=[ps[7]])
            for hv in range(4):
                P.op("pe", lambda e, hv=hv, cs=cs: e.transpose(ptb[0:64, 256 + hv * 128:256 + (hv + 1) * 128], vT[hv][:, cs], identb[:]),
                     reads=[vT[hv], identb], writes=[ps[7]])
            for hv in range(4):
                hq = hv // 2
                P.op("act", lambda e, hv=hv, hq=hq, c=c: e.activation(out=kdec[:, hv, :], in_=ptb[0:64, hq * 128:(hq + 1) * 128], func=AF.Copy,
                                                                 scale=kds[:, c, hv:hv + 1]), reads=[ps[7], kds], writes=[kdec])
                P.op("dve", lambda e, hv=hv, hq=hq, c=c: e.tensor_scalar(out=kbw[:, hv, :], in0=ptb[0:64, hq * 128:(hq + 1) * 128], scalar1=bw[:, c, hv:hv + 1],
                                                                    scalar2=None, op0=ALU.mult), reads=[ps[7], bw], writes=[kbw])
                P.op("act", lambda e, hv=hv, c=c: e.activation(out=bv[:, hv, :], in_=ptb[0:64, 256 + hv * 128:256 + (hv + 1) * 128], func=AF.Copy,
                                                          scale=beta[:, c, hv:hv + 1]), reads=[ps[7], beta], writes=[bv])
            for hq in range(2):
                P.op("pe", lambda e, hq=hq, cs=cs: e.matmul(pKQ[:, hq * 64:(hq + 1) * 64], lhsT=qkT[2 + hq][:, cs], rhs=qkT[2 + hq][:, cs], start=True, stop=True),
                     reads=[qkT[2 + hq]], writes=[ps[3]])
                P.op("pe", lambda e, hq=hq, cs=cs: e.matmul(pKQ[:, 128 + hq * 64:128 + (hq + 1) * 64], lhsT=qkT[2 + hq][:, cs], rhs=qkT[hq][:, cs], start=True, stop=True),
                     reads=[qkT[2 + hq], qkT[hq]], writes=[ps[3]])
            for hv in range(4):
                hq = hv // 2
                P.op("dve", lambda e, hv=hv, hq=hq, c=c: e.scalar_tensor_tensor(out=Y0[:, hv, :], in0=pKQ[:, hq * 64:(hq + 1) * 64], scalar=nbeta[:, c, hv:hv + 1],
                                                                           in1=dec[:, hv, :], op0=ALU.mult, op1=ALU.mult),
                     reads=[ps[3], nbeta, dec], writes=[Y0])
                P.op("dve", lambda e, hv=hv, hq=hq: e.tensor_tensor(out=attnT[:, hv, :], in0=pKQ[:, 128 + hq * 64:128 + (hq + 1) * 64], in1=decT[:, hv, :], op=ALU.mult),
                     reads=[ps[3], decT], writes=[attnT])
            for hv in range(4):
                P.op("pe", lambda e, hv=hv: e.transpose(pX0[:, hv * 64:(hv + 1) * 64], Y0[:, hv, :], I64), reads=[Y0, I4], writes=[ps[2]])
            Xc, Yc, Rc = XX[0], Y0, RR[0]
            X2 = lambda t: t[:].rearrange("p h i -> p (h i)")
            P.op("act", lambda e, Xc=Xc: e.activation(out=X2(Xc), in_=pX0, func=AF.Copy), reads=[ps[2]], writes=[Xc])
            P.op("dve", lambda e, Rc=Rc: e.tensor_tensor(out=X2(Rc), in0=pX0, in1=I4[:], op=ALU.add), reads=[ps[2], I4], writes=[Rc])
            for lvl in range(5):
                Yn = YY[lvl % 2]
                Xn = XX[(lvl + 1) % 2]
                Rn = RR[(lvl + 1) % 2]
                for hv in range(4):
                    P.op("pe", lambda e, hv=hv, Xc=Xc, Yc=Yc: e.matmul(pY[:, hv * 64:(hv + 1) * 64], lhsT=Xc[:, hv, :], rhs=Yc[:, hv, :], start=True, stop=True),
                         reads=[Xc, Yc], writes=[ps[4]])
                if lvl < 4:
                    for hv in range(4):
                        P.op("pe", lambda e, hv=hv, Xc=Xc, Yc=Yc: e.matmul(pX[:, hv * 64:(hv + 1) * 64], lhsT=Yc[:, hv, :], rhs=Xc[:, hv, :], start=True, stop=True),
                             reads=[Xc, Yc], writes=[ps[4]])
                P.op("act", lambda e, Yn=Yn: e.activation(out=X2(Yn), in_=pY, func=AF.Copy), reads=[ps[4]], writes=[Yn])
                if lvl < 4:
                    P.op("dve", lambda e, Xn=Xn: e.tensor_copy(out=X2(Xn), in_=pX), reads=[ps[4]], writes=[Xn])
                for hv in range(4):
                    P.op("pe", lambda e, hv=hv, Rc=Rc: e.matmul(pR[:, hv * 64:(hv + 1) * 64], lhsT=I64, rhs=Rc[:, hv, :], start=True, stop=False),
                         reads=[I4, Rc], writes=[ps[5]])
                    P.op("pe", lambda e, hv=hv, Rc=Rc, Yn=Yn: e.matmul(pR[:, hv * 64:(hv + 1) * 64], lhsT=Yn[:, hv, :], rhs=Rc[:, hv, :], start=False, stop=True),
                         reads=[Yn, Rc], writes=[ps[5]])
                if lvl < 4:
                    P.op("dve", lambda e, Rn=Rn: e.tensor_copy(out=X2(Rn), in_=pR), reads=[ps[5]], writes=[Rn])
                else:
                    P.op("dve", lambda e: e.tensor_copy(out=X2(TT), in_=pR), reads=[ps[5]], writes=[TT])
                Xc, Yc, Rc = Xn, Yn, Rn
            for hv in range(4):
                P.op("pe", lambda e, hv=hv: e.matmul(pwT[:, hv * 64:(hv + 1) * 64], lhsT=kbw[:, hv, :], rhs=TT[:, hv, :], start=True, stop=True),
                     reads=[kbw, TT], writes=[ps[5]])
            P.op("act", lambda e: e.activation(out=nwT[:].rearrange("p h i -> p (h i)"), in_=pwT, func=AF.Copy, scale=-1.0), reads=[ps[5]], writes=[nwT])
            for hv in range(4):
                P.op("pe", lambda e, hv=hv: e.matmul(pVN[:, hv * 128:(hv + 1) * 128], lhsT=TT[:, hv, :], rhs=bv[:, hv, :], start=True, stop=False),
                     reads=[TT, bv], writes=[ps[6]])
                P.op("pe", lambda e, hv=hv: e.matmul(pVN[:, hv * 128:(hv + 1) * 128], lhsT=nwT[:, hv, :], rhs=Sb[:, hv, :], start=False, stop=True),
                     reads=[nwT, Sb], writes=[ps[6]])
            P.op("dve", lambda e: e.tensor_copy(out=vn[:], in_=pVN), reads=[ps[6]], writes=[vn])
            for pr in range(2):
                for k in range(2):
                    hv = pr * 2 + k
                    hq = hv // 2
                    P.op("pe", lambda e, hv=hv, hq=hq, k=k, cs=cs: e.matmul(pO[:, k * 128:(k + 1) * 128], lhsT=qkT[hq][:, cs], rhs=Sb[:, hv, :], start=True, stop=True),
                         reads=[qkT[hq], Sb], writes=[ps[0]])
                    P.op("pe", lambda e, hv=hv, k=k: e.matmul(pO[:, 256 + k * 128:256 + (k + 1) * 128], lhsT=attnT[:, hv, :], rhs=vn[:, hv * 128:(hv + 1) * 128],
                                                         start=True, stop=True), reads=[attnT, vn], writes=[ps[0]])
                for k in range(2):
                    hv = pr * 2 + k
                    tmpo = ostmp[k]
                    P.op("act", lambda e, hv=hv, k=k, c=c, tmpo=tmpo: e.activation(out=tmpo[:], in_=pO[:, k * 128:(k + 1) * 128], func=AF.Copy, scale=eg[:, c, hv:hv + 1]),
                         reads=[ps[0], eg], writes=[tmpo])
                    P.op("dve", lambda e, hv=hv, k=k, c=c, tmpo=tmpo: e.tensor_tensor(out=Oall[:, c * 4 + hv, :], in0=tmpo[:], in1=pO[:, 256 + k * 128:256 + (k + 1) * 128], op=ALU.add),
                         reads=[ps[0], tmpo], writes=[Oall])
            for hv in range(4):
                P.op("pe", lambda e, hv=hv: e.matmul(pSU[:, hv * 128:(hv + 1) * 128], lhsT=kdec[:, hv, :], rhs=vn[:, hv * 128:(hv + 1) * 128], start=True, stop=True),
                     reads=[kdec, vn], writes=[pSU])
            for hv in range(4):
                P.op("dve", lambda e, hv=hv, c=c: e.scalar_tensor_tensor(out=Sf[:, hv, :], in0=Sf[:, hv, :], scalar=egl[:, c * 4 + hv:c * 4 + hv + 1],
                                                                    in1=pSU[:, hv * 128:(hv + 1) * 128], op0=ALU.mult, op1=ALU.add),
                     reads=[Sf, egl, pSU], writes=[Sf])
            P.op("act", lambda e: e.activation(out=Sb[:], in_=Sf[:], func=AF.Copy), reads=[Sf], writes=[Sb])

        for c in range(8):
            P.op("pool", lambda e, c=c: e.tensor_tensor(out=Osq[:], in0=Oall[:, c * 4:(c + 1) * 4, :], in1=Oall[:, c * 4:(c + 1) * 4, :], op=ALU.mult),
                 reads=[Oall], writes=[Osq])
            P.op("dve", lambda e, c=c: e.tensor_reduce(out=ss[:, c * 4:(c + 1) * 4], in_=Osq[:], axis=AX.X, op=ALU.add), reads=[Osq], writes=[ss])
        P.op("act", lambda e: e.activation(out=ss[:], in_=ss[:], func=AF.Ln, scale=1.0 / 128.0, bias=C.epsb[0:64, :]), reads=[ss, C.epsb], writes=[ss])
        P.op("act", lambda e: e.activation(out=ss[:], in_=ss[:], func=AF.Exp, scale=-0.5), reads=[ss], writes=[ss])
        for c in range(8):
            for hv in range(4):
                eng = "dve"
                P.op(eng, lambda e, c=c, hv=hv: e.scalar_tensor_tensor(out=ot[:, c, hv * 128:(hv + 1) * 128], in0=Oall[:, c * 4 + hv, :],
                                                                      scalar=ss[:, c * 4 + hv:c * 4 + hv + 1], in1=gz[:, c, hv * 128:(hv + 1) * 128],
                                                                      op0=ALU.mult, op1=ALU.mult),
                     reads=[Oall, ss, (gz, c)], writes=[(ot, c)])
        P.dma("sp", ov[:, ti * 8:(ti + 1) * 8, :], ot[:], reads=[(ot, c) for c in range(8)], writes=[("out", ti)])
        out_keys.append(("out", ti))
    P.wait_all("sp", out_keys)
    P.emit()
    return nc


def stage1_inputs(inp, i, consts, NT=16):
    T = NT * 512
    w = inp["gdn_w_in"][0]
    q = w[:, 256 * i:256 * i + 256]
    k = w[:, 2048 + 256 * i:2048 + 256 * i + 256]
    v = w[:, 4096 + 512 * i:4096 + 512 * i + 512]
    z = w[:, 8192 + 512 * i:8192 + 512 * i + 512]
    wba = np.concatenate([w[:, 12288 + 4 * i:12288 + 4 * i + 4], w[:, 12320 + 4 * i:12320 + 4 * i + 4]], axis=1)
    cw = inp["gdn_conv"][0]
    chans = np.concatenate([np.arange(256 * i, 256 * i + 256), 2048 + np.arange(256 * i, 256 * i + 256), 4096 + np.arange(512 * i, 512 * i + 512)])
    convw = np.ascontiguousarray(cw[:, chans].reshape(4, 8, 128).transpose(2, 1, 0))
    rep = lambda a: np.ascontiguousarray(np.broadcast_to(a[None, None, :], (64, 8, 4)), dtype=np.float32)
    m = dict(xT=np.ascontiguousarray(inp["x"][0][:T].T), c=col_layout(inp["c"][0]),
             adaW=np.ascontiguousarray(inp["ada_w"][0][:, 0:4096]), adaB=col_layout(inp["ada_b"][0][0:4096]),
             nmix=col_layout(inp["norm_mix"][0]), wqkvz=np.ascontiguousarray(np.concatenate([q, k, v, z], axis=1)),
             wba=np.ascontiguousarray(wba), convw=convw, alog=rep(inp["gdn_a_log"][0][4 * i:4 * i + 4]), dtb=rep(inp["gdn_dt_bias"][0][4 * i:4 * i + 4]),
             gnorm=np.ascontiguousarray(np.broadcast_to(np.tile(inp["gdn_norm"][0], 4)[None, :], (64, 512)), dtype=np.float32))
    m.update(consts)
    return m


def _run(nc, maps):
    res = run_bass_kernel_spmd(nc, maps, core_ids=list(range(NCORES)))
    return res.results


def kernel(**inp):
    import ml_dtypes
    inp = {k: np.asarray(v) for k, v in inp.items()}
    x = inp["x"][0]
    cc = col_layout(inp["c"][0])
    consts = stage1_host_consts()
    r1 = _run(build_stage1(), [stage1_inputs(inp, i, consts) for i in range(NCORES)])
    o_full = np.concatenate([r1[i]["o"] for i in range(NCORES)], axis=1)
    maps = []
    fb = np.ascontiguousarray(np.broadcast_to(inp["forget_b"][None, :], (128, 16)), dtype=np.float32)
    for i in range(NCORES):
        ts = slice(i * TOK, (i + 1) * TOK)
        maps.append(dict(xT=np.ascontiguousarray(x[ts].T), oT=np.ascontiguousarray(o_full[ts].T), c=cc,
                         adaW=np.ascontiguousarray(inp["ada_w"][0][:, 4096:]), adaB=col_layout(inp["ada_b"][0][4096:]),
                         kvaW=inp["kv_ada_w"], kvaB=col_layout(inp["kv_ada_b"]), nffn=col_layout(inp["norm_ffn"][0]),
                         nkv=col_layout(inp["kv_norm"]), wout=inp["gdn_w_out"][0], win=inp["ffn_w_in"][0], wo2=inp["ffn_w_out"][0],
                         kvw=inp["kv_w"], knorm=col_layout(inp["k_norm"]), fb=fb))
    r2 = _run(build_stage2(), maps)
    x1 = np.concatenate([r2[i]["x1T"].T for i in range(NCORES)], axis=0)
    KT = np.concatenate([r2[i]["kT"] for i in range(NCORES)], axis=1).reshape(2, 256, SEQ)
    V = np.concatenate([r2[i]["v"] for i in range(NCORES)], axis=0)
    lf = np.concatenate([r2[i]["lf"] for i in range(NCORES)], axis=0)
    U = (np.arange(128)[:, None] <= np.arange(128)[None, :]).astype(np.float32)
    toks = [np.concatenate([np.arange(b * 128, (b + 1) * 128) for b in core_blocks(i)]) for i in range(NCORES)]
    maps = []
    for i in range(NCORES):
        masks, selx = stage3a_host_consts(i)
        maps.append(dict(xT=np.ascontiguousarray(x1[toks[i]].T), c=cc, adaW=np.ascontiguousarray(inp["ada_w"][1][:, 0:4096]),
                         adaB=col_layout(inp["ada_b"][1][0:4096]), nmix=col_layout(inp["norm_mix"][1]), fwin=inp["fox_w_in"][0],
                         qnorm=col_layout(inp["q_norm"][0]), KT=np.ascontiguousarray(KT), V=V, lf=lf, U=U, masks=masks, selx=selx))
    r3 = _run(build_stage3a(), maps)
    maps = []
    for i in range(NCORES):
        maps.append(dict(xT=np.ascontiguousarray(x1[toks[i]].T), oT=r3[i]["ogT"], c=cc,
                         adaW=np.ascontiguousarray(inp["ada_w"][1][:, 4096:]), adaB=col_layout(inp["ada_b"][1][4096:]),
                         kvaW=inp["out_ada_w"], kvaB=col_layout(inp["out_ada_b"]), nffn=col_layout(inp["norm_ffn"][1]),
                         nkv=col_layout(inp["out_norm"]), wout=inp["fox_w_out"][0], win=inp["ffn_w_in"][1], wo2=inp["ffn_w_out"][1]))
    r4 = _run(build_stage2(final=True), maps)
    out = np.zeros((1, SEQ, D), np.float32)
    for i in range(NCORES):
        out[0, toks[i]] = r4[i]["x1T"].T
    return out
```

```python
import contextlib
import numpy as np
import concourse.bass as bass
import concourse.mybir as mybir
from concourse.bass_utils import run_bass_kernel_spmd

F32 = mybir.dt.float32
BF16 = mybir.dt.bfloat16
AF = mybir.ActivationFunctionType
ALU = mybir.AluOpType
AX = mybir.AxisListType

NCORES = 8
SEM_LIMIT = 30000


class Prog:
    ENGS = ("pe", "act", "dve", "pool", "sp")

    def __init__(self, nc, n_dma_sems=40, self_sync=True):
        self.nc = nc
        self.stack = contextlib.ExitStack()
        self.streams = {e: [] for e in self.ENGS}
        self.cnt = {e: 0 for e in self.ENGS}
        self.esem = {e: nc.alloc_semaphore("s_" + e + "0") for e in self.ENGS}
        self.egen = {e: 0 for e in self.ENGS}
        self.seen = {e: {} for e in self.ENGS}
        self.lastw = {}
        self.readers = {}
        self.self_sync = self_sync
        self.dsem = [nc.alloc_semaphore("s_dma%d" % i) for i in range(n_dma_sems)]
        self.dcnt = [0] * n_dma_sems
        self.drr = 0
        self.sems = {}
        for e in self.ENGS:
            self.sems[id(self.esem[e])] = self.esem[e]
        for s in self.dsem:
            self.sems[id(s)] = s
        self.n_ins = 0
        self.uid = 0

    def sb(self, shape, dtype, name=None):
        self.uid += 1
        return self.stack.enter_context(self.nc.sbuf_tensor(name or "sb%d" % self.uid, list(shape), dtype))

    def ps(self, shape, dtype, name=None):
        self.uid += 1
        return self.stack.enter_context(self.nc.psum_tensor(name or "ps%d" % self.uid, list(shape), dtype))

    @staticmethod
    def _key(k):
        if isinstance(k, tuple):
            return tuple(Prog._key(x) for x in k)
        if isinstance(k, (str, int)):
            return k
        return k.name

    def _deps(self, reads, writes):
        need = {}
        raw = {}
        reads = [self._key(k) for k in reads]
        writes = [self._key(k) for k in writes]

        def add(d, sk, v):
            if d.get(sk, 0) < v:
                d[sk] = v
        for k in reads:
            lw = self.lastw.get(k)
            if lw is not None:
                add(need, *lw)
                add(raw, *lw)
        for k in writes:
            lw = self.lastw.get(k)
            if lw is not None:
                add(need, *lw)
            for sk, v in self.readers.get(k, {}).items():
                add(need, sk, v)
        return need, raw

    def _emit_waits(self, eng, deps):
        need, raw = deps
        own = id(self.esem[eng])
        for sk, v in need.items():
            if sk == own:
                if eng == "pe" or not self.self_sync:
                    continue
            if self.seen[eng].get(sk, 0) >= v:
                continue
            self.seen[eng][sk] = v
            sem = self.sems[sk]
            self.streams[eng].append(lambda e, sem=sem, v=v: e.wait_ge(sem, v))
            self.n_ins += 1

    def _record(self, reads, writes, sk, v):
        reads = [self._key(k) for k in reads]
        writes = [self._key(k) for k in writes]
        for k in writes:
            self.lastw[k] = (sk, v)
            self.readers[k] = {}
        for k in reads:
            r = self.readers.setdefault(k, {})
            if r.get(sk, 0) < v:
                r[sk] = v

    def _excl(self, eng, reads, writes):
        extra = {}
        for k in reads:
            k = self._key(k)
            if isinstance(k, str) and k.startswith("psb"):
                for sk, v in self.readers.get(k, {}).items():
                    if sk != id(self.esem[eng]) and extra.get(sk, 0) < v:
                        extra[sk] = v
        return extra

    def op(self, eng, fn, reads=(), writes=()):
        extra = self._excl(eng, reads, writes)
        need, raw = self._deps(reads, writes)
        for sk, v in extra.items():
            if need.get(sk, 0) < v:
                need[sk] = v
        self._emit_waits(eng, (need, raw))
        if self.cnt[eng] >= SEM_LIMIT:
            self.egen[eng] += 1
            s = self.nc.alloc_semaphore("s_%s%d" % (eng, self.egen[eng]))
            self.esem[eng] = s
            self.sems[id(s)] = s
            self.cnt[eng] = 0
        self.cnt[eng] += 1
        sem = self.esem[eng]
        v = self.cnt[eng]
        self.streams[eng].append(lambda e, fn=fn, sem=sem: fn(e).then_inc(sem, 1))
        self.n_ins += 1
        self._record(reads, writes, id(sem), v)

    def dma(self, eng, out, in_, reads=(), writes=(), **kw):
        half = len(self.dsem) // 2
        if eng == "pool":
            self.drr_sw = (getattr(self, "drr_sw", -1) + 1) % half
            i = half + self.drr_sw
        else:
            self.drr = (self.drr + 1) % half
            i = self.drr
        sem = self.dsem[i]
        need, raw = self._deps(reads, writes)
        if self.dcnt[i] > 0:
            sk = id(sem)
            if need.get(sk, 0) < self.dcnt[i]:
                need[sk] = self.dcnt[i]
        self._emit_waits(eng, (need, raw))
        self.dcnt[i] += 16
        v = self.dcnt[i]
        self.streams[eng].append(
            lambda e, out=out, in_=in_, sem=sem, kw=kw: e.dma_start(out=out, in_=in_, **kw).then_inc(sem, 16))
        self.n_ins += 1
        self._record(reads, writes, id(sem), v)

    def wait_all(self, eng, keys):
        need, _ = self._deps((), keys)
        for sk, v in need.items():
            if self.seen[eng].get(sk, 0) >= v:
                continue
            self.seen[eng][sk] = v
            sem = self.sems[sk]
            self.streams[eng].append(lambda e, sem=sem, v=v: e.wait_ge(sem, v))

    def end_barrier(self, eng="sp"):
        for e2 in self.ENGS:
            if e2 == eng or self.cnt[e2] == 0:
                continue
            sem, v = self.esem[e2], self.cnt[e2]
            self.streams[eng].append(lambda e, sem=sem, v=v: e.wait_ge(sem, v))
        for i, sem in enumerate(self.dsem):
            if self.dcnt[i] > 0:
                v = self.dcnt[i]
                self.streams[eng].append(lambda e, sem=sem, v=v: e.wait_ge(sem, v))

    def emit(self):
        self.end_barrier("sp")
        with self.nc.Block() as block:
            @block.tensor
            def _(e):
                for f in self.streams["pe"]:
                    f(e)

            @block.scalar
            def _(e):
                for f in self.streams["act"]:
                    f(e)

            @block.vector
            def _(e):
                for f in self.streams["dve"]:
                    f(e)

            @block.gpsimd
            def _(e):
                for f in self.streams["pool"]:
                    f(e)

            @block.sync
            def _(e):
                for f in self.streams["sp"]:
                    f(e)
        self.stack.close()


D = 2048
KC = D // 128
EPS = 1e-6
FFN_H = 5632


def col_layout(v):
    v = np.ascontiguousarray(v, dtype=np.float32)
    return np.ascontiguousarray(v.reshape(-1, 128).T)


class Ctx:
    def __init__(self, P, mvblk=None):
        self.P = P
        self.ps = [P.ps([128, 512], F32, name="psb%d" % i) for i in range(8)]
        self.ones = P.sb([128, 128], F32, name="ones_f")
        P.op("pool", lambda e: e.memset(self.ones[:], 1.0), writes=[self.ones])
        self.sq = [P.sb([128, 512], F32, name="sq%d" % i) for i in range(2)]
        self.sqi = 0
        self.rstd = P.sb([128, 512], F32, name="rstd")
        self.tmp = [P.sb([128, 512], F32, name="tmpf%d" % i) for i in range(2)]
        self.tmpi = 0
        if mvblk is None:
            mv = [P.sb([128, KC, 128], F32, name="mvblk%d" % i) for i in range(2)]
            mvblk = [(t[:], t) for t in mv]
        self.mvblk = mvblk
        self.mvi = 0
        self.epsb = P.sb([128, 1], F32, name="epsb")
        P.op("pool", lambda e: e.memset(self.epsb[:], EPS), writes=[self.epsb])
        self.oneb = P.sb([128, 1], F32, name="oneb")
        P.op("pool", lambda e: e.memset(self.oneb[:], 1.0), writes=[self.oneb])

    def next_sq(self):
        self.sqi = (self.sqi + 1) % len(self.sq)
        return self.sq[self.sqi]

    def next_tmp(self):
        self.tmpi = (self.tmpi + 1) % len(self.tmp)
        return self.tmp[self.tmpi]


def load_cond(P, C, c_dram):
    craw = P.sb([128, KC], F32, name="craw")
    cond = P.sb([128, KC], F32, name="cond")
    P.dma("sp", craw[:], c_dram, writes=[craw])
    P.op("act", lambda e: e.activation(out=cond[:], in_=craw[:], func=AF.Silu), reads=[craw], writes=[cond])
    return cond


def matvec(P, C, w_dram, ncols, cond, bias_tile, out_tile, ps):
    noc = ncols // 128
    wv = w_dram.rearrange("(kc p) n -> p kc n", p=128)
    for oc in range(noc):
        C.mvi ^= 1
        blk, bkey = C.mvblk[C.mvi]
        P.dma("sp", blk, wv[:, :, oc * 128:(oc + 1) * 128], writes=[bkey])
        for kc in range(KC):
            P.op("pe", lambda e, blk=blk, kc=kc, oc=oc: e.matmul(
                ps[:, oc:oc + 1], lhsT=blk[:, kc, :], rhs=cond[:, kc:kc + 1], start=(kc == 0), stop=(kc == KC - 1)),
                reads=[bkey, cond], writes=[ps])
    P.op("dve", lambda e: e.tensor_tensor(out=out_tile[:, 0:noc], in0=ps[:, 0:noc], in1=bias_tile[:, 0:noc], op=ALU.add),
         reads=[ps, bias_tile], writes=[out_tile])


def mod_coeffs(P, normw, scale_ap, name, adakey):
    a = P.sb([128, KC], F32, name=name)
    P.op("dve", lambda e: e.scalar_tensor_tensor(out=a[:], in0=scale_ap, scalar=1.0, in1=normw[:], op0=ALU.add, op1=ALU.mult),
         reads=[normw, adakey], writes=[a])
    return a


def rms_rstd(P, C, ps, n_feat, T, rkeys):
    P.op("act", lambda e: e.activation(out=C.rstd[:, 0:T], in_=ps[:, 0:T], func=AF.Ln, scale=1.0 / n_feat, bias=C.epsb[:]),
         reads=[ps, C.epsb] + list(rkeys), writes=[C.rstd])
    P.op("act", lambda e: e.activation(out=C.rstd[:, 0:T], in_=C.rstd[:, 0:T], func=AF.Exp, scale=-0.5),
         reads=[C.rstd], writes=[C.rstd])


def rmsnorm_mod(P, C, xT, xkey, hT, hkey, acol, bcol, ntt, T=512, inplace=False, tts=None, xoff=None):
    ps = C.ps[7]
    for tt in (tts if tts is not None else range(ntt)):
        ts = slice(tt * T, (tt + 1) * T)
        hs = ts
        if xoff is not None:
            ts = slice(xoff, xoff + T)
        for kc in range(KC):
            sq = C.next_sq()
            P.op("act", lambda e, sq=sq, kc=kc, ts=ts: e.activation(out=sq[:, 0:T], in_=xT[:, kc, ts], func=AF.Square),
                 reads=[(xkey, kc, tt)], writes=[sq])
            P.op("pe", lambda e, sq=sq, kc=kc: e.matmul(ps[:, 0:T], lhsT=C.ones[:], rhs=sq[:, 0:T], start=(kc == 0), stop=(kc == KC - 1)),
                 reads=[sq, C.ones], writes=[ps])
        rms_rstd(P, C, ps, D, T, [])
        for kc in range(KC):
            tmp = C.next_tmp()
            P.op("dve", lambda e, tmp=tmp, kc=kc, ts=ts: e.scalar_tensor_tensor(
                out=tmp[:, 0:T], in0=xT[:, kc, ts], scalar=acol[:, kc:kc + 1], in1=C.rstd[:, 0:T], op0=ALU.mult, op1=ALU.mult),
                reads=[(xkey, kc, tt), acol, C.rstd], writes=[tmp])
            P.op("act", lambda e, tmp=tmp, kc=kc, hs=hs: e.activation(
                out=hT[:, kc, hs], in_=tmp[:, 0:T], func=AF.Identity, bias=bcol[:, kc:kc + 1], scale=1.0),
                reads=[tmp, bcol], writes=[(xkey, kc, tt) if inplace else (hkey, tt)])


def ffn(P, C, xT, xkey, hT, hkey, gcol, gkey, w_in_dram, w_out_dram, wA, wB, hid, ntt, T=512):
    nblk = FFN_H // 256
    wiv = w_in_dram.rearrange("(kc p) n -> p kc n", p=128)
    wov = w_out_dram.rearrange("(c p) n -> p c n", p=128)
    for j in range(nblk):
        wa = wA[j % 2]
        wb = wB[j % 2]
        hd = hid[j % 2]
        wa3 = wa[:].rearrange("p (kc n) -> p kc n", kc=KC)
        wb3 = wb[:].rearrange("p (c n) -> p c n", c=2)
        P.dma("pool", wa3[:, :, 0:256], wiv[:, :, j * 256:(j + 1) * 256], writes=[(wa, 0)])
        P.dma("pool", wa3[:, :, 256:512], wiv[:, :, FFN_H + j * 256:FFN_H + (j + 1) * 256], writes=[(wa, 1)])
        P.dma("pool", wb3, wov[:, 2 * j:2 * j + 2, :], writes=[wb])
        for tt in range(ntt):
            ts = slice(tt * T, (tt + 1) * T)
            for c2 in range(2):
                pg = C.ps[(2 * tt + c2) % 2]
                pu = C.ps[2 + (2 * tt + c2) % 2]
                for kc in range(KC):
                    P.op("pe", lambda e, pg=pg, kc=kc, c2=c2, ts=ts, wa3=wa3: e.matmul(
                        pg[:, 0:T], lhsT=wa3[:, kc, c2 * 128:(c2 + 1) * 128], rhs=hT[:, kc, ts], start=(kc == 0), stop=(kc == KC - 1)),
                        reads=[(wa, 0), (hkey, tt)], writes=[pg])
                for kc in range(KC):
                    P.op("pe", lambda e, pu=pu, kc=kc, c2=c2, ts=ts, wa3=wa3: e.matmul(
                        pu[:, 0:T], lhsT=wa3[:, kc, 256 + c2 * 128:256 + (c2 + 1) * 128], rhs=hT[:, kc, ts], start=(kc == 0), stop=(kc == KC - 1)),
                        reads=[(wa, 1), (hkey, tt)], writes=[pu])
                tmp = C.next_tmp()
                P.op("act", lambda e, tmp=tmp, pg=pg: e.activation(out=tmp[:, 0:T], in_=pg[:, 0:T], func=AF.Silu),
                     reads=[pg], writes=[tmp])
                P.op("dve", lambda e, tmp=tmp, pu=pu, c2=c2, ts=ts, hd=hd: e.tensor_tensor(
                    out=hd[:, c2, ts], in0=pu[:, 0:T], in1=tmp[:, 0:T], op=ALU.mult),
                    reads=[pu, tmp], writes=[(hd, tt)])
            for oc in range(KC):
                po = C.ps[4 + oc % 3]
                for c2 in range(2):
                    P.op("pe", lambda e, po=po, c2=c2, oc=oc, ts=ts, wb3=wb3, hd=hd: e.matmul(
                        po[:, 0:T], lhsT=wb3[:, c2, oc * 128:(oc + 1) * 128], rhs=hd[:, c2, ts], start=(c2 == 0), stop=(c2 == 1)),
                        reads=[wb, (hd, tt)], writes=[po])
                P.op("dve", lambda e, po=po, oc=oc, ts=ts: e.scalar_tensor_tensor(
                    out=xT[:, oc, ts], in0=po[:, 0:T], scalar=gcol[:, oc:oc + 1], in1=xT[:, oc, ts], op0=ALU.mult, op1=ALU.add),
                    reads=[po, (xkey, oc, tt), gkey], writes=[(xkey, oc, tt)])


TOK = 1024


def build_stage2(stop=99, final=False):
    nc = bass.Bass("TRN2", target_bir_lowering=False)
    dt = nc.dram_tensor
    xT_d = dt("xT", [D, TOK], F32, kind="ExternalInput").ap()
    oT_d = dt("oT", [4096, TOK], BF16, kind="ExternalInput").ap()
    c_d = dt("c", [128, KC], F32, kind="ExternalInput").ap()
    adaW_d = dt("adaW", [D, 4 * D], F32, kind="ExternalInput").ap()
    adaB_d = dt("adaB", [128, 64], F32, kind="ExternalInput").ap()
    kvaW_d = dt("kvaW", [D, 2 * D], F32, kind="ExternalInput").ap()
    kvaB_d = dt("kvaB", [128, 32], F32, kind="ExternalInput").ap()
    nffn_d = dt("nffn", [128, KC], F32, kind="ExternalInput").ap()
    nkv_d = dt("nkv", [128, KC], F32, kind="ExternalInput").ap()
    wout_d = dt("wout", [4096, D], F32, kind="ExternalInput").ap()
    win_d = dt("win", [D, 2 * FFN_H], F32, kind="ExternalInput").ap()
    wo2_d = dt("wo2", [FFN_H, D], F32, kind="ExternalInput").ap()
    if not final:
        kvw_d = dt("kvw", [D, 1040], F32, kind="ExternalInput").ap()
        knorm_d = dt("knorm", [128, 2], F32, kind="ExternalInput").ap()
        fb_d = dt("fb", [128, 16], F32, kind="ExternalInput").ap()
    x1T_d = dt("x1T", [D, TOK], F32, kind="ExternalOutput").ap()
    if not final:
        kT_d = dt("kT", [512, TOK], BF16, kind="ExternalOutput").ap()
        v_d = dt("v", [TOK, 512], BF16, kind="ExternalOutput").ap()
        lf_d = dt("lf", [TOK, 16], F32, kind="ExternalOutput").ap()

    P = Prog(nc)
    C = Ctx(P)
    xT = P.sb([128, KC, TOK], F32, name="xT_sb")
    hbuf = P.sb([128, KC * TOK], BF16, name="hbuf")
    hT = hbuf[:].rearrange("p (c t) -> p c t", c=KC)
    wA = [P.sb([128, 8192], BF16, name="wA%d" % i) for i in range(2)]
    wB = [P.sb([128, 4096], BF16, name="wB%d" % i) for i in range(2)]
    hid = [P.sb([128, 2, TOK], BF16, name="hid%d" % i) for i in range(2)]
    small = {}
    smalls = [("adaB", adaB_d, [128, 64]), ("kvaB", kvaB_d, [128, 32]), ("nffn", nffn_d, [128, KC]), ("nkv", nkv_d, [128, KC])]
    if not final:
        smalls += [("knorm", knorm_d, [128, 2]), ("fb", fb_d, [128, 16])]
    for nm, d_, shp in smalls:
        t = P.sb(shp, F32, name="sm_" + nm)
        P.dma("sp", t[:], d_, writes=[t])
        small[nm] = t
    xkeys = [("x", kc, tt) for kc in range(KC) for tt in range(2)]
    P.dma("sp", xT[:], xT_d.rearrange("(c p) t -> p c t", p=128), writes=xkeys)
    cond = load_cond(P, C, c_d)
    ada = P.sb([128, 64], F32, name="ada")
    matvec(P, C, adaW_d, 4 * D, cond, small["adaB"], ada, C.ps[6])
    kva = P.sb([128, 32], F32, name="kva")
    matvec(P, C, kvaW_d, 2 * D, cond, small["kvaB"], kva, C.ps[6])

    def finish():
        P.dma("sp", x1T_d.rearrange("(c p) t -> p c t", p=128), xT[:], reads=xkeys + [ada, kva], writes=["out_x1"])
        P.wait_all("sp", ["out_x1"])
        P.emit()
        return nc
    if stop == 1:
        return finish()
    ov = oT_d.rearrange("(c p) t -> p c t", p=128)
    wov = wout_d.rearrange("(c p) n -> p c n", p=128)
    o3 = hbuf[:].rearrange("p (c t) -> p c t", c=32)
    for tt in range(2):
        ts = slice(tt * 512, (tt + 1) * 512)
        P.dma("sp", o3, ov[:, :, ts], writes=[("h", 0), ("h", 1)])
        for ob in range(8):
            w = wA[ob % 2]
            w3 = w[:].rearrange("p (c n) -> p c n", c=32)
            P.dma("pool", w3, wov[:, :, ob * 256:(ob + 1) * 256], writes=[(w, 0), (w, 1)])
            for o2 in range(2):
                oc = ob * 2 + o2
                po = C.ps[4 + oc % 3]
                for kc in range(32):
                    P.op("pe", lambda e, po=po, kc=kc, o2=o2, w3=w3: e.matmul(
                        po[:], lhsT=w3[:, kc, o2 * 128:(o2 + 1) * 128], rhs=o3[:, kc, :], start=(kc == 0), stop=(kc == 31)),
                        reads=[(w, 0), (w, 1), ("h", 0), ("h", 1)], writes=[po])
                P.op("dve", lambda e, po=po, oc=oc, ts=ts: e.scalar_tensor_tensor(
                    out=xT[:, oc, ts], in0=po[:], scalar=ada[:, oc:oc + 1], in1=xT[:, oc, ts], op0=ALU.mult, op1=ALU.add),
                    reads=[po, ("x", oc, tt), ada], writes=[("x", oc, tt)])

    if stop == 2:
        return finish()
    a_f = mod_coeffs(P, small["nffn"], ada[:, 32:48], "a_f", ada)
    rmsnorm_mod(P, C, xT, "x", hT, "h", a_f, ada[:, 16:32], 2)
    if stop == 3:
        return finish()
    ffn(P, C, xT, "x", hT, "h", ada[:, 48:64], ada, win_d, wo2_d, wA, wB, hid, 2)
    if stop == 4:
        return finish()
    a_kv = mod_coeffs(P, small["nkv"], kva[:, 16:32], "a_kv", kva)
    if final:
        rmsnorm_mod(P, C, xT, "x", xT, "x", a_kv, kva[:, 0:16], 2, inplace=True)
        return finish()
    P.dma("sp", x1T_d.rearrange("(c p) t -> p c t", p=128), xT[:], reads=xkeys, writes=["out_x1"])

    rmsnorm_mod(P, C, xT, "x", hT, "h", a_kv, kva[:, 0:16], 2)
    kvv = kvw_d.rearrange("(kc p) n -> p kc n", p=128)
    wk3 = wA[0][:].rearrange("p (kc n) -> p kc n", kc=KC)
    wv3 = wA[1][:].rearrange("p (kc n) -> p kc n", kc=KC)
    wf = P.sb([128, KC, 16], BF16, name="wf")
    P.dma("pool", wk3, kvv[:, :, 0:512], writes=[(wA[0], 0), (wA[0], 1)])
    P.dma("pool", wv3, kvv[:, :, 512:1024], writes=[(wA[1], 0), (wA[1], 1)])
    P.dma("pool", wf[:], kvv[:, :, 1024:1040], writes=[wf])
    kraw = P.sb([128, 2, 512], F32, name="kraw")
    kout = P.sb([128, 4, TOK], BF16, name="kout")
    for tt in range(2):
        ts = slice(tt * 512, (tt + 1) * 512)
        for kh in range(2):
            for cc in range(2):
                c = kh * 2 + cc
                pk = C.ps[cc]
                for kc in range(KC):
                    P.op("pe", lambda e, pk=pk, kc=kc, c=c, ts=ts: e.matmul(
                        pk[:], lhsT=wk3[:, kc, c * 128:(c + 1) * 128], rhs=hT[:, kc, ts], start=(kc == 0), stop=(kc == KC - 1)),
                        reads=[(wA[0], 0), (wA[0], 1), ("h", tt)], writes=[pk])
                P.op("dve", lambda e, pk=pk, cc=cc: e.tensor_copy(out=kraw[:, cc, :], in_=pk[:]), reads=[pk], writes=[(kraw, cc)])
                sq = C.next_sq()
                P.op("act", lambda e, sq=sq, pk=pk: e.activation(out=sq[:], in_=pk[:], func=AF.Square), reads=[pk], writes=[sq])
                P.op("pe", lambda e, sq=sq, cc=cc: e.matmul(C.ps[7][:], lhsT=C.ones[:], rhs=sq[:], start=(cc == 0), stop=(cc == 1)),
                     reads=[sq, C.ones], writes=[C.ps[7]])
            rms_rstd(P, C, C.ps[7], 256, 512, [])
            for cc in range(2):
                c = kh * 2 + cc
                P.op("dve", lambda e, cc=cc, c=c, ts=ts: e.scalar_tensor_tensor(
                    out=kout[:, c, ts], in0=kraw[:, cc, :], scalar=small["knorm"][:, cc:cc + 1], in1=C.rstd[:], op0=ALU.mult, op1=ALU.mult),
                    reads=[(kraw, cc), small["knorm"], C.rstd], writes=[(kout, c)])
    P.dma("sp", kT_d.rearrange("(c p) t -> p c t", p=128), kout[:], reads=[(kout, c) for c in range(4)], writes=["out_k"])
    vout = P.sb([128, 8, 512], BF16, name="vout")
    lfo = P.sb([128, 8, 16], F32, name="lfo")
    lft = P.sb([128, 8, 16], F32, name="lft")
    for tb in range(8):
        tbs = slice(tb * 128, (tb + 1) * 128)
        pv = C.ps[tb % 2]
        for kc in range(KC):
            P.op("pe", lambda e, pv=pv, kc=kc, tbs=tbs: e.matmul(
                pv[:], lhsT=hT[:, kc, tbs], rhs=wv3[:, kc, :], start=(kc == 0), stop=(kc == KC - 1)),
                reads=[(wA[1], 0), (wA[1], 1), ("h", tb // 4)], writes=[pv])
        P.op("act", lambda e, pv=pv, tb=tb: e.activation(out=vout[:, tb, :], in_=pv[:], func=AF.Copy), reads=[pv], writes=[(vout, tb)])
        pf = C.ps[2 + tb % 2]
        for kc in range(KC):
            P.op("pe", lambda e, pf=pf, kc=kc, tbs=tbs: e.matmul(
                pf[:, 0:16], lhsT=hT[:, kc, tbs], rhs=wf[:, kc, :], start=(kc == 0), stop=(kc == KC - 1)),
                reads=[wf, ("h", tb // 4)], writes=[pf])
        P.op("dve", lambda e, pf=pf, tb=tb: e.tensor_tensor(out=lft[:, tb, :], in0=pf[:, 0:16], in1=small["fb"][:], op=ALU.add),
             reads=[pf, small["fb"]], writes=[(lft, tb)])
    P.op("act", lambda e: e.activation(out=lft[:], in_=lft[:], func=AF.Exp, scale=-1.0), reads=[(lft, tb) for tb in range(8)], writes=[lft])
    P.op("act", lambda e: e.activation(out=lft[:], in_=lft[:], func=AF.Ln, bias=C.oneb[:], scale=1.0), reads=[lft, C.oneb], writes=[lft])
    P.op("dve", lambda e: e.tensor_scalar(out=lfo[:], in0=lft[:], scalar1=-1.0, scalar2=None, op0=ALU.mult), reads=[lft], writes=[lfo])
    P.dma("sp", v_d.rearrange("(b p) n -> p b n", p=128), vout[:], reads=[(vout, tb) for tb in range(8)], writes=["out_v"])
    P.dma("sp", lf_d.rearrange("(b p) n -> p b n", p=128), lfo[:], reads=[lfo], writes=["out_lf"])
    P.wait_all("sp", ["out_x1", "out_k", "out_v", "out_lf"])
    P.emit()
    return nc


def core_blocks(i):
    return [i, 15 - i, 16 + i, 31 - i, 32 + i, 47 - i, 48 + i, 63 - i]


SLOT_EXT = [8, 16, 24, 32, 40, 48, 56, 64]


def stage3a_host_consts(i):
    import ml_dtypes
    blks = core_blocks(i)
    tri = (np.arange(128)[:, None] <= np.arange(128)[None, :]).astype(np.float32)
    masks = np.zeros((128, 8, 8, 128), np.float32)
    selx = np.zeros((8, 128, 64, 16), np.float32)
    for s_, gb in enumerate(blks):
        for m in range(8):
            jb = 8 * s_ + m
            if jb < gb:
                masks[:, s_, m, :] = 1.0
            elif jb == gb:
                masks[:, s_, m, :] = tri
        selx[s_, :, gb, :] = 1.0
    return masks.astype(ml_dtypes.bfloat16), selx


def build_stage3a():
    nc = bass.Bass("TRN2", target_bir_lowering=False)
    dt = nc.dram_tensor
    xT_d = dt("xT", [D, TOK], F32, kind="ExternalInput").ap()
    c_d = dt("c", [128, KC], F32, kind="ExternalInput").ap()
    adaW_d = dt("adaW", [D, 2 * D], F32, kind="ExternalInput").ap()
    adaB_d = dt("adaB", [128, 32], F32, kind="ExternalInput").ap()
    nmix_d = dt("nmix", [128, KC], F32, kind="ExternalInput").ap()
    win_d = dt("fwin", [D, 8192], F32, kind="ExternalInput").ap()
    qn_d = dt("qnorm", [128, 2], F32, kind="ExternalInput").ap()
    K_d = dt("KT", [2, 256, 8192], BF16, kind="ExternalInput").ap()
    V_d = dt("V", [8192, 512], BF16, kind="ExternalInput").ap()
    lf_d = dt("lf", [8192, 16], F32, kind="ExternalInput").ap()
    U_d = dt("U", [128, 128], F32, kind="ExternalInput").ap()
    mask_d = dt("masks", [128, 8, 8, 128], BF16, kind="ExternalInput").ap()
    selx_d = dt("selx", [8, 128, 64, 16], F32, kind="ExternalInput").ap()
    og_d = dt("ogT", [4096, TOK], BF16, kind="ExternalOutput").ap()
    Qs_d = dt("Qs", [4096, TOK], BF16, kind="Internal").ap()
    Gs_d = dt("Gs", [4096, TOK], BF16, kind="Internal").ap()

    P = Prog(nc)
    C = Ctx(P)
    bufK = P.sb([128, 8192], F32, name="bufK")
    bufV = P.sb([128, 16384], BF16, name="bufV")
    xt3 = bufK[:].rearrange("p (c t) -> p c t", c=KC)
    K3 = bufK[:].bitcast(BF16).rearrange("p (c t) -> p c t", c=2)
    hT = bufV[:].rearrange("p (c t) -> p c t", c=KC)
    V3 = bufV[:].rearrange("p (b d) -> p b d", b=64)
    small = {}
    for nm, d_, shp in (("adaB", adaB_d, [128, 32]), ("nmix", nmix_d, [128, KC]), ("qnorm", qn_d, [128, 2]), ("U", U_d, [128, 128])):
        t = P.sb(shp, F32, name="sm_" + nm)
        P.dma("sp", t[:], d_, writes=[t])
        small[nm] = t
    cond = load_cond(P, C, c_d)
    ada = P.sb([128, 32], F32, name="ada")
    matvec(P, C, adaW_d, 2 * D, cond, small["adaB"], ada, C.ps[6])
    a_m = mod_coeffs(P, small["nmix"], ada[:, 16:32], "a_m", ada)
    xv = xT_d.rearrange("(c p) t -> p c t", p=128)
    for tt in range(2):
        P.dma("sp", xt3, xv[:, :, tt * 512:(tt + 1) * 512], writes=[("xt", kc, t2) for kc in range(KC) for t2 in range(2)] + ["bufK"])
        rmsnorm_mod(P, C, xt3, "xt", hT, "h", a_m, ada[:, 0:16], 2, tts=[tt], xoff=0)

    wq = [P.sb([128, KC, 256], BF16, name="wq%d" % i) for i in range(2)]
    qraw = P.sb([128, 2, TOK], F32, name="qraw")
    qo = [P.sb([128, 2, TOK], BF16, name="qo%d" % i) for i in range(2)]
    wiv = win_d.rearrange("(kc p) n -> p kc n", p=128)
    Qsv = Qs_d.rearrange("(h c p) t -> h p c t", p=128, c=2)
    Gsv = Gs_d.rearrange("(h c p) t -> h p c t", p=128, c=2)
    for h in range(16):
        for isg in range(2):
            w = wq[isg]
            P.dma("pool", w[:], wiv[:, :, isg * 4096 + h * 256: isg * 4096 + (h + 1) * 256], writes=[w])
            out = qo[isg]
            for tt in range(2):
                ts = slice(tt * 512, (tt + 1) * 512)
                for cc in range(2):
                    pq = C.ps[(2 * tt + cc) % 4]
                    for kc in range(KC):
                        P.op("pe", lambda e, pq=pq, kc=kc, cc=cc, ts=ts, w=w: e.matmul(
                            pq[:], lhsT=w[:, kc, cc * 128:(cc + 1) * 128], rhs=hT[:, kc, ts], start=(kc == 0), stop=(kc == KC - 1)),
                            reads=[w, ("h", tt)], writes=[pq])
                    if isg:
                        P.op("act", lambda e, pq=pq, cc=cc, ts=ts, out=out: e.activation(out=out[:, cc, ts], in_=pq[:], func=AF.Sigmoid),
                             reads=[pq], writes=[(out, cc, tt)])
                    else:
                        P.op("dve", lambda e, pq=pq, cc=cc, ts=ts: e.tensor_copy(out=qraw[:, cc, ts], in_=pq[:]), reads=[pq], writes=[(qraw, cc, tt)])
                        sq = C.next_sq()
                        P.op("act", lambda e, sq=sq, pq=pq: e.activation(out=sq[:], in_=pq[:], func=AF.Square), reads=[pq], writes=[sq])
                        P.op("pe", lambda e, sq=sq, cc=cc: e.matmul(C.ps[7][:], lhsT=C.ones[:], rhs=sq[:], start=(cc == 0), stop=(cc == 1)),
                             reads=[sq, C.ones], writes=[C.ps[7]])
                if not isg:
                    rms_rstd(P, C, C.ps[7], 256, 512, [])
                    for cc in range(2):
                        P.op("dve", lambda e, cc=cc, ts=ts, out=out: e.scalar_tensor_tensor(
                            out=out[:, cc, ts], in0=qraw[:, cc, ts], scalar=small["qnorm"][:, cc:cc + 1], in1=C.rstd[:], op0=ALU.mult, op1=ALU.mult),
                            reads=[(qraw, cc, tt), small["qnorm"], C.rstd], writes=[(out, cc, tt)])
            P.dma("sp", (Gsv if isg else Qsv)[h], out[:], reads=[(out, cc, tt) for cc in range(2) for tt in range(2)],
                  writes=[("QG", isg, h)])

    lft = P.sb([128, 64, 16], F32, name="lft")
    csl = P.sb([128, 64, 16], F32, name="csl")
    tot = P.sb([128, 64, 16], F32, name="tot")
    incl = P.sb([128, 64, 16], F32, name="incl")
    P.dma("sp", lft[:], lf_d.rearrange("(b p) h -> p b h", p=128), writes=[lft])
    lf2 = lft[:].rearrange("p b h -> p (b h)")
    for half in range(2):
        hs = slice(half * 512, (half + 1) * 512)
        pa = C.ps[half]
        pb = C.ps[2 + half]
        P.op("pe", lambda e, pa=pa, hs=hs: e.matmul(pa[:], lhsT=small["U"][:], rhs=lf2[:, hs], start=True, stop=True),
             reads=[small["U"], lft], writes=[pa])
        P.op("pe", lambda e, pb=pb, hs=hs: e.matmul(pb[:], lhsT=C.ones[:], rhs=lf2[:, hs], start=True, stop=True),
             reads=[C.ones, lft], writes=[pb])
        P.op("dve", lambda e, pa=pa, hs=hs: e.tensor_copy(out=csl[:].rearrange("p b h -> p (b h)")[:, hs], in_=pa[:]), reads=[pa], writes=[csl])
        P.op("dve", lambda e, pb=pb, hs=hs: e.tensor_copy(out=tot[:].rearrange("p b h -> p (b h)")[:, hs], in_=pb[:]), reads=[pb], writes=[tot])
    P.op("dve", lambda e: e.tensor_copy(out=incl[:, 0, :], in_=tot[:, 0, :]), reads=[tot], writes=[incl])
    for b in range(1, 64):
        P.op("dve", lambda e, b=b: e.tensor_tensor(out=incl[:, b, :], in0=incl[:, b - 1, :], in1=tot[:, b, :], op=ALU.add),
             reads=[incl, tot], writes=[incl])
    P.op("dve", lambda e: e.tensor_tensor(out=csl[:], in0=csl[:], in1=incl[:], op=ALU.add), reads=[csl, incl], writes=[csl])
    P.op("dve", lambda e: e.tensor_tensor(out=lft[:], in0=tot[:], in1=csl[:], op=ALU.subtract), reads=[csl, tot], writes=[lft])
    Fneg = lft
    fref = P.sb([128, 8, 16], F32, name="fref")
    for s_ in range(8):
        P.dma("sp", tot[:], selx_d[s_], writes=[tot])
        P.op("dve", lambda e: e.tensor_tensor(out=tot[:], in0=tot[:], in1=incl[:], op=ALU.mult), reads=[tot, incl], writes=[tot])
        P.op("dve", lambda e, s_=s_: e.tensor_reduce(out=fref[:, s_, :], in_=tot[:].rearrange("p b h -> p h b"), axis=AX.X, op=ALU.add),
             reads=[tot], writes=[fref])

    masks = P.sb([128, 8, 8, 128], BF16, name="masks_sb")
    P.dma("sp", masks[:], mask_d, writes=[masks])
    onesb = P.sb([128, 128], BF16, name="onesb")
    P.op("pool", lambda e: e.memset(onesb[:], 1.0), writes=[onesb])
    Qt = P.sb([128, 4, 2, TOK], BF16, name="Qt")
    Gt = P.sb([128, 4, 2, TOK], BF16, name="Gt")
    bias = P.sb([128, 64, 4], F32, name="bias")
    pT = [P.sb([128, 4, 128], BF16, name="pT%d" % i) for i in range(3)]
    rec = P.sb([128, 512], F32, name="rec")
    pti = 0
    setsel = 0
    Kv = K_d.rearrange("k (c p) t -> k p c t", p=128)
    Vv = V_d.rearrange("(b p) n -> p b n", p=128)
    ogv = og_d.rearrange("(h c p) t -> p h c t", p=128, c=2)
    for kvh in range(2):
        P.dma("sp", K3, Kv[kvh], writes=["bufK"] + [("xt", kc, tt) for kc in range(KC) for tt in range(2)])
        P.dma("sp", V3, Vv[:, :, kvh * 256:(kvh + 1) * 256], writes=[("h", 0), ("h", 1)])
        for hg in range(2):
            h0 = kvh * 8 + hg * 4
            for j in range(4):
                P.dma("sp", Qt[:, j], Qsv[h0 + j], reads=[("QG", 0, h0 + j)], writes=[Qt])
                P.dma("sp", Gt[:, j], Gsv[h0 + j], reads=[("QG", 1, h0 + j)], writes=[Gt])
            for s_ in range(8):
                qs = slice(s_ * 128, (s_ + 1) * 128)
                ext = SLOT_EXT[s_]
                for j in range(4):
                    P.op("dve", lambda e, j=j, s_=s_, ext=ext, h0=h0: e.tensor_scalar(
                        out=bias[:, 0:ext, j], in0=Fneg[:, 0:ext, h0 + j], scalar1=fref[:, s_, h0 + j:h0 + j + 1], scalar2=0.0,
                        op0=ALU.add, op1=ALU.min), reads=[Fneg, fref], writes=[bias])
                setsel ^= 1
                acc = [C.ps[2 + 3 * setsel + k] for k in range(3)]
                def emit_S(jb):
                    ks = slice(jb * 128, (jb + 1) * 128)
                    pS = C.ps[jb % 2]
                    for cc in range(2):
                        P.op("pe", lambda e, pS=pS, cc=cc, ks=ks, qs=qs: e.matmul(
                            pS[:].rearrange("p (j t) -> p j t", j=4), lhsT=K3[:, cc, ks], rhs=Qt[:, :, cc, qs], start=(cc == 0), stop=(cc == 1)),
                            reads=["bufK", Qt], writes=[pS])
                emit_S(0)
                for jb in range(ext):
                    pS = C.ps[jb % 2]
                    if jb + 1 < ext:
                        emit_S(jb + 1)
                    pti = (pti + 1) % 3
                    pt = pT[pti]
                    for j in range(4):
                        P.op("act", lambda e, pS=pS, pt=pt, j=j, jb=jb: e.activation(
                            out=pt[:, j, :], in_=pS[:, j * 128:(j + 1) * 128], func=AF.Exp, scale=1.0 / 16.0, bias=bias[:, jb, j:j + 1]),
                            reads=[pS, bias], writes=[(pt, j)])
                    m = jb - 8 * s_
                    if m >= 0:
                        for j in range(4):
                            P.op("pool", lambda e, pt=pt, j=j, s_=s_, m=m: e.tensor_tensor(
                                out=pt[:, j, :], in0=pt[:, j, :], in1=masks[:, s_, m, :], op=ALU.mult),
                                reads=[(pt, j), masks], writes=[(pt, j)])
                    pt2 = pt[:].rearrange("p j t -> p (j t)")
                    for k in range(3):
                        lhs = onesb[:] if k == 2 else V3[:, jb, k * 128:(k + 1) * 128]
                        P.op("pe", lambda e, k=k, lhs=lhs, pt2=pt2, jb=jb, ext=ext, acc=acc: e.matmul(
                            acc[k][:], lhsT=lhs, rhs=pt2, start=(jb == 0), stop=(jb == ext - 1)),
                            reads=[(pt, 0), (pt, 1), (pt, 2), (pt, 3), ("h", 0), ("h", 1), onesb], writes=[acc[k]])
                P.op("dve", lambda e, acc=acc: e.reciprocal(out=rec[:], in_=acc[2][:]), reads=[acc[2]], writes=[rec])
                for k in range(2):
                    tmp = C.next_tmp()
                    P.op("dve", lambda e, tmp=tmp, k=k, acc=acc: e.tensor_tensor(out=tmp[:], in0=acc[k][:], in1=rec[:], op=ALU.mult),
                         reads=[acc[k], rec], writes=[tmp])
                    P.op("pool", lambda e, tmp=tmp, k=k, qs=qs: e.tensor_tensor(
                        out=Gt[:, :, k, qs], in0=tmp[:].rearrange("p (j t) -> p j t", j=4), in1=Gt[:, :, k, qs], op=ALU.mult),
                        reads=[tmp, Gt], writes=[Gt])
            P.dma("sp", ogv[:, h0:h0 + 4], Gt[:], reads=[Gt], writes=[("og", h0)])
    P.wait_all("sp", [("og", h0) for h0 in (0, 4, 8, 12)])
    P.emit()
    return nc


SEQ = 8192


def stage1_host_consts():
    import ml_dtypes
    I64 = np.eye(64, dtype=np.float32)
    U64 = (np.arange(64)[:, None] <= np.arange(64)[None, :]).astype(np.float32)
    LS = (np.arange(64)[None, :] < np.arange(64)[:, None]).astype(np.float32)
    return dict(I4=np.ascontiguousarray(np.tile(I64, (1, 4))), U64=U64,
                UM4=np.ascontiguousarray(np.tile(U64, (1, 4))), LS4=np.ascontiguousarray(np.tile(LS, (1, 4))),
                identb=np.eye(128, dtype=np.float32).astype(ml_dtypes.bfloat16))


def build_stage1(NT=16, stop=99):
    nc = bass.Bass("TRN2", target_bir_lowering=False)
    dt = nc.dram_tensor
    T = NT * 512
    xT_d = dt("xT", [D, T], F32, kind="ExternalInput").ap()
    c_d = dt("c", [128, KC], F32, kind="ExternalInput").ap()
    adaW_d = dt("adaW", [D, 2 * D], F32, kind="ExternalInput").ap()
    adaB_d = dt("adaB", [128, 32], F32, kind="ExternalInput").ap()
    nmix_d = dt("nmix", [128, KC], F32, kind="ExternalInput").ap()
    w_d = dt("wqkvz", [D, 1536], F32, kind="ExternalInput").ap()
    wba_d = dt("wba", [D, 8], F32, kind="ExternalInput").ap()
    convw_d = dt("convw", [128, 8, 4], F32, kind="ExternalInput").ap()
    alog_d = dt("alog", [64, 8, 4], F32, kind="ExternalInput").ap()
    dtb_d = dt("dtb", [64, 8, 4], F32, kind="ExternalInput").ap()
    gn_d = dt("gnorm", [64, 512], F32, kind="ExternalInput").ap()
    I4_d = dt("I4", [64, 256], F32, kind="ExternalInput").ap()
    U64_d = dt("U64", [64, 64], F32, kind="ExternalInput").ap()
    UM4_d = dt("UM4", [64, 256], F32, kind="ExternalInput").ap()
    LS4_d = dt("LS4", [64, 256], F32, kind="ExternalInput").ap()
    idb_d = dt("identb", [128, 128], BF16, kind="ExternalInput").ap()
    o_d = dt("o", [T, 512], BF16, kind="ExternalOutput").ap()

    P = Prog(nc)
    xbuf = P.sb([128, KC * 512], F32, name="xbuf")
    xt3 = xbuf[:].rearrange("p (c t) -> p c t", c=KC)
    mv = [(xbuf[:, i * 2048:(i + 1) * 2048].rearrange("p (c n) -> p c n", c=KC), ("mvb", i)) for i in range(2)]
    C = Ctx(P, mvblk=mv)
    ps = C.ps
    ptb = ps[7][:].bitcast(BF16)

    def ld(name, d_, shp, dtype=F32, eng="sp"):
        t = P.sb(shp, dtype, name="c_" + name)
        P.dma(eng, t[:], d_, writes=[t])
        return t
    adaB = ld("adaB", adaB_d, [128, 32])
    nmix = ld("nmix", nmix_d, [128, KC])
    convw = ld("convw", convw_d, [128, 8, 4])
    alog = ld("alog", alog_d, [64, 8, 4])
    dtb = ld("dtb", dtb_d, [64, 8, 4])
    gn = ld("gn", gn_d, [64, 512])
    I4 = ld("I4", I4_d, [64, 256])
    U64 = ld("U64", U64_d, [64, 64])
    UM4 = ld("UM4", UM4_d, [64, 256])
    LS4 = ld("LS4", LS4_d, [64, 256])
    identb = ld("identb", idb_d, [128, 128], BF16)
    I64 = I4[:, 0:64]
    wqk = P.sb([128, KC, 1536], BF16, name="wqkvz_sb")
    wba = P.sb([128, KC, 8], BF16, name="wba_sb")
    wv_ = w_d.rearrange("(kc p) n -> p kc n", p=128)
    for j in range(3):
        P.dma("pool", wqk[:, :, j * 512:(j + 1) * 512], wv_[:, :, j * 512:(j + 1) * 512], writes=[(wqk, j)])
    P.dma("pool", wba[:], wba_d.rearrange("(kc p) n -> p kc n", p=128), writes=[wba])
    wkeys = [(wqk, j) for j in range(3)]

    cond = load_cond(P, C, c_d)
    ada = P.sb([128, 32], F32, name="ada")
    matvec(P, C, adaW_d, 2 * D, cond, adaB, ada, ps[6])
    a_m = mod_coeffs(P, nmix, ada[:, 16:32], "a_m", ada)
    eA = P.sb([64, 8, 4], F32, name="eA")
    P.op("act", lambda e: e.activation(out=eA[:], in_=alog[:], func=AF.Exp), reads=[alog], writes=[eA])

    hT = P.sb([128, KC, 512], BF16, name="hT")
    pre = P.sb([128, 8, 515], F32, name="pre")
    P.op("pool", lambda e: e.memset(pre[:], 0.0), writes=[(pre, n) for n in range(8)])
    qkT = [P.sb([128, 512], BF16, name="qkT%d" % n) for n in range(4)]
    vT = [P.sb([128, 512], BF16, name="vT%d" % n) for n in range(4)]
    actf = P.sb([128, 512], F32, name="actf") if False else None
    gz = P.sb([64, 8, 512], F32, name="gz")
    Sf = P.sb([128, 4, 128], F32, name="Sf")
    Sb = P.sb([128, 4, 128], BF16, name="Sb")
    P.op("pool", lambda e: e.memset(Sf[:], 0.0), writes=[Sf])
    P.op("pool", lambda e: e.memset(Sb[:], 0.0), writes=[Sb])
    sm = lambda name, dtype=F32: P.sb([64, 8, 4], dtype, name=name)
    beta, nbeta, xg, graw, gcum, eg, kds, bw = [sm(n) for n in ("beta", "nbeta", "xg", "graw", "gcum", "eg", "kds", "bw")]
    egl = P.sb([128, 32], F32, name="egl")
    t64 = lambda name, dtype=F32: P.sb([64, 4, 64], dtype, name=name)
    Dg, d1, d2, decT, dec, Y0 = [t64(n) for n in ("Dg", "d1", "d2", "decT", "dec", "Y0")]
    XX = [t64("XX%d" % i) for i in range(2)]
    YY = [t64("YY%d" % i) for i in range(2)]
    RR = [t64("RR%d" % i) for i in range(2)]
    HB = [(t64("TT%d" % i, BF16), t64("attnT%d" % i, BF16), P.sb([64, 4, 128], BF16, name="kdec%d" % i),
           P.sb([64, 4, 128], BF16, name="bv%d" % i), P.sb([128, 4, 64], BF16, name="nwT%d" % i)) for i in range(2)]
    kbw = P.sb([64, 4, 128], BF16, name="kbw")
    osb = P.sb([64, 4, 128], F32, name="osb")
    vn = P.sb([64, 512], BF16, name="vn")
    Oall = P.sb([64, 32, 128], F32, name="Oall")
    Osq = P.sb([64, 4, 128], F32, name="Osq")
    ss = P.sb([64, 32], F32, name="ss")
    ot = P.sb([64, 8, 512], BF16, name="ot")
    xv = xT_d.rearrange("(c p) t -> p c t", p=128)
    ov = o_d.rearrange("(c p) n -> p c n", p=64)
    pba = ps[2][0:64, 0:64]
    pgc = ps[2][0:64, 64:96]
    pgl = ps[2][:, 96:128]
    pX0 = ps[2][0:64, 128:384]
    pKQ = ps[3][0:64, 0:256]
    pG = ps[3][0:64, 256:512]
    pX = ps[4][0:64, 0:256]
    pY = ps[4][0:64, 256:512]
    pR = ps[5][0:64, 0:256]
    pwT = ps[5][:, 256:512]
    pVN = ps[6][0:64, :]
    pO = ps[0][0:64, :]
    pSU = ps[1]
    out_keys = []

    def finish_now(extra):
        P.dma("sp", ov[:, 0:8, :], ot[:], reads=list(extra) + [(ot, c) for c in range(8)], writes=[("out", 0)])
        P.wait_all("sp", [("out", 0)])
        P.emit()
        return nc
    if stop == 1:
        return finish_now([ada, a_m, eA, wba, Sf, Sb] + wkeys)

    for ti in range(NT):
        tsl = slice(ti * 512, (ti + 1) * 512)
        P.dma("sp", xt3, xv[:, :, tsl], writes=[("xt", kc, 0) for kc in range(KC)] + [("mvb", 0), ("mvb", 1)])
        rmsnorm_mod(P, C, xt3, "xt", hT, "h", a_m, ada[:, 0:16], 1, tts=[0])
        if stop == 2:
            return finish_now([("h", 0)])
        for c in range(8):
            for kc in range(KC):
                P.op("pe", lambda e, c=c, kc=kc: e.matmul(pba[:, c * 8:(c + 1) * 8], lhsT=hT[:, kc, c * 64:(c + 1) * 64], rhs=wba[:, kc, :],
                                                      start=(kc == 0), stop=(kc == KC - 1)), reads=[("h", 0), wba], writes=[ps[2]])
        pba3 = pba.rearrange("p (c n) -> p c n", n=8)
        P.op("act", lambda e: e.activation(out=beta[:], in_=pba3[:, :, 0:4], func=AF.Sigmoid), reads=[ps[2]], writes=[beta])
        P.op("dve", lambda e: e.tensor_tensor(out=xg[:], in0=pba3[:, :, 4:8], in1=dtb[:], op=ALU.add), reads=[ps[2], dtb], writes=[xg])
        if stop == 30:
            return finish_now([beta, xg])
        P.op("dve", lambda e: e.tensor_scalar(out=nbeta[:], in0=beta[:], scalar1=-1.0, scalar2=None, op0=ALU.mult), reads=[beta], writes=[nbeta])
        P.op("act", lambda e: e.activation(out=xg[:], in_=xg[:], func=AF.Exp), reads=[xg], writes=[xg])
        P.op("act", lambda e: e.activation(out=xg[:], in_=xg[:], func=AF.Ln, bias=C.oneb[0:64, :], scale=1.0), reads=[xg, C.oneb], writes=[xg])
        P.op("dve", lambda e: e.scalar_tensor_tensor(out=graw[:], in0=xg[:], scalar=-1.0, in1=eA[:], op0=ALU.mult, op1=ALU.mult),
             reads=[xg, eA], writes=[graw])
        if stop == 31:
            return finish_now([beta, nbeta, graw])
        g2 = graw[:].rearrange("p c h -> p (c h)")
        P.op("pe", lambda e: e.matmul(pgc, lhsT=U64[:], rhs=g2, start=True, stop=True), reads=[U64, graw], writes=[ps[2]])
        P.op("pe", lambda e: e.matmul(pgl, lhsT=C.ones[0:64, :], rhs=g2, start=True, stop=True), reads=[C.ones, graw], writes=[ps[2]])
        gc2 = gcum[:].rearrange("p c h -> p (c h)")
        P.op("dve", lambda e: e.tensor_copy(out=gc2, in_=pgc), reads=[ps[2]], writes=[gcum])
        P.op("act", lambda e: e.activation(out=eg[:].rearrange("p c h -> p (c h)"), in_=pgc, func=AF.Exp), reads=[ps[2]], writes=[eg])
        P.op("act", lambda e: e.activation(out=egl[:], in_=pgl, func=AF.Exp), reads=[ps[2]], writes=[egl])
        if stop == 32:
            return finish_now([beta, nbeta, graw, gcum, eg, egl])
        kd2 = kds[:].rearrange("p c h -> p (c h)")
        P.op("dve", lambda e: e.tensor_tensor(out=kd2, in0=pgl[0:64, :], in1=gc2, op=ALU.subtract), reads=[ps[2], gcum], writes=[kds])
        if stop == 33:
            return finish_now([beta, nbeta, graw, gcum, eg, egl, kds])
        P.op("act", lambda e: e.activation(out=kd2, in_=kd2, func=AF.Exp), reads=[kds], writes=[kds])
        if stop == 34:
            return finish_now([beta, nbeta, graw, gcum, eg, egl, kds])
        P.op("dve", lambda e: e.tensor_tensor(out=bw[:], in0=beta[:], in1=eg[:], op=ALU.mult), reads=[beta, eg], writes=[bw])
        if stop == 3:
            return finish_now([beta, nbeta, gcum, eg, egl, kds, bw])
        for n in range(8):
            pp = ps[n % 2]
            for kc in range(KC):
                P.op("pe", lambda e, pp=pp, kc=kc, n=n: e.matmul(pp[:], lhsT=wqk[:, kc, n * 128:(n + 1) * 128], rhs=hT[:, kc, :],
                                                            start=(kc == 0), stop=(kc == KC - 1)), reads=[("h", 0)] + wkeys, writes=[pp])
            P.op("pool", lambda e, n=n: e.tensor_copy(out=pre[:, n, 0:3], in_=pre[:, n, 512:515]), reads=[(pre, n)], writes=[(pre, n)])
            P.op("act", lambda e, pp=pp, n=n: e.activation(out=pre[:, n, 3:515], in_=pp[:], func=AF.Copy), reads=[pp], writes=[(pre, n)])
            acc = C.next_tmp()
            P.op("dve", lambda e, acc=acc, n=n: e.tensor_scalar(out=acc[:], in0=pre[:, n, 0:512], scalar1=convw[:, n, 0:1], scalar2=None, op0=ALU.mult),
                 reads=[(pre, n), convw], writes=[acc])
            for i in range(1, 4):
                P.op("dve", lambda e, acc=acc, n=n, i=i: e.scalar_tensor_tensor(out=acc[:], in0=pre[:, n, i:i + 512], scalar=convw[:, n, i:i + 1],
                                                                             in1=acc[:], op0=ALU.mult, op1=ALU.add),
                     reads=[(pre, n), convw, acc], writes=[acc])
            if n >= 4:
                P.op("act", lambda e, acc=acc, n=n: e.activation(out=vT[n - 4][:], in_=acc[:], func=AF.Silu), reads=[acc], writes=[vT[n - 4]])
            else:
                actf = C.next_tmp()
                P.op("act", lambda e, acc=acc, actf=actf: e.activation(out=actf[:], in_=acc[:], func=AF.Silu), reads=[acc], writes=[actf])
                sq = C.next_sq()
                P.op("act", lambda e, sq=sq, actf=actf: e.activation(out=sq[:], in_=actf[:], func=AF.Square), reads=[actf], writes=[sq])
                P.op("pe", lambda e, sq=sq: e.matmul(ps[6][:], lhsT=C.ones[:], rhs=sq[:], start=True, stop=True), reads=[sq, C.ones], writes=[ps[6]])
                P.op("act", lambda e: e.activation(out=C.rstd[:], in_=ps[6][:], func=AF.Ln, scale=1.0, bias=C.epsb[:]),
                     reads=[ps[6], C.epsb], writes=[C.rstd])
                P.op("act", lambda e: e.activation(out=C.rstd[:], in_=C.rstd[:], func=AF.Exp, scale=-0.5), reads=[C.rstd], writes=[C.rstd])
                sc = (128.0 ** -0.5) if n < 2 else 1.0
                P.op("dve", lambda e, n=n, sc=sc, actf=actf: e.scalar_tensor_tensor(out=qkT[n][:], in0=actf[:], scalar=sc, in1=C.rstd[:], op0=ALU.mult, op1=ALU.mult),
                     reads=[actf, C.rstd], writes=[qkT[n]])
        if stop == 4:
            return finish_now(qkT + vT)
        for c in range(8):
            pz = ps[c % 2]
            for kc in range(KC):
                P.op("pe", lambda e, pz=pz, kc=kc, c=c: e.matmul(pz[0:64, :], lhsT=hT[:, kc, c * 64:(c + 1) * 64], rhs=wqk[:, kc, 1024:1536],
                                                            start=(kc == 0), stop=(kc == KC - 1)), reads=[("h", 0)] + wkeys, writes=[pz])
            zt = C.next_tmp()
            P.op("act", lambda e, pz=pz, zt=zt: e.activation(out=zt[0:64, :], in_=pz[0:64, :], func=AF.Silu), reads=[pz], writes=[zt])
            P.op("pool", lambda e, zt=zt, c=c: e.tensor_tensor(out=gz[:, c, :], in0=zt[0:64, :], in1=gn[:], op=ALU.mult), reads=[zt, gn], writes=[(gz, c)])

        if stop == 5:
            return finish_now([(gz, c) for c in range(8)])
        X2 = lambda t: t[:].rearrange("p h i -> p (h i)")

        def local(c, hb):
            TT, attnT, kdec, bv, nwT = hb
            cs = slice(c * 64, (c + 1) * 64)
            for h in range(4):
                P.op("dve", lambda e, h=h, c=c: e.tensor_scalar(out=Dg[:, h, :], in0=I64, scalar1=gcum[:, c, h:h + 1], scalar2=None, op0=ALU.mult),
                     reads=[I4, gcum], writes=[Dg])
            for hq in range(2):
                P.op("pe", lambda e, hq=hq, cs=cs: e.transpose(ptb[0:64, hq * 128:(hq + 1) * 128], qkT[2 + hq][:, cs], identb[:]),
                     reads=[qkT[2 + hq], identb], writes=[ps[7]])
            for hv in range(4):
                P.op("pe", lambda e, hv=hv, cs=cs: e.transpose(ptb[0:64, 256 + hv * 128:256 + (hv + 1) * 128], vT[hv][:, cs], identb[:]),
                     reads=[vT[hv], identb], writes=[ps[7]])
            for hq in range(2):
                P.op("pe", lambda e, hq=hq, cs=cs: e.matmul(pKQ[:, hq * 64:(hq + 1) * 64], lhsT=qkT[2 + hq][:, cs], rhs=qkT[2 + hq][:, cs], start=True, stop=True),
                     reads=[qkT[2 + hq]], writes=[ps[3]])
                P.op("pe", lambda e, hq=hq, cs=cs: e.matmul(pKQ[:, 128 + hq * 64:128 + (hq + 1) * 64], lhsT=qkT[2 + hq][:, cs], rhs=qkT[hq][:, cs], start=True, stop=True),
                     reads=[qkT[2 + hq], qkT[hq]], writes=[ps[3]])
            yield
            P.op("pe", lambda e: e.matmul(pG, lhsT=C.ones[0:64, 0:64], rhs=Dg[:].rearrange("p h i -> p (h i)"), start=True, stop=True),
                 reads=[C.ones, Dg], writes=[ps[3]])
            for hv in range(4):
                hq = hv // 2
                P.op("act", lambda e, hv=hv, hq=hq, c=c: e.activation(out=kdec[:, hv, :], in_=ptb[0:64, hq * 128:(hq + 1) * 128], func=AF.Copy,
                                                                 scale=kds[:, c, hv:hv + 1]), reads=[ps[7], kds], writes=[kdec])
                P.op("act", lambda e, hv=hv, hq=hq, c=c: e.activation(out=kbw[:, hv, :], in_=ptb[0:64, hq * 128:(hq + 1) * 128], func=AF.Copy,
                                                                 scale=bw[:, c, hv:hv + 1]), reads=[ps[7], bw], writes=[kbw])
                P.op("act", lambda e, hv=hv, c=c: e.activation(out=bv[:, hv, :], in_=ptb[0:64, 256 + hv * 128:256 + (hv + 1) * 128], func=AF.Copy,
                                                          scale=beta[:, c, hv:hv + 1]), reads=[ps[7], beta], writes=[bv])
            yield
            pG3 = pG.rearrange("p (h i) -> p h i", h=4)
            for h in range(4):
                P.op("dve", lambda e, h=h, c=c: e.tensor_scalar(out=d1[:, h, :], in0=pG3[:, h, :], scalar1=gcum[:, c, h:h + 1], scalar2=0.0,
                                                           op0=ALU.subtract, op1=ALU.min), reads=[ps[3], gcum], writes=[d1])
                P.op("dve", lambda e, h=h, c=c: e.tensor_scalar(out=d2[:, h, :], in0=pG3[:, h, :], scalar1=gcum[:, c, h:h + 1], scalar2=0.0,
                                                           op0=ALU.subtract, op1=ALU.max), reads=[ps[3], gcum], writes=[d2])
            yield
            P.op("act", lambda e: e.activation(out=decT[:], in_=d1[:], func=AF.Exp), reads=[d1], writes=[decT])
            P.op("act", lambda e: e.activation(out=dec[:], in_=d2[:], func=AF.Exp, scale=-1.0), reads=[d2], writes=[dec])
            yield
            P.op("pool", lambda e: e.tensor_tensor(out=X2(decT), in0=X2(decT), in1=UM4[:], op=ALU.mult), reads=[decT, UM4], writes=[decT])
            P.op("pool", lambda e: e.tensor_tensor(out=X2(dec), in0=X2(dec), in1=LS4[:], op=ALU.mult), reads=[dec, LS4], writes=[dec])
            yield
            for hv in range(4):
                hq = hv // 2
                P.op("dve", lambda e, hv=hv, hq=hq, c=c: e.scalar_tensor_tensor(out=Y0[:, hv, :], in0=pKQ[:, hq * 64:(hq + 1) * 64], scalar=nbeta[:, c, hv:hv + 1],
                                                                           in1=dec[:, hv, :], op0=ALU.mult, op1=ALU.mult),
                     reads=[ps[3], nbeta, dec], writes=[Y0])
            for hv in range(4):
                hq = hv // 2
                P.op("dve", lambda e, hv=hv, hq=hq: e.tensor_tensor(out=attnT[:, hv, :], in0=pKQ[:, 128 + hq * 64:128 + (hq + 1) * 64], in1=decT[:, hv, :], op=ALU.mult),
                     reads=[ps[3], decT], writes=[attnT])
            yield
            for hv in range(4):
                P.op("pe", lambda e, hv=hv: e.transpose(pX0[:, hv * 64:(hv + 1) * 64], Y0[:, hv, :], I64), reads=[Y0, I4], writes=[ps[2]])
            yield
            Xc, Yc, Rc = XX[0], Y0, RR[0]
            P.op("act", lambda e, Xc=Xc: e.activation(out=X2(Xc), in_=pX0, func=AF.Copy), reads=[ps[2]], writes=[Xc])
            P.op("dve", lambda e, Rc=Rc: e.tensor_tensor(out=X2(Rc), in0=pX0, in1=I4[:], op=ALU.add), reads=[ps[2], I4], writes=[Rc])
            yield
            pendR = None
            for lvl in range(5):
                Yn = YY[lvl % 2]
                Xn = XX[(lvl + 1) % 2]
                Rn = RR[(lvl + 1) % 2]
                for hv in range(4):
                    P.op("pe", lambda e, hv=hv, Xc=Xc, Yc=Yc: e.matmul(pY[:, hv * 64:(hv + 1) * 64], lhsT=Xc[:, hv, :], rhs=Yc[:, hv, :], start=True, stop=True),
                         reads=[Xc, Yc], writes=[ps[4]])
                if lvl < 4:
                    for hv in range(4):
                        P.op("pe", lambda e, hv=hv, Xc=Xc, Yc=Yc: e.matmul(pX[:, hv * 64:(hv + 1) * 64], lhsT=Yc[:, hv, :], rhs=Xc[:, hv, :], start=True, stop=True),
                             reads=[Xc, Yc], writes=[ps[4]])
                yield
                P.op("act", lambda e, Yn=Yn: e.activation(out=X2(Yn), in_=pY, func=AF.Copy), reads=[ps[4]], writes=[Yn])
                if lvl < 4:
                    P.op("dve", lambda e, Xn=Xn: e.tensor_copy(out=X2(Xn), in_=pX), reads=[ps[4]], writes=[Xn])
                yield
                for hv in range(4):
                    P.op("pe", lambda e, hv=hv, Rc=Rc: e.matmul(pR[:, hv * 64:(hv + 1) * 64], lhsT=I64, rhs=Rc[:, hv, :], start=True, stop=False),
                         reads=[I4, Rc], writes=[ps[5]])
                    P.op("pe", lambda e, hv=hv, Rc=Rc, Yn=Yn: e.matmul(pR[:, hv * 64:(hv + 1) * 64], lhsT=Yn[:, hv, :], rhs=Rc[:, hv, :], start=False, stop=True),
                         reads=[Yn, Rc], writes=[ps[5]])
                if lvl < 4:
                    P.op("dve", lambda e, Rn=Rn: e.tensor_copy(out=X2(Rn), in_=pR), reads=[ps[5]], writes=[Rn])
                else:
                    yield
                    P.op("dve", lambda e: e.tensor_copy(out=X2(TT), in_=pR), reads=[ps[5]], writes=[TT])
                Xc, Yc, Rc = Xn, Yn, Rn
            yield
            for hv in range(4):
                P.op("pe", lambda e, hv=hv: e.matmul(pwT[:, hv * 64:(hv + 1) * 64], lhsT=kbw[:, hv, :], rhs=TT[:, hv, :], start=True, stop=True),
                     reads=[kbw, TT], writes=[ps[5]])
            yield
            P.op("act", lambda e: e.activation(out=nwT[:].rearrange("p h i -> p (h i)"), in_=pwT, func=AF.Copy, scale=-1.0), reads=[ps[5]], writes=[nwT])
            yield

        def state(c, hb):
            TT, attnT, kdec, bv, nwT = hb
            cs = slice(c * 64, (c + 1) * 64)
            for hv in range(4):
                P.op("pe", lambda e, hv=hv: e.matmul(pVN[:, hv * 128:(hv + 1) * 128], lhsT=TT[:, hv, :], rhs=bv[:, hv, :], start=True, stop=False),
                     reads=[TT, bv], writes=[ps[6]])
                P.op("pe", lambda e, hv=hv: e.matmul(pVN[:, hv * 128:(hv + 1) * 128], lhsT=nwT[:, hv, :], rhs=Sb[:, hv, :], start=False, stop=True),
                     reads=[nwT, Sb], writes=[ps[6]])
            for hv in range(4):
                hq = hv // 2
                P.op("pe", lambda e, hv=hv, hq=hq, cs=cs: e.matmul(pO[:, hv * 128:(hv + 1) * 128], lhsT=qkT[hq][:, cs], rhs=Sb[:, hv, :], start=True, stop=True),
                     reads=[qkT[hq], Sb], writes=[ps[0]])
            yield
            P.op("dve", lambda e: e.tensor_copy(out=vn[:], in_=pVN), reads=[ps[6]], writes=[vn])
            for hv in range(4):
                P.op("act", lambda e, hv=hv, c=c: e.activation(out=osb[:, hv, :], in_=pO[:, hv * 128:(hv + 1) * 128], func=AF.Copy, scale=eg[:, c, hv:hv + 1]),
                     reads=[ps[0], eg], writes=[osb])
            yield
            for hv in range(4):
                P.op("pe", lambda e, hv=hv: e.matmul(pSU[:, hv * 128:(hv + 1) * 128], lhsT=kdec[:, hv, :], rhs=vn[:, hv * 128:(hv + 1) * 128], start=True, stop=True),
                     reads=[kdec, vn], writes=[pSU])
            for hv in range(4):
                P.op("pe", lambda e, hv=hv: e.matmul(pO[:, hv * 128:(hv + 1) * 128], lhsT=attnT[:, hv, :], rhs=vn[:, hv * 128:(hv + 1) * 128],
                                                start=True, stop=True), reads=[attnT, vn], writes=[ps[0]])
            yield
            for hv in range(4):
                P.op("dve", lambda e, hv=hv, c=c: e.scalar_tensor_tensor(out=Sf[:, hv, :], in0=Sf[:, hv, :], scalar=egl[:, c * 4 + hv:c * 4 + hv + 1],
                                                                    in1=pSU[:, hv * 128:(hv + 1) * 128], op0=ALU.mult, op1=ALU.add),
                     reads=[Sf, egl, pSU], writes=[Sf])
            yield
            P.op("act", lambda e: e.activation(out=Sb[:], in_=Sf[:], func=AF.Copy), reads=[Sf], writes=[Sb])
            P.op("dve", lambda e, c=c: e.tensor_tensor(out=Oall[:, c * 4:(c + 1) * 4, :], in0=osb[:], in1=pO.rearrange("p (h d) -> p h d", h=4), op=ALU.add),
                 reads=[ps[0], osb], writes=[Oall])
            yield

        def drive(gens):
            gens = [g for g in gens if g is not None]
            while gens:
                for g in list(gens):
                    try:
                        next(g)
                    except StopIteration:
                        gens.remove(g)
        drive([local(0, HB[0])])
        for c in range(8):
            drive([state(c, HB[c % 2]), local(c + 1, HB[(c + 1) % 2]) if c < 7 else None])

        for c in range(8):
            P.op("pool", lambda e, c=c: e.tensor_tensor(out=Osq[:], in0=Oall[:, c * 4:(c + 1) * 4, :], in1=Oall[:, c * 4:(c + 1) * 4, :], op=ALU.mult),
                 reads=[Oall], writes=[Osq])
            P.op("dve", lambda e, c=c: e.tensor_reduce(out=ss[:, c * 4:(c + 1) * 4], in_=Osq[:], axis=AX.X, op=ALU.add), reads=[Osq], writes=[ss])
        P.op("act", lambda e: e.activation(out=ss[:], in_=ss[:], func=AF.Ln, scale=1.0 / 128.0, bias=C.epsb[0:64, :]), reads=[ss, C.epsb], writes=[ss])
        P.op("act", lambda e: e.activation(out=ss[:], in_=ss[:], func=AF.Exp, scale=-0.5), reads=[ss], writes=[ss])
        for c in range(8):
            for hv in range(4):
                eng = "dve"
                P.op(eng, lambda e, c=c, hv=hv: e.scalar_tensor_tensor(out=ot[:, c, hv * 128:(hv + 1) * 128], in0=Oall[:, c * 4 + hv, :],
                                                                      scalar=ss[:, c * 4 + hv:c * 4 + hv + 1], in1=gz[:, c, hv * 128:(hv + 1) * 128],
                                                                      op0=ALU.mult, op1=ALU.mult),
                     reads=[Oall, ss, (gz, c)], writes=[(ot, c)])
        P.dma("sp", ov[:, ti * 8:(ti + 1) * 8, :], ot[:], reads=[(ot, c) for c in range(8)], writes=[("out", ti)])
        out_keys.append(("out", ti))
    P.wait_all("sp", out_keys)
    P.emit()
    return nc


def stage1_inputs(inp, i, consts, NT=16):
    T = NT * 512
    w = inp["gdn_w_in"][0]
    q = w[:, 256 * i:256 * i + 256]
    k = w[:, 2048 + 256 * i:2048 + 256 * i + 256]
    v = w[:, 4096 + 512 * i:4096 + 512 * i + 512]
    z = w[:, 8192 + 512 * i:8192 + 512 * i + 512]
    wba = np.concatenate([w[:, 12288 + 4 * i:12288 + 4 * i + 4], w[:, 12320 + 4 * i:12320 + 4 * i + 4]], axis=1)
    cw = inp["gdn_conv"][0]
    chans = np.concatenate([np.arange(256 * i, 256 * i + 256), 2048 + np.arange(256 * i, 256 * i + 256), 4096 + np.arange(512 * i, 512 * i + 512)])
    convw = np.ascontiguousarray(cw[:, chans].reshape(4, 8, 128).transpose(2, 1, 0))
    rep = lambda a: np.ascontiguousarray(np.broadcast_to(a[None, None, :], (64, 8, 4)), dtype=np.float32)
    m = dict(xT=np.ascontiguousarray(inp["x"][0][:T].T), c=col_layout(inp["c"][0]),
             adaW=np.ascontiguousarray(inp["ada_w"][0][:, 0:4096]), adaB=col_layout(inp["ada_b"][0][0:4096]),
             nmix=col_layout(inp["norm_mix"][0]), wqkvz=np.ascontiguousarray(np.concatenate([q, k, v, z], axis=1)),
             wba=np.ascontiguousarray(wba), convw=convw, alog=rep(inp["gdn_a_log"][0][4 * i:4 * i + 4]), dtb=rep(inp["gdn_dt_bias"][0][4 * i:4 * i + 4]),
             gnorm=np.ascontiguousarray(np.broadcast_to(np.tile(inp["gdn_norm"][0], 4)[None, :], (64, 512)), dtype=np.float32))
    m.update(consts)
    return m


def _run(nc, maps):
    res = run_bass_kernel_spmd(nc, maps, core_ids=list(range(NCORES)))
    return res.results


def kernel(**inp):
    import ml_dtypes
    inp = {k: np.asarray(v) for k, v in inp.items()}
    x = inp["x"][0]
    cc = col_layout(inp["c"][0])
    consts = stage1_host_consts()
    r1 = _run(build_stage1(), [stage1_inputs(inp, i, consts) for i in range(NCORES)])
    o_full = np.concatenate([r1[i]["o"] for i in range(NCORES)], axis=1)
    maps = []
    fb = np.ascontiguousarray(np.broadcast_to(inp["forget_b"][None, :], (128, 16)), dtype=np.float32)
    for i in range(NCORES):
        ts = slice(i * TOK, (i + 1) * TOK)
        maps.append(dict(xT=np.ascontiguousarray(x[ts].T), oT=np.ascontiguousarray(o_full[ts].T), c=cc,
                         adaW=np.ascontiguousarray(inp["ada_w"][0][:, 4096:]), adaB=col_layout(inp["ada_b"][0][4096:]),
                         kvaW=inp["kv_ada_w"], kvaB=col_layout(inp["kv_ada_b"]), nffn=col_layout(inp["norm_ffn"][0]),
                         nkv=col_layout(inp["kv_norm"]), wout=inp["gdn_w_out"][0], win=inp["ffn_w_in"][0], wo2=inp["ffn_w_out"][0],
                         kvw=inp["kv_w"], knorm=col_layout(inp["k_norm"]), fb=fb))
    r2 = _run(build_stage2(), maps)
    x1 = np.concatenate([r2[i]["x1T"].T for i in range(NCORES)], axis=0)
    KT = np.concatenate([r2[i]["kT"] for i in range(NCORES)], axis=1).reshape(2, 256, SEQ)
    V = np.concatenate([r2[i]["v"] for i in range(NCORES)], axis=0)
    lf = np.concatenate([r2[i]["lf"] for i in range(NCORES)], axis=0)
    U = (np.arange(128)[:, None] <= np.arange(128)[None, :]).astype(np.float32)
    toks = [np.concatenate([np.arange(b * 128, (b + 1) * 128) for b in core_blocks(i)]) for i in range(NCORES)]
    maps = []
    for i in range(NCORES):
        masks, selx = stage3a_host_consts(i)
        maps.append(dict(xT=np.ascontiguousarray(x1[toks[i]].T), c=cc, adaW=np.ascontiguousarray(inp["ada_w"][1][:, 0:4096]),
                         adaB=col_layout(inp["ada_b"][1][0:4096]), nmix=col_layout(inp["norm_mix"][1]), fwin=inp["fox_w_in"][0],
                         qnorm=col_layout(inp["q_norm"][0]), KT=np.ascontiguousarray(KT), V=V, lf=lf, U=U, masks=masks, selx=selx))
    r3 = _run(build_stage3a(), maps)
    maps = []
    for i in range(NCORES):
        maps.append(dict(xT=np.ascontiguousarray(x1[toks[i]].T), oT=r3[i]["ogT"], c=cc,
                         adaW=np.ascontiguousarray(inp["ada_w"][1][:, 4096:]), adaB=col_layout(inp["ada_b"][1][4096:]),
                         kvaW=inp["out_ada_w"], kvaB=col_layout(inp["out_ada_b"]), nffn=col_layout(inp["norm_ffn"][1]),
                         nkv=col_layout(inp["out_norm"]), wout=inp["fox_w_out"][0], win=inp["ffn_w_in"][1], wo2=inp["ffn_w_out"][1]))
    r4 = _run(build_stage2(final=True), maps)
    out = np.zeros((1, SEQ, D), np.float32)
    for i in range(NCORES):
        out[0, toks[i]] = r4[i]["x1T"].T
    return out
```

```python
import contextlib
import numpy as np
import concourse.bass as bass
import concourse.mybir as mybir
from concourse.bass_utils import run_bass_kernel_spmd

F32 = mybir.dt.float32
BF16 = mybir.dt.bfloat16
AF = mybir.ActivationFunctionType
ALU = mybir.AluOpType
AX = mybir.AxisListType

NCORES = 8
SEM_LIMIT = 30000


class Prog:
    ENGS = ("pe", "act", "dve", "pool", "sp")

    def __init__(self, nc, n_dma_sems=40, self_sync=True):
        self.nc = nc
        self.stack = contextlib.ExitStack()
        self.streams = {e: [] for e in self.ENGS}
        self.cnt = {e: 0 for e in self.ENGS}
        self.esem = {e: nc.alloc_semaphore("s_" + e + "0") for e in self.ENGS}
        self.egen = {e: 0 for e in self.ENGS}
        self.seen = {e: {} for e in self.ENGS}
        self.lastw = {}
        self.readers = {}
        self.self_sync = self_sync
        self.dsem = [nc.alloc_semaphore("s_dma%d" % i) for i in range(n_dma_sems)]
        self.dcnt = [0] * n_dma_sems
        self.drr = 0
        self.sems = {}
        for e in self.ENGS:
            self.sems[id(self.esem[e])] = self.esem[e]
        for s in self.dsem:
            self.sems[id(s)] = s
        self.n_ins = 0
        self.uid = 0

    def sb(self, shape, dtype, name=None):
        self.uid += 1
        return self.stack.enter_context(self.nc.sbuf_tensor(name or "sb%d" % self.uid, list(shape), dtype))

    def ps(self, shape, dtype, name=None):
        self.uid += 1
        return self.stack.enter_context(self.nc.psum_tensor(name or "ps%d" % self.uid, list(shape), dtype))

    @staticmethod
    def _key(k):
        if isinstance(k, tuple):
            return tuple(Prog._key(x) for x in k)
        if isinstance(k, (str, int)):
            return k
        return k.name

    def _deps(self, reads, writes):
        need = {}
        raw = {}
        reads = [self._key(k) for k in reads]
        writes = [self._key(k) for k in writes]

        def add(d, sk, v):
            if d.get(sk, 0) < v:
                d[sk] = v
        for k in reads:
            lw = self.lastw.get(k)
            if lw is not None:
                add(need, *lw)
                add(raw, *lw)
        for k in writes:
            lw = self.lastw.get(k)
            if lw is not None:
                add(need, *lw)
            for sk, v in self.readers.get(k, {}).items():
                add(need, sk, v)
        return need, raw

    def _emit_waits(self, eng, deps):
        need, raw = deps
        own = id(self.esem[eng])
        for sk, v in need.items():
            if sk == own:
                if eng == "pe" or not self.self_sync:
                    continue
            if self.seen[eng].get(sk, 0) >= v:
                continue
            self.seen[eng][sk] = v
            sem = self.sems[sk]
            self.streams[eng].append(lambda e, sem=sem, v=v: e.wait_ge(sem, v))
            self.n_ins += 1

    def _record(self, reads, writes, sk, v):
        reads = [self._key(k) for k in reads]
        writes = [self._key(k) for k in writes]
        for k in writes:
            self.lastw[k] = (sk, v)
            self.readers[k] = {}
        for k in reads:
            r = self.readers.setdefault(k, {})
            if r.get(sk, 0) < v:
                r[sk] = v

    def _excl(self, eng, reads, writes):
        extra = {}
        for k in reads:
            k = self._key(k)
            if isinstance(k, str) and k.startswith("psb"):
                for sk, v in self.readers.get(k, {}).items():
                    if sk != id(self.esem[eng]) and extra.get(sk, 0) < v:
                        extra[sk] = v
        return extra

    def op(self, eng, fn, reads=(), writes=()):
        extra = self._excl(eng, reads, writes)
        need, raw = self._deps(reads, writes)
        for sk, v in extra.items():
            if need.get(sk, 0) < v:
                need[sk] = v
        self._emit_waits(eng, (need, raw))
        if self.cnt[eng] >= SEM_LIMIT:
            self.egen[eng] += 1
            s = self.nc.alloc_semaphore("s_%s%d" % (eng, self.egen[eng]))
            self.esem[eng] = s
            self.sems[id(s)] = s
            self.cnt[eng] = 0
        self.cnt[eng] += 1
        sem = self.esem[eng]
        v = self.cnt[eng]
        self.streams[eng].append(lambda e, fn=fn, sem=sem: fn(e).then_inc(sem, 1))
        self.n_ins += 1
        self._record(reads, writes, id(sem), v)

    def dma(self, eng, out, in_, reads=(), writes=(), **kw):
        half = len(self.dsem) // 2
        if eng == "pool":
            self.drr_sw = (getattr(self, "drr_sw", -1) + 1) % half
            i = half + self.drr_sw
        else:
            self.drr = (self.drr + 1) % half
            i = self.drr
        sem = self.dsem[i]
        need, raw = self._deps(reads, writes)
        if self.dcnt[i] > 0:
            sk = id(sem)
            if need.get(sk, 0) < self.dcnt[i]:
                need[sk] = self.dcnt[i]
        self._emit_waits(eng, (need, raw))
        self.dcnt[i] += 16
        v = self.dcnt[i]
        self.streams[eng].append(
            lambda e, out=out, in_=in_, sem=sem, kw=kw: e.dma_start(out=out, in_=in_, **kw).then_inc(sem, 16))
        self.n_ins += 1
        self._record(reads, writes, id(sem), v)

    def wait_all(self, eng, keys):
        need, _ = self._deps((), keys)
        for sk, v in need.items():
            if self.seen[eng].get(sk, 0) >= v:
                continue
            self.seen[eng][sk] = v
            sem = self.sems[sk]
            self.streams[eng].append(lambda e, sem=sem, v=v: e.wait_ge(sem, v))

    def end_barrier(self, eng="sp"):
        for e2 in self.ENGS:
            if e2 == eng or self.cnt[e2] == 0:
                continue
            sem, v = self.esem[e2], self.cnt[e2]
            self.streams[eng].append(lambda e, sem=sem, v=v: e.wait_ge(sem, v))
        for i, sem in enumerate(self.dsem):
            if self.dcnt[i] > 0:
                v = self.dcnt[i]
                self.streams[eng].append(lambda e, sem=sem, v=v: e.wait_ge(sem, v))

    def emit(self):
        self.end_barrier("sp")
        with self.nc.Block() as block:
            @block.tensor
            def _(e):
                for f in self.streams["pe"]:
                    f(e)

            @block.scalar
            def _(e):
                for f in self.streams["act"]:
                    f(e)

            @block.vector
            def _(e):
                for f in self.streams["dve"]:
                    f(e)

            @block.gpsimd
            def _(e):
                for f in self.streams["pool"]:
                    f(e)

            @block.sync
            def _(e):
                for f in self.streams["sp"]:
                    f(e)
        self.stack.close()


D = 2048
KC = D // 128
EPS = 1e-6
FFN_H = 5632


def mv_layout(w):
    w = np.asarray(w, dtype=np.float32)
    noc = w.shape[1] // 128
    return np.ascontiguousarray(w.reshape(KC, 128, noc, 128).transpose(1, 2, 0, 3).reshape(128, noc, KC * 128))


def col_layout(v):
    v = np.ascontiguousarray(v, dtype=np.float32)
    return np.ascontiguousarray(v.reshape(-1, 128).T)


class Ctx:
    def __init__(self, P, mvblk=None):
        self.P = P
        self.ps = [P.ps([128, 512], F32, name="psb%d" % i) for i in range(8)]
        self.ones = P.sb([128, 128], F32, name="ones_f")
        P.op("pool", lambda e: e.memset(self.ones[:], 1.0), writes=[self.ones])
        self.sq = [P.sb([128, 512], F32, name="sq%d" % i) for i in range(2)]
        self.sqi = 0
        self.rstd = P.sb([128, 512], F32, name="rstd")
        self.tmp = [P.sb([128, 512], F32, name="tmpf%d" % i) for i in range(2)]
        self.tmpi = 0
        if mvblk is None:
            mv = [P.sb([128, KC, 128], F32, name="mvblk%d" % i) for i in range(2)]
            mvblk = [(t[:], t) for t in mv]
        self.mvblk = mvblk
        self.mvi = 0
        self.epsb = P.sb([128, 1], F32, name="epsb")
        P.op("pool", lambda e: e.memset(self.epsb[:], EPS), writes=[self.epsb])
        self.oneb = P.sb([128, 1], F32, name="oneb")
        P.op("pool", lambda e: e.memset(self.oneb[:], 1.0), writes=[self.oneb])

    def next_sq(self):
        self.sqi = (self.sqi + 1) % len(self.sq)
        return self.sq[self.sqi]

    def next_tmp(self):
        self.tmpi = (self.tmpi + 1) % len(self.tmp)
        return self.tmp[self.tmpi]


def load_cond(P, C, c_dram):
    craw = P.sb([128, KC], F32, name="craw")
    cond = P.sb([128, KC], F32, name="cond")
    P.dma("sp", craw[:], c_dram, writes=[craw])
    P.op("act", lambda e: e.activation(out=cond[:], in_=craw[:], func=AF.Silu), reads=[craw], writes=[cond])
    return cond


def matvec(P, C, w_dram, ncols, cond, bias_tile, out_tile, ps):
    noc = ncols // 128
    for oc in range(noc):
        C.mvi ^= 1
        blk, bkey = C.mvblk[C.mvi]
        P.dma("sp", blk, w_dram[:, oc, :].rearrange("p (kc n) -> p kc n", kc=KC), writes=[bkey])
        for kc in range(KC):
            P.op("pe", lambda e, blk=blk, kc=kc, oc=oc: e.matmul(
                ps[:, oc:oc + 1], lhsT=blk[:, kc, :], rhs=cond[:, kc:kc + 1], start=(kc == 0), stop=(kc == KC - 1)),
                reads=[bkey, cond], writes=[ps])
    P.op("dve", lambda e: e.tensor_tensor(out=out_tile[:, 0:noc], in0=ps[:, 0:noc], in1=bias_tile[:, 0:noc], op=ALU.add),
         reads=[ps, bias_tile], writes=[out_tile])


def mod_coeffs(P, normw, scale_ap, name, adakey):
    a = P.sb([128, KC], F32, name=name)
    P.op("dve", lambda e: e.scalar_tensor_tensor(out=a[:], in0=scale_ap, scalar=1.0, in1=normw[:], op0=ALU.add, op1=ALU.mult),
         reads=[normw, adakey], writes=[a])
    return a


def rms_rstd(P, C, ps, n_feat, T, rkeys):
    P.op("act", lambda e: e.activation(out=C.rstd[:, 0:T], in_=ps[:, 0:T], func=AF.Ln, scale=1.0 / n_feat, bias=C.epsb[:]),
         reads=[ps, C.epsb] + list(rkeys), writes=[C.rstd])
    P.op("act", lambda e: e.activation(out=C.rstd[:, 0:T], in_=C.rstd[:, 0:T], func=AF.Exp, scale=-0.5),
         reads=[C.rstd], writes=[C.rstd])


def rmsnorm_mod(P, C, xT, xkey, hT, hkey, acol, bcol, ntt, T=512, inplace=False, tts=None, xoff=None):
    ps = C.ps[7]
    for tt in (tts if tts is not None else range(ntt)):
        ts = slice(tt * T, (tt + 1) * T)
        hs = ts
        if xoff is not None:
            ts = slice(xoff, xoff + T)
        for kc in range(KC):
            sq = C.next_sq()
            P.op("act", lambda e, sq=sq, kc=kc, ts=ts: e.activation(out=sq[:, 0:T], in_=xT[:, kc, ts], func=AF.Square),
                 reads=[(xkey, kc, tt)], writes=[sq])
            P.op("pe", lambda e, sq=sq, kc=kc: e.matmul(ps[:, 0:T], lhsT=C.ones[:], rhs=sq[:, 0:T], start=(kc == 0), stop=(kc == KC - 1)),
                 reads=[sq, C.ones], writes=[ps])
        rms_rstd(P, C, ps, D, T, [])
        for kc in range(KC):
            tmp = C.next_tmp()
            P.op("dve", lambda e, tmp=tmp, kc=kc, ts=ts: e.scalar_tensor_tensor(
                out=tmp[:, 0:T], in0=xT[:, kc, ts], scalar=acol[:, kc:kc + 1], in1=C.rstd[:, 0:T], op0=ALU.mult, op1=ALU.mult),
                reads=[(xkey, kc, tt), acol, C.rstd], writes=[tmp])
            P.op("act", lambda e, tmp=tmp, kc=kc, hs=hs: e.activation(
                out=hT[:, kc, hs], in_=tmp[:, 0:T], func=AF.Identity, bias=bcol[:, kc:kc + 1], scale=1.0),
                reads=[tmp, bcol], writes=[(xkey, kc, tt) if inplace else (hkey, tt)])


def ffn(P, C, xT, xkey, hT, hkey, gcol, gkey, w_in_dram, w_out_dram, wA, wB, hid, ntt, T=512):
    nblk = FFN_H // 256
    wiv = w_in_dram.rearrange("(kc p) n -> p kc n", p=128)
    wov = w_out_dram.rearrange("(c p) n -> p c n", p=128)
    for j in range(nblk):
        wa = wA[j % 2]
        wb = wB[j % 2]
        hd = hid[j % 2]
        wa3 = wa[:].rearrange("p (kc n) -> p kc n", kc=KC)
        wb3 = wb[:].rearrange("p (c n) -> p c n", c=2)
        P.dma("pool", wa3[:, :, 0:256], wiv[:, :, j * 256:(j + 1) * 256], writes=[(wa, 0)])
        P.dma("pool", wa3[:, :, 256:512], wiv[:, :, FFN_H + j * 256:FFN_H + (j + 1) * 256], writes=[(wa, 1)])
        P.dma("pool", wb3, wov[:, 2 * j:2 * j + 2, :], writes=[wb])
        for tt in range(ntt):
            ts = slice(tt * T, (tt + 1) * T)
            for c2 in range(2):
                pg = C.ps[(2 * tt + c2) % 2]
                pu = C.ps[2 + (2 * tt + c2) % 2]
                for kc in range(KC):
                    P.op("pe", lambda e, pg=pg, kc=kc, c2=c2, ts=ts, wa3=wa3: e.matmul(
                        pg[:, 0:T], lhsT=wa3[:, kc, c2 * 128:(c2 + 1) * 128], rhs=hT[:, kc, ts], start=(kc == 0), stop=(kc == KC - 1)),
                        reads=[(wa, 0), (hkey, tt)], writes=[pg])
                for kc in range(KC):
                    P.op("pe", lambda e, pu=pu, kc=kc, c2=c2, ts=ts, wa3=wa3: e.matmul(
                        pu[:, 0:T], lhsT=wa3[:, kc, 256 + c2 * 128:256 + (c2 + 1) * 128], rhs=hT[:, kc, ts], start=(kc == 0), stop=(kc == KC - 1)),
                        reads=[(wa, 1), (hkey, tt)], writes=[pu])
                tmp = C.next_tmp()
                P.op("act", lambda e, tmp=tmp, pg=pg: e.activation(out=tmp[:, 0:T], in_=pg[:, 0:T], func=AF.Silu),
                     reads=[pg], writes=[tmp])
                P.op("dve", lambda e, tmp=tmp, pu=pu, c2=c2, ts=ts, hd=hd: e.tensor_tensor(
                    out=hd[:, c2, ts], in0=pu[:, 0:T], in1=tmp[:, 0:T], op=ALU.mult),
                    reads=[pu, tmp], writes=[(hd, tt)])
            for oc in range(KC):
                po = C.ps[4 + oc % 3]
                for c2 in range(2):
                    P.op("pe", lambda e, po=po, c2=c2, oc=oc, ts=ts, wb3=wb3, hd=hd: e.matmul(
                        po[:, 0:T], lhsT=wb3[:, c2, oc * 128:(oc + 1) * 128], rhs=hd[:, c2, ts], start=(c2 == 0), stop=(c2 == 1)),
                        reads=[wb, (hd, tt)], writes=[po])
                P.op("dve", lambda e, po=po, oc=oc, ts=ts: e.scalar_tensor_tensor(
                    out=xT[:, oc, ts], in0=po[:, 0:T], scalar=gcol[:, oc:oc + 1], in1=xT[:, oc, ts], op0=ALU.mult, op1=ALU.add),
                    reads=[po, (xkey, oc, tt), gkey], writes=[(xkey, oc, tt)])


TOK = 1024


def build_stage2(stop=99, final=False):
    nc = bass.Bass("TRN2", target_bir_lowering=False)
    dt = nc.dram_tensor
    xT_d = dt("xT", [D, TOK], F32, kind="ExternalInput").ap()
    oT_d = dt("oT", [4096, TOK], BF16, kind="ExternalInput").ap()
    c_d = dt("c", [128, KC], F32, kind="ExternalInput").ap()
    adaW_d = dt("adaW", [128, 64, D], F32, kind="ExternalInput").ap()
    adaB_d = dt("adaB", [128, 64], F32, kind="ExternalInput").ap()
    kvaW_d = dt("kvaW", [128, 32, D], F32, kind="ExternalInput").ap()
    kvaB_d = dt("kvaB", [128, 32], F32, kind="ExternalInput").ap()
    nffn_d = dt("nffn", [128, KC], F32, kind="ExternalInput").ap()
    nkv_d = dt("nkv", [128, KC], F32, kind="ExternalInput").ap()
    wout_d = dt("wout", [4096, D], F32, kind="ExternalInput").ap()
    win_d = dt("win", [D, 2 * FFN_H], F32, kind="ExternalInput").ap()
    wo2_d = dt("wo2", [FFN_H, D], F32, kind="ExternalInput").ap()
    if not final:
        kvw_d = dt("kvw", [D, 1040], F32, kind="ExternalInput").ap()
        knorm_d = dt("knorm", [128, 2], F32, kind="ExternalInput").ap()
        fb_d = dt("fb", [128, 16], F32, kind="ExternalInput").ap()
    x1T_d = dt("x1T", [D, TOK], F32, kind="ExternalOutput").ap()
    if not final:
        kT_d = dt("kT", [512, TOK], BF16, kind="ExternalOutput").ap()
        v_d = dt("v", [TOK, 512], BF16, kind="ExternalOutput").ap()
        lf_d = dt("lf", [TOK, 16], F32, kind="ExternalOutput").ap()

    P = Prog(nc)
    C = Ctx(P)
    xT = P.sb([128, KC, TOK], F32, name="xT_sb")
    hbuf = P.sb([128, KC * TOK], BF16, name="hbuf")
    hT = hbuf[:].rearrange("p (c t) -> p c t", c=KC)
    wA = [P.sb([128, 8192], BF16, name="wA%d" % i) for i in range(2)]
    wB = [P.sb([128, 4096], BF16, name="wB%d" % i) for i in range(2)]
    hid = [P.sb([128, 2, TOK], BF16, name="hid%d" % i) for i in range(2)]
    small = {}
    smalls = [("adaB", adaB_d, [128, 64]), ("kvaB", kvaB_d, [128, 32]), ("nffn", nffn_d, [128, KC]), ("nkv", nkv_d, [128, KC])]
    if not final:
        smalls += [("knorm", knorm_d, [128, 2]), ("fb", fb_d, [128, 16])]
    for nm, d_, shp in smalls:
        t = P.sb(shp, F32, name="sm_" + nm)
        P.dma("sp", t[:], d_, writes=[t])
        small[nm] = t
    xkeys = [("x", kc, tt) for kc in range(KC) for tt in range(2)]
    P.dma("sp", xT[:], xT_d.rearrange("(c p) t -> p c t", p=128), writes=xkeys)
    cond = load_cond(P, C, c_d)
    ada = P.sb([128, 64], F32, name="ada")
    matvec(P, C, adaW_d, 4 * D, cond, small["adaB"], ada, C.ps[6])
    kva = P.sb([128, 32], F32, name="kva")
    matvec(P, C, kvaW_d, 2 * D, cond, small["kvaB"], kva, C.ps[6])

    def finish():
        P.dma("sp", x1T_d.rearrange("(c p) t -> p c t", p=128), xT[:], reads=xkeys + [ada, kva], writes=["out_x1"])
        P.wait_all("sp", ["out_x1"])
        P.emit()
        return nc
    if stop == 1:
        return finish()
    ov = oT_d.rearrange("(c p) t -> p c t", p=128)
    wov = wout_d.rearrange("(c p) n -> p c n", p=128)
    o3 = hbuf[:].rearrange("p (c t) -> p c t", c=32)
    for tt in range(2):
        ts = slice(tt * 512, (tt + 1) * 512)
        P.dma("sp", o3, ov[:, :, ts], writes=[("h", 0), ("h", 1)])
        for ob in range(8):
            w = wA[ob % 2]
            w3 = w[:].rearrange("p (c n) -> p c n", c=32)
            P.dma("pool", w3, wov[:, :, ob * 256:(ob + 1) * 256], writes=[(w, 0), (w, 1)])
            for o2 in range(2):
                oc = ob * 2 + o2
                po = C.ps[4 + oc % 3]
                for kc in range(32):
                    P.op("pe", lambda e, po=po, kc=kc, o2=o2, w3=w3: e.matmul(
                        po[:], lhsT=w3[:, kc, o2 * 128:(o2 + 1) * 128], rhs=o3[:, kc, :], start=(kc == 0), stop=(kc == 31)),
                        reads=[(w, 0), (w, 1), ("h", 0), ("h", 1)], writes=[po])
                P.op("dve", lambda e, po=po, oc=oc, ts=ts: e.scalar_tensor_tensor(
                    out=xT[:, oc, ts], in0=po[:], scalar=ada[:, oc:oc + 1], in1=xT[:, oc, ts], op0=ALU.mult, op1=ALU.add),
                    reads=[po, ("x", oc, tt), ada], writes=[("x", oc, tt)])

    if stop == 2:
        return finish()
    a_f = mod_coeffs(P, small["nffn"], ada[:, 32:48], "a_f", ada)
    rmsnorm_mod(P, C, xT, "x", hT, "h", a_f, ada[:, 16:32], 2)
    if stop == 3:
        return finish()
    ffn(P, C, xT, "x", hT, "h", ada[:, 48:64], ada, win_d, wo2_d, wA, wB, hid, 2)
    if stop == 4:
        return finish()
    a_kv = mod_coeffs(P, small["nkv"], kva[:, 16:32], "a_kv", kva)
    if final:
        rmsnorm_mod(P, C, xT, "x", xT, "x", a_kv, kva[:, 0:16], 2, inplace=True)
        return finish()
    P.dma("sp", x1T_d.rearrange("(c p) t -> p c t", p=128), xT[:], reads=xkeys, writes=["out_x1"])

    rmsnorm_mod(P, C, xT, "x", hT, "h", a_kv, kva[:, 0:16], 2)
    kvv = kvw_d.rearrange("(kc p) n -> p kc n", p=128)
    wk3 = wA[0][:].rearrange("p (kc n) -> p kc n", kc=KC)
    wv3 = wA[1][:].rearrange("p (kc n) -> p kc n", kc=KC)
    wf = P.sb([128, KC, 16], BF16, name="wf")
    P.dma("pool", wk3, kvv[:, :, 0:512], writes=[(wA[0], 0), (wA[0], 1)])
    P.dma("pool", wv3, kvv[:, :, 512:1024], writes=[(wA[1], 0), (wA[1], 1)])
    P.dma("pool", wf[:], kvv[:, :, 1024:1040], writes=[wf])
    kraw = P.sb([128, 2, 512], F32, name="kraw")
    kout = P.sb([128, 4, TOK], BF16, name="kout")
    for tt in range(2):
        ts = slice(tt * 512, (tt + 1) * 512)
        for kh in range(2):
            for cc in range(2):
                c = kh * 2 + cc
                pk = C.ps[cc]
                for kc in range(KC):
                    P.op("pe", lambda e, pk=pk, kc=kc, c=c, ts=ts: e.matmul(
                        pk[:], lhsT=wk3[:, kc, c * 128:(c + 1) * 128], rhs=hT[:, kc, ts], start=(kc == 0), stop=(kc == KC - 1)),
                        reads=[(wA[0], 0), (wA[0], 1), ("h", tt)], writes=[pk])
                P.op("dve", lambda e, pk=pk, cc=cc: e.tensor_copy(out=kraw[:, cc, :], in_=pk[:]), reads=[pk], writes=[(kraw, cc)])
                sq = C.next_sq()
                P.op("act", lambda e, sq=sq, pk=pk: e.activation(out=sq[:], in_=pk[:], func=AF.Square), reads=[pk], writes=[sq])
                P.op("pe", lambda e, sq=sq, cc=cc: e.matmul(C.ps[7][:], lhsT=C.ones[:], rhs=sq[:], start=(cc == 0), stop=(cc == 1)),
                     reads=[sq, C.ones], writes=[C.ps[7]])
            rms_rstd(P, C, C.ps[7], 256, 512, [])
            for cc in range(2):
                c = kh * 2 + cc
                P.op("dve", lambda e, cc=cc, c=c, ts=ts: e.scalar_tensor_tensor(
                    out=kout[:, c, ts], in0=kraw[:, cc, :], scalar=small["knorm"][:, cc:cc + 1], in1=C.rstd[:], op0=ALU.mult, op1=ALU.mult),
                    reads=[(kraw, cc), small["knorm"], C.rstd], writes=[(kout, c)])
    P.dma("sp", kT_d.rearrange("(c p) t -> p c t", p=128), kout[:], reads=[(kout, c) for c in range(4)], writes=["out_k"])
    vout = P.sb([128, 8, 512], BF16, name="vout")
    lfo = P.sb([128, 8, 16], F32, name="lfo")
    lft = P.sb([128, 8, 16], F32, name="lft")
    for tb in range(8):
        tbs = slice(tb * 128, (tb + 1) * 128)
        pv = C.ps[tb % 2]
        for kc in range(KC):
            P.op("pe", lambda e, pv=pv, kc=kc, tbs=tbs: e.matmul(
                pv[:], lhsT=hT[:, kc, tbs], rhs=wv3[:, kc, :], start=(kc == 0), stop=(kc == KC - 1)),
                reads=[(wA[1], 0), (wA[1], 1), ("h", tb // 4)], writes=[pv])
        P.op("act", lambda e, pv=pv, tb=tb: e.activation(out=vout[:, tb, :], in_=pv[:], func=AF.Copy), reads=[pv], writes=[(vout, tb)])
        pf = C.ps[2 + tb % 2]
        for kc in range(KC):
            P.op("pe", lambda e, pf=pf, kc=kc, tbs=tbs: e.matmul(
                pf[:, 0:16], lhsT=hT[:, kc, tbs], rhs=wf[:, kc, :], start=(kc == 0), stop=(kc == KC - 1)),
                reads=[wf, ("h", tb // 4)], writes=[pf])
        P.op("dve", lambda e, pf=pf, tb=tb: e.tensor_tensor(out=lft[:, tb, :], in0=pf[:, 0:16], in1=small["fb"][:], op=ALU.add),
             reads=[pf, small["fb"]], writes=[(lft, tb)])
    P.op("act", lambda e: e.activation(out=lft[:], in_=lft[:], func=AF.Exp, scale=-1.0), reads=[(lft, tb) for tb in range(8)], writes=[lft])
    P.op("act", lambda e: e.activation(out=lft[:], in_=lft[:], func=AF.Ln, bias=C.oneb[:], scale=1.0), reads=[lft, C.oneb], writes=[lft])
    P.op("dve", lambda e: e.tensor_scalar(out=lfo[:], in0=lft[:], scalar1=-1.0, scalar2=None, op0=ALU.mult), reads=[lft], writes=[lfo])
    P.dma("sp", v_d.rearrange("(b p) n -> p b n", p=128), vout[:], reads=[(vout, tb) for tb in range(8)], writes=["out_v"])
    P.dma("sp", lf_d.rearrange("(b p) n -> p b n", p=128), lfo[:], reads=[lfo], writes=["out_lf"])
    P.wait_all("sp", ["out_x1", "out_k", "out_v", "out_lf"])
    P.emit()
    return nc


def core_blocks(i):
    return [i, 15 - i, 16 + i, 31 - i, 32 + i, 47 - i, 48 + i, 63 - i]


SLOT_EXT = [8, 16, 24, 32, 40, 48, 56, 64]


def stage3a_host_consts(i):
    import ml_dtypes
    blks = core_blocks(i)
    tri = (np.arange(128)[:, None] <= np.arange(128)[None, :]).astype(np.float32)
    masks = np.zeros((128, 8, 8, 128), np.float32)
    selx = np.zeros((8, 128, 64, 16), np.float32)
    for s_, gb in enumerate(blks):
        for m in range(8):
            jb = 8 * s_ + m
            if jb < gb:
                masks[:, s_, m, :] = 1.0
            elif jb == gb:
                masks[:, s_, m, :] = tri
        selx[s_, :, gb, :] = 1.0
    return masks.astype(ml_dtypes.bfloat16), selx


def build_stage3a():
    nc = bass.Bass("TRN2", target_bir_lowering=False)
    dt = nc.dram_tensor
    xT_d = dt("xT", [D, TOK], F32, kind="ExternalInput").ap()
    c_d = dt("c", [128, KC], F32, kind="ExternalInput").ap()
    adaW_d = dt("adaW", [128, 32, D], F32, kind="ExternalInput").ap()
    adaB_d = dt("adaB", [128, 32], F32, kind="ExternalInput").ap()
    nmix_d = dt("nmix", [128, KC], F32, kind="ExternalInput").ap()
    win_d = dt("fwin", [D, 8192], F32, kind="ExternalInput").ap()
    qn_d = dt("qnorm", [128, 2], F32, kind="ExternalInput").ap()
    K_d = dt("KT", [2, 256, 8192], BF16, kind="ExternalInput").ap()
    V_d = dt("V", [8192, 512], BF16, kind="ExternalInput").ap()
    lf_d = dt("lf", [8192, 16], F32, kind="ExternalInput").ap()
    U_d = dt("U", [128, 128], F32, kind="ExternalInput").ap()
    mask_d = dt("masks", [128, 8, 8, 128], BF16, kind="ExternalInput").ap()
    selx_d = dt("selx", [8, 128, 64, 16], F32, kind="ExternalInput").ap()
    og_d = dt("ogT", [4096, TOK], BF16, kind="ExternalOutput").ap()
    Qs_d = dt("Qs", [4096, TOK], BF16, kind="Internal").ap()
    Gs_d = dt("Gs", [4096, TOK], BF16, kind="Internal").ap()

    P = Prog(nc)
    C = Ctx(P)
    bufK = P.sb([128, 8192], F32, name="bufK")
    bufV = P.sb([128, 16384], BF16, name="bufV")
    xt3 = bufK[:].rearrange("p (c t) -> p c t", c=KC)
    K3 = bufK[:].bitcast(BF16).rearrange("p (c t) -> p c t", c=2)
    hT = bufV[:].rearrange("p (c t) -> p c t", c=KC)
    V3 = bufV[:].rearrange("p (b d) -> p b d", b=64)
    small = {}
    for nm, d_, shp in (("adaB", adaB_d, [128, 32]), ("nmix", nmix_d, [128, KC]), ("qnorm", qn_d, [128, 2]), ("U", U_d, [128, 128])):
        t = P.sb(shp, F32, name="sm_" + nm)
        P.dma("sp", t[:], d_, writes=[t])
        small[nm] = t
    cond = load_cond(P, C, c_d)
    ada = P.sb([128, 32], F32, name="ada")
    matvec(P, C, adaW_d, 2 * D, cond, small["adaB"], ada, C.ps[6])
    a_m = mod_coeffs(P, small["nmix"], ada[:, 16:32], "a_m", ada)
    xv = xT_d.rearrange("(c p) t -> p c t", p=128)
    for tt in range(2):
        P.dma("sp", xt3, xv[:, :, tt * 512:(tt + 1) * 512], writes=[("xt", kc, t2) for kc in range(KC) for t2 in range(2)] + ["bufK"])
        rmsnorm_mod(P, C, xt3, "xt", hT, "h", a_m, ada[:, 0:16], 2, tts=[tt], xoff=0)

    wq = [P.sb([128, KC, 256], BF16, name="wq%d" % i) for i in range(2)]
    qraw = P.sb([128, 2, TOK], F32, name="qraw")
    qo = [P.sb([128, 2, TOK], BF16, name="qo%d" % i) for i in range(2)]
    wiv = win_d.rearrange("(kc p) n -> p kc n", p=128)
    Qsv = Qs_d.rearrange("(h c p) t -> h p c t", p=128, c=2)
    Gsv = Gs_d.rearrange("(h c p) t -> h p c t", p=128, c=2)
    for h in range(16):
        for isg in range(2):
            w = wq[isg]
            P.dma("pool", w[:], wiv[:, :, isg * 4096 + h * 256: isg * 4096 + (h + 1) * 256], writes=[w])
            out = qo[isg]
            for tt in range(2):
                ts = slice(tt * 512, (tt + 1) * 512)
                for cc in range(2):
                    pq = C.ps[(2 * tt + cc) % 4]
                    for kc in range(KC):
                        P.op("pe", lambda e, pq=pq, kc=kc, cc=cc, ts=ts, w=w: e.matmul(
                            pq[:], lhsT=w[:, kc, cc * 128:(cc + 1) * 128], rhs=hT[:, kc, ts], start=(kc == 0), stop=(kc == KC - 1)),
                            reads=[w, ("h", tt)], writes=[pq])
                    if isg:
                        P.op("act", lambda e, pq=pq, cc=cc, ts=ts, out=out: e.activation(out=out[:, cc, ts], in_=pq[:], func=AF.Sigmoid),
                             reads=[pq], writes=[(out, cc, tt)])
                    else:
                        P.op("dve", lambda e, pq=pq, cc=cc, ts=ts: e.tensor_copy(out=qraw[:, cc, ts], in_=pq[:]), reads=[pq], writes=[(qraw, cc, tt)])
                        sq = C.next_sq()
                        P.op("act", lambda e, sq=sq, pq=pq: e.activation(out=sq[:], in_=pq[:], func=AF.Square), reads=[pq], writes=[sq])
                        P.op("pe", lambda e, sq=sq, cc=cc: e.matmul(C.ps[7][:], lhsT=C.ones[:], rhs=sq[:], start=(cc == 0), stop=(cc == 1)),
                             reads=[sq, C.ones], writes=[C.ps[7]])
                if not isg:
                    rms_rstd(P, C, C.ps[7], 256, 512, [])
                    for cc in range(2):
                        P.op("dve", lambda e, cc=cc, ts=ts, out=out: e.scalar_tensor_tensor(
                            out=out[:, cc, ts], in0=qraw[:, cc, ts], scalar=small["qnorm"][:, cc:cc + 1], in1=C.rstd[:], op0=ALU.mult, op1=ALU.mult),
                            reads=[(qraw, cc, tt), small["qnorm"], C.rstd], writes=[(out, cc, tt)])
            P.dma("sp", (Gsv if isg else Qsv)[h], out[:], reads=[(out, cc, tt) for cc in range(2) for tt in range(2)],
                  writes=[("QG", isg, h)])

    lft = P.sb([128, 64, 16], F32, name="lft")
    csl = P.sb([128, 64, 16], F32, name="csl")
    tot = P.sb([128, 64, 16], F32, name="tot")
    incl = P.sb([128, 64, 16], F32, name="incl")
    P.dma("sp", lft[:], lf_d.rearrange("(b p) h -> p b h", p=128), writes=[lft])
    lf2 = lft[:].rearrange("p b h -> p (b h)")
    for half in range(2):
        hs = slice(half * 512, (half + 1) * 512)
        pa = C.ps[half]
        pb = C.ps[2 + half]
        P.op("pe", lambda e, pa=pa, hs=hs: e.matmul(pa[:], lhsT=small["U"][:], rhs=lf2[:, hs], start=True, stop=True),
             reads=[small["U"], lft], writes=[pa])
        P.op("pe", lambda e, pb=pb, hs=hs: e.matmul(pb[:], lhsT=C.ones[:], rhs=lf2[:, hs], start=True, stop=True),
             reads=[C.ones, lft], writes=[pb])
        P.op("dve", lambda e, pa=pa, hs=hs: e.tensor_copy(out=csl[:].rearrange("p b h -> p (b h)")[:, hs], in_=pa[:]), reads=[pa], writes=[csl])
        P.op("dve", lambda e, pb=pb, hs=hs: e.tensor_copy(out=tot[:].rearrange("p b h -> p (b h)")[:, hs], in_=pb[:]), reads=[pb], writes=[tot])
    P.op("dve", lambda e: e.tensor_copy(out=incl[:, 0, :], in_=tot[:, 0, :]), reads=[tot], writes=[incl])
    for b in range(1, 64):
        P.op("dve", lambda e, b=b: e.tensor_tensor(out=incl[:, b, :], in0=incl[:, b - 1, :], in1=tot[:, b, :], op=ALU.add),
             reads=[incl, tot], writes=[incl])
    P.op("dve", lambda e: e.tensor_tensor(out=csl[:], in0=csl[:], in1=incl[:], op=ALU.add), reads=[csl, incl], writes=[csl])
    P.op("dve", lambda e: e.tensor_tensor(out=lft[:], in0=tot[:], in1=csl[:], op=ALU.subtract), reads=[csl, tot], writes=[lft])
    Fneg = lft
    fref = P.sb([128, 8, 16], F32, name="fref")
    for s_ in range(8):
        P.dma("sp", tot[:], selx_d[s_], writes=[tot])
        P.op("dve", lambda e: e.tensor_tensor(out=tot[:], in0=tot[:], in1=incl[:], op=ALU.mult), reads=[tot, incl], writes=[tot])
        P.op("dve", lambda e, s_=s_: e.tensor_reduce(out=fref[:, s_, :], in_=tot[:].rearrange("p b h -> p h b"), axis=AX.X, op=ALU.add),
             reads=[tot], writes=[fref])

    masks = P.sb([128, 8, 8, 128], BF16, name="masks_sb")
    P.dma("sp", masks[:], mask_d, writes=[masks])
    onesb = P.sb([128, 128], BF16, name="onesb")
    P.op("pool", lambda e: e.memset(onesb[:], 1.0), writes=[onesb])
    Qt = P.sb([128, 4, 2, TOK], BF16, name="Qt")
    Gt = P.sb([128, 4, 2, TOK], BF16, name="Gt")
    bias = P.sb([128, 64, 4], F32, name="bias")
    pT = [P.sb([128, 4, 128], BF16, name="pT%d" % i) for i in range(3)]
    rec = P.sb([128, 512], F32, name="rec")
    pti = 0
    setsel = 0
    Kv = K_d.rearrange("k (c p) t -> k p c t", p=128)
    Vv = V_d.rearrange("(b p) n -> p b n", p=128)
    ogv = og_d.rearrange("(h c p) t -> p h c t", p=128, c=2)
    for kvh in range(2):
        P.dma("sp", K3, Kv[kvh], writes=["bufK"] + [("xt", kc, tt) for kc in range(KC) for tt in range(2)])
        P.dma("sp", V3, Vv[:, :, kvh * 256:(kvh + 1) * 256], writes=[("h", 0), ("h", 1)])
        for hg in range(2):
            h0 = kvh * 8 + hg * 4
            for j in range(4):
                P.dma("sp", Qt[:, j], Qsv[h0 + j], reads=[("QG", 0, h0 + j)], writes=[Qt])
                P.dma("sp", Gt[:, j], Gsv[h0 + j], reads=[("QG", 1, h0 + j)], writes=[Gt])
            for s_ in range(8):
                qs = slice(s_ * 128, (s_ + 1) * 128)
                ext = SLOT_EXT[s_]
                for j in range(4):
                    P.op("dve", lambda e, j=j, s_=s_, ext=ext, h0=h0: e.tensor_scalar(
                        out=bias[:, 0:ext, j], in0=Fneg[:, 0:ext, h0 + j], scalar1=fref[:, s_, h0 + j:h0 + j + 1], scalar2=0.0,
                        op0=ALU.add, op1=ALU.min), reads=[Fneg, fref], writes=[bias])
                setsel ^= 1
                acc = [C.ps[2 + 3 * setsel + k] for k in range(3)]
                def emit_S(jb):
                    ks = slice(jb * 128, (jb + 1) * 128)
                    pS = C.ps[jb % 2]
                    for cc in range(2):
                        P.op("pe", lambda e, pS=pS, cc=cc, ks=ks, qs=qs: e.matmul(
                            pS[:].rearrange("p (j t) -> p j t", j=4), lhsT=K3[:, cc, ks], rhs=Qt[:, :, cc, qs], start=(cc == 0), stop=(cc == 1)),
                            reads=["bufK", Qt], writes=[pS])
                emit_S(0)
                for jb in range(ext):
                    pS = C.ps[jb % 2]
                    if jb + 1 < ext:
                        emit_S(jb + 1)
                    pti = (pti + 1) % 3
                    pt = pT[pti]
                    for j in range(4):
                        P.op("act", lambda e, pS=pS, pt=pt, j=j, jb=jb: e.activation(
                            out=pt[:, j, :], in_=pS[:, j * 128:(j + 1) * 128], func=AF.Exp, scale=1.0 / 16.0, bias=bias[:, jb, j:j + 1]),
                            reads=[pS, bias], writes=[(pt, j)])
                    m = jb - 8 * s_
                    if m >= 0:
                        for j in range(4):
                            P.op("pool", lambda e, pt=pt, j=j, s_=s_, m=m: e.tensor_tensor(
                                out=pt[:, j, :], in0=pt[:, j, :], in1=masks[:, s_, m, :], op=ALU.mult),
                                reads=[(pt, j), masks], writes=[(pt, j)])
                    pt2 = pt[:].rearrange("p j t -> p (j t)")
                    for k in range(3):
                        lhs = onesb[:] if k == 2 else V3[:, jb, k * 128:(k + 1) * 128]
                        P.op("pe", lambda e, k=k, lhs=lhs, pt2=pt2, jb=jb, ext=ext, acc=acc: e.matmul(
                            acc[k][:], lhsT=lhs, rhs=pt2, start=(jb == 0), stop=(jb == ext - 1)),
                            reads=[(pt, 0), (pt, 1), (pt, 2), (pt, 3), ("h", 0), ("h", 1), onesb], writes=[acc[k]])
                P.op("dve", lambda e, acc=acc: e.reciprocal(out=rec[:], in_=acc[2][:]), reads=[acc[2]], writes=[rec])
                for k in range(2):
                    tmp = C.next_tmp()
                    P.op("dve", lambda e, tmp=tmp, k=k, acc=acc: e.tensor_tensor(out=tmp[:], in0=acc[k][:], in1=rec[:], op=ALU.mult),
                         reads=[acc[k], rec], writes=[tmp])
                    P.op("pool", lambda e, tmp=tmp, k=k, qs=qs: e.tensor_tensor(
                        out=Gt[:, :, k, qs], in0=tmp[:].rearrange("p (j t) -> p j t", j=4), in1=Gt[:, :, k, qs], op=ALU.mult),
                        reads=[tmp, Gt], writes=[Gt])
            P.dma("sp", ogv[:, h0:h0 + 4], Gt[:], reads=[Gt], writes=[("og", h0)])
    P.wait_all("sp", [("og", h0) for h0 in (0, 4, 8, 12)])
    P.emit()
    return nc


SEQ = 8192


def stage1_host_consts():
    import ml_dtypes
    I64 = np.eye(64, dtype=np.float32)
    U64 = (np.arange(64)[:, None] <= np.arange(64)[None, :]).astype(np.float32)
    LS = (np.arange(64)[None, :] < np.arange(64)[:, None]).astype(np.float32)
    return dict(I4=np.ascontiguousarray(np.tile(I64, (1, 4))), U64=U64,
                UM4=np.ascontiguousarray(np.tile(U64, (1, 4))), LS4=np.ascontiguousarray(np.tile(LS, (1, 4))),
                identb=np.eye(128, dtype=np.float32).astype(ml_dtypes.bfloat16))


def build_stage1(NT=16, stop=99):
    nc = bass.Bass("TRN2", target_bir_lowering=False)
    dt = nc.dram_tensor
    T = NT * 512
    xT_d = dt("xT", [D, T], F32, kind="ExternalInput").ap()
    c_d = dt("c", [128, KC], F32, kind="ExternalInput").ap()
    adaW_d = dt("adaW", [128, 32, D], F32, kind="ExternalInput").ap()
    adaB_d = dt("adaB", [128, 32], F32, kind="ExternalInput").ap()
    nmix_d = dt("nmix", [128, KC], F32, kind="ExternalInput").ap()
    w_d = dt("wqkvz", [D, 1536], F32, kind="ExternalInput").ap()
    wba_d = dt("wba", [D, 8], F32, kind="ExternalInput").ap()
    convw_d = dt("convw", [128, 8, 4], F32, kind="ExternalInput").ap()
    alog_d = dt("alog", [64, 8, 4], F32, kind="ExternalInput").ap()
    dtb_d = dt("dtb", [64, 8, 4], F32, kind="ExternalInput").ap()
    gn_d = dt("gnorm", [64, 512], F32, kind="ExternalInput").ap()
    I4_d = dt("I4", [64, 256], F32, kind="ExternalInput").ap()
    U64_d = dt("U64", [64, 64], F32, kind="ExternalInput").ap()
    UM4_d = dt("UM4", [64, 256], F32, kind="ExternalInput").ap()
    LS4_d = dt("LS4", [64, 256], F32, kind="ExternalInput").ap()
    idb_d = dt("identb", [128, 128], BF16, kind="ExternalInput").ap()
    o_d = dt("o", [T, 512], BF16, kind="ExternalOutput").ap()

    P = Prog(nc)
    xbuf = P.sb([128, KC * 512], F32, name="xbuf")
    xt3 = xbuf[:].rearrange("p (c t) -> p c t", c=KC)
    mv = [(xbuf[:, i * 2048:(i + 1) * 2048].rearrange("p (c n) -> p c n", c=KC), ("mvb", i)) for i in range(2)]
    C = Ctx(P, mvblk=mv)
    ps = C.ps
    ptb = ps[7][:].bitcast(BF16)

    def ld(name, d_, shp, dtype=F32, eng="sp"):
        t = P.sb(shp, dtype, name="c_" + name)
        P.dma(eng, t[:], d_, writes=[t])
        return t
    adaB = ld("adaB", adaB_d, [128, 32])
    nmix = ld("nmix", nmix_d, [128, KC])
    convw = ld("convw", convw_d, [128, 8, 4])
    alog = ld("alog", alog_d, [64, 8, 4])
    dtb = ld("dtb", dtb_d, [64, 8, 4])
    gn = ld("gn", gn_d, [64, 512])
    I4 = ld("I4", I4_d, [64, 256])
    U64 = ld("U64", U64_d, [64, 64])
    UM4 = ld("UM4", UM4_d, [64, 256])
    LS4 = ld("LS4", LS4_d, [64, 256])
    identb = ld("identb", idb_d, [128, 128], BF16)
    I64 = I4[:, 0:64]
    wqk = P.sb([128, KC, 1536], BF16, name="wqkvz_sb")
    wba = P.sb([128, KC, 8], BF16, name="wba_sb")
    wv_ = w_d.rearrange("(kc p) n -> p kc n", p=128)
    for j in range(3):
        P.dma("pool", wqk[:, :, j * 512:(j + 1) * 512], wv_[:, :, j * 512:(j + 1) * 512], writes=[(wqk, j)])
    P.dma("pool", wba[:], wba_d.rearrange("(kc p) n -> p kc n", p=128), writes=[wba])
    wkeys = [(wqk, j) for j in range(3)]

    cond = load_cond(P, C, c_d)
    ada = P.sb([128, 32], F32, name="ada")
    matvec(P, C, adaW_d, 2 * D, cond, adaB, ada, ps[6])
    a_m = mod_coeffs(P, nmix, ada[:, 16:32], "a_m", ada)
    eA = P.sb([64, 8, 4], F32, name="eA")
    P.op("act", lambda e: e.activation(out=eA[:], in_=alog[:], func=AF.Exp), reads=[alog], writes=[eA])

    hT = P.sb([128, KC, 512], BF16, name="hT")
    pre = P.sb([128, 8, 515], F32, name="pre")
    P.op("pool", lambda e: e.memset(pre[:], 0.0), writes=[(pre, n) for n in range(8)])
    qkT = [P.sb([128, 512], BF16, name="qkT%d" % n) for n in range(4)]
    vT = [P.sb([128, 512], BF16, name="vT%d" % n) for n in range(4)]
    actf = P.sb([128, 512], F32, name="actf") if False else None
    gz = P.sb([64, 8, 512], F32, name="gz")
    Sf = P.sb([128, 4, 128], F32, name="Sf")
    Sb = P.sb([128, 4, 128], BF16, name="Sb")
    P.op("pool", lambda e: e.memset(Sf[:], 0.0), writes=[Sf])
    P.op("pool", lambda e: e.memset(Sb[:], 0.0), writes=[Sb])
    sm = lambda name, dtype=F32: P.sb([64, 8, 4], dtype, name=name)
    beta, nbeta, xg, graw, gcum, eg, kds, bw = [sm(n) for n in ("beta", "nbeta", "xg", "graw", "gcum", "eg", "kds", "bw")]
    egl = P.sb([128, 32], F32, name="egl")
    t64 = lambda name, dtype=F32: P.sb([64, 4, 64], dtype, name=name)
    Dg, d1, d2, decT, dec, Y0 = [t64(n) for n in ("Dg", "d1", "d2", "decT", "dec", "Y0")]
    XX = [t64("XX%d" % i) for i in range(2)]
    YY = [t64("YY%d" % i) for i in range(2)]
    RR = [t64("RR%d" % i) for i in range(2)]
    HB = [(t64("TT%d" % i, BF16), t64("attnT%d" % i, BF16), P.sb([64, 4, 128], BF16, name="kdec%d" % i),
           P.sb([64, 4, 128], BF16, name="bv%d" % i), P.sb([128, 4, 64], BF16, name="nwT%d" % i)) for i in range(2)]
    kbw = P.sb([64, 4, 128], BF16, name="kbw")
    osb = P.sb([64, 4, 128], F32, name="osb")
    vn = P.sb([64, 512], BF16, name="vn")
    Oall = P.sb([64, 32, 128], F32, name="Oall")
    Osq = P.sb([64, 4, 128], F32, name="Osq")
    ss = P.sb([64, 32], F32, name="ss")
    ot = P.sb([64, 8, 512], BF16, name="ot")
    xv = xT_d.rearrange("(c p) t -> p c t", p=128)
    ov = o_d.rearrange("(c p) n -> p c n", p=64)
    pba = ps[2][0:64, 0:64]
    pgc = ps[2][0:64, 64:96]
    pgl = ps[2][:, 96:128]
    pX0 = ps[2][0:64, 128:384]
    pKQ = ps[3][0:64, 0:256]
    pG = ps[3][0:64, 256:512]
    pX = ps[4][0:64, 0:256]
    pY = ps[4][0:64, 256:512]
    pR = ps[5][0:64, 0:256]
    pwT = ps[5][:, 256:512]
    pVN = ps[6][0:64, :]
    pO = ps[0][0:64, :]
    pSU = ps[1]
    out_keys = []

    def finish_now(extra):
        P.dma("sp", ov[:, 0:8, :], ot[:], reads=list(extra) + [(ot, c) for c in range(8)], writes=[("out", 0)])
        P.wait_all("sp", [("out", 0)])
        P.emit()
        return nc
    if stop == 1:
        return finish_now([ada, a_m, eA, wba, Sf, Sb] + wkeys)

    for ti in range(NT):
        tsl = slice(ti * 512, (ti + 1) * 512)
        P.dma("sp", xt3, xv[:, :, tsl], writes=[("xt", kc, 0) for kc in range(KC)] + [("mvb", 0), ("mvb", 1)])
        rmsnorm_mod(P, C, xt3, "xt", hT, "h", a_m, ada[:, 0:16], 1, tts=[0])
        if stop == 2:
            return finish_now([("h", 0)])
        for c in range(8):
            for kc in range(KC):
                P.op("pe", lambda e, c=c, kc=kc: e.matmul(pba[:, c * 8:(c + 1) * 8], lhsT=hT[:, kc, c * 64:(c + 1) * 64], rhs=wba[:, kc, :],
                                                      start=(kc == 0), stop=(kc == KC - 1)), reads=[("h", 0), wba], writes=[ps[2]])
        pba3 = pba.rearrange("p (c n) -> p c n", n=8)
        P.op("act", lambda e: e.activation(out=beta[:], in_=pba3[:, :, 0:4], func=AF.Sigmoid), reads=[ps[2]], writes=[beta])
        P.op("dve", lambda e: e.tensor_tensor(out=xg[:], in0=pba3[:, :, 4:8], in1=dtb[:], op=ALU.add), reads=[ps[2], dtb], writes=[xg])
        if stop == 30:
            return finish_now([beta, xg])
        P.op("dve", lambda e: e.tensor_scalar(out=nbeta[:], in0=beta[:], scalar1=-1.0, scalar2=None, op0=ALU.mult), reads=[beta], writes=[nbeta])
        P.op("act", lambda e: e.activation(out=xg[:], in_=xg[:], func=AF.Exp), reads=[xg], writes=[xg])
        P.op("act", lambda e: e.activation(out=xg[:], in_=xg[:], func=AF.Ln, bias=C.oneb[0:64, :], scale=1.0), reads=[xg, C.oneb], writes=[xg])
        P.op("dve", lambda e: e.scalar_tensor_tensor(out=graw[:], in0=xg[:], scalar=-1.0, in1=eA[:], op0=ALU.mult, op1=ALU.mult),
             reads=[xg, eA], writes=[graw])
        if stop == 31:
            return finish_now([beta, nbeta, graw])
        g2 = graw[:].rearrange("p c h -> p (c h)")
        P.op("pe", lambda e: e.matmul(pgc, lhsT=U64[:], rhs=g2, start=True, stop=True), reads=[U64, graw], writes=[ps[2]])
        P.op("pe", lambda e: e.matmul(pgl, lhsT=C.ones[0:64, :], rhs=g2, start=True, stop=True), reads=[C.ones, graw], writes=[ps[2]])
        gc2 = gcum[:].rearrange("p c h -> p (c h)")
        P.op("dve", lambda e: e.tensor_copy(out=gc2, in_=pgc), reads=[ps[2]], writes=[gcum])
        P.op("act", lambda e: e.activation(out=eg[:].rearrange("p c h -> p (c h)"), in_=pgc, func=AF.Exp), reads=[ps[2]], writes=[eg])
        P.op("act", lambda e: e.activation(out=egl[:], in_=pgl, func=AF.Exp), reads=[ps[2]], writes=[egl])
        if stop == 32:
            return finish_now([beta, nbeta, graw, gcum, eg, egl])
        kd2 = kds[:].rearrange("p c h -> p (c h)")
        P.op("dve", lambda e: e.tensor_tensor(out=kd2, in0=pgl[0:64, :], in1=gc2, op=ALU.subtract), reads=[ps[2], gcum], writes=[kds])
        if stop == 33:
            return finish_now([beta, nbeta, graw, gcum, eg, egl, kds])
        P.op("act", lambda e: e.activation(out=kd2, in_=kd2, func=AF.Exp), reads=[kds], writes=[kds])
        if stop == 34:
            return finish_now([beta, nbeta, graw, gcum, eg, egl, kds])
        P.op("dve", lambda e: e.tensor_tensor(out=bw[:], in0=beta[:], in1=eg[:], op=ALU.mult), reads=[beta, eg], writes=[bw])
        if stop == 3:
            return finish_now([beta, nbeta, gcum, eg, egl, kds, bw])
        for n in range(8):
            pp = ps[n % 2]
            for kc in range(KC):
                P.op("pe", lambda e, pp=pp, kc=kc, n=n: e.matmul(pp[:], lhsT=wqk[:, kc, n * 128:(n + 1) * 128], rhs=hT[:, kc, :],
                                                            start=(kc == 0), stop=(kc == KC - 1)), reads=[("h", 0)] + wkeys, writes=[pp])
            P.op("pool", lambda e, n=n: e.tensor_copy(out=pre[:, n, 0:3], in_=pre[:, n, 512:515]), reads=[(pre, n)], writes=[(pre, n)])
            P.op("act", lambda e, pp=pp, n=n: e.activation(out=pre[:, n, 3:515], in_=pp[:], func=AF.Copy), reads=[pp], writes=[(pre, n)])
            acc = C.next_tmp()
            P.op("dve", lambda e, acc=acc, n=n: e.tensor_scalar(out=acc[:], in0=pre[:, n, 0:512], scalar1=convw[:, n, 0:1], scalar2=None, op0=ALU.mult),
                 reads=[(pre, n), convw], writes=[acc])
            for i in range(1, 4):
                P.op("dve", lambda e, acc=acc, n=n, i=i: e.scalar_tensor_tensor(out=acc[:], in0=pre[:, n, i:i + 512], scalar=convw[:, n, i:i + 1],
                                                                             in1=acc[:], op0=ALU.mult, op1=ALU.add),
                     reads=[(pre, n), convw, acc], writes=[acc])
            if n >= 4:
                P.op("act", lambda e, acc=acc, n=n: e.activation(out=vT[n - 4][:], in_=acc[:], func=AF.Silu), reads=[acc], writes=[vT[n - 4]])
            else:
                actf = C.next_tmp()
                P.op("act", lambda e, acc=acc, actf=actf: e.activation(out=actf[:], in_=acc[:], func=AF.Silu), reads=[acc], writes=[actf])
                sq = C.next_sq()
                P.op("act", lambda e, sq=sq, actf=actf: e.activation(out=sq[:], in_=actf[:], func=AF.Square), reads=[actf], writes=[sq])
                P.op("pe", lambda e, sq=sq: e.matmul(ps[6][:], lhsT=C.ones[:], rhs=sq[:], start=True, stop=True), reads=[sq, C.ones], writes=[ps[6]])
                P.op("act", lambda e: e.activation(out=C.rstd[:], in_=ps[6][:], func=AF.Ln, scale=1.0, bias=C.epsb[:]),
                     reads=[ps[6], C.epsb], writes=[C.rstd])
                P.op("act", lambda e: e.activation(out=C.rstd[:], in_=C.rstd[:], func=AF.Exp, scale=-0.5), reads=[C.rstd], writes=[C.rstd])
                sc = (128.0 ** -0.5) if n < 2 else 1.0
                P.op("dve", lambda e, n=n, sc=sc, actf=actf: e.scalar_tensor_tensor(out=qkT[n][:], in0=actf[:], scalar=sc, in1=C.rstd[:], op0=ALU.mult, op1=ALU.mult),
                     reads=[actf, C.rstd], writes=[qkT[n]])
        if stop == 4:
            return finish_now(qkT + vT)
        for c in range(8):
            pz = ps[c % 2]
            for kc in range(KC):
                P.op("pe", lambda e, pz=pz, kc=kc, c=c: e.matmul(pz[0:64, :], lhsT=hT[:, kc, c * 64:(c + 1) * 64], rhs=wqk[:, kc, 1024:1536],
                                                            start=(kc == 0), stop=(kc == KC - 1)), reads=[("h", 0)] + wkeys, writes=[pz])
            zt = C.next_tmp()
            P.op("act", lambda e, pz=pz, zt=zt: e.activation(out=zt[0:64, :], in_=pz[0:64, :], func=AF.Silu), reads=[pz], writes=[zt])
            P.op("pool", lambda e, zt=zt, c=c: e.tensor_tensor(out=gz[:, c, :], in0=zt[0:64, :], in1=gn[:], op=ALU.mult), reads=[zt, gn], writes=[(gz, c)])

        if stop == 5:
            return finish_now([(gz, c) for c in range(8)])
        X2 = lambda t: t[:].rearrange("p h i -> p (h i)")

        def local(c, hb):
            TT, attnT, kdec, bv, nwT = hb
            cs = slice(c * 64, (c + 1) * 64)
            for h in range(4):
                P.op("dve", lambda e, h=h, c=c: e.tensor_scalar(out=Dg[:, h, :], in0=I64, scalar1=gcum[:, c, h:h + 1], scalar2=None, op0=ALU.mult),
                     reads=[I4, gcum], writes=[Dg])
            for hq in range(2):
                P.op("pe", lambda e, hq=hq, cs=cs: e.transpose(ptb[0:64, hq * 128:(hq + 1) * 128], qkT[2 + hq][:, cs], identb[:]),
                     reads=[qkT[2 + hq], identb], writes=[ps[7]])
            for hv in range(4):
                P.op("pe", lambda e, hv=hv, cs=cs: e.transpose(ptb[0:64, 256 + hv * 128:256 + (hv + 1) * 128], vT[hv][:, cs], identb[:]),
                     reads=[vT[hv], identb], writes=[ps[7]])
            for hq in range(2):
                P.op("pe", lambda e, hq=hq, cs=cs: e.matmul(pKQ[:, hq * 64:(hq + 1) * 64], lhsT=qkT[2 + hq][:, cs], rhs=qkT[2 + hq][:, cs], start=True, stop=True),
                     reads=[qkT[2 + hq]], writes=[ps[3]])
                P.op("pe", lambda e, hq=hq, cs=cs: e.matmul(pKQ[:, 128 + hq * 64:128 + (hq + 1) * 64], lhsT=qkT[2 + hq][:, cs], rhs=qkT[hq][:, cs], start=True, stop=True),
                     reads=[qkT[2 + hq], qkT[hq]], writes=[ps[3]])
            yield
            P.op("pe", lambda e: e.matmul(pG, lhsT=C.ones[0:64, 0:64], rhs=Dg[:].rearrange("p h i -> p (h i)"), start=True, stop=True),
                 reads=[C.ones, Dg], writes=[ps[3]])
            for hv in range(4):
                hq = hv // 2
                P.op("act", lambda e, hv=hv, hq=hq, c=c: e.activation(out=kdec[:, hv, :], in_=ptb[0:64, hq * 128:(hq + 1) * 128], func=AF.Copy,
                                                                 scale=kds[:, c, hv:hv + 1]), reads=[ps[7], kds], writes=[kdec])
                P.op("act", lambda e, hv=hv, hq=hq, c=c: e.activation(out=kbw[:, hv, :], in_=ptb[0:64, hq * 128:(hq + 1) * 128], func=AF.Copy,
                                                                 scale=bw[:, c, hv:hv + 1]), reads=[ps[7], bw], writes=[kbw])
                P.op("act", lambda e, hv=hv, c=c: e.activation(out=bv[:, hv, :], in_=ptb[0:64, 256 + hv * 128:256 + (hv + 1) * 128], func=AF.Copy,
                                                          scale=beta[:, c, hv:hv + 1]), reads=[ps[7], beta], writes=[bv])
            yield
            pG3 = pG.rearrange("p (h i) -> p h i", h=4)
            for h in range(4):
                P.op("dve", lambda e, h=h, c=c: e.tensor_scalar(out=d1[:, h, :], in0=pG3[:, h, :], scalar1=gcum[:, c, h:h + 1], scalar2=0.0,
                                                           op0=ALU.subtract, op1=ALU.min), reads=[ps[3], gcum], writes=[d1])
                P.op("dve", lambda e, h=h, c=c: e.tensor_scalar(out=d2[:, h, :], in0=pG3[:, h, :], scalar1=gcum[:, c, h:h + 1], scalar2=0.0,
                                                           op0=ALU.subtract, op1=ALU.max), reads=[ps[3], gcum], writes=[d2])
            yield
            P.op("act", lambda e: e.activation(out=decT[:], in_=d1[:], func=AF.Exp), reads=[d1], writes=[decT])
            P.op("act", lambda e: e.activation(out=dec[:], in_=d2[:], func=AF.Exp, scale=-1.0), reads=[d2], writes=[dec])
            yield
            P.op("pool", lambda e: e.tensor_tensor(out=X2(decT), in0=X2(decT), in1=UM4[:], op=ALU.mult), reads=[decT, UM4], writes=[decT])
            P.op("pool", lambda e: e.tensor_tensor(out=X2(dec), in0=X2(dec), in1=LS4[:], op=ALU.mult), reads=[dec, LS4], writes=[dec])
            yield
            for hv in range(4):
                hq = hv // 2
                P.op("dve", lambda e, hv=hv, hq=hq, c=c: e.scalar_tensor_tensor(out=Y0[:, hv, :], in0=pKQ[:, hq * 64:(hq + 1) * 64], scalar=nbeta[:, c, hv:hv + 1],
                                                                           in1=dec[:, hv, :], op0=ALU.mult, op1=ALU.mult),
                     reads=[ps[3], nbeta, dec], writes=[Y0])
            for hv in range(4):
                hq = hv // 2
                P.op("dve", lambda e, hv=hv, hq=hq: e.tensor_tensor(out=attnT[:, hv, :], in0=pKQ[:, 128 + hq * 64:128 + (hq + 1) * 64], in1=decT[:, hv, :], op=ALU.mult),
                     reads=[ps[3], decT], writes=[attnT])
            yield
            for hv in range(4):
                P.op("pe", lambda e, hv=hv: e.transpose(pX0[:, hv * 64:(hv + 1) * 64], Y0[:, hv, :], I64), reads=[Y0, I4], writes=[ps[2]])
            yield
            Xc, Yc, Rc = XX[0], Y0, RR[0]
            P.op("act", lambda e, Xc=Xc: e.activation(out=X2(Xc), in_=pX0, func=AF.Copy), reads=[ps[2]], writes=[Xc])
            P.op("dve", lambda e, Rc=Rc: e.tensor_tensor(out=X2(Rc), in0=pX0, in1=I4[:], op=ALU.add), reads=[ps[2], I4], writes=[Rc])
            yield
            pendR = None
            for lvl in range(5):
                Yn = YY[lvl % 2]
                Xn = XX[(lvl + 1) % 2]
                Rn = RR[(lvl + 1) % 2]
                for hv in range(4):
                    P.op("pe", lambda e, hv=hv, Xc=Xc, Yc=Yc: e.matmul(pY[:, hv * 64:(hv + 1) * 64], lhsT=Xc[:, hv, :], rhs=Yc[:, hv, :], start=True, stop=True),
                         reads=[Xc, Yc], writes=[ps[4]])
                if lvl < 4:
                    for hv in range(4):
                        P.op("pe", lambda e, hv=hv, Xc=Xc, Yc=Yc: e.matmul(pX[:, hv * 64:(hv + 1) * 64], lhsT=Yc[:, hv, :], rhs=Xc[:, hv, :], start=True, stop=True),
                             reads=[Xc, Yc], writes=[ps[4]])
                yield
                P.op("act", lambda e, Yn=Yn: e.activation(out=X2(Yn), in_=pY, func=AF.Copy), reads=[ps[4]], writes=[Yn])
                if lvl < 4:
                    P.op("dve", lambda e, Xn=Xn: e.tensor_copy(out=X2(Xn), in_=pX), reads=[ps[4]], writes=[Xn])
                yield
                for hv in range(4):
                    P.op("pe", lambda e, hv=hv, Rc=Rc: e.matmul(pR[:, hv * 64:(hv + 1) * 64], lhsT=I64, rhs=Rc[:, hv, :], start=True, stop=False),
                         reads=[I4, Rc], writes=[ps[5]])
                    P.op("pe", lambda e, hv=hv, Rc=Rc, Yn=Yn: e.matmul(pR[:, hv * 64:(hv + 1) * 64], lhsT=Yn[:, hv, :], rhs=Rc[:, hv, :], start=False, stop=True),
                         reads=[Yn, Rc], writes=[ps[5]])
                if lvl < 4:
                    P.op("dve", lambda e, Rn=Rn: e.tensor_copy(out=X2(Rn), in_=pR), reads=[ps[5]], writes=[Rn])
                else:
                    yield
                    P.op("dve", lambda e: e.tensor_copy(out=X2(TT), in_=pR), reads=[ps[5]], writes=[TT])
                Xc, Yc, Rc = Xn, Yn, Rn
            yield
            for hv in range(4):
                P.op("pe", lambda e, hv=hv: e.matmul(pwT[:, hv * 64:(hv + 1) * 64], lhsT=kbw[:, hv, :], rhs=TT[:, hv, :], start=True, stop=True),
                     reads=[kbw, TT], writes=[ps[5]])
            yield
            P.op("act", lambda e: e.activation(out=nwT[:].rearrange("p h i -> p (h i)"), in_=pwT, func=AF.Copy, scale=-1.0), reads=[ps[5]], writes=[nwT])
            yield

        def state(c, hb):
            TT, attnT, kdec, bv, nwT = hb
            cs = slice(c * 64, (c + 1) * 64)
            for hv in range(4):
                P.op("pe", lambda e, hv=hv: e.matmul(pVN[:, hv * 128:(hv + 1) * 128], lhsT=TT[:, hv, :], rhs=bv[:, hv, :], start=True, stop=False),
                     reads=[TT, bv], writes=[ps[6]])
                P.op("pe", lambda e, hv=hv: e.matmul(pVN[:, hv * 128:(hv + 1) * 128], lhsT=nwT[:, hv, :], rhs=Sb[:, hv, :], start=False, stop=True),
                     reads=[nwT, Sb], writes=[ps[6]])
            for hv in range(4):
                hq = hv // 2
                P.op("pe", lambda e, hv=hv, hq=hq, cs=cs: e.matmul(pO[:, hv * 128:(hv + 1) * 128], lhsT=qkT[hq][:, cs], rhs=Sb[:, hv, :], start=True, stop=True),
                     reads=[qkT[hq], Sb], writes=[ps[0]])
            yield
            P.op("dve", lambda e: e.tensor_copy(out=vn[:], in_=pVN), reads=[ps[6]], writes=[vn])
            for hv in range(4):
                P.op("act", lambda e, hv=hv, c=c: e.activation(out=osb[:, hv, :], in_=pO[:, hv * 128:(hv + 1) * 128], func=AF.Copy, scale=eg[:, c, hv:hv + 1]),
                     reads=[ps[0], eg], writes=[osb])
            yield
            for hv in range(4):
                P.op("pe", lambda e, hv=hv: e.matmul(pSU[:, hv * 128:(hv + 1) * 128], lhsT=kdec[:, hv, :], rhs=vn[:, hv * 128:(hv + 1) * 128], start=True, stop=True),
                     reads=[kdec, vn], writes=[pSU])
            for hv in range(4):
                P.op("pe", lambda e, hv=hv: e.matmul(pO[:, hv * 128:(hv + 1) * 128], lhsT=attnT[:, hv, :], rhs=vn[:, hv * 128:(hv + 1) * 128],
                                                start=True, stop=True), reads=[attnT, vn], writes=[ps[0]])
            yield
            for hv in range(4):
                P.op("dve", lambda e, hv=hv, c=c: e.scalar_tensor_tensor(out=Sf[:, hv, :], in0=Sf[:, hv, :], scalar=egl[:, c * 4 + hv:c * 4 + hv + 1],
                                                                    in1=pSU[:, hv * 128:(hv + 1) * 128], op0=ALU.mult, op1=ALU.add),
                     reads=[Sf, egl, pSU], writes=[Sf])
            yield
            P.op("act", lambda e: e.activation(out=Sb[:], in_=Sf[:], func=AF.Copy), reads=[Sf], writes=[Sb])
            P.op("dve", lambda e, c=c: e.tensor_tensor(out=Oall[:, c * 4:(c + 1) * 4, :], in0=osb[:], in1=pO.rearrange("p (h d) -> p h d", h=4), op=ALU.add),
                 reads=[ps[0], osb], writes=[Oall])
            yield

        def drive(gens):
            gens = [g for g in gens if g is not None]
            while gens:
                for g in list(gens):
                    try:
                        next(g)
                    except StopIteration:
                        gens.remove(g)
        drive([local(0, HB[0])])
        for c in range(8):
            drive([state(c, HB[c % 2]), local(c + 1, HB[(c + 1) % 2]) if c < 7 else None])

        for c in range(8):
            P.op("pool", lambda e, c=c: e.tensor_tensor(out=Osq[:], in0=Oall[:, c * 4:(c + 1) * 4, :], in1=Oall[:, c * 4:(c + 1) * 4, :], op=ALU.mult),
                 reads=[Oall], writes=[Osq])
            P.op("dve", lambda e, c=c: e.tensor_reduce(out=ss[:, c * 4:(c + 1) * 4], in_=Osq[:], axis=AX.X, op=ALU.add), reads=[Osq], writes=[ss])
        P.op("act", lambda e: e.activation(out=ss[:], in_=ss[:], func=AF.Ln, scale=1.0 / 128.0, bias=C.epsb[0:64, :]), reads=[ss, C.epsb], writes=[ss])
        P.op("act", lambda e: e.activation(out=ss[:], in_=ss[:], func=AF.Exp, scale=-0.5), reads=[ss], writes=[ss])
        for c in range(8):
            for hv in range(4):
                eng = "dve"
                P.op(eng, lambda e, c=c, hv=hv: e.scalar_tensor_tensor(out=ot[:, c, hv * 128:(hv + 1) * 128], in0=Oall[:, c * 4 + hv, :],
                                                                      scalar=ss[:, c * 4 + hv:c * 4 + hv + 1], in1=gz[:, c, hv * 128:(hv + 1) * 128],
                                                                      op0=ALU.mult, op1=ALU.mult),
                     reads=[Oall, ss, (gz, c)], writes=[(ot, c)])
        P.dma("sp", ov[:, ti * 8:(ti + 1) * 8, :], ot[:], reads=[(ot, c) for c in range(8)], writes=[("out", ti)])
        out_keys.append(("out", ti))
    P.wait_all("sp", out_keys)
    P.emit()
    return nc


def stage1_inputs(inp, i, consts, NT=16):
    T = NT * 512
    w = inp["gdn_w_in"][0]
    q = w[:, 256 * i:256 * i + 256]
    k = w[:, 2048 + 256 * i:2048 + 256 * i + 256]
    v = w[:, 4096 + 512 * i:4096 + 512 * i + 512]
    z = w[:, 8192 + 512 * i:8192 + 512 * i + 512]
    wba = np.concatenate([w[:, 12288 + 4 * i:12288 + 4 * i + 4], w[:, 12320 + 4 * i:12320 + 4 * i + 4]], axis=1)
    cw = inp["gdn_conv"][0]
    chans = np.concatenate([np.arange(256 * i, 256 * i + 256), 2048 + np.arange(256 * i, 256 * i + 256), 4096 + np.arange(512 * i, 512 * i + 512)])
    convw = np.ascontiguousarray(cw[:, chans].reshape(4, 8, 128).transpose(2, 1, 0))
    rep = lambda a: np.ascontiguousarray(np.broadcast_to(a[None, None, :], (64, 8, 4)), dtype=np.float32)
    m = dict(xT=np.ascontiguousarray(inp["x"][0][:T].T), c=col_layout(inp["c"][0]),
             adaW=consts["adaW0"], adaB=col_layout(inp["ada_b"][0][0:4096]),
             nmix=col_layout(inp["norm_mix"][0]), wqkvz=np.ascontiguousarray(np.concatenate([q, k, v, z], axis=1)),
             wba=np.ascontiguousarray(wba), convw=convw, alog=rep(inp["gdn_a_log"][0][4 * i:4 * i + 4]), dtb=rep(inp["gdn_dt_bias"][0][4 * i:4 * i + 4]),
             gnorm=np.ascontiguousarray(np.broadcast_to(np.tile(inp["gdn_norm"][0], 4)[None, :], (64, 512)), dtype=np.float32))
    m.update({k: v for k, v in consts.items() if k != "adaW0"})
    return m


def _run(nc, maps):
    res = run_bass_kernel_spmd(nc, maps, core_ids=list(range(NCORES)))
    return res.results


def kernel(**inp):
    import ml_dtypes
    inp = {k: np.asarray(v) for k, v in inp.items()}
    x = inp["x"][0]
    cc = col_layout(inp["c"][0])
    consts = stage1_host_consts()
    consts["adaW0"] = mv_layout(inp["ada_w"][0][:, 0:4096])
    r1 = _run(build_stage1(), [stage1_inputs(inp, i, consts) for i in range(NCORES)])
    o_full = np.concatenate([r1[i]["o"] for i in range(NCORES)], axis=1)
    mv = dict(a0=mv_layout(inp["ada_w"][0][:, 4096:]), kv=mv_layout(inp["kv_ada_w"]), a1m=mv_layout(inp["ada_w"][1][:, 0:4096]),
              a1=mv_layout(inp["ada_w"][1][:, 4096:]), out=mv_layout(inp["out_ada_w"]))
    maps = []
    fb = np.ascontiguousarray(np.broadcast_to(inp["forget_b"][None, :], (128, 16)), dtype=np.float32)
    for i in range(NCORES):
        ts = slice(i * TOK, (i + 1) * TOK)
        maps.append(dict(xT=np.ascontiguousarray(x[ts].T), oT=np.ascontiguousarray(o_full[ts].T), c=cc,
                         adaW=mv["a0"], adaB=col_layout(inp["ada_b"][0][4096:]),
                         kvaW=mv["kv"], kvaB=col_layout(inp["kv_ada_b"]), nffn=col_layout(inp["norm_ffn"][0]),
                         nkv=col_layout(inp["kv_norm"]), wout=inp["gdn_w_out"][0], win=inp["ffn_w_in"][0], wo2=inp["ffn_w_out"][0],
                         kvw=inp["kv_w"], knorm=col_layout(inp["k_norm"]), fb=fb))
    r2 = _run(build_stage2(), maps)
    x1 = np.concatenate([r2[i]["x1T"].T for i in range(NCORES)], axis=0)
    KT = np.concatenate([r2[i]["kT"] for i in range(NCORES)], axis=1).reshape(2, 256, SEQ)
    V = np.concatenate([r2[i]["v"] for i in range(NCORES)], axis=0)
    lf = np.concatenate([r2[i]["lf"] for i in range(NCORES)], axis=0)
    U = (np.arange(128)[:, None] <= np.arange(128)[None, :]).astype(np.float32)
    toks = [np.concatenate([np.arange(b * 128, (b + 1) * 128) for b in core_blocks(i)]) for i in range(NCORES)]
    maps = []
    for i in range(NCORES):
        masks, selx = stage3a_host_consts(i)
        maps.append(dict(xT=np.ascontiguousarray(x1[toks[i]].T), c=cc, adaW=mv["a1m"],
                         adaB=col_layout(inp["ada_b"][1][0:4096]), nmix=col_layout(inp["norm_mix"][1]), fwin=inp["fox_w_in"][0],
                         qnorm=col_layout(inp["q_norm"][0]), KT=np.ascontiguousarray(KT), V=V, lf=lf, U=U, masks=masks, selx=selx))
    r3 = _run(build_stage3a(), maps)
    maps = []
    for i in range(NCORES):
        maps.append(dict(xT=np.ascontiguousarray(x1[toks[i]].T), oT=r3[i]["ogT"], c=cc,
                         adaW=mv["a1"], adaB=col_layout(inp["ada_b"][1][4096:]),
                         kvaW=mv["out"], kvaB=col_layout(inp["out_ada_b"]), nffn=col_layout(inp["norm_ffn"][1]),
                         nkv=col_layout(inp["out_norm"]), wout=inp["fox_w_out"][0], win=inp["ffn_w_in"][1], wo2=inp["ffn_w_out"][1]))
    r4 = _run(build_stage2(final=True), maps)
    out = np.zeros((1, SEQ, D), np.float32)
    for i in range(NCORES):
        out[0, toks[i]] = r4[i]["x1T"].T
    return out
```

```python
import contextlib
import numpy as np
import concourse.bass as bass
import concourse.mybir as mybir
from concourse.bass_utils import run_bass_kernel_spmd

F32 = mybir.dt.float32
BF16 = mybir.dt.bfloat16
AF = mybir.ActivationFunctionType
ALU = mybir.AluOpType
AX = mybir.AxisListType

NCORES = 8
SEM_LIMIT = 30000


class Prog:
    ENGS = ("pe", "act", "dve", "pool", "sp")

    def __init__(self, nc, n_dma_sems=40, self_sync=True):
        self.nc = nc
        self.stack = contextlib.ExitStack()
        self.streams = {e: [] for e in self.ENGS}
        self.cnt = {e: 0 for e in self.ENGS}
        self.esem = {e: nc.alloc_semaphore("s_" + e + "0") for e in self.ENGS}
        self.egen = {e: 0 for e in self.ENGS}
        self.seen = {e: {} for e in self.ENGS}
        self.lastw = {}
        self.readers = {}
        self.self_sync = self_sync
        self.dsem = [nc.alloc_semaphore("s_dma%d" % i) for i in range(n_dma_sems)]
        self.dcnt = [0] * n_dma_sems
        self.drr = 0
        self.sems = {}
        for e in self.ENGS:
            self.sems[id(self.esem[e])] = self.esem[e]
        for s in self.dsem:
            self.sems[id(s)] = s
        self.n_ins = 0
        self.uid = 0

    def sb(self, shape, dtype, name=None):
        self.uid += 1
        return self.stack.enter_context(self.nc.sbuf_tensor(name or "sb%d" % self.uid, list(shape), dtype))

    def ps(self, shape, dtype, name=None):
        self.uid += 1
        return self.stack.enter_context(self.nc.psum_tensor(name or "ps%d" % self.uid, list(shape), dtype))

    @staticmethod
    def _key(k):
        if isinstance(k, tuple):
            return tuple(Prog._key(x) for x in k)
        if isinstance(k, (str, int)):
            return k
        return k.name

    def _deps(self, reads, writes):
        need = {}
        raw = {}
        reads = [self._key(k) for k in reads]
        writes = [self._key(k) for k in writes]

        def add(d, sk, v):
            if d.get(sk, 0) < v:
                d[sk] = v
        for k in reads:
            lw = self.lastw.get(k)
            if lw is not None:
                add(need, *lw)
                add(raw, *lw)
        for k in writes:
            lw = self.lastw.get(k)
            if lw is not None:
                add(need, *lw)
            for sk, v in self.readers.get(k, {}).items():
                add(need, sk, v)
        return need, raw

    def _emit_waits(self, eng, deps):
        need, raw = deps
        own = id(self.esem[eng])
        for sk, v in need.items():
            if sk == own:
                if eng == "pe" or not self.self_sync:
                    continue
            if self.seen[eng].get(sk, 0) >= v:
                continue
            self.seen[eng][sk] = v
            sem = self.sems[sk]
            self.streams[eng].append(lambda e, sem=sem, v=v: e.wait_ge(sem, v))
            self.n_ins += 1

    def _record(self, reads, writes, sk, v):
        reads = [self._key(k) for k in reads]
        writes = [self._key(k) for k in writes]
        for k in writes:
            self.lastw[k] = (sk, v)
            self.readers[k] = {}
        for k in reads:
            r = self.readers.setdefault(k, {})
            if r.get(sk, 0) < v:
                r[sk] = v

    def _excl(self, eng, reads, writes):
        extra = {}
        for k in reads:
            k = self._key(k)
            if isinstance(k, str) and k.startswith("psb"):
                for sk, v in self.readers.get(k, {}).items():
                    if sk != id(self.esem[eng]) and extra.get(sk, 0) < v:
                        extra[sk] = v
        return extra

    def op(self, eng, fn, reads=(), writes=()):
        extra = self._excl(eng, reads, writes)
        need, raw = self._deps(reads, writes)
        for sk, v in extra.items():
            if need.get(sk, 0) < v:
                need[sk] = v
        self._emit_waits(eng, (need, raw))
        if self.cnt[eng] >= SEM_LIMIT:
            self.egen[eng] += 1
            s = self.nc.alloc_semaphore("s_%s%d" % (eng, self.egen[eng]))
            self.esem[eng] = s
            self.sems[id(s)] = s
            self.cnt[eng] = 0
        self.cnt[eng] += 1
        sem = self.esem[eng]
        v = self.cnt[eng]
        self.streams[eng].append(lambda e, fn=fn, sem=sem: fn(e).then_inc(sem, 1))
        self.n_ins += 1
        self._record(reads, writes, id(sem), v)

    def dma(self, eng, out, in_, reads=(), writes=(), **kw):
        half = len(self.dsem) // 2
        if eng == "pool":
            self.drr_sw = (getattr(self, "drr_sw", -1) + 1) % half
            i = half + self.drr_sw
        else:
            self.drr = (self.drr + 1) % half
            i = self.drr
        sem = self.dsem[i]
        need, raw = self._deps(reads, writes)
        if self.dcnt[i] > 0:
            sk = id(sem)
            if need.get(sk, 0) < self.dcnt[i]:
                need[sk] = self.dcnt[i]
        self._emit_waits(eng, (need, raw))
        self.dcnt[i] += 16
        v = self.dcnt[i]
        self.streams[eng].append(
            lambda e, out=out, in_=in_, sem=sem, kw=kw: e.dma_start(out=out, in_=in_, **kw).then_inc(sem, 16))
        self.n_ins += 1
        self._record(reads, writes, id(sem), v)

    def wait_all(self, eng, keys):
        need, _ = self._deps((), keys)
        for sk, v in need.items():
            if self.seen[eng].get(sk, 0) >= v:
                continue
            self.seen[eng][sk] = v
            sem = self.sems[sk]
            self.streams[eng].append(lambda e, sem=sem, v=v: e.wait_ge(sem, v))

    def end_barrier(self, eng="sp"):
        for e2 in self.ENGS:
            if e2 == eng or self.cnt[e2] == 0:
                continue
            sem, v = self.esem[e2], self.cnt[e2]
            self.streams[eng].append(lambda e, sem=sem, v=v: e.wait_ge(sem, v))
        for i, sem in enumerate(self.dsem):
            if self.dcnt[i] > 0:
                v = self.dcnt[i]
                self.streams[eng].append(lambda e, sem=sem, v=v: e.wait_ge(sem, v))

    def emit(self):
        self.end_barrier("sp")
        with self.nc.Block() as block:
            @block.tensor
            def _(e):
                for f in self.streams["pe"]:
                    f(e)

            @block.scalar
            def _(e):
                for f in self.streams["act"]:
                    f(e)

            @block.vector
            def _(e):
                for f in self.streams["dve"]:
                    f(e)

            @block.gpsimd
            def _(e):
                for f in self.streams["pool"]:
                    f(e)

            @block.sync
            def _(e):
                for f in self.streams["sp"]:
                    f(e)
        self.stack.close()


D = 2048
KC = D // 128
EPS = 1e-6
FFN_H = 5632


def mv_layout(w):
    w = np.asarray(w, dtype=np.float32)
    noc = w.shape[1] // 128
    return np.ascontiguousarray(w.reshape(KC, 128, noc, 128).transpose(1, 2, 0, 3).reshape(128, noc, KC * 128))


def col_layout(v):
    v = np.ascontiguousarray(v, dtype=np.float32)
    return np.ascontiguousarray(v.reshape(-1, 128).T)


class Ctx:
    def __init__(self, P, mvblk=None):
        self.P = P
        self.ps = [P.ps([128, 512], F32, name="psb%d" % i) for i in range(8)]
        self.ones = P.sb([128, 128], F32, name="ones_f")
        P.op("pool", lambda e: e.memset(self.ones[:], 1.0), writes=[self.ones])
        self.sq = [P.sb([128, 512], F32, name="sq%d" % i) for i in range(2)]
        self.sqi = 0
        self.rstd = P.sb([128, 512], F32, name="rstd")
        self.tmp = [P.sb([128, 512], F32, name="tmpf%d" % i) for i in range(2)]
        self.tmpi = 0
        if mvblk is None:
            mv = [P.sb([128, KC, 128], F32, name="mvblk%d" % i) for i in range(2)]
            mvblk = [(t[:], t) for t in mv]
        self.mvblk = mvblk
        self.mvi = 0
        self.epsb = P.sb([128, 1], F32, name="epsb")
        P.op("pool", lambda e: e.memset(self.epsb[:], EPS), writes=[self.epsb])
        self.oneb = P.sb([128, 1], F32, name="oneb")
        P.op("pool", lambda e: e.memset(self.oneb[:], 1.0), writes=[self.oneb])

    def next_sq(self):
        self.sqi = (self.sqi + 1) % len(self.sq)
        return self.sq[self.sqi]

    def next_tmp(self):
        self.tmpi = (self.tmpi + 1) % len(self.tmp)
        return self.tmp[self.tmpi]


def load_cond(P, C, c_dram):
    craw = P.sb([128, KC], F32, name="craw")
    cond = P.sb([128, KC], F32, name="cond")
    P.dma("sp", craw[:], c_dram, writes=[craw])
    P.op("act", lambda e: e.activation(out=cond[:], in_=craw[:], func=AF.Silu), reads=[craw], writes=[cond])
    return cond


def matvec(P, C, w_dram, ncols, cond, bias_tile, out_tile, ps):
    noc = ncols // 128
    for oc in range(noc):
        C.mvi ^= 1
        blk, bkey = C.mvblk[C.mvi]
        P.dma("sp", blk, w_dram[:, oc, :].rearrange("p (kc n) -> p kc n", kc=KC), writes=[bkey])
        for kc in range(KC):
            P.op("pe", lambda e, blk=blk, kc=kc, oc=oc: e.matmul(
                ps[:, oc:oc + 1], lhsT=blk[:, kc, :], rhs=cond[:, kc:kc + 1], start=(kc == 0), stop=(kc == KC - 1)),
                reads=[bkey, cond], writes=[ps])
    P.op("dve", lambda e: e.tensor_tensor(out=out_tile[:, 0:noc], in0=ps[:, 0:noc], in1=bias_tile[:, 0:noc], op=ALU.add),
         reads=[ps, bias_tile], writes=[out_tile])


def mod_coeffs(P, normw, scale_ap, name, adakey):
    a = P.sb([128, KC], F32, name=name)
    P.op("dve", lambda e: e.scalar_tensor_tensor(out=a[:], in0=scale_ap, scalar=1.0, in1=normw[:], op0=ALU.add, op1=ALU.mult),
         reads=[normw, adakey], writes=[a])
    return a


def rms_rstd(P, C, ps, n_feat, T, rkeys):
    P.op("act", lambda e: e.activation(out=C.rstd[:, 0:T], in_=ps[:, 0:T], func=AF.Ln, scale=1.0 / n_feat, bias=C.epsb[:]),
         reads=[ps, C.epsb] + list(rkeys), writes=[C.rstd])
    P.op("act", lambda e: e.activation(out=C.rstd[:, 0:T], in_=C.rstd[:, 0:T], func=AF.Exp, scale=-0.5),
         reads=[C.rstd], writes=[C.rstd])


def rmsnorm_mod(P, C, xT, xkey, hT, hkey, acol, bcol, ntt, T=512, inplace=False, tts=None, xoff=None):
    ps = C.ps[7]
    for tt in (tts if tts is not None else range(ntt)):
        ts = slice(tt * T, (tt + 1) * T)
        hs = ts
        if xoff is not None:
            ts = slice(xoff, xoff + T)
        for kc in range(KC):
            sq = C.next_sq()
            P.op("act", lambda e, sq=sq, kc=kc, ts=ts: e.activation(out=sq[:, 0:T], in_=xT[:, kc, ts], func=AF.Square),
                 reads=[(xkey, kc, tt)], writes=[sq])
            P.op("pe", lambda e, sq=sq, kc=kc: e.matmul(ps[:, 0:T], lhsT=C.ones[:], rhs=sq[:, 0:T], start=(kc == 0), stop=(kc == KC - 1)),
                 reads=[sq, C.ones], writes=[ps])
        rms_rstd(P, C, ps, D, T, [])
        for kc in range(KC):
            tmp = C.next_tmp()
            P.op("dve", lambda e, tmp=tmp, kc=kc, ts=ts: e.scalar_tensor_tensor(
                out=tmp[:, 0:T], in0=xT[:, kc, ts], scalar=acol[:, kc:kc + 1], in1=C.rstd[:, 0:T], op0=ALU.mult, op1=ALU.mult),
                reads=[(xkey, kc, tt), acol, C.rstd], writes=[tmp])
            P.op("act", lambda e, tmp=tmp, kc=kc, hs=hs: e.activation(
                out=hT[:, kc, hs], in_=tmp[:, 0:T], func=AF.Identity, bias=bcol[:, kc:kc + 1], scale=1.0),
                reads=[tmp, bcol], writes=[(xkey, kc, tt) if inplace else (hkey, tt)])


def ffn(P, C, xT, xkey, hT, hkey, gcol, gkey, w_in_dram, w_out_dram, wA, wB, hid, ntt, T=512):
    nblk = FFN_H // 256
    wiv = w_in_dram.rearrange("(kc p) n -> p kc n", p=128)
    wov = w_out_dram.rearrange("(c p) n -> p c n", p=128)
    for j in range(nblk):
        wa = wA[j % 2]
        wb = wB[j % 2]
        hd = hid[j % 2]
        wa3 = wa[:].rearrange("p (kc n) -> p kc n", kc=KC)
        wb3 = wb[:].rearrange("p (c n) -> p c n", c=2)
        P.dma("pool", wa3[:, :, 0:256], wiv[:, :, j * 256:(j + 1) * 256], writes=[(wa, 0)])
        P.dma("pool", wa3[:, :, 256:512], wiv[:, :, FFN_H + j * 256:FFN_H + (j + 1) * 256], writes=[(wa, 1)])
        P.dma("pool", wb3, wov[:, 2 * j:2 * j + 2, :], writes=[wb])
        for tt in range(ntt):
            ts = slice(tt * T, (tt + 1) * T)
            for c2 in range(2):
                pg = C.ps[(2 * tt + c2) % 2]
                pu = C.ps[2 + (2 * tt + c2) % 2]
                for kc in range(KC):
                    P.op("pe", lambda e, pg=pg, kc=kc, c2=c2, ts=ts, wa3=wa3: e.matmul(
                        pg[:, 0:T], lhsT=wa3[:, kc, c2 * 128:(c2 + 1) * 128], rhs=hT[:, kc, ts], start=(kc == 0), stop=(kc == KC - 1)),
                        reads=[(wa, 0), (hkey, tt)], writes=[pg])
                for kc in range(KC):
                    P.op("pe", lambda e, pu=pu, kc=kc, c2=c2, ts=ts, wa3=wa3: e.matmul(
                        pu[:, 0:T], lhsT=wa3[:, kc, 256 + c2 * 128:256 + (c2 + 1) * 128], rhs=hT[:, kc, ts], start=(kc == 0), stop=(kc == KC - 1)),
                        reads=[(wa, 1), (hkey, tt)], writes=[pu])
                tmp = C.next_tmp()
                P.op("act", lambda e, tmp=tmp, pg=pg: e.activation(out=tmp[:, 0:T], in_=pg[:, 0:T], func=AF.Silu),
                     reads=[pg], writes=[tmp])
                P.op("dve", lambda e, tmp=tmp, pu=pu, c2=c2, ts=ts, hd=hd: e.tensor_tensor(
                    out=hd[:, c2, ts], in0=pu[:, 0:T], in1=tmp[:, 0:T], op=ALU.mult),
                    reads=[pu, tmp], writes=[(hd, tt)])
            for oc in range(KC):
                po = C.ps[4 + oc % 3]
                for c2 in range(2):
                    P.op("pe", lambda e, po=po, c2=c2, oc=oc, ts=ts, wb3=wb3, hd=hd: e.matmul(
                        po[:, 0:T], lhsT=wb3[:, c2, oc * 128:(oc + 1) * 128], rhs=hd[:, c2, ts], start=(c2 == 0), stop=(c2 == 1)),
                        reads=[wb, (hd, tt)], writes=[po])
                P.op("dve", lambda e, po=po, oc=oc, ts=ts: e.scalar_tensor_tensor(
                    out=xT[:, oc, ts], in0=po[:, 0:T], scalar=gcol[:, oc:oc + 1], in1=xT[:, oc, ts], op0=ALU.mult, op1=ALU.add),
                    reads=[po, (xkey, oc, tt), gkey], writes=[(xkey, oc, tt)])


TOK = 1024


def build_stage2(stop=99, final=False):
    nc = bass.Bass("TRN2", target_bir_lowering=False)
    dt = nc.dram_tensor
    xT_d = dt("xT", [D, TOK], F32, kind="ExternalInput").ap()
    oT_d = dt("oT", [4096, TOK], BF16, kind="ExternalInput").ap()
    ada_d = dt("ada", [128, 64], F32, kind="ExternalInput").ap()
    kva_d = dt("kva", [128, 32], F32, kind="ExternalInput").ap()
    nffn_d = dt("nffn", [128, KC], F32, kind="ExternalInput").ap()
    nkv_d = dt("nkv", [128, KC], F32, kind="ExternalInput").ap()
    wout_d = dt("wout", [4096, D], F32, kind="ExternalInput").ap()
    win_d = dt("win", [D, 2 * FFN_H], F32, kind="ExternalInput").ap()
    wo2_d = dt("wo2", [FFN_H, D], F32, kind="ExternalInput").ap()
    if not final:
        kvw_d = dt("kvw", [D, 1040], F32, kind="ExternalInput").ap()
        knorm_d = dt("knorm", [128, 2], F32, kind="ExternalInput").ap()
        fb_d = dt("fb", [128, 16], F32, kind="ExternalInput").ap()
    x1T_d = dt("x1T", [D, TOK], F32, kind="ExternalOutput").ap()
    if not final:
        kT_d = dt("kT", [512, TOK], BF16, kind="ExternalOutput").ap()
        v_d = dt("v", [TOK, 512], BF16, kind="ExternalOutput").ap()
        lf_d = dt("lf", [TOK, 16], F32, kind="ExternalOutput").ap()

    P = Prog(nc)
    C = Ctx(P)
    xT = P.sb([128, KC, TOK], F32, name="xT_sb")
    hbuf = P.sb([128, KC * TOK], BF16, name="hbuf")
    hT = hbuf[:].rearrange("p (c t) -> p c t", c=KC)
    wA = [P.sb([128, 8192], BF16, name="wA%d" % i) for i in range(2)]
    wB = [P.sb([128, 4096], BF16, name="wB%d" % i) for i in range(2)]
    hid = [P.sb([128, 2, TOK], BF16, name="hid%d" % i) for i in range(2)]
    small = {}
    smalls = [("nffn", nffn_d, [128, KC]), ("nkv", nkv_d, [128, KC])]
    if not final:
        smalls += [("knorm", knorm_d, [128, 2]), ("fb", fb_d, [128, 16])]
    for nm, d_, shp in smalls:
        t = P.sb(shp, F32, name="sm_" + nm)
        P.dma("sp", t[:], d_, writes=[t])
        small[nm] = t
    xkeys = [("x", kc, tt) for kc in range(KC) for tt in range(2)]
    P.dma("sp", xT[:], xT_d.rearrange("(c p) t -> p c t", p=128), writes=xkeys)
    ada = P.sb([128, 64], F32, name="ada_sb")
    P.dma("sp", ada[:], ada_d, writes=[ada])
    kva = P.sb([128, 32], F32, name="kva_sb")
    P.dma("sp", kva[:], kva_d, writes=[kva])

    def finish():
        P.dma("sp", x1T_d.rearrange("(c p) t -> p c t", p=128), xT[:], reads=xkeys + [ada, kva], writes=["out_x1"])
        P.wait_all("sp", ["out_x1"])
        P.emit()
        return nc
    if stop == 1:
        return finish()
    ov = oT_d.rearrange("(c p) t -> p c t", p=128)
    wov = wout_d.rearrange("(c p) n -> p c n", p=128)
    o3 = hbuf[:].rearrange("p (c t) -> p c t", c=32)
    for tt in range(2):
        ts = slice(tt * 512, (tt + 1) * 512)
        P.dma("sp", o3, ov[:, :, ts], writes=[("h", 0), ("h", 1)])
        for ob in range(8):
            w = wA[ob % 2]
            w3 = w[:].rearrange("p (c n) -> p c n", c=32)
            P.dma("pool", w3, wov[:, :, ob * 256:(ob + 1) * 256], writes=[(w, 0), (w, 1)])
            for o2 in range(2):
                oc = ob * 2 + o2
                po = C.ps[4 + oc % 3]
                for kc in range(32):
                    P.op("pe", lambda e, po=po, kc=kc, o2=o2, w3=w3: e.matmul(
                        po[:], lhsT=w3[:, kc, o2 * 128:(o2 + 1) * 128], rhs=o3[:, kc, :], start=(kc == 0), stop=(kc == 31)),
                        reads=[(w, 0), (w, 1), ("h", 0), ("h", 1)], writes=[po])
                P.op("dve", lambda e, po=po, oc=oc, ts=ts: e.scalar_tensor_tensor(
                    out=xT[:, oc, ts], in0=po[:], scalar=ada[:, oc:oc + 1], in1=xT[:, oc, ts], op0=ALU.mult, op1=ALU.add),
                    reads=[po, ("x", oc, tt), ada], writes=[("x", oc, tt)])

    if stop == 2:
        return finish()
    a_f = mod_coeffs(P, small["nffn"], ada[:, 32:48], "a_f", ada)
    rmsnorm_mod(P, C, xT, "x", hT, "h", a_f, ada[:, 16:32], 2)
    if stop == 3:
        return finish()
    ffn(P, C, xT, "x", hT, "h", ada[:, 48:64], ada, win_d, wo2_d, wA, wB, hid, 2)
    if stop == 4:
        return finish()
    a_kv = mod_coeffs(P, small["nkv"], kva[:, 16:32], "a_kv", kva)
    if final:
        rmsnorm_mod(P, C, xT, "x", xT, "x", a_kv, kva[:, 0:16], 2, inplace=True)
        return finish()
    P.dma("sp", x1T_d.rearrange("(c p) t -> p c t", p=128), xT[:], reads=xkeys, writes=["out_x1"])

    rmsnorm_mod(P, C, xT, "x", hT, "h", a_kv, kva[:, 0:16], 2)
    kvv = kvw_d.rearrange("(kc p) n -> p kc n", p=128)
    wk3 = wA[0][:].rearrange("p (kc n) -> p kc n", kc=KC)
    wv3 = wA[1][:].rearrange("p (kc n) -> p kc n", kc=KC)
    wf = P.sb([128, KC, 16], BF16, name="wf")
    P.dma("pool", wk3, kvv[:, :, 0:512], writes=[(wA[0], 0), (wA[0], 1)])
    P.dma("pool", wv3, kvv[:, :, 512:1024], writes=[(wA[1], 0), (wA[1], 1)])
    P.dma("pool", wf[:], kvv[:, :, 1024:1040], writes=[wf])
    kraw = P.sb([128, 2, 512], F32, name="kraw")
    kout = P.sb([128, 4, TOK], BF16, name="kout")
    for tt in range(2):
        ts = slice(tt * 512, (tt + 1) * 512)
        for kh in range(2):
            for cc in range(2):
                c = kh * 2 + cc
                pk = C.ps[cc]
                for kc in range(KC):
                    P.op("pe", lambda e, pk=pk, kc=kc, c=c, ts=ts: e.matmul(
                        pk[:], lhsT=wk3[:, kc, c * 128:(c + 1) * 128], rhs=hT[:, kc, ts], start=(kc == 0), stop=(kc == KC - 1)),
                        reads=[(wA[0], 0), (wA[0], 1), ("h", tt)], writes=[pk])
                P.op("dve", lambda e, pk=pk, cc=cc: e.tensor_copy(out=kraw[:, cc, :], in_=pk[:]), reads=[pk], writes=[(kraw, cc)])
                sq = C.next_sq()
                P.op("act", lambda e, sq=sq, pk=pk: e.activation(out=sq[:], in_=pk[:], func=AF.Square), reads=[pk], writes=[sq])
                P.op("pe", lambda e, sq=sq, cc=cc: e.matmul(C.ps[7][:], lhsT=C.ones[:], rhs=sq[:], start=(cc == 0), stop=(cc == 1)),
                     reads=[sq, C.ones], writes=[C.ps[7]])
            rms_rstd(P, C, C.ps[7], 256, 512, [])
            for cc in range(2):
                c = kh * 2 + cc
                P.op("dve", lambda e, cc=cc, c=c, ts=ts: e.scalar_tensor_tensor(
                    out=kout[:, c, ts], in0=kraw[:, cc, :], scalar=small["knorm"][:, cc:cc + 1], in1=C.rstd[:], op0=ALU.mult, op1=ALU.mult),
                    reads=[(kraw, cc), small["knorm"], C.rstd], writes=[(kout, c)])
    P.dma("sp", kT_d.rearrange("(c p) t -> p c t", p=128), kout[:], reads=[(kout, c) for c in range(4)], writes=["out_k"])
    vout = P.sb([128, 8, 512], BF16, name="vout")
    lfo = P.sb([128, 8, 16], F32, name="lfo")
    lft = P.sb([128, 8, 16], F32, name="lft")
    for tb in range(8):
        tbs = slice(tb * 128, (tb + 1) * 128)
        pv = C.ps[tb % 2]
        for kc in range(KC):
            P.op("pe", lambda e, pv=pv, kc=kc, tbs=tbs: e.matmul(
                pv[:], lhsT=hT[:, kc, tbs], rhs=wv3[:, kc, :], start=(kc == 0), stop=(kc == KC - 1)),
                reads=[(wA[1], 0), (wA[1], 1), ("h", tb // 4)], writes=[pv])
        P.op("act", lambda e, pv=pv, tb=tb: e.activation(out=vout[:, tb, :], in_=pv[:], func=AF.Copy), reads=[pv], writes=[(vout, tb)])
        pf = C.ps[2 + tb % 2]
        for kc in range(KC):
            P.op("pe", lambda e, pf=pf, kc=kc, tbs=tbs: e.matmul(
                pf[:, 0:16], lhsT=hT[:, kc, tbs], rhs=wf[:, kc, :], start=(kc == 0), stop=(kc == KC - 1)),
                reads=[wf, ("h", tb // 4)], writes=[pf])
        P.op("dve", lambda e, pf=pf, tb=tb: e.tensor_tensor(out=lft[:, tb, :], in0=pf[:, 0:16], in1=small["fb"][:], op=ALU.add),
             reads=[pf, small["fb"]], writes=[(lft, tb)])
    P.op("act", lambda e: e.activation(out=lft[:], in_=lft[:], func=AF.Exp, scale=-1.0), reads=[(lft, tb) for tb in range(8)], writes=[lft])
    P.op("act", lambda e: e.activation(out=lft[:], in_=lft[:], func=AF.Ln, bias=C.oneb[:], scale=1.0), reads=[lft, C.oneb], writes=[lft])
    P.op("dve", lambda e: e.tensor_scalar(out=lfo[:], in0=lft[:], scalar1=-1.0, scalar2=None, op0=ALU.mult), reads=[lft], writes=[lfo])
    P.dma("sp", v_d.rearrange("(b p) n -> p b n", p=128), vout[:], reads=[(vout, tb) for tb in range(8)], writes=["out_v"])
    P.dma("sp", lf_d.rearrange("(b p) n -> p b n", p=128), lfo[:], reads=[lfo], writes=["out_lf"])
    P.wait_all("sp", ["out_x1", "out_k", "out_v", "out_lf"])
    P.emit()
    return nc


def core_blocks(i):
    return [i, 15 - i, 16 + i, 31 - i, 32 + i, 47 - i, 48 + i, 63 - i]


SLOT_EXT = [8, 16, 24, 32, 40, 48, 56, 64]


def stage3a_host_consts(i):
    import ml_dtypes
    blks = core_blocks(i)
    tri = (np.arange(128)[:, None] <= np.arange(128)[None, :]).astype(np.float32)
    masks = np.zeros((128, 8, 8, 128), np.float32)
    selx = np.zeros((8, 128, 64, 16), np.float32)
    for s_, gb in enumerate(blks):
        for m in range(8):
            jb = 8 * s_ + m
            if jb < gb:
                masks[:, s_, m, :] = 1.0
            elif jb == gb:
                masks[:, s_, m, :] = tri
        selx[s_, :, gb, :] = 1.0
    return masks.astype(ml_dtypes.bfloat16), selx


def build_stage3a():
    nc = bass.Bass("TRN2", target_bir_lowering=False)
    dt = nc.dram_tensor
    xT_d = dt("xT", [D, TOK], F32, kind="ExternalInput").ap()
    ada_d = dt("ada", [128, 32], F32, kind="ExternalInput").ap()
    nmix_d = dt("nmix", [128, KC], F32, kind="ExternalInput").ap()
    win_d = dt("fwin", [D, 8192], F32, kind="ExternalInput").ap()
    qn_d = dt("qnorm", [128, 2], F32, kind="ExternalInput").ap()
    K_d = dt("KT", [2, 256, 8192], BF16, kind="ExternalInput").ap()
    V_d = dt("V", [8192, 512], BF16, kind="ExternalInput").ap()
    lf_d = dt("lf", [8192, 16], F32, kind="ExternalInput").ap()
    U_d = dt("U", [128, 128], F32, kind="ExternalInput").ap()
    mask_d = dt("masks", [128, 8, 8, 128], BF16, kind="ExternalInput").ap()
    selx_d = dt("selx", [8, 128, 64, 16], F32, kind="ExternalInput").ap()
    og_d = dt("ogT", [4096, TOK], BF16, kind="ExternalOutput").ap()
    Qs_d = dt("Qs", [4096, TOK], BF16, kind="Internal").ap()
    Gs_d = dt("Gs", [4096, TOK], BF16, kind="Internal").ap()

    P = Prog(nc)
    C = Ctx(P)
    bufK = P.sb([128, 8192], F32, name="bufK")
    bufV = P.sb([128, 16384], BF16, name="bufV")
    xt3 = bufK[:].rearrange("p (c t) -> p c t", c=KC)
    K3 = bufK[:].bitcast(BF16).rearrange("p (c t) -> p c t", c=2)
    hT = bufV[:].rearrange("p (c t) -> p c t", c=KC)
    V3 = bufV[:].rearrange("p (b d) -> p b d", b=64)
    small = {}
    for nm, d_, shp in (("nmix", nmix_d, [128, KC]), ("qnorm", qn_d, [128, 2]), ("U", U_d, [128, 128])):
        t = P.sb(shp, F32, name="sm_" + nm)
        P.dma("sp", t[:], d_, writes=[t])
        small[nm] = t
    ada = P.sb([128, 32], F32, name="ada_sb")
    P.dma("sp", ada[:], ada_d, writes=[ada])
    a_m = mod_coeffs(P, small["nmix"], ada[:, 16:32], "a_m", ada)
    xv = xT_d.rearrange("(c p) t -> p c t", p=128)
    for tt in range(2):
        P.dma("sp", xt3, xv[:, :, tt * 512:(tt + 1) * 512], writes=[("xt", kc, t2) for kc in range(KC) for t2 in range(2)] + ["bufK"])
        rmsnorm_mod(P, C, xt3, "xt", hT, "h", a_m, ada[:, 0:16], 2, tts=[tt], xoff=0)

    wq = [P.sb([128, KC, 256], BF16, name="wq%d" % i) for i in range(2)]
    qraw = P.sb([128, 2, TOK], F32, name="qraw")
    qo = [P.sb([128, 2, TOK], BF16, name="qo%d" % i) for i in range(2)]
    wiv = win_d.rearrange("(kc p) n -> p kc n", p=128)
    Qsv = Qs_d.rearrange("(h c p) t -> h p c t", p=128, c=2)
    Gsv = Gs_d.rearrange("(h c p) t -> h p c t", p=128, c=2)
    for h in range(16):
        for isg in range(2):
            w = wq[isg]
            P.dma("pool", w[:], wiv[:, :, isg * 4096 + h * 256: isg * 4096 + (h + 1) * 256], writes=[w])
            out = qo[isg]
            for tt in range(2):
                ts = slice(tt * 512, (tt + 1) * 512)
                for cc in range(2):
                    pq = C.ps[(2 * tt + cc) % 4]
                    for kc in range(KC):
                        P.op("pe", lambda e, pq=pq, kc=kc, cc=cc, ts=ts, w=w: e.matmul(
                            pq[:], lhsT=w[:, kc, cc * 128:(cc + 1) * 128], rhs=hT[:, kc, ts], start=(kc == 0), stop=(kc == KC - 1)),
                            reads=[w, ("h", tt)], writes=[pq])
                    if isg:
                        P.op("act", lambda e, pq=pq, cc=cc, ts=ts, out=out: e.activation(out=out[:, cc, ts], in_=pq[:], func=AF.Sigmoid),
                             reads=[pq], writes=[(out, cc, tt)])
                    else:
                        P.op("dve", lambda e, pq=pq, cc=cc, ts=ts: e.tensor_copy(out=qraw[:, cc, ts], in_=pq[:]), reads=[pq], writes=[(qraw, cc, tt)])
                        sq = C.next_sq()
                        P.op("act", lambda e, sq=sq, pq=pq: e.activation(out=sq[:], in_=pq[:], func=AF.Square), reads=[pq], writes=[sq])
                        P.op("pe", lambda e, sq=sq, cc=cc: e.matmul(C.ps[7][:], lhsT=C.ones[:], rhs=sq[:], start=(cc == 0), stop=(cc == 1)),
                             reads=[sq, C.ones], writes=[C.ps[7]])
                if not isg:
                    rms_rstd(P, C, C.ps[7], 256, 512, [])
                    for cc in range(2):
                        P.op("dve", lambda e, cc=cc, ts=ts, out=out: e.scalar_tensor_tensor(
                            out=out[:, cc, ts], in0=qraw[:, cc, ts], scalar=small["qnorm"][:, cc:cc + 1], in1=C.rstd[:], op0=ALU.mult, op1=ALU.mult),
                            reads=[(qraw, cc, tt), small["qnorm"], C.rstd], writes=[(out, cc, tt)])
            P.dma("sp", (Gsv if isg else Qsv)[h], out[:], reads=[(out, cc, tt) for cc in range(2) for tt in range(2)],
                  writes=[("QG", isg, h)])

    lft = P.sb([128, 64, 16], F32, name="lft")
    csl = P.sb([128, 64, 16], F32, name="csl")
    tot = P.sb([128, 64, 16], F32, name="tot")
    incl = P.sb([128, 64, 16], F32, name="incl")
    P.dma("sp", lft[:], lf_d.rearrange("(b p) h -> p b h", p=128), writes=[lft])
    lf2 = lft[:].rearrange("p b h -> p (b h)")
    for half in range(2):
        hs = slice(half * 512, (half + 1) * 512)
        pa = C.ps[half]
        pb = C.ps[2 + half]
        P.op("pe", lambda e, pa=pa, hs=hs: e.matmul(pa[:], lhsT=small["U"][:], rhs=lf2[:, hs], start=True, stop=True),
             reads=[small["U"], lft], writes=[pa])
        P.op("pe", lambda e, pb=pb, hs=hs: e.matmul(pb[:], lhsT=C.ones[:], rhs=lf2[:, hs], start=True, stop=True),
             reads=[C.ones, lft], writes=[pb])
        P.op("dve", lambda e, pa=pa, hs=hs: e.tensor_copy(out=csl[:].rearrange("p b h -> p (b h)")[:, hs], in_=pa[:]), reads=[pa], writes=[csl])
        P.op("dve", lambda e, pb=pb, hs=hs: e.tensor_copy(out=tot[:].rearrange("p b h -> p (b h)")[:, hs], in_=pb[:]), reads=[pb], writes=[tot])
    P.op("dve", lambda e: e.tensor_copy(out=incl[:, 0, :], in_=tot[:, 0, :]), reads=[tot], writes=[incl])
    for b in range(1, 64):
        P.op("dve", lambda e, b=b: e.tensor_tensor(out=incl[:, b, :], in0=incl[:, b - 1, :], in1=tot[:, b, :], op=ALU.add),
             reads=[incl, tot], writes=[incl])
    P.op("dve", lambda e: e.tensor_tensor(out=csl[:], in0=csl[:], in1=incl[:], op=ALU.add), reads=[csl, incl], writes=[csl])
    P.op("dve", lambda e: e.tensor_tensor(out=lft[:], in0=tot[:], in1=csl[:], op=ALU.subtract), reads=[csl, tot], writes=[lft])
    Fneg = lft
    fref = P.sb([128, 8, 16], F32, name="fref")
    for s_ in range(8):
        P.dma("sp", tot[:], selx_d[s_], writes=[tot])
        P.op("dve", lambda e: e.tensor_tensor(out=tot[:], in0=tot[:], in1=incl[:], op=ALU.mult), reads=[tot, incl], writes=[tot])
        P.op("dve", lambda e, s_=s_: e.tensor_reduce(out=fref[:, s_, :], in_=tot[:].rearrange("p b h -> p h b"), axis=AX.X, op=ALU.add),
             reads=[tot], writes=[fref])

    masks = P.sb([128, 8, 8, 128], BF16, name="masks_sb")
    P.dma("sp", masks[:], mask_d, writes=[masks])
    onesb = P.sb([128, 128], BF16, name="onesb")
    P.op("pool", lambda e: e.memset(onesb[:], 1.0), writes=[onesb])
    Qt = P.sb([128, 4, 2, TOK], BF16, name="Qt")
    Gt = P.sb([128, 4, 2, TOK], BF16, name="Gt")
    bias = P.sb([128, 64, 4], F32, name="bias")
    pT = [P.sb([128, 4, 128], BF16, name="pT%d" % i) for i in range(3)]
    rec = P.sb([128, 512], F32, name="rec")
    pti = 0
    setsel = 0
    Kv = K_d.rearrange("k (c p) t -> k p c t", p=128)
    Vv = V_d.rearrange("(b p) n -> p b n", p=128)
    ogv = og_d.rearrange("(h c p) t -> p h c t", p=128, c=2)
    for kvh in range(2):
        P.dma("sp", K3, Kv[kvh], writes=["bufK"] + [("xt", kc, tt) for kc in range(KC) for tt in range(2)])
        P.dma("sp", V3, Vv[:, :, kvh * 256:(kvh + 1) * 256], writes=[("h", 0), ("h", 1)])
        for hg in range(2):
            h0 = kvh * 8 + hg * 4
            for j in range(4):
                P.dma("sp", Qt[:, j], Qsv[h0 + j], reads=[("QG", 0, h0 + j)], writes=[Qt])
                P.dma("sp", Gt[:, j], Gsv[h0 + j], reads=[("QG", 1, h0 + j)], writes=[Gt])
            for s_ in range(8):
                qs = slice(s_ * 128, (s_ + 1) * 128)
                ext = SLOT_EXT[s_]
                for j in range(4):
                    P.op("dve", lambda e, j=j, s_=s_, ext=ext, h0=h0: e.tensor_scalar(
                        out=bias[:, 0:ext, j], in0=Fneg[:, 0:ext, h0 + j], scalar1=fref[:, s_, h0 + j:h0 + j + 1], scalar2=0.0,
                        op0=ALU.add, op1=ALU.min), reads=[Fneg, fref], writes=[bias])
                setsel ^= 1
                acc = [C.ps[2 + 3 * setsel + k] for k in range(3)]
                def emit_S(jb):
                    ks = slice(jb * 128, (jb + 1) * 128)
                    pS = C.ps[jb % 2]
                    for cc in range(2):
                        P.op("pe", lambda e, pS=pS, cc=cc, ks=ks, qs=qs: e.matmul(
                            pS[:].rearrange("p (j t) -> p j t", j=4), lhsT=K3[:, cc, ks], rhs=Qt[:, :, cc, qs], start=(cc == 0), stop=(cc == 1)),
                            reads=["bufK", Qt], writes=[pS])
                emit_S(0)
                for jb in range(ext):
                    pS = C.ps[jb % 2]
                    if jb + 1 < ext:
                        emit_S(jb + 1)
                    pti = (pti + 1) % 3
                    pt = pT[pti]
                    for j in range(4):
                        P.op("act", lambda e, pS=pS, pt=pt, j=j, jb=jb: e.activation(
                            out=pt[:, j, :], in_=pS[:, j * 128:(j + 1) * 128], func=AF.Exp, scale=1.0 / 16.0, bias=bias[:, jb, j:j + 1]),
                            reads=[pS, bias], writes=[(pt, j)])
                    m = jb - 8 * s_
                    if m >= 0:
                        for j in range(4):
                            P.op("pool", lambda e, pt=pt, j=j, s_=s_, m=m: e.tensor_tensor(
                                out=pt[:, j, :], in0=pt[:, j, :], in1=masks[:, s_, m, :], op=ALU.mult),
                                reads=[(pt, j), masks], writes=[(pt, j)])
                    pt2 = pt[:].rearrange("p j t -> p (j t)")
                    for k in range(3):
                        lhs = onesb[:] if k == 2 else V3[:, jb, k * 128:(k + 1) * 128]
                        P.op("pe", lambda e, k=k, lhs=lhs, pt2=pt2, jb=jb, ext=ext, acc=acc: e.matmul(
                            acc[k][:], lhsT=lhs, rhs=pt2, start=(jb == 0), stop=(jb == ext - 1)),
                            reads=[(pt, 0), (pt, 1), (pt, 2), (pt, 3), ("h", 0), ("h", 1), onesb], writes=[acc[k]])
                P.op("dve", lambda e, acc=acc: e.reciprocal(out=rec[:], in_=acc[2][:]), reads=[acc[2]], writes=[rec])
                for k in range(2):
                    tmp = C.next_tmp()
                    P.op("dve", lambda e, tmp=tmp, k=k, acc=acc: e.tensor_tensor(out=tmp[:], in0=acc[k][:], in1=rec[:], op=ALU.mult),
                         reads=[acc[k], rec], writes=[tmp])
                    P.op("pool", lambda e, tmp=tmp, k=k, qs=qs: e.tensor_tensor(
                        out=Gt[:, :, k, qs], in0=tmp[:].rearrange("p (j t) -> p j t", j=4), in1=Gt[:, :, k, qs], op=ALU.mult),
                        reads=[tmp, Gt], writes=[Gt])
            P.dma("sp", ogv[:, h0:h0 + 4], Gt[:], reads=[Gt], writes=[("og", h0)])
    P.wait_all("sp", [("og", h0) for h0 in (0, 4, 8, 12)])
    P.emit()
    return nc


SEQ = 8192


def stage1_host_consts():
    import ml_dtypes
    I64 = np.eye(64, dtype=np.float32)
    U64 = (np.arange(64)[:, None] <= np.arange(64)[None, :]).astype(np.float32)
    LS = (np.arange(64)[None, :] < np.arange(64)[:, None]).astype(np.float32)
    return dict(I4=np.ascontiguousarray(np.tile(I64, (1, 4))), U64=U64,
                UM4=np.ascontiguousarray(np.tile(U64, (1, 4))), LS4=np.ascontiguousarray(np.tile(LS, (1, 4))),
                identb=np.eye(128, dtype=np.float32).astype(ml_dtypes.bfloat16))


def build_stage1(NT=16, stop=99):
    nc = bass.Bass("TRN2", target_bir_lowering=False)
    dt = nc.dram_tensor
    T = NT * 512
    xT_d = dt("xT", [D, T], F32, kind="ExternalInput").ap()
    c_d = dt("c", [128, KC], F32, kind="ExternalInput").ap()
    adaW_d = dt("adaW", [128, 32, D], F32, kind="ExternalInput").ap()
    adaB_d = dt("adaB", [128, 32], F32, kind="ExternalInput").ap()
    nmix_d = dt("nmix", [128, KC], F32, kind="ExternalInput").ap()
    w_d = dt("wqkvz", [D, 1536], F32, kind="ExternalInput").ap()
    wba_d = dt("wba", [D, 8], F32, kind="ExternalInput").ap()
    convw_d = dt("convw", [128, 8, 4], F32, kind="ExternalInput").ap()
    alog_d = dt("alog", [64, 8, 4], F32, kind="ExternalInput").ap()
    dtb_d = dt("dtb", [64, 8, 4], F32, kind="ExternalInput").ap()
    gn_d = dt("gnorm", [64, 512], F32, kind="ExternalInput").ap()
    I4_d = dt("I4", [64, 256], F32, kind="ExternalInput").ap()
    U64_d = dt("U64", [64, 64], F32, kind="ExternalInput").ap()
    UM4_d = dt("UM4", [64, 256], F32, kind="ExternalInput").ap()
    LS4_d = dt("LS4", [64, 256], F32, kind="ExternalInput").ap()
    idb_d = dt("identb", [128, 128], BF16, kind="ExternalInput").ap()
    Wx_d = dt("Wx", [128, 28, D], F32, kind="ExternalInput").ap()
    bx_d = dt("bx", [128, 28], F32, kind="ExternalInput").ap()
    adax_d = dt("adax", [128, 28], F32, kind="ExternalOutput").ap()
    o_d = dt("o", [T, 512], BF16, kind="ExternalOutput").ap()

    P = Prog(nc)
    xbuf = P.sb([128, KC * 512], F32, name="xbuf")
    xt3 = xbuf[:].rearrange("p (c t) -> p c t", c=KC)
    mv = [(xbuf[:, i * 2048:(i + 1) * 2048].rearrange("p (c n) -> p c n", c=KC), ("mvb", i)) for i in range(2)]
    C = Ctx(P, mvblk=mv)
    ps = C.ps
    ptb = ps[7][:].bitcast(BF16)

    def ld(name, d_, shp, dtype=F32, eng="sp"):
        t = P.sb(shp, dtype, name="c_" + name)
        P.dma(eng, t[:], d_, writes=[t])
        return t
    adaB = ld("adaB", adaB_d, [128, 32])
    nmix = ld("nmix", nmix_d, [128, KC])
    convw = ld("convw", convw_d, [128, 8, 4])
    alog = ld("alog", alog_d, [64, 8, 4])
    dtb = ld("dtb", dtb_d, [64, 8, 4])
    gn = ld("gn", gn_d, [64, 512])
    I4 = ld("I4", I4_d, [64, 256])
    U64 = ld("U64", U64_d, [64, 64])
    UM4 = ld("UM4", UM4_d, [64, 256])
    LS4 = ld("LS4", LS4_d, [64, 256])
    identb = ld("identb", idb_d, [128, 128], BF16)
    I64 = I4[:, 0:64]
    wqk = P.sb([128, KC, 1536], BF16, name="wqkvz_sb")
    wba = P.sb([128, KC, 8], BF16, name="wba_sb")
    wv_ = w_d.rearrange("(kc p) n -> p kc n", p=128)
    for j in range(3):
        P.dma("pool", wqk[:, :, j * 512:(j + 1) * 512], wv_[:, :, j * 512:(j + 1) * 512], writes=[(wqk, j)])
    P.dma("pool", wba[:], wba_d.rearrange("(kc p) n -> p kc n", p=128), writes=[wba])
    wkeys = [(wqk, j) for j in range(3)]

    cond = load_cond(P, C, c_d)
    ada = P.sb([128, 32], F32, name="ada")
    matvec(P, C, adaW_d, 2 * D, cond, adaB, ada, ps[6])
    a_m = mod_coeffs(P, nmix, ada[:, 16:32], "a_m", ada)
    bx = ld("bx", bx_d, [128, 28])
    adax = P.sb([128, 28], F32, name="adax_sb")
    matvec(P, C, Wx_d, 28 * 128, cond, bx, adax, ps[6])
    P.dma("sp", adax_d, adax[:], reads=[adax], writes=["out_adax"])
    eA = P.sb([64, 8, 4], F32, name="eA")
    P.op("act", lambda e: e.activation(out=eA[:], in_=alog[:], func=AF.Exp), reads=[alog], writes=[eA])

    hT = P.sb([128, KC, 512], BF16, name="hT")
    pre = P.sb([128, 8, 515], F32, name="pre")
    P.op("pool", lambda e: e.memset(pre[:], 0.0), writes=[(pre, n) for n in range(8)])
    qkT = [P.sb([128, 512], BF16, name="qkT%d" % n) for n in range(4)]
    vT = [P.sb([128, 512], BF16, name="vT%d" % n) for n in range(4)]
    actf = P.sb([128, 512], F32, name="actf") if False else None
    gz = P.sb([64, 8, 512], F32, name="gz")
    Sf = P.sb([128, 4, 128], F32, name="Sf")
    Sb = P.sb([128, 4, 128], BF16, name="Sb")
    P.op("pool", lambda e: e.memset(Sf[:], 0.0), writes=[Sf])
    P.op("pool", lambda e: e.memset(Sb[:], 0.0), writes=[Sb])
    sm = lambda name, dtype=F32: P.sb([64, 8, 4], dtype, name=name)
    beta, nbeta, xg, graw, gcum, eg, kds, bw = [sm(n) for n in ("beta", "nbeta", "xg", "graw", "gcum", "eg", "kds", "bw")]
    egl = P.sb([128, 32], F32, name="egl")
    t64 = lambda name, dtype=F32: P.sb([64, 4, 64], dtype, name=name)
    Dg, d1, d2, decT, dec, Y0 = [t64(n) for n in ("Dg", "d1", "d2", "decT", "dec", "Y0")]
    XX = [t64("XX%d" % i) for i in range(2)]
    YY = [t64("YY%d" % i) for i in range(2)]
    RR = [t64("RR%d" % i) for i in range(2)]
    HB = [(t64("TT%d" % i, BF16), t64("attnT%d" % i, BF16), P.sb([64, 4, 128], BF16, name="kdec%d" % i),
           P.sb([64, 4, 128], BF16, name="bv%d" % i), P.sb([128, 4, 64], BF16, name="nwT%d" % i)) for i in range(2)]
    kbw = P.sb([64, 4, 128], BF16, name="kbw")
    osb = P.sb([64, 4, 128], F32, name="osb")
    vn = P.sb([64, 512], BF16, name="vn")
    Oall = P.sb([64, 32, 128], F32, name="Oall")
    Osq = P.sb([64, 4, 128], F32, name="Osq")
    ss = P.sb([64, 32], F32, name="ss")
    ot = P.sb([64, 8, 512], BF16, name="ot")
    xv = xT_d.rearrange("(c p) t -> p c t", p=128)
    ov = o_d.rearrange("(c p) n -> p c n", p=64)
    pba = ps[2][0:64, 0:64]
    pgc = ps[2][0:64, 64:96]
    pgl = ps[2][:, 96:128]
    pX0 = ps[2][0:64, 128:384]
    pKQ = ps[3][0:64, 0:256]
    pG = ps[3][0:64, 256:512]
    pX = ps[4][0:64, 0:256]
    pY = ps[4][0:64, 256:512]
    pR = ps[5][0:64, 0:256]
    pwT = ps[5][:, 256:512]
    pVN = ps[6][0:64, :]
    pO = ps[0][0:64, :]
    pSU = ps[1]
    out_keys = []

    def finish_now(extra):
        P.dma("sp", ov[:, 0:8, :], ot[:], reads=list(extra) + [(ot, c) for c in range(8)], writes=[("out", 0)])
        P.wait_all("sp", [("out", 0)])
        P.emit()
        return nc
    if stop == 1:
        return finish_now([ada, a_m, eA, wba, Sf, Sb] + wkeys)

    for ti in range(NT):
        tsl = slice(ti * 512, (ti + 1) * 512)
        P.dma("sp", xt3, xv[:, :, tsl], writes=[("xt", kc, 0) for kc in range(KC)] + [("mvb", 0), ("mvb", 1)])
        rmsnorm_mod(P, C, xt3, "xt", hT, "h", a_m, ada[:, 0:16], 1, tts=[0])
        if stop == 2:
            return finish_now([("h", 0)])
        for c in range(8):
            for kc in range(KC):
                P.op("pe", lambda e, c=c, kc=kc: e.matmul(pba[:, c * 8:(c + 1) * 8], lhsT=hT[:, kc, c * 64:(c + 1) * 64], rhs=wba[:, kc, :],
                                                      start=(kc == 0), stop=(kc == KC - 1)), reads=[("h", 0), wba], writes=[ps[2]])
        pba3 = pba.rearrange("p (c n) -> p c n", n=8)
        P.op("act", lambda e: e.activation(out=beta[:], in_=pba3[:, :, 0:4], func=AF.Sigmoid), reads=[ps[2]], writes=[beta])
        P.op("dve", lambda e: e.tensor_tensor(out=xg[:], in0=pba3[:, :, 4:8], in1=dtb[:], op=ALU.add), reads=[ps[2], dtb], writes=[xg])
        if stop == 30:
            return finish_now([beta, xg])
        P.op("dve", lambda e: e.tensor_scalar(out=nbeta[:], in0=beta[:], scalar1=-1.0, scalar2=None, op0=ALU.mult), reads=[beta], writes=[nbeta])
        P.op("act", lambda e: e.activation(out=xg[:], in_=xg[:], func=AF.Exp), reads=[xg], writes=[xg])
        P.op("act", lambda e: e.activation(out=xg[:], in_=xg[:], func=AF.Ln, bias=C.oneb[0:64, :], scale=1.0), reads=[xg, C.oneb], writes=[xg])
        P.op("dve", lambda e: e.scalar_tensor_tensor(out=graw[:], in0=xg[:], scalar=-1.0, in1=eA[:], op0=ALU.mult, op1=ALU.mult),
             reads=[xg, eA], writes=[graw])
        if stop == 31:
            return finish_now([beta, nbeta, graw])
        g2 = graw[:].rearrange("p c h -> p (c h)")
        P.op("pe", lambda e: e.matmul(pgc, lhsT=U64[:], rhs=g2, start=True, stop=True), reads=[U64, graw], writes=[ps[2]])
        P.op("pe", lambda e: e.matmul(pgl, lhsT=C.ones[0:64, :], rhs=g2, start=True, stop=True), reads=[C.ones, graw], writes=[ps[2]])
        gc2 = gcum[:].rearrange("p c h -> p (c h)")
        P.op("dve", lambda e: e.tensor_copy(out=gc2, in_=pgc), reads=[ps[2]], writes=[gcum])
        P.op("act", lambda e: e.activation(out=eg[:].rearrange("p c h -> p (c h)"), in_=pgc, func=AF.Exp), reads=[ps[2]], writes=[eg])
        P.op("act", lambda e: e.activation(out=egl[:], in_=pgl, func=AF.Exp), reads=[ps[2]], writes=[egl])
        if stop == 32:
            return finish_now([beta, nbeta, graw, gcum, eg, egl])
        kd2 = kds[:].rearrange("p c h -> p (c h)")
        P.op("dve", lambda e: e.tensor_tensor(out=kd2, in0=pgl[0:64, :], in1=gc2, op=ALU.subtract), reads=[ps[2], gcum], writes=[kds])
        if stop == 33:
            return finish_now([beta, nbeta, graw, gcum, eg, egl, kds])
        P.op("act", lambda e: e.activation(out=kd2, in_=kd2, func=AF.Exp), reads=[kds], writes=[kds])
        if stop == 34:
            return finish_now([beta, nbeta, graw, gcum, eg, egl, kds])
        P.op("dve", lambda e: e.tensor_tensor(out=bw[:], in0=beta[:], in1=eg[:], op=ALU.mult), reads=[beta, eg], writes=[bw])
        if stop == 3:
            return finish_now([beta, nbeta, gcum, eg, egl, kds, bw])
        for n in range(8):
            pp = ps[n % 2]
            for kc in range(KC):
                P.op("pe", lambda e, pp=pp, kc=kc, n=n: e.matmul(pp[:], lhsT=wqk[:, kc, n * 128:(n + 1) * 128], rhs=hT[:, kc, :],
                                                            start=(kc == 0), stop=(kc == KC - 1)), reads=[("h", 0)] + wkeys, writes=[pp])
            P.op("pool", lambda e, n=n: e.tensor_copy(out=pre[:, n, 0:3], in_=pre[:, n, 512:515]), reads=[(pre, n)], writes=[(pre, n)])
            P.op("act", lambda e, pp=pp, n=n: e.activation(out=pre[:, n, 3:515], in_=pp[:], func=AF.Copy), reads=[pp], writes=[(pre, n)])
            acc = C.next_tmp()
            P.op("dve", lambda e, acc=acc, n=n: e.tensor_scalar(out=acc[:], in0=pre[:, n, 0:512], scalar1=convw[:, n, 0:1], scalar2=None, op0=ALU.mult),
                 reads=[(pre, n), convw], writes=[acc])
            for i in range(1, 4):
                P.op("dve", lambda e, acc=acc, n=n, i=i: e.scalar_tensor_tensor(out=acc[:], in0=pre[:, n, i:i + 512], scalar=convw[:, n, i:i + 1],
                                                                             in1=acc[:], op0=ALU.mult, op1=ALU.add),
                     reads=[(pre, n), convw, acc], writes=[acc])
            if n >= 4:
                P.op("act", lambda e, acc=acc, n=n: e.activation(out=vT[n - 4][:], in_=acc[:], func=AF.Silu), reads=[acc], writes=[vT[n - 4]])
            else:
                actf = C.next_tmp()
                P.op("act", lambda e, acc=acc, actf=actf: e.activation(out=actf[:], in_=acc[:], func=AF.Silu), reads=[acc], writes=[actf])
                sq = C.next_sq()
                P.op("act", lambda e, sq=sq, actf=actf: e.activation(out=sq[:], in_=actf[:], func=AF.Square), reads=[actf], writes=[sq])
                P.op("pe", lambda e, sq=sq: e.matmul(ps[6][:], lhsT=C.ones[:], rhs=sq[:], start=True, stop=True), reads=[sq, C.ones], writes=[ps[6]])
                P.op("act", lambda e: e.activation(out=C.rstd[:], in_=ps[6][:], func=AF.Ln, scale=1.0, bias=C.epsb[:]),
                     reads=[ps[6], C.epsb], writes=[C.rstd])
                P.op("act", lambda e: e.activation(out=C.rstd[:], in_=C.rstd[:], func=AF.Exp, scale=-0.5), reads=[C.rstd], writes=[C.rstd])
                sc = (128.0 ** -0.5) if n < 2 else 1.0
                P.op("dve", lambda e, n=n, sc=sc, actf=actf: e.scalar_tensor_tensor(out=qkT[n][:], in0=actf[:], scalar=sc, in1=C.rstd[:], op0=ALU.mult, op1=ALU.mult),
                     reads=[actf, C.rstd], writes=[qkT[n]])
        if stop == 4:
            return finish_now(qkT + vT)
        for c in range(8):
            pz = ps[c % 2]
            for kc in range(KC):
                P.op("pe", lambda e, pz=pz, kc=kc, c=c: e.matmul(pz[0:64, :], lhsT=hT[:, kc, c * 64:(c + 1) * 64], rhs=wqk[:, kc, 1024:1536],
                                                            start=(kc == 0), stop=(kc == KC - 1)), reads=[("h", 0)] + wkeys, writes=[pz])
            zt = C.next_tmp()
            P.op("act", lambda e, pz=pz, zt=zt: e.activation(out=zt[0:64, :], in_=pz[0:64, :], func=AF.Silu), reads=[pz], writes=[zt])
            P.op("pool", lambda e, zt=zt, c=c: e.tensor_tensor(out=gz[:, c, :], in0=zt[0:64, :], in1=gn[:], op=ALU.mult), reads=[zt, gn], writes=[(gz, c)])

        if stop == 5:
            return finish_now([(gz, c) for c in range(8)])
        X2 = lambda t: t[:].rearrange("p h i -> p (h i)")

        def local(c, hb):
            TT, attnT, kdec, bv, nwT = hb
            cs = slice(c * 64, (c + 1) * 64)
            for h in range(4):
                P.op("dve", lambda e, h=h, c=c: e.tensor_scalar(out=Dg[:, h, :], in0=I64, scalar1=gcum[:, c, h:h + 1], scalar2=None, op0=ALU.mult),
                     reads=[I4, gcum], writes=[Dg])
            for hq in range(2):
                P.op("pe", lambda e, hq=hq, cs=cs: e.transpose(ptb[0:64, hq * 128:(hq + 1) * 128], qkT[2 + hq][:, cs], identb[:]),
                     reads=[qkT[2 + hq], identb], writes=[ps[7]])
            for hv in range(4):
                P.op("pe", lambda e, hv=hv, cs=cs: e.transpose(ptb[0:64, 256 + hv * 128:256 + (hv + 1) * 128], vT[hv][:, cs], identb[:]),
                     reads=[vT[hv], identb], writes=[ps[7]])
            for hq in range(2):
                P.op("pe", lambda e, hq=hq, cs=cs: e.matmul(pKQ[:, hq * 64:(hq + 1) * 64], lhsT=qkT[2 + hq][:, cs], rhs=qkT[2 + hq][:, cs], start=True, stop=True),
                     reads=[qkT[2 + hq]], writes=[ps[3]])
                P.op("pe", lambda e, hq=hq, cs=cs: e.matmul(pKQ[:, 128 + hq * 64:128 + (hq + 1) * 64], lhsT=qkT[2 + hq][:, cs], rhs=qkT[hq][:, cs], start=True, stop=True),
                     reads=[qkT[2 + hq], qkT[hq]], writes=[ps[3]])
            yield
            P.op("pe", lambda e: e.matmul(pG, lhsT=C.ones[0:64, 0:64], rhs=Dg[:].rearrange("p h i -> p (h i)"), start=True, stop=True),
                 reads=[C.ones, Dg], writes=[ps[3]])
            for hv in range(4):
                hq = hv // 2
                P.op("act", lambda e, hv=hv, hq=hq, c=c: e.activation(out=kdec[:, hv, :], in_=ptb[0:64, hq * 128:(hq + 1) * 128], func=AF.Copy,
                                                                 scale=kds[:, c, hv:hv + 1]), reads=[ps[7], kds], writes=[kdec])
                P.op("act", lambda e, hv=hv, hq=hq, c=c: e.activation(out=kbw[:, hv, :], in_=ptb[0:64, hq * 128:(hq + 1) * 128], func=AF.Copy,
                                                                 scale=bw[:, c, hv:hv + 1]), reads=[ps[7], bw], writes=[kbw])
                P.op("act", lambda e, hv=hv, c=c: e.activation(out=bv[:, hv, :], in_=ptb[0:64, 256 + hv * 128:256 + (hv + 1) * 128], func=AF.Copy,
                                                          scale=beta[:, c, hv:hv + 1]), reads=[ps[7], beta], writes=[bv])
            yield
            pG3 = pG.rearrange("p (h i) -> p h i", h=4)
            for h in range(4):
                P.op("dve", lambda e, h=h, c=c: e.tensor_scalar(out=d1[:, h, :], in0=pG3[:, h, :], scalar1=gcum[:, c, h:h + 1], scalar2=0.0,
                                                           op0=ALU.subtract, op1=ALU.min), reads=[ps[3], gcum], writes=[d1])
                P.op("dve", lambda e, h=h, c=c: e.tensor_scalar(out=d2[:, h, :], in0=pG3[:, h, :], scalar1=gcum[:, c, h:h + 1], scalar2=0.0,
                                                           op0=ALU.subtract, op1=ALU.max), reads=[ps[3], gcum], writes=[d2])
            yield
            P.op("act", lambda e: e.activation(out=decT[:], in_=d1[:], func=AF.Exp), reads=[d1], writes=[decT])
            P.op("act", lambda e: e.activation(out=dec[:], in_=d2[:], func=AF.Exp, scale=-1.0), reads=[d2], writes=[dec])
            yield
            P.op("pool", lambda e: e.tensor_tensor(out=X2(decT), in0=X2(decT), in1=UM4[:], op=ALU.mult), reads=[decT, UM4], writes=[decT])
            P.op("pool", lambda e: e.tensor_tensor(out=X2(dec), in0=X2(dec), in1=LS4[:], op=ALU.mult), reads=[dec, LS4], writes=[dec])
            yield
            for hv in range(4):
                hq = hv // 2
                P.op("dve", lambda e, hv=hv, hq=hq, c=c: e.scalar_tensor_tensor(out=Y0[:, hv, :], in0=pKQ[:, hq * 64:(hq + 1) * 64], scalar=nbeta[:, c, hv:hv + 1],
                                                                           in1=dec[:, hv, :], op0=ALU.mult, op1=ALU.mult),
                     reads=[ps[3], nbeta, dec], writes=[Y0])
            for hv in range(4):
                hq = hv // 2
                P.op("dve", lambda e, hv=hv, hq=hq: e.tensor_tensor(out=attnT[:, hv, :], in0=pKQ[:, 128 + hq * 64:128 + (hq + 1) * 64], in1=decT[:, hv, :], op=ALU.mult),
                     reads=[ps[3], decT], writes=[attnT])
            yield
            for hv in range(4):
                P.op("pe", lambda e, hv=hv: e.transpose(pX0[:, hv * 64:(hv + 1) * 64], Y0[:, hv, :], I64), reads=[Y0, I4], writes=[ps[2]])
            yield
            Xc, Yc, Rc = XX[0], Y0, RR[0]
            P.op("act", lambda e, Xc=Xc: e.activation(out=X2(Xc), in_=pX0, func=AF.Copy), reads=[ps[2]], writes=[Xc])
            P.op("dve", lambda e, Rc=Rc: e.tensor_tensor(out=X2(Rc), in0=pX0, in1=I4[:], op=ALU.add), reads=[ps[2], I4], writes=[Rc])
            yield
            pendR = None
            for lvl in range(5):
                Yn = YY[lvl % 2]
                Xn = XX[(lvl + 1) % 2]
                Rn = RR[(lvl + 1) % 2]
                for hv in range(4):
                    P.op("pe", lambda e, hv=hv, Xc=Xc, Yc=Yc: e.matmul(pY[:, hv * 64:(hv + 1) * 64], lhsT=Xc[:, hv, :], rhs=Yc[:, hv, :], start=True, stop=True),
                         reads=[Xc, Yc], writes=[ps[4]])
                if lvl < 4:
                    for hv in range(4):
                        P.op("pe", lambda e, hv=hv, Xc=Xc, Yc=Yc: e.matmul(pX[:, hv * 64:(hv + 1) * 64], lhsT=Yc[:, hv, :], rhs=Xc[:, hv, :], start=True, stop=True),
                             reads=[Xc, Yc], writes=[ps[4]])
                yield
                P.op("act", lambda e, Yn=Yn: e.activation(out=X2(Yn), in_=pY, func=AF.Copy), reads=[ps[4]], writes=[Yn])
                if lvl < 4:
                    P.op("dve", lambda e, Xn=Xn: e.tensor_copy(out=X2(Xn), in_=pX), reads=[ps[4]], writes=[Xn])
                yield
                for hv in range(4):
                    P.op("pe", lambda e, hv=hv, Rc=Rc: e.matmul(pR[:, hv * 64:(hv + 1) * 64], lhsT=I64, rhs=Rc[:, hv, :], start=True, stop=False),
                         reads=[I4, Rc], writes=[ps[5]])
                    P.op("pe", lambda e, hv=hv, Rc=Rc, Yn=Yn: e.matmul(pR[:, hv * 64:(hv + 1) * 64], lhsT=Yn[:, hv, :], rhs=Rc[:, hv, :], start=False, stop=True),
                         reads=[Yn, Rc], writes=[ps[5]])
                if lvl < 4:
                    P.op("dve", lambda e, Rn=Rn: e.tensor_copy(out=X2(Rn), in_=pR), reads=[ps[5]], writes=[Rn])
                else:
                    yield
                    P.op("dve", lambda e: e.tensor_copy(out=X2(TT), in_=pR), reads=[ps[5]], writes=[TT])
                Xc, Yc, Rc = Xn, Yn, Rn
            yield
            for hv in range(4):
                P.op("pe", lambda e, hv=hv: e.matmul(pwT[:, hv * 64:(hv + 1) * 64], lhsT=kbw[:, hv, :], rhs=TT[:, hv, :], start=True, stop=True),
                     reads=[kbw, TT], writes=[ps[5]])
            yield
            P.op("act", lambda e: e.activation(out=nwT[:].rearrange("p h i -> p (h i)"), in_=pwT, func=AF.Copy, scale=-1.0), reads=[ps[5]], writes=[nwT])
            yield

        def state(c, hb):
            TT, attnT, kdec, bv, nwT = hb
            cs = slice(c * 64, (c + 1) * 64)
            for hv in range(4):
                P.op("pe", lambda e, hv=hv: e.matmul(pVN[:, hv * 128:(hv + 1) * 128], lhsT=TT[:, hv, :], rhs=bv[:, hv, :], start=True, stop=False),
                     reads=[TT, bv], writes=[ps[6]])
                P.op("pe", lambda e, hv=hv: e.matmul(pVN[:, hv * 128:(hv + 1) * 128], lhsT=nwT[:, hv, :], rhs=Sb[:, hv, :], start=False, stop=True),
                     reads=[nwT, Sb], writes=[ps[6]])
            for hv in range(4):
                hq = hv // 2
                P.op("pe", lambda e, hv=hv, hq=hq, cs=cs: e.matmul(pO[:, hv * 128:(hv + 1) * 128], lhsT=qkT[hq][:, cs], rhs=Sb[:, hv, :], start=True, stop=True),
                     reads=[qkT[hq], Sb], writes=[ps[0]])
            yield
            P.op("dve", lambda e: e.tensor_copy(out=vn[:], in_=pVN), reads=[ps[6]], writes=[vn])
            for hv in range(4):
                P.op("act", lambda e, hv=hv, c=c: e.activation(out=osb[:, hv, :], in_=pO[:, hv * 128:(hv + 1) * 128], func=AF.Copy, scale=eg[:, c, hv:hv + 1]),
                     reads=[ps[0], eg], writes=[osb])
            yield
            for hv in range(4):
                P.op("pe", lambda e, hv=hv: e.matmul(pSU[:, hv * 128:(hv + 1) * 128], lhsT=kdec[:, hv, :], rhs=vn[:, hv * 128:(hv + 1) * 128], start=True, stop=True),
                     reads=[kdec, vn], writes=[pSU])
            for hv in range(4):
                P.op("pe", lambda e, hv=hv: e.matmul(pO[:, hv * 128:(hv + 1) * 128], lhsT=attnT[:, hv, :], rhs=vn[:, hv * 128:(hv + 1) * 128],
                                                start=True, stop=True), reads=[attnT, vn], writes=[ps[0]])
            yield
            for hv in range(4):
                P.op("dve", lambda e, hv=hv, c=c: e.scalar_tensor_tensor(out=Sf[:, hv, :], in0=Sf[:, hv, :], scalar=egl[:, c * 4 + hv:c * 4 + hv + 1],
                                                                    in1=pSU[:, hv * 128:(hv + 1) * 128], op0=ALU.mult, op1=ALU.add),
                     reads=[Sf, egl, pSU], writes=[Sf])
            yield
            P.op("act", lambda e: e.activation(out=Sb[:], in_=Sf[:], func=AF.Copy), reads=[Sf], writes=[Sb])
            P.op("dve", lambda e, c=c: e.tensor_tensor(out=Oall[:, c * 4:(c + 1) * 4, :], in0=osb[:], in1=pO.rearrange("p (h d) -> p h d", h=4), op=ALU.add),
                 reads=[ps[0], osb], writes=[Oall])
            yield

        def drive(gens):
            gens = [g for g in gens if g is not None]
            while gens:
                for g in list(gens):
                    try:
                        next(g)
                    except StopIteration:
                        gens.remove(g)
        drive([local(0, HB[0])])
        for c in range(8):
            drive([state(c, HB[c % 2]), local(c + 1, HB[(c + 1) % 2]) if c < 7 else None])

        for c in range(8):
            P.op("pool", lambda e, c=c: e.tensor_tensor(out=Osq[:], in0=Oall[:, c * 4:(c + 1) * 4, :], in1=Oall[:, c * 4:(c + 1) * 4, :], op=ALU.mult),
                 reads=[Oall], writes=[Osq])
            P.op("dve", lambda e, c=c: e.tensor_reduce(out=ss[:, c * 4:(c + 1) * 4], in_=Osq[:], axis=AX.X, op=ALU.add), reads=[Osq], writes=[ss])
        P.op("act", lambda e: e.activation(out=ss[:], in_=ss[:], func=AF.Ln, scale=1.0 / 128.0, bias=C.epsb[0:64, :]), reads=[ss, C.epsb], writes=[ss])
        P.op("act", lambda e: e.activation(out=ss[:], in_=ss[:], func=AF.Exp, scale=-0.5), reads=[ss], writes=[ss])
        for c in range(8):
            for hv in range(4):
                eng = "dve"
                P.op(eng, lambda e, c=c, hv=hv: e.scalar_tensor_tensor(out=ot[:, c, hv * 128:(hv + 1) * 128], in0=Oall[:, c * 4 + hv, :],
                                                                      scalar=ss[:, c * 4 + hv:c * 4 + hv + 1], in1=gz[:, c, hv * 128:(hv + 1) * 128],
                                                                      op0=ALU.mult, op1=ALU.mult),
                     reads=[Oall, ss, (gz, c)], writes=[(ot, c)])
        P.dma("sp", ov[:, ti * 8:(ti + 1) * 8, :], ot[:], reads=[(ot, c) for c in range(8)], writes=[("out", ti)])
        out_keys.append(("out", ti))
    P.wait_all("sp", out_keys + ["out_adax"])
    P.emit()
    return nc


def stage1_inputs(inp, i, consts, NT=16):
    T = NT * 512
    w = inp["gdn_w_in"][0]
    q = w[:, 256 * i:256 * i + 256]
    k = w[:, 2048 + 256 * i:2048 + 256 * i + 256]
    v = w[:, 4096 + 512 * i:4096 + 512 * i + 512]
    z = w[:, 8192 + 512 * i:8192 + 512 * i + 512]
    wba = np.concatenate([w[:, 12288 + 4 * i:12288 + 4 * i + 4], w[:, 12320 + 4 * i:12320 + 4 * i + 4]], axis=1)
    cw = inp["gdn_conv"][0]
    chans = np.concatenate([np.arange(256 * i, 256 * i + 256), 2048 + np.arange(256 * i, 256 * i + 256), 4096 + np.arange(512 * i, 512 * i + 512)])
    convw = np.ascontiguousarray(cw[:, chans].reshape(4, 8, 128).transpose(2, 1, 0))
    rep = lambda a: np.ascontiguousarray(np.broadcast_to(a[None, None, :], (64, 8, 4)), dtype=np.float32)
    m = dict(xT=np.ascontiguousarray(inp["x"][0][:T].T), c=col_layout(inp["c"][0]),
             adaW=consts["adaW0"], adaB=col_layout(inp["ada_b"][0][0:4096]),
             nmix=col_layout(inp["norm_mix"][0]), wqkvz=np.ascontiguousarray(np.concatenate([q, k, v, z], axis=1)),
             wba=np.ascontiguousarray(wba), convw=convw, alog=rep(inp["gdn_a_log"][0][4 * i:4 * i + 4]), dtb=rep(inp["gdn_dt_bias"][0][4 * i:4 * i + 4]),
             gnorm=np.ascontiguousarray(np.broadcast_to(np.tile(inp["gdn_norm"][0], 4)[None, :], (64, 512)), dtype=np.float32))
    m.update({k: v for k, v in consts.items() if k not in ("adaW0", "Wx", "bx")})
    m["Wx"] = np.ascontiguousarray(consts["Wx"][:, 28 * i:28 * (i + 1), :])
    m["bx"] = np.ascontiguousarray(consts["bx"][:, 28 * i:28 * (i + 1)])
    return m


def _run(nc, maps):
    res = run_bass_kernel_spmd(nc, maps, core_ids=list(range(NCORES)))
    return res.results


def kernel(**inp):
    import ml_dtypes
    inp = {k: np.asarray(v) for k, v in inp.items()}
    x = inp["x"][0]
    cc = col_layout(inp["c"][0])
    consts = stage1_host_consts()
    consts["adaW0"] = mv_layout(inp["ada_w"][0][:, 0:4096])
    consts["Wx"] = mv_layout(np.concatenate([inp["ada_w"][0][:, 4096:], inp["kv_ada_w"], inp["ada_w"][1], inp["out_ada_w"]], axis=1))
    consts["bx"] = col_layout(np.concatenate([inp["ada_b"][0][4096:], inp["kv_ada_b"], inp["ada_b"][1], inp["out_ada_b"]]))
    r1 = _run(build_stage1(), [stage1_inputs(inp, i, consts) for i in range(NCORES)])
    o_full = np.concatenate([r1[i]["o"] for i in range(NCORES)], axis=1)
    adax = np.ascontiguousarray(np.concatenate([r1[i]["adax"] for i in range(NCORES)], axis=1))
    ada0x, kvax, ada1m, ada1x, outax = (np.ascontiguousarray(adax[:, a:b]) for a, b in ((0, 64), (64, 96), (96, 128), (128, 192), (192, 224)))
    maps = []
    fb = np.ascontiguousarray(np.broadcast_to(inp["forget_b"][None, :], (128, 16)), dtype=np.float32)
    for i in range(NCORES):
        ts = slice(i * TOK, (i + 1) * TOK)
        maps.append(dict(xT=np.ascontiguousarray(x[ts].T), oT=np.ascontiguousarray(o_full[ts].T),
                         ada=ada0x, kva=kvax, nffn=col_layout(inp["norm_ffn"][0]),
                         nkv=col_layout(inp["kv_norm"]), wout=inp["gdn_w_out"][0], win=inp["ffn_w_in"][0], wo2=inp["ffn_w_out"][0],
                         kvw=inp["kv_w"], knorm=col_layout(inp["k_norm"]), fb=fb))
    r2 = _run(build_stage2(), maps)
    x1 = np.concatenate([r2[i]["x1T"].T for i in range(NCORES)], axis=0)
    KT = np.concatenate([r2[i]["kT"] for i in range(NCORES)], axis=1).reshape(2, 256, SEQ)
    V = np.concatenate([r2[i]["v"] for i in range(NCORES)], axis=0)
    lf = np.concatenate([r2[i]["lf"] for i in range(NCORES)], axis=0)
    U = (np.arange(128)[:, None] <= np.arange(128)[None, :]).astype(np.float32)
    toks = [np.concatenate([np.arange(b * 128, (b + 1) * 128) for b in core_blocks(i)]) for i in range(NCORES)]
    maps = []
    for i in range(NCORES):
        masks, selx = stage3a_host_consts(i)
        maps.append(dict(xT=np.ascontiguousarray(x1[toks[i]].T), ada=ada1m,
                         nmix=col_layout(inp["norm_mix"][1]), fwin=inp["fox_w_in"][0],
                         qnorm=col_layout(inp["q_norm"][0]), KT=np.ascontiguousarray(KT), V=V, lf=lf, U=U, masks=masks, selx=selx))
    r3 = _run(build_stage3a(), maps)
    maps = []
    for i in range(NCORES):
        maps.append(dict(xT=np.ascontiguousarray(x1[toks[i]].T), oT=r3[i]["ogT"],
                         ada=ada1x, kva=outax, nffn=col_layout(inp["norm_ffn"][1]),
                         nkv=col_layout(inp["out_norm"]), wout=inp["fox_w_out"][0], win=inp["ffn_w_in"][1], wo2=inp["ffn_w_out"][1]))
    r4 = _run(build_stage2(final=True), maps)
    out = np.zeros((1, SEQ, D), np.float32)
    for i in range(NCORES):
        out[0, toks[i]] = r4[i]["x1T"].T
    return out
```

```python
import contextlib
import numpy as np
import concourse.bass as bass
import concourse.mybir as mybir
from concourse.bass_utils import run_bass_kernel_spmd

F32 = mybir.dt.float32
BF16 = mybir.dt.bfloat16
AF = mybir.ActivationFunctionType
ALU = mybir.AluOpType
AX = mybir.AxisListType

NCORES = 8
SEM_LIMIT = 30000


class Prog:
    ENGS = ("pe", "act", "dve", "pool", "sp")

    def __init__(self, nc, n_dma_sems=40, self_sync=True):
        self.nc = nc
        self.stack = contextlib.ExitStack()
        self.streams = {e: [] for e in self.ENGS}
        self.cnt = {e: 0 for e in self.ENGS}
        self.esem = {e: nc.alloc_semaphore("s_" + e + "0") for e in self.ENGS}
        self.egen = {e: 0 for e in self.ENGS}
        self.seen = {e: {} for e in self.ENGS}
        self.lastw = {}
        self.readers = {}
        self.self_sync = self_sync
        self.dsem = [nc.alloc_semaphore("s_dma%d" % i) for i in range(n_dma_sems)]
        self.dcnt = [0] * n_dma_sems
        self.drr = 0
        self.sems = {}
        for e in self.ENGS:
            self.sems[id(self.esem[e])] = self.esem[e]
        for s in self.dsem:
            self.sems[id(s)] = s
        self.n_ins = 0
        self.uid = 0

    def sb(self, shape, dtype, name=None):
        self.uid += 1
        return self.stack.enter_context(self.nc.sbuf_tensor(name or "sb%d" % self.uid, list(shape), dtype))

    def ps(self, shape, dtype, name=None):
        self.uid += 1
        return self.stack.enter_context(self.nc.psum_tensor(name or "ps%d" % self.uid, list(shape), dtype))

    @staticmethod
    def _key(k):
        if isinstance(k, tuple):
            return tuple(Prog._key(x) for x in k)
        if isinstance(k, (str, int)):
            return k
        return k.name

    def _deps(self, reads, writes):
        need = {}
        raw = {}
        reads = [self._key(k) for k in reads]
        writes = [self._key(k) for k in writes]

        def add(d, sk, v):
            if d.get(sk, 0) < v:
                d[sk] = v
        for k in reads:
            lw = self.lastw.get(k)
            if lw is not None:
                add(need, *lw)
                add(raw, *lw)
        for k in writes:
            lw = self.lastw.get(k)
            if lw is not None:
                add(need, *lw)
            for sk, v in self.readers.get(k, {}).items():
                add(need, sk, v)
        return need, raw

    def _emit_waits(self, eng, deps):
        need, raw = deps
        own = id(self.esem[eng])
        for sk, v in need.items():
            if sk == own:
                if eng == "pe" or not self.self_sync:
                    continue
            if self.seen[eng].get(sk, 0) >= v:
                continue
            self.seen[eng][sk] = v
            sem = self.sems[sk]
            self.streams[eng].append(lambda e, sem=sem, v=v: e.wait_ge(sem, v))
            self.n_ins += 1

    def _record(self, reads, writes, sk, v):
        reads = [self._key(k) for k in reads]
        writes = [self._key(k) for k in writes]
        for k in writes:
            self.lastw[k] = (sk, v)
            self.readers[k] = {}
        for k in reads:
            r = self.readers.setdefault(k, {})
            if r.get(sk, 0) < v:
                r[sk] = v

    def _excl(self, eng, reads, writes):
        extra = {}
        for k in reads:
            k = self._key(k)
            if isinstance(k, str) and k.startswith("psb"):
                for sk, v in self.readers.get(k, {}).items():
                    if sk != id(self.esem[eng]) and extra.get(sk, 0) < v:
                        extra[sk] = v
        return extra

    def op(self, eng, fn, reads=(), writes=()):
        extra = self._excl(eng, reads, writes)
        need, raw = self._deps(reads, writes)
        for sk, v in extra.items():
            if need.get(sk, 0) < v:
                need[sk] = v
        self._emit_waits(eng, (need, raw))
        if self.cnt[eng] >= SEM_LIMIT:
            self.egen[eng] += 1
            s = self.nc.alloc_semaphore("s_%s%d" % (eng, self.egen[eng]))
            self.esem[eng] = s
            self.sems[id(s)] = s
            self.cnt[eng] = 0
        self.cnt[eng] += 1
        sem = self.esem[eng]
        v = self.cnt[eng]
        self.streams[eng].append(lambda e, fn=fn, sem=sem: fn(e).then_inc(sem, 1))
        self.n_ins += 1
        self._record(reads, writes, id(sem), v)

    def dma(self, eng, out, in_, reads=(), writes=(), **kw):
        half = len(self.dsem) // 2
        if eng == "pool":
            self.drr_sw = (getattr(self, "drr_sw", -1) + 1) % half
            i = half + self.drr_sw
        else:
            self.drr = (self.drr + 1) % half
            i = self.drr
        sem = self.dsem[i]
        need, raw = self._deps(reads, writes)
        if self.dcnt[i] > 0:
            sk = id(sem)
            if need.get(sk, 0) < self.dcnt[i]:
                need[sk] = self.dcnt[i]
        self._emit_waits(eng, (need, raw))
        self.dcnt[i] += 16
        v = self.dcnt[i]
        self.streams[eng].append(
            lambda e, out=out, in_=in_, sem=sem, kw=kw: e.dma_start(out=out, in_=in_, **kw).then_inc(sem, 16))
        self.n_ins += 1
        self._record(reads, writes, id(sem), v)

    def wait_all(self, eng, keys):
        need, _ = self._deps((), keys)
        for sk, v in need.items():
            if self.seen[eng].get(sk, 0) >= v:
                continue
            self.seen[eng][sk] = v
            sem = self.sems[sk]
            self.streams[eng].append(lambda e, sem=sem, v=v: e.wait_ge(sem, v))

    def end_barrier(self, eng="sp"):
        for e2 in self.ENGS:
            if e2 == eng or self.cnt[e2] == 0:
                continue
            sem, v = self.esem[e2], self.cnt[e2]
            self.streams[eng].append(lambda e, sem=sem, v=v: e.wait_ge(sem, v))
        for i, sem in enumerate(self.dsem):
            if self.dcnt[i] > 0:
                v = self.dcnt[i]
                self.streams[eng].append(lambda e, sem=sem, v=v: e.wait_ge(sem, v))

    def emit(self):
        self.end_barrier("sp")
        with self.nc.Block() as block:
            @block.tensor
            def _(e):
                for f in self.streams["pe"]:
                    f(e)

            @block.scalar
            def _(e):
                for f in self.streams["act"]:
                    f(e)

            @block.vector
            def _(e):
                for f in self.streams["dve"]:
                    f(e)

            @block.gpsimd
            def _(e):
                for f in self.streams["pool"]:
                    f(e)

            @block.sync
            def _(e):
                for f in self.streams["sp"]:
                    f(e)
        self.stack.close()


D = 2048
KC = D // 128
EPS = 1e-6
FFN_H = 5632


def mv_layout(w):
    w = np.asarray(w, dtype=np.float32)
    noc = w.shape[1] // 128
    return np.ascontiguousarray(w.reshape(KC, 128, noc, 128).transpose(1, 2, 0, 3).reshape(128, noc, KC * 128))


def col_layout(v):
    v = np.ascontiguousarray(v, dtype=np.float32)
    return np.ascontiguousarray(v.reshape(-1, 128).T)


class Ctx:
    def __init__(self, P, mvblk=None):
        self.P = P
        self.ps = [P.ps([128, 512], F32, name="psb%d" % i) for i in range(8)]
        self.ones = P.sb([128, 128], F32, name="ones_f")
        P.op("pool", lambda e: e.memset(self.ones[:], 1.0), writes=[self.ones])
        self.sq = [P.sb([128, 512], F32, name="sq%d" % i) for i in range(2)]
        self.sqi = 0
        self.rstd = P.sb([128, 512], F32, name="rstd")
        self.tmp = [P.sb([128, 512], F32, name="tmpf%d" % i) for i in range(2)]
        self.tmpi = 0
        if mvblk is None:
            mv = [P.sb([128, KC, 128], F32, name="mvblk%d" % i) for i in range(2)]
            mvblk = [(t[:], t) for t in mv]
        self.mvblk = mvblk
        self.mvi = 0
        self.epsb = P.sb([128, 1], F32, name="epsb")
        P.op("pool", lambda e: e.memset(self.epsb[:], EPS), writes=[self.epsb])
        self.oneb = P.sb([128, 1], F32, name="oneb")
        P.op("pool", lambda e: e.memset(self.oneb[:], 1.0), writes=[self.oneb])

    def next_sq(self):
        self.sqi = (self.sqi + 1) % len(self.sq)
        return self.sq[self.sqi]

    def next_tmp(self):
        self.tmpi = (self.tmpi + 1) % len(self.tmp)
        return self.tmp[self.tmpi]


def load_cond(P, C, c_dram):
    craw = P.sb([128, KC], F32, name="craw")
    cond = P.sb([128, KC], F32, name="cond")
    P.dma("sp", craw[:], c_dram, writes=[craw])
    P.op("act", lambda e: e.activation(out=cond[:], in_=craw[:], func=AF.Silu), reads=[craw], writes=[cond])
    return cond


def matvec(P, C, w_dram, ncols, cond, bias_tile, out_tile, ps):
    noc = ncols // 128
    for oc in range(noc):
        C.mvi ^= 1
        blk, bkey = C.mvblk[C.mvi]
        P.dma("sp", blk, w_dram[:, oc, :].rearrange("p (kc n) -> p kc n", kc=KC), writes=[bkey])
        for kc in range(KC):
            P.op("pe", lambda e, blk=blk, kc=kc, oc=oc: e.matmul(
                ps[:, oc:oc + 1], lhsT=blk[:, kc, :], rhs=cond[:, kc:kc + 1], start=(kc == 0), stop=(kc == KC - 1)),
                reads=[bkey, cond], writes=[ps])
    P.op("dve", lambda e: e.tensor_tensor(out=out_tile[:, 0:noc], in0=ps[:, 0:noc], in1=bias_tile[:, 0:noc], op=ALU.add),
         reads=[ps, bias_tile], writes=[out_tile])


def mod_coeffs(P, normw, scale_ap, name, adakey):
    a = P.sb([128, KC], F32, name=name)
    P.op("dve", lambda e: e.scalar_tensor_tensor(out=a[:], in0=scale_ap, scalar=1.0, in1=normw[:], op0=ALU.add, op1=ALU.mult),
         reads=[normw, adakey], writes=[a])
    return a


def rms_rstd(P, C, ps, n_feat, T, rkeys):
    P.op("act", lambda e: e.activation(out=C.rstd[:, 0:T], in_=ps[:, 0:T], func=AF.Ln, scale=1.0 / n_feat, bias=C.epsb[:]),
         reads=[ps, C.epsb] + list(rkeys), writes=[C.rstd])
    P.op("act", lambda e: e.activation(out=C.rstd[:, 0:T], in_=C.rstd[:, 0:T], func=AF.Exp, scale=-0.5),
         reads=[C.rstd], writes=[C.rstd])


def rmsnorm_mod(P, C, xT, xkey, hT, hkey, acol, bcol, ntt, T=512, inplace=False, tts=None, xoff=None):
    ps = C.ps[7]
    for tt in (tts if tts is not None else range(ntt)):
        ts = slice(tt * T, (tt + 1) * T)
        hs = ts
        if xoff is not None:
            ts = slice(xoff, xoff + T)
        for kc in range(KC):
            sq = C.next_sq()
            P.op("act", lambda e, sq=sq, kc=kc, ts=ts: e.activation(out=sq[:, 0:T], in_=xT[:, kc, ts], func=AF.Square),
                 reads=[(xkey, kc, tt)], writes=[sq])
            P.op("pe", lambda e, sq=sq, kc=kc: e.matmul(ps[:, 0:T], lhsT=C.ones[:], rhs=sq[:, 0:T], start=(kc == 0), stop=(kc == KC - 1)),
                 reads=[sq, C.ones], writes=[ps])
        rms_rstd(P, C, ps, D, T, [])
        for kc in range(KC):
            tmp = C.next_tmp()
            P.op("dve", lambda e, tmp=tmp, kc=kc, ts=ts: e.scalar_tensor_tensor(
                out=tmp[:, 0:T], in0=xT[:, kc, ts], scalar=acol[:, kc:kc + 1], in1=C.rstd[:, 0:T], op0=ALU.mult, op1=ALU.mult),
                reads=[(xkey, kc, tt), acol, C.rstd], writes=[tmp])
            P.op("act", lambda e, tmp=tmp, kc=kc, hs=hs: e.activation(
                out=hT[:, kc, hs], in_=tmp[:, 0:T], func=AF.Identity, bias=bcol[:, kc:kc + 1], scale=1.0),
                reads=[tmp, bcol], writes=[(xkey, kc, tt) if inplace else (hkey, tt)])


def ffn(P, C, xT, xkey, hT, hkey, gcol, gkey, w_in_dram, w_out_dram, wA, wB, hid, ntt, T=512):
    nblk = FFN_H // 256
    wiv = w_in_dram.rearrange("(kc p) n -> p kc n", p=128)
    wov = w_out_dram.rearrange("(c p) n -> p c n", p=128)
    for j in range(nblk):
        wa = wA[j % len(wA)]
        wb = wB[j % len(wB)]
        hd = hid[j % len(hid)]
        wa3 = wa[:].rearrange("p (kc n) -> p kc n", kc=KC)
        wb3 = wb[:].rearrange("p (c n) -> p c n", c=2)
        P.dma("pool", wa3[:, :, 0:256], wiv[:, :, j * 256:(j + 1) * 256], writes=[(wa, 0)])
        P.dma("pool", wa3[:, :, 256:512], wiv[:, :, FFN_H + j * 256:FFN_H + (j + 1) * 256], writes=[(wa, 1)])
        P.dma("pool", wb3, wov[:, 2 * j:2 * j + 2, :], writes=[wb])
        for tt in range(ntt):
            ts = slice(tt * T, (tt + 1) * T)
            for c2 in range(2):
                pg = C.ps[(2 * tt + c2) % 2]
                pu = C.ps[2 + (2 * tt + c2) % 2]
                for kc in range(KC):
                    P.op("pe", lambda e, pg=pg, kc=kc, c2=c2, ts=ts, wa3=wa3: e.matmul(
                        pg[:, 0:T], lhsT=wa3[:, kc, c2 * 128:(c2 + 1) * 128], rhs=hT[:, kc, ts], start=(kc == 0), stop=(kc == KC - 1)),
                        reads=[(wa, 0), (hkey, tt)], writes=[pg])
                for kc in range(KC):
                    P.op("pe", lambda e, pu=pu, kc=kc, c2=c2, ts=ts, wa3=wa3: e.matmul(
                        pu[:, 0:T], lhsT=wa3[:, kc, 256 + c2 * 128:256 + (c2 + 1) * 128], rhs=hT[:, kc, ts], start=(kc == 0), stop=(kc == KC - 1)),
                        reads=[(wa, 1), (hkey, tt)], writes=[pu])
                tmp = C.next_tmp()
                P.op("act", lambda e, tmp=tmp, pg=pg: e.activation(out=tmp[:, 0:T], in_=pg[:, 0:T], func=AF.Silu),
                     reads=[pg], writes=[tmp])
                P.op("dve", lambda e, tmp=tmp, pu=pu, c2=c2, ts=ts, hd=hd: e.tensor_tensor(
                    out=hd[:, c2, ts], in0=pu[:, 0:T], in1=tmp[:, 0:T], op=ALU.mult),
                    reads=[pu, tmp], writes=[(hd, tt)])
            for oc in range(KC):
                po = C.ps[4 + oc % 3]
                for c2 in range(2):
                    P.op("pe", lambda e, po=po, c2=c2, oc=oc, ts=ts, wb3=wb3, hd=hd: e.matmul(
                        po[:, 0:T], lhsT=wb3[:, c2, oc * 128:(oc + 1) * 128], rhs=hd[:, c2, ts], start=(c2 == 0), stop=(c2 == 1)),
                        reads=[wb, (hd, tt)], writes=[po])
                P.op("dve", lambda e, po=po, oc=oc, ts=ts: e.scalar_tensor_tensor(
                    out=xT[:, oc, ts], in0=po[:, 0:T], scalar=gcol[:, oc:oc + 1], in1=xT[:, oc, ts], op0=ALU.mult, op1=ALU.add),
                    reads=[po, (xkey, oc, tt), gkey], writes=[(xkey, oc, tt)])


TOK = 1024


def build_stage2(stop=99, final=False):
    nc = bass.Bass("TRN2", target_bir_lowering=False)
    dt = nc.dram_tensor
    xT_d = dt("xT", [D, TOK], F32, kind="ExternalInput").ap()
    oT_d = dt("oT", [4096, TOK], BF16, kind="ExternalInput").ap()
    ada_d = dt("ada", [128, 64], F32, kind="ExternalInput").ap()
    kva_d = dt("kva", [128, 32], F32, kind="ExternalInput").ap()
    nffn_d = dt("nffn", [128, KC], F32, kind="ExternalInput").ap()
    nkv_d = dt("nkv", [128, KC], F32, kind="ExternalInput").ap()
    wout_d = dt("wout", [4096, D], F32, kind="ExternalInput").ap()
    win_d = dt("win", [D, 2 * FFN_H], F32, kind="ExternalInput").ap()
    wo2_d = dt("wo2", [FFN_H, D], F32, kind="ExternalInput").ap()
    if not final:
        kvw_d = dt("kvw", [D, 1040], F32, kind="ExternalInput").ap()
        knorm_d = dt("knorm", [128, 2], F32, kind="ExternalInput").ap()
        fb_d = dt("fb", [128, 16], F32, kind="ExternalInput").ap()
    x1T_d = dt("x1T", [D, TOK], F32, kind="ExternalOutput").ap()
    if not final:
        kT_d = dt("kT", [512, TOK], BF16, kind="ExternalOutput").ap()
        v_d = dt("v", [TOK, 512], BF16, kind="ExternalOutput").ap()
        lf_d = dt("lf", [TOK, 16], F32, kind="ExternalOutput").ap()

    P = Prog(nc)
    C = Ctx(P, mvblk=[])
    xT = P.sb([128, KC, TOK], F32, name="xT_sb")
    hbuf = P.sb([128, KC * TOK], BF16, name="hbuf")
    hT = hbuf[:].rearrange("p (c t) -> p c t", c=KC)
    wA = [P.sb([128, 8192], BF16, name="wA%d" % i) for i in range(3)]
    wB = [P.sb([128, 4096], BF16, name="wB%d" % i) for i in range(2)]
    hid = [P.sb([128, 2, TOK], BF16, name="hid%d" % i) for i in range(2)]
    small = {}
    smalls = [("nffn", nffn_d, [128, KC]), ("nkv", nkv_d, [128, KC])]
    if not final:
        smalls += [("knorm", knorm_d, [128, 2]), ("fb", fb_d, [128, 16])]
    for nm, d_, shp in smalls:
        t = P.sb(shp, F32, name="sm_" + nm)
        P.dma("sp", t[:], d_, writes=[t])
        small[nm] = t
    xkeys = [("x", kc, tt) for kc in range(KC) for tt in range(2)]
    P.dma("sp", xT[:], xT_d.rearrange("(c p) t -> p c t", p=128), writes=xkeys)
    ada = P.sb([128, 64], F32, name="ada_sb")
    P.dma("sp", ada[:], ada_d, writes=[ada])
    kva = P.sb([128, 32], F32, name="kva_sb")
    P.dma("sp", kva[:], kva_d, writes=[kva])

    def finish():
        P.dma("sp", x1T_d.rearrange("(c p) t -> p c t", p=128), xT[:], reads=xkeys + [ada, kva], writes=["out_x1"])
        P.wait_all("sp", ["out_x1"])
        P.emit()
        return nc
    if stop == 1:
        return finish()
    ov = oT_d.rearrange("(c p) t -> p c t", p=128)
    wov = wout_d.rearrange("(c p) n -> p c n", p=128)
    o3 = hbuf[:].rearrange("p (c t) -> p c t", c=32)
    for tt in range(2):
        ts = slice(tt * 512, (tt + 1) * 512)
        P.dma("sp", o3, ov[:, :, ts], writes=[("h", 0), ("h", 1)])
        for ob in range(8):
            w = wA[ob % 2]
            w3 = w[:].rearrange("p (c n) -> p c n", c=32)
            P.dma("pool", w3, wov[:, :, ob * 256:(ob + 1) * 256], writes=[(w, 0), (w, 1)])
            for o2 in range(2):
                oc = ob * 2 + o2
                po = C.ps[4 + oc % 3]
                for kc in range(32):
                    P.op("pe", lambda e, po=po, kc=kc, o2=o2, w3=w3: e.matmul(
                        po[:], lhsT=w3[:, kc, o2 * 128:(o2 + 1) * 128], rhs=o3[:, kc, :], start=(kc == 0), stop=(kc == 31)),
                        reads=[(w, 0), (w, 1), ("h", 0), ("h", 1)], writes=[po])
                P.op("dve", lambda e, po=po, oc=oc, ts=ts: e.scalar_tensor_tensor(
                    out=xT[:, oc, ts], in0=po[:], scalar=ada[:, oc:oc + 1], in1=xT[:, oc, ts], op0=ALU.mult, op1=ALU.add),
                    reads=[po, ("x", oc, tt), ada], writes=[("x", oc, tt)])

    if stop == 2:
        return finish()
    a_f = mod_coeffs(P, small["nffn"], ada[:, 32:48], "a_f", ada)
    rmsnorm_mod(P, C, xT, "x", hT, "h", a_f, ada[:, 16:32], 2)
    if stop == 3:
        return finish()
    ffn(P, C, xT, "x", hT, "h", ada[:, 48:64], ada, win_d, wo2_d, wA, wB, hid, 2)
    if stop == 4:
        return finish()
    a_kv = mod_coeffs(P, small["nkv"], kva[:, 16:32], "a_kv", kva)
    if final:
        rmsnorm_mod(P, C, xT, "x", xT, "x", a_kv, kva[:, 0:16], 2, inplace=True)
        return finish()
    P.dma("sp", x1T_d.rearrange("(c p) t -> p c t", p=128), xT[:], reads=xkeys, writes=["out_x1"])

    rmsnorm_mod(P, C, xT, "x", hT, "h", a_kv, kva[:, 0:16], 2)
    kvv = kvw_d.rearrange("(kc p) n -> p kc n", p=128)
    wk3 = wA[0][:].rearrange("p (kc n) -> p kc n", kc=KC)
    wv3 = wA[1][:].rearrange("p (kc n) -> p kc n", kc=KC)
    wf = P.sb([128, KC, 16], BF16, name="wf")
    P.dma("pool", wk3, kvv[:, :, 0:512], writes=[(wA[0], 0), (wA[0], 1)])
    P.dma("pool", wv3, kvv[:, :, 512:1024], writes=[(wA[1], 0), (wA[1], 1)])
    P.dma("pool", wf[:], kvv[:, :, 1024:1040], writes=[wf])
    kraw = P.sb([128, 2, 512], F32, name="kraw")
    kout = P.sb([128, 4, TOK], BF16, name="kout")
    for tt in range(2):
        ts = slice(tt * 512, (tt + 1) * 512)
        for kh in range(2):
            for cc in range(2):
                c = kh * 2 + cc
                pk = C.ps[cc]
                for kc in range(KC):
                    P.op("pe", lambda e, pk=pk, kc=kc, c=c, ts=ts: e.matmul(
                        pk[:], lhsT=wk3[:, kc, c * 128:(c + 1) * 128], rhs=hT[:, kc, ts], start=(kc == 0), stop=(kc == KC - 1)),
                        reads=[(wA[0], 0), (wA[0], 1), ("h", tt)], writes=[pk])
                P.op("dve", lambda e, pk=pk, cc=cc: e.tensor_copy(out=kraw[:, cc, :], in_=pk[:]), reads=[pk], writes=[(kraw, cc)])
                sq = C.next_sq()
                P.op("act", lambda e, sq=sq, pk=pk: e.activation(out=sq[:], in_=pk[:], func=AF.Square), reads=[pk], writes=[sq])
                P.op("pe", lambda e, sq=sq, cc=cc: e.matmul(C.ps[7][:], lhsT=C.ones[:], rhs=sq[:], start=(cc == 0), stop=(cc == 1)),
                     reads=[sq, C.ones], writes=[C.ps[7]])
            rms_rstd(P, C, C.ps[7], 256, 512, [])
            for cc in range(2):
                c = kh * 2 + cc
                P.op("dve", lambda e, cc=cc, c=c, ts=ts: e.scalar_tensor_tensor(
                    out=kout[:, c, ts], in0=kraw[:, cc, :], scalar=small["knorm"][:, cc:cc + 1], in1=C.rstd[:], op0=ALU.mult, op1=ALU.mult),
                    reads=[(kraw, cc), small["knorm"], C.rstd], writes=[(kout, c)])
    P.dma("sp", kT_d.rearrange("(c p) t -> p c t", p=128), kout[:], reads=[(kout, c) for c in range(4)], writes=["out_k"])
    vout = P.sb([128, 8, 512], BF16, name="vout")
    lfo = P.sb([128, 8, 16], F32, name="lfo")
    lft = P.sb([128, 8, 16], F32, name="lft")
    for tb in range(8):
        tbs = slice(tb * 128, (tb + 1) * 128)
        pv = C.ps[tb % 2]
        for kc in range(KC):
            P.op("pe", lambda e, pv=pv, kc=kc, tbs=tbs: e.matmul(
                pv[:], lhsT=hT[:, kc, tbs], rhs=wv3[:, kc, :], start=(kc == 0), stop=(kc == KC - 1)),
                reads=[(wA[1], 0), (wA[1], 1), ("h", tb // 4)], writes=[pv])
        P.op("act", lambda e, pv=pv, tb=tb: e.activation(out=vout[:, tb, :], in_=pv[:], func=AF.Copy), reads=[pv], writes=[(vout, tb)])
        pf = C.ps[2 + tb % 2]
        for kc in range(KC):
            P.op("pe", lambda e, pf=pf, kc=kc, tbs=tbs: e.matmul(
                pf[:, 0:16], lhsT=hT[:, kc, tbs], rhs=wf[:, kc, :], start=(kc == 0), stop=(kc == KC - 1)),
                reads=[wf, ("h", tb // 4)], writes=[pf])
        P.op("dve", lambda e, pf=pf, tb=tb: e.tensor_tensor(out=lft[:, tb, :], in0=pf[:, 0:16], in1=small["fb"][:], op=ALU.add),
             reads=[pf, small["fb"]], writes=[(lft, tb)])
    P.op("act", lambda e: e.activation(out=lft[:], in_=lft[:], func=AF.Exp, scale=-1.0), reads=[(lft, tb) for tb in range(8)], writes=[lft])
    P.op("act", lambda e: e.activation(out=lft[:], in_=lft[:], func=AF.Ln, bias=C.oneb[:], scale=1.0), reads=[lft, C.oneb], writes=[lft])
    P.op("dve", lambda e: e.tensor_scalar(out=lfo[:], in0=lft[:], scalar1=-1.0, scalar2=None, op0=ALU.mult), reads=[lft], writes=[lfo])
    P.dma("sp", v_d.rearrange("(b p) n -> p b n", p=128), vout[:], reads=[(vout, tb) for tb in range(8)], writes=["out_v"])
    P.dma("sp", lf_d.rearrange("(b p) n -> p b n", p=128), lfo[:], reads=[lfo], writes=["out_lf"])
    P.wait_all("sp", ["out_x1", "out_k", "out_v", "out_lf"])
    P.emit()
    return nc


def core_blocks(i):
    return [i, 15 - i, 16 + i, 31 - i, 32 + i, 47 - i, 48 + i, 63 - i]


SLOT_EXT = [8, 16, 24, 32, 40, 48, 56, 64]


def stage3a_host_consts(i):
    import ml_dtypes
    blks = core_blocks(i)
    tri = (np.arange(128)[:, None] <= np.arange(128)[None, :]).astype(np.float32)
    masks = np.zeros((128, 8, 8, 128), np.float32)
    selx = np.zeros((8, 128, 64, 16), np.float32)
    for s_, gb in enumerate(blks):
        for m in range(8):
            jb = 8 * s_ + m
            if jb < gb:
                masks[:, s_, m, :] = 1.0
            elif jb == gb:
                masks[:, s_, m, :] = tri
        selx[s_, :, gb, :] = 1.0
    return masks.astype(ml_dtypes.bfloat16), selx


def build_stage3a():
    nc = bass.Bass("TRN2", target_bir_lowering=False)
    dt = nc.dram_tensor
    xT_d = dt("xT", [D, TOK], F32, kind="ExternalInput").ap()
    ada_d = dt("ada", [128, 32], F32, kind="ExternalInput").ap()
    nmix_d = dt("nmix", [128, KC], F32, kind="ExternalInput").ap()
    win_d = dt("fwin", [D, 8192], F32, kind="ExternalInput").ap()
    qn_d = dt("qnorm", [128, 2], F32, kind="ExternalInput").ap()
    K_d = dt("KT", [2, 256, 8192], BF16, kind="ExternalInput").ap()
    V_d = dt("V", [8192, 512], BF16, kind="ExternalInput").ap()
    lf_d = dt("lf", [8192, 16], F32, kind="ExternalInput").ap()
    U_d = dt("U", [128, 128], F32, kind="ExternalInput").ap()
    mask_d = dt("masks", [128, 8, 8, 128], BF16, kind="ExternalInput").ap()
    selx_d = dt("selx", [8, 128, 64, 16], F32, kind="ExternalInput").ap()
    og_d = dt("ogT", [4096, TOK], BF16, kind="ExternalOutput").ap()
    Qs_d = dt("Qs", [4096, TOK], BF16, kind="Internal").ap()
    Gs_d = dt("Gs", [4096, TOK], BF16, kind="Internal").ap()

    P = Prog(nc)
    C = Ctx(P, mvblk=[])
    bufK = P.sb([128, 8192], F32, name="bufK")
    bufV = P.sb([128, 16384], BF16, name="bufV")
    xt3 = bufK[:].rearrange("p (c t) -> p c t", c=KC)
    K3 = bufK[:].bitcast(BF16).rearrange("p (c t) -> p c t", c=2)
    hT = bufV[:].rearrange("p (c t) -> p c t", c=KC)
    V3 = bufV[:].rearrange("p (b d) -> p b d", b=64)
    small = {}
    for nm, d_, shp in (("nmix", nmix_d, [128, KC]), ("qnorm", qn_d, [128, 2]), ("U", U_d, [128, 128])):
        t = P.sb(shp, F32, name="sm_" + nm)
        P.dma("sp", t[:], d_, writes=[t])
        small[nm] = t
    ada = P.sb([128, 32], F32, name="ada_sb")
    P.dma("sp", ada[:], ada_d, writes=[ada])
    a_m = mod_coeffs(P, small["nmix"], ada[:, 16:32], "a_m", ada)
    xv = xT_d.rearrange("(c p) t -> p c t", p=128)
    for tt in range(2):
        P.dma("sp", xt3, xv[:, :, tt * 512:(tt + 1) * 512], writes=[("xt", kc, t2) for kc in range(KC) for t2 in range(2)] + ["bufK"])
        rmsnorm_mod(P, C, xt3, "xt", hT, "h", a_m, ada[:, 0:16], 2, tts=[tt], xoff=0)

    wq = [P.sb([128, KC, 256], BF16, name="wq%d" % i) for i in range(4)]
    qraw = P.sb([128, 2, TOK], F32, name="qraw")
    qo = [P.sb([128, 2, TOK], BF16, name="qo%d" % i) for i in range(2)]
    wiv = win_d.rearrange("(kc p) n -> p kc n", p=128)
    Qsv = Qs_d.rearrange("(h c p) t -> h p c t", p=128, c=2)
    Gsv = Gs_d.rearrange("(h c p) t -> h p c t", p=128, c=2)
    for h in range(16):
        for isg in range(2):
            w = wq[(2 * h + isg) % 4]
            P.dma("pool", w[:], wiv[:, :, isg * 4096 + h * 256: isg * 4096 + (h + 1) * 256], writes=[w])
            out = qo[isg]
            for tt in range(2):
                ts = slice(tt * 512, (tt + 1) * 512)
                for cc in range(2):
                    pq = C.ps[(2 * tt + cc) % 4]
                    for kc in range(KC):
                        P.op("pe", lambda e, pq=pq, kc=kc, cc=cc, ts=ts, w=w: e.matmul(
                            pq[:], lhsT=w[:, kc, cc * 128:(cc + 1) * 128], rhs=hT[:, kc, ts], start=(kc == 0), stop=(kc == KC - 1)),
                            reads=[w, ("h", tt)], writes=[pq])
                    if isg:
                        P.op("act", lambda e, pq=pq, cc=cc, ts=ts, out=out: e.activation(out=out[:, cc, ts], in_=pq[:], func=AF.Sigmoid),
                             reads=[pq], writes=[(out, cc, tt)])
                    else:
                        P.op("dve", lambda e, pq=pq, cc=cc, ts=ts: e.tensor_copy(out=qraw[:, cc, ts], in_=pq[:]), reads=[pq], writes=[(qraw, cc, tt)])
                        sq = C.next_sq()
                        P.op("act", lambda e, sq=sq, pq=pq: e.activation(out=sq[:], in_=pq[:], func=AF.Square), reads=[pq], writes=[sq])
                        P.op("pe", lambda e, sq=sq, cc=cc: e.matmul(C.ps[7][:], lhsT=C.ones[:], rhs=sq[:], start=(cc == 0), stop=(cc == 1)),
                             reads=[sq, C.ones], writes=[C.ps[7]])
                if not isg:
                    rms_rstd(P, C, C.ps[7], 256, 512, [])
                    for cc in range(2):
                        P.op("dve", lambda e, cc=cc, ts=ts, out=out: e.scalar_tensor_tensor(
                            out=out[:, cc, ts], in0=qraw[:, cc, ts], scalar=small["qnorm"][:, cc:cc + 1], in1=C.rstd[:], op0=ALU.mult, op1=ALU.mult),
                            reads=[(qraw, cc, tt), small["qnorm"], C.rstd], writes=[(out, cc, tt)])
            P.dma("sp", (Gsv if isg else Qsv)[h], out[:], reads=[(out, cc, tt) for cc in range(2) for tt in range(2)],
                  writes=[("QG", isg, h)])

    lft = P.sb([128, 64, 16], F32, name="lft")
    csl = P.sb([128, 64, 16], F32, name="csl")
    tot = P.sb([128, 64, 16], F32, name="tot")
    incl = P.sb([128, 64, 16], F32, name="incl")
    P.dma("sp", lft[:], lf_d.rearrange("(b p) h -> p b h", p=128), writes=[lft])
    lf2 = lft[:].rearrange("p b h -> p (b h)")
    for half in range(2):
        hs = slice(half * 512, (half + 1) * 512)
        pa = C.ps[half]
        pb = C.ps[2 + half]
        P.op("pe", lambda e, pa=pa, hs=hs: e.matmul(pa[:], lhsT=small["U"][:], rhs=lf2[:, hs], start=True, stop=True),
             reads=[small["U"], lft], writes=[pa])
        P.op("pe", lambda e, pb=pb, hs=hs: e.matmul(pb[:], lhsT=C.ones[:], rhs=lf2[:, hs], start=True, stop=True),
             reads=[C.ones, lft], writes=[pb])
        P.op("dve", lambda e, pa=pa, hs=hs: e.tensor_copy(out=csl[:].rearrange("p b h -> p (b h)")[:, hs], in_=pa[:]), reads=[pa], writes=[csl])
        P.op("dve", lambda e, pb=pb, hs=hs: e.tensor_copy(out=tot[:].rearrange("p b h -> p (b h)")[:, hs], in_=pb[:]), reads=[pb], writes=[tot])
    P.op("dve", lambda e: e.tensor_copy(out=incl[:, 0, :], in_=tot[:, 0, :]), reads=[tot], writes=[incl])
    for b in range(1, 64):
        P.op("dve", lambda e, b=b: e.tensor_tensor(out=incl[:, b, :], in0=incl[:, b - 1, :], in1=tot[:, b, :], op=ALU.add),
             reads=[incl, tot], writes=[incl])
    P.op("dve", lambda e: e.tensor_tensor(out=csl[:], in0=csl[:], in1=incl[:], op=ALU.add), reads=[csl, incl], writes=[csl])
    P.op("dve", lambda e: e.tensor_tensor(out=lft[:], in0=tot[:], in1=csl[:], op=ALU.subtract), reads=[csl, tot], writes=[lft])
    Fneg = lft
    fref = P.sb([128, 8, 16], F32, name="fref")
    for s_ in range(8):
        P.dma("sp", tot[:], selx_d[s_], writes=[tot])
        P.op("dve", lambda e: e.tensor_tensor(out=tot[:], in0=tot[:], in1=incl[:], op=ALU.mult), reads=[tot, incl], writes=[tot])
        P.op("dve", lambda e, s_=s_: e.tensor_reduce(out=fref[:, s_, :], in_=tot[:].rearrange("p b h -> p h b"), axis=AX.X, op=ALU.add),
             reads=[tot], writes=[fref])

    masks = P.sb([128, 8, 8, 128], BF16, name="masks_sb")
    P.dma("sp", masks[:], mask_d, writes=[masks])
    onesb = P.sb([128, 128], BF16, name="onesb")
    P.op("pool", lambda e: e.memset(onesb[:], 1.0), writes=[onesb])
    Qt = P.sb([128, 4, 2, TOK], BF16, name="Qt")
    Gt = P.sb([128, 4, 2, TOK], BF16, name="Gt")
    bias = P.sb([128, 64, 4], F32, name="bias")
    pT = [P.sb([128, 4, 128], BF16, name="pT%d" % i) for i in range(3)]
    rec = P.sb([128, 512], F32, name="rec")
    pti = 0
    setsel = 0
    Kv = K_d.rearrange("k (c p) t -> k p c t", p=128)
    Vv = V_d.rearrange("(b p) n -> p b n", p=128)
    ogv = og_d.rearrange("(h c p) t -> p h c t", p=128, c=2)
    for kvh in range(2):
        P.dma("sp", K3, Kv[kvh], writes=["bufK"] + [("xt", kc, tt) for kc in range(KC) for tt in range(2)])
        P.dma("sp", V3, Vv[:, :, kvh * 256:(kvh + 1) * 256], writes=[("h", 0), ("h", 1)])
        for hg in range(2):
            h0 = kvh * 8 + hg * 4
            for j in range(4):
                P.dma("sp", Qt[:, j], Qsv[h0 + j], reads=[("QG", 0, h0 + j)], writes=[Qt])
                P.dma("sp", Gt[:, j], Gsv[h0 + j], reads=[("QG", 1, h0 + j)], writes=[Gt])
            for s_ in range(8):
                qs = slice(s_ * 128, (s_ + 1) * 128)
                ext = SLOT_EXT[s_]
                for j in range(4):
                    P.op("dve", lambda e, j=j, s_=s_, ext=ext, h0=h0: e.tensor_scalar(
                        out=bias[:, 0:ext, j], in0=Fneg[:, 0:ext, h0 + j], scalar1=fref[:, s_, h0 + j:h0 + j + 1], scalar2=0.0,
                        op0=ALU.add, op1=ALU.min), reads=[Fneg, fref], writes=[bias])
                setsel ^= 1
                acc = [C.ps[2 + 3 * setsel + k] for k in range(3)]
                def emit_S(jb):
                    ks = slice(jb * 128, (jb + 1) * 128)
                    pS = C.ps[jb % 2]
                    for cc in range(2):
                        P.op("pe", lambda e, pS=pS, cc=cc, ks=ks, qs=qs: e.matmul(
                            pS[:].rearrange("p (j t) -> p j t", j=4), lhsT=K3[:, cc, ks], rhs=Qt[:, :, cc, qs], start=(cc == 0), stop=(cc == 1)),
                            reads=["bufK", Qt], writes=[pS])
                emit_S(0)
                for jb in range(ext):
                    pS = C.ps[jb % 2]
                    if jb + 1 < ext:
                        emit_S(jb + 1)
                    pti = (pti + 1) % 3
                    pt = pT[pti]
                    for j in range(4):
                        P.op("act", lambda e, pS=pS, pt=pt, j=j, jb=jb: e.activation(
                            out=pt[:, j, :], in_=pS[:, j * 128:(j + 1) * 128], func=AF.Exp, scale=1.0 / 16.0, bias=bias[:, jb, j:j + 1]),
                            reads=[pS, bias], writes=[(pt, j)])
                    m = jb - 8 * s_
                    if m >= 0:
                        for j in range(4):
                            P.op("pool", lambda e, pt=pt, j=j, s_=s_, m=m: e.tensor_tensor(
                                out=pt[:, j, :], in0=pt[:, j, :], in1=masks[:, s_, m, :], op=ALU.mult),
                                reads=[(pt, j), masks], writes=[(pt, j)])
                    pt2 = pt[:].rearrange("p j t -> p (j t)")
                    for k in range(3):
                        lhs = onesb[:] if k == 2 else V3[:, jb, k * 128:(k + 1) * 128]
                        P.op("pe", lambda e, k=k, lhs=lhs, pt2=pt2, jb=jb, ext=ext, acc=acc: e.matmul(
                            acc[k][:], lhsT=lhs, rhs=pt2, start=(jb == 0), stop=(jb == ext - 1)),
                            reads=[(pt, 0), (pt, 1), (pt, 2), (pt, 3), ("h", 0), ("h", 1), onesb], writes=[acc[k]])
                P.op("dve", lambda e, acc=acc: e.reciprocal(out=rec[:], in_=acc[2][:]), reads=[acc[2]], writes=[rec])
                for k in range(2):
                    tmp = C.next_tmp()
                    P.op("dve", lambda e, tmp=tmp, k=k, acc=acc: e.tensor_tensor(out=tmp[:], in0=acc[k][:], in1=rec[:], op=ALU.mult),
                         reads=[acc[k], rec], writes=[tmp])
                    P.op("pool", lambda e, tmp=tmp, k=k, qs=qs: e.tensor_tensor(
                        out=Gt[:, :, k, qs], in0=tmp[:].rearrange("p (j t) -> p j t", j=4), in1=Gt[:, :, k, qs], op=ALU.mult),
                        reads=[tmp, Gt], writes=[Gt])
            P.dma("sp", ogv[:, h0:h0 + 4], Gt[:], reads=[Gt], writes=[("og", h0)])
    P.wait_all("sp", [("og", h0) for h0 in (0, 4, 8, 12)])
    P.emit()
    return nc


SEQ = 8192


def stage1_host_consts():
    import ml_dtypes
    I64 = np.eye(64, dtype=np.float32)
    U64 = (np.arange(64)[:, None] <= np.arange(64)[None, :]).astype(np.float32)
    LS = (np.arange(64)[None, :] < np.arange(64)[:, None]).astype(np.float32)
    return dict(I4=np.ascontiguousarray(np.tile(I64, (1, 4))), U64=U64,
                UM4=np.ascontiguousarray(np.tile(U64, (1, 4))), LS4=np.ascontiguousarray(np.tile(LS, (1, 4))),
                identb=np.eye(128, dtype=np.float32).astype(ml_dtypes.bfloat16))


def build_stage1(NT=16, stop=99):
    nc = bass.Bass("TRN2", target_bir_lowering=False)
    dt = nc.dram_tensor
    T = NT * 512
    xT_d = dt("xT", [D, T], F32, kind="ExternalInput").ap()
    c_d = dt("c", [128, KC], F32, kind="ExternalInput").ap()
    adaW_d = dt("adaW", [128, 32, D], F32, kind="ExternalInput").ap()
    adaB_d = dt("adaB", [128, 32], F32, kind="ExternalInput").ap()
    nmix_d = dt("nmix", [128, KC], F32, kind="ExternalInput").ap()
    w_d = dt("wqkvz", [D, 1536], F32, kind="ExternalInput").ap()
    wba_d = dt("wba", [D, 8], F32, kind="ExternalInput").ap()
    convw_d = dt("convw", [128, 8, 4], F32, kind="ExternalInput").ap()
    alog_d = dt("alog", [64, 8, 4], F32, kind="ExternalInput").ap()
    dtb_d = dt("dtb", [64, 8, 4], F32, kind="ExternalInput").ap()
    gn_d = dt("gnorm", [64, 512], F32, kind="ExternalInput").ap()
    I4_d = dt("I4", [64, 256], F32, kind="ExternalInput").ap()
    U64_d = dt("U64", [64, 64], F32, kind="ExternalInput").ap()
    UM4_d = dt("UM4", [64, 256], F32, kind="ExternalInput").ap()
    LS4_d = dt("LS4", [64, 256], F32, kind="ExternalInput").ap()
    idb_d = dt("identb", [128, 128], BF16, kind="ExternalInput").ap()
    Wx_d = dt("Wx", [128, 28, D], F32, kind="ExternalInput").ap()
    bx_d = dt("bx", [128, 28], F32, kind="ExternalInput").ap()
    adax_d = dt("adax", [128, 28], F32, kind="ExternalOutput").ap()
    o_d = dt("o", [T, 512], BF16, kind="ExternalOutput").ap()

    P = Prog(nc)
    xbuf = P.sb([128, KC * 512], F32, name="xbuf")
    xt3 = xbuf[:].rearrange("p (c t) -> p c t", c=KC)
    mv = [(xbuf[:, i * 2048:(i + 1) * 2048].rearrange("p (c n) -> p c n", c=KC), ("mvb", i)) for i in range(2)]
    C = Ctx(P, mvblk=mv)
    ps = C.ps
    ptb = ps[7][:].bitcast(BF16)

    def ld(name, d_, shp, dtype=F32, eng="sp"):
        t = P.sb(shp, dtype, name="c_" + name)
        P.dma(eng, t[:], d_, writes=[t])
        return t
    adaB = ld("adaB", adaB_d, [128, 32])
    nmix = ld("nmix", nmix_d, [128, KC])
    convw = ld("convw", convw_d, [128, 8, 4])
    alog = ld("alog", alog_d, [64, 8, 4])
    dtb = ld("dtb", dtb_d, [64, 8, 4])
    gn = ld("gn", gn_d, [64, 512])
    I4 = ld("I4", I4_d, [64, 256])
    U64 = ld("U64", U64_d, [64, 64])
    UM4 = ld("UM4", UM4_d, [64, 256])
    LS4 = ld("LS4", LS4_d, [64, 256])
    identb = ld("identb", idb_d, [128, 128], BF16)
    I64 = I4[:, 0:64]
    wqk = P.sb([128, KC, 1536], BF16, name="wqkvz_sb")
    wba = P.sb([128, KC, 8], BF16, name="wba_sb")
    wv_ = w_d.rearrange("(kc p) n -> p kc n", p=128)
    for j in range(3):
        P.dma("pool", wqk[:, :, j * 512:(j + 1) * 512], wv_[:, :, j * 512:(j + 1) * 512], writes=[(wqk, j)])
    P.dma("pool", wba[:], wba_d.rearrange("(kc p) n -> p kc n", p=128), writes=[wba])
    wkeys = [(wqk, j) for j in range(3)]

    cond = load_cond(P, C, c_d)
    ada = P.sb([128, 32], F32, name="ada")
    matvec(P, C, adaW_d, 2 * D, cond, adaB, ada, ps[6])
    a_m = mod_coeffs(P, nmix, ada[:, 16:32], "a_m", ada)
    bx = ld("bx", bx_d, [128, 28])
    adax = P.sb([128, 28], F32, name="adax_sb")
    matvec(P, C, Wx_d, 28 * 128, cond, bx, adax, ps[6])
    P.dma("sp", adax_d, adax[:], reads=[adax], writes=["out_adax"])
    eA = P.sb([64, 8, 4], F32, name="eA")
    P.op("act", lambda e: e.activation(out=eA[:], in_=alog[:], func=AF.Exp), reads=[alog], writes=[eA])

    hT = P.sb([128, KC, 512], BF16, name="hT")
    pre = P.sb([128, 8, 515], F32, name="pre")
    P.op("pool", lambda e: e.memset(pre[:], 0.0), writes=[(pre, n) for n in range(8)])
    qkT = [P.sb([128, 512], BF16, name="qkT%d" % n) for n in range(4)]
    vT = [P.sb([128, 512], BF16, name="vT%d" % n) for n in range(4)]
    actf = P.sb([128, 512], F32, name="actf") if False else None
    gz = P.sb([64, 8, 512], F32, name="gz")
    Sf = P.sb([128, 4, 128], F32, name="Sf")
    Sb = P.sb([128, 4, 128], BF16, name="Sb")
    P.op("pool", lambda e: e.memset(Sf[:], 0.0), writes=[Sf])
    P.op("pool", lambda e: e.memset(Sb[:], 0.0), writes=[Sb])
    sm = lambda name, dtype=F32: P.sb([64, 8, 4], dtype, name=name)
    beta, nbeta, xg, graw, gcum, eg, kds, bw = [sm(n) for n in ("beta", "nbeta", "xg", "graw", "gcum", "eg", "kds", "bw")]
    egl = P.sb([128, 32], F32, name="egl")
    t64 = lambda name, dtype=F32: P.sb([64, 4, 64], dtype, name=name)
    Dg, d1, d2, decT, dec, Y0 = [t64(n) for n in ("Dg", "d1", "d2", "decT", "dec", "Y0")]
    XX = [t64("XX%d" % i) for i in range(2)]
    YY = [t64("YY%d" % i) for i in range(2)]
    RR = [t64("RR%d" % i) for i in range(2)]
    HB = [(t64("TT%d" % i, BF16), t64("attnT%d" % i, BF16), P.sb([64, 4, 128], BF16, name="kdec%d" % i),
           P.sb([64, 4, 128], BF16, name="bv%d" % i), P.sb([128, 4, 64], BF16, name="nwT%d" % i)) for i in range(2)]
    kbw = P.sb([64, 4, 128], BF16, name="kbw")
    osb = P.sb([64, 4, 128], F32, name="osb")
    vn = P.sb([64, 512], BF16, name="vn")
    Oall = P.sb([64, 32, 128], F32, name="Oall")
    Osq = P.sb([64, 4, 128], F32, name="Osq")
    ss = P.sb([64, 32], F32, name="ss")
    ot = P.sb([64, 8, 512], BF16, name="ot")
    xv = xT_d.rearrange("(c p) t -> p c t", p=128)
    ov = o_d.rearrange("(c p) n -> p c n", p=64)
    pba = ps[2][0:64, 0:64]
    pgc = ps[2][0:64, 64:96]
    pgl = ps[2][:, 96:128]
    pX0 = ps[2][0:64, 128:384]
    pKQ = ps[3][0:64, 0:256]
    pG = ps[3][0:64, 256:512]
    pX = ps[4][0:64, 0:256]
    pY = ps[4][0:64, 256:512]
    pR = ps[5][0:64, 0:256]
    pwT = ps[5][:, 256:512]
    pVN = ps[6][0:64, :]
    pO = ps[0][0:64, :]
    pSU = ps[1]
    out_keys = []

    def finish_now(extra):
        P.dma("sp", ov[:, 0:8, :], ot[:], reads=list(extra) + [(ot, c) for c in range(8)], writes=[("out", 0)])
        P.wait_all("sp", [("out", 0)])
        P.emit()
        return nc
    if stop == 1:
        return finish_now([ada, a_m, eA, wba, Sf, Sb] + wkeys)

    for ti in range(NT):
        tsl = slice(ti * 512, (ti + 1) * 512)
        P.dma("sp", xt3, xv[:, :, tsl], writes=[("xt", kc, 0) for kc in range(KC)] + [("mvb", 0), ("mvb", 1)])
        rmsnorm_mod(P, C, xt3, "xt", hT, "h", a_m, ada[:, 0:16], 1, tts=[0])
        if stop == 2:
            return finish_now([("h", 0)])
        for c in range(8):
            for kc in range(KC):
                P.op("pe", lambda e, c=c, kc=kc: e.matmul(pba[:, c * 8:(c + 1) * 8], lhsT=hT[:, kc, c * 64:(c + 1) * 64], rhs=wba[:, kc, :],
                                                      start=(kc == 0), stop=(kc == KC - 1)), reads=[("h", 0), wba], writes=[ps[2]])
        pba3 = pba.rearrange("p (c n) -> p c n", n=8)
        P.op("act", lambda e: e.activation(out=beta[:], in_=pba3[:, :, 0:4], func=AF.Sigmoid), reads=[ps[2]], writes=[beta])
        P.op("dve", lambda e: e.tensor_tensor(out=xg[:], in0=pba3[:, :, 4:8], in1=dtb[:], op=ALU.add), reads=[ps[2], dtb], writes=[xg])
        if stop == 30:
            return finish_now([beta, xg])
        P.op("dve", lambda e: e.tensor_scalar(out=nbeta[:], in0=beta[:], scalar1=-1.0, scalar2=None, op0=ALU.mult), reads=[beta], writes=[nbeta])
        P.op("act", lambda e: e.activation(out=xg[:], in_=xg[:], func=AF.Exp), reads=[xg], writes=[xg])
        P.op("act", lambda e: e.activation(out=xg[:], in_=xg[:], func=AF.Ln, bias=C.oneb[0:64, :], scale=1.0), reads=[xg, C.oneb], writes=[xg])
        P.op("dve", lambda e: e.scalar_tensor_tensor(out=graw[:], in0=xg[:], scalar=-1.0, in1=eA[:], op0=ALU.mult, op1=ALU.mult),
             reads=[xg, eA], writes=[graw])
        if stop == 31:
            return finish_now([beta, nbeta, graw])
        g2 = graw[:].rearrange("p c h -> p (c h)")
        P.op("pe", lambda e: e.matmul(pgc, lhsT=U64[:], rhs=g2, start=True, stop=True), reads=[U64, graw], writes=[ps[2]])
        P.op("pe", lambda e: e.matmul(pgl, lhsT=C.ones[0:64, :], rhs=g2, start=True, stop=True), reads=[C.ones, graw], writes=[ps[2]])
        gc2 = gcum[:].rearrange("p c h -> p (c h)")
        P.op("dve", lambda e: e.tensor_copy(out=gc2, in_=pgc), reads=[ps[2]], writes=[gcum])
        P.op("act", lambda e: e.activation(out=eg[:].rearrange("p c h -> p (c h)"), in_=pgc, func=AF.Exp), reads=[ps[2]], writes=[eg])
        P.op("act", lambda e: e.activation(out=egl[:], in_=pgl, func=AF.Exp), reads=[ps[2]], writes=[egl])
        if stop == 32:
            return finish_now([beta, nbeta, graw, gcum, eg, egl])
        kd2 = kds[:].rearrange("p c h -> p (c h)")
        P.op("dve", lambda e: e.tensor_tensor(out=kd2, in0=pgl[0:64, :], in1=gc2, op=ALU.subtract), reads=[ps[2], gcum], writes=[kds])
        if stop == 33:
            return finish_now([beta, nbeta, graw, gcum, eg, egl, kds])
        P.op("act", lambda e: e.activation(out=kd2, in_=kd2, func=AF.Exp), reads=[kds], writes=[kds])
        if stop == 34:
            return finish_now([beta, nbeta, graw, gcum, eg, egl, kds])
        P.op("dve", lambda e: e.tensor_tensor(out=bw[:], in0=beta[:], in1=eg[:], op=ALU.mult), reads=[beta, eg], writes=[bw])
        if stop == 3:
            return finish_now([beta, nbeta, gcum, eg, egl, kds, bw])
        for n in range(8):
            pp = ps[n % 2]
            for kc in range(KC):
                P.op("pe", lambda e, pp=pp, kc=kc, n=n: e.matmul(pp[:], lhsT=wqk[:, kc, n * 128:(n + 1) * 128], rhs=hT[:, kc, :],
                                                            start=(kc == 0), stop=(kc == KC - 1)), reads=[("h", 0)] + wkeys, writes=[pp])
            P.op("pool", lambda e, n=n: e.tensor_copy(out=pre[:, n, 0:3], in_=pre[:, n, 512:515]), reads=[(pre, n)], writes=[(pre, n)])
            P.op("act", lambda e, pp=pp, n=n: e.activation(out=pre[:, n, 3:515], in_=pp[:], func=AF.Copy), reads=[pp], writes=[(pre, n)])
            acc = C.next_tmp()
            P.op("dve", lambda e, acc=acc, n=n: e.tensor_scalar(out=acc[:], in0=pre[:, n, 0:512], scalar1=convw[:, n, 0:1], scalar2=None, op0=ALU.mult),
                 reads=[(pre, n), convw], writes=[acc])
            for i in range(1, 4):
                P.op("dve", lambda e, acc=acc, n=n, i=i: e.scalar_tensor_tensor(out=acc[:], in0=pre[:, n, i:i + 512], scalar=convw[:, n, i:i + 1],
                                                                             in1=acc[:], op0=ALU.mult, op1=ALU.add),
                     reads=[(pre, n), convw, acc], writes=[acc])
            if n >= 4:
                P.op("act", lambda e, acc=acc, n=n: e.activation(out=vT[n - 4][:], in_=acc[:], func=AF.Silu), reads=[acc], writes=[vT[n - 4]])
            else:
                actf = C.next_tmp()
                P.op("act", lambda e, acc=acc, actf=actf: e.activation(out=actf[:], in_=acc[:], func=AF.Silu), reads=[acc], writes=[actf])
                sq = C.next_sq()
                P.op("act", lambda e, sq=sq, actf=actf: e.activation(out=sq[:], in_=actf[:], func=AF.Square), reads=[actf], writes=[sq])
                P.op("pe", lambda e, sq=sq: e.matmul(ps[6][:], lhsT=C.ones[:], rhs=sq[:], start=True, stop=True), reads=[sq, C.ones], writes=[ps[6]])
                P.op("act", lambda e: e.activation(out=C.rstd[:], in_=ps[6][:], func=AF.Ln, scale=1.0, bias=C.epsb[:]),
                     reads=[ps[6], C.epsb], writes=[C.rstd])
                P.op("act", lambda e: e.activation(out=C.rstd[:], in_=C.rstd[:], func=AF.Exp, scale=-0.5), reads=[C.rstd], writes=[C.rstd])
                sc = (128.0 ** -0.5) if n < 2 else 1.0
                P.op("dve", lambda e, n=n, sc=sc, actf=actf: e.scalar_tensor_tensor(out=qkT[n][:], in0=actf[:], scalar=sc, in1=C.rstd[:], op0=ALU.mult, op1=ALU.mult),
                     reads=[actf, C.rstd], writes=[qkT[n]])
        if stop == 4:
            return finish_now(qkT + vT)
        for c in range(8):
            pz = ps[c % 2]
            for kc in range(KC):
                P.op("pe", lambda e, pz=pz, kc=kc, c=c: e.matmul(pz[0:64, :], lhsT=hT[:, kc, c * 64:(c + 1) * 64], rhs=wqk[:, kc, 1024:1536],
                                                            start=(kc == 0), stop=(kc == KC - 1)), reads=[("h", 0)] + wkeys, writes=[pz])
            zt = C.next_tmp()
            P.op("act", lambda e, pz=pz, zt=zt: e.activation(out=zt[0:64, :], in_=pz[0:64, :], func=AF.Silu), reads=[pz], writes=[zt])
            P.op("pool", lambda e, zt=zt, c=c: e.tensor_tensor(out=gz[:, c, :], in0=zt[0:64, :], in1=gn[:], op=ALU.mult), reads=[zt, gn], writes=[(gz, c)])

        if stop == 5:
            return finish_now([(gz, c) for c in range(8)])
        X2 = lambda t: t[:].rearrange("p h i -> p (h i)")

        def local(c, hb):
            TT, attnT, kdec, bv, nwT = hb
            cs = slice(c * 64, (c + 1) * 64)
            for h in range(4):
                P.op("dve", lambda e, h=h, c=c: e.tensor_scalar(out=Dg[:, h, :], in0=I64, scalar1=gcum[:, c, h:h + 1], scalar2=None, op0=ALU.mult),
                     reads=[I4, gcum], writes=[Dg])
            for hq in range(2):
                P.op("pe", lambda e, hq=hq, cs=cs: e.transpose(ptb[0:64, hq * 128:(hq + 1) * 128], qkT[2 + hq][:, cs], identb[:]),
                     reads=[qkT[2 + hq], identb], writes=[ps[7]])
            for hv in range(4):
                P.op("pe", lambda e, hv=hv, cs=cs: e.transpose(ptb[0:64, 256 + hv * 128:256 + (hv + 1) * 128], vT[hv][:, cs], identb[:]),
                     reads=[vT[hv], identb], writes=[ps[7]])
            for hq in range(2):
                P.op("pe", lambda e, hq=hq, cs=cs: e.matmul(pKQ[:, hq * 64:(hq + 1) * 64], lhsT=qkT[2 + hq][:, cs], rhs=qkT[2 + hq][:, cs], start=True, stop=True),
                     reads=[qkT[2 + hq]], writes=[ps[3]])
                P.op("pe", lambda e, hq=hq, cs=cs: e.matmul(pKQ[:, 128 + hq * 64:128 + (hq + 1) * 64], lhsT=qkT[2 + hq][:, cs], rhs=qkT[hq][:, cs], start=True, stop=True),
                     reads=[qkT[2 + hq], qkT[hq]], writes=[ps[3]])
            yield
            P.op("pe", lambda e: e.matmul(pG, lhsT=C.ones[0:64, 0:64], rhs=Dg[:].rearrange("p h i -> p (h i)"), start=True, stop=True),
                 reads=[C.ones, Dg], writes=[ps[3]])
            for hv in range(4):
                hq = hv // 2
                P.op("act", lambda e, hv=hv, hq=hq, c=c: e.activation(out=kdec[:, hv, :], in_=ptb[0:64, hq * 128:(hq + 1) * 128], func=AF.Copy,
                                                                 scale=kds[:, c, hv:hv + 1]), reads=[ps[7], kds], writes=[kdec])
                P.op("act", lambda e, hv=hv, hq=hq, c=c: e.activation(out=kbw[:, hv, :], in_=ptb[0:64, hq * 128:(hq + 1) * 128], func=AF.Copy,
                                                                 scale=bw[:, c, hv:hv + 1]), reads=[ps[7], bw], writes=[kbw])
                P.op("act", lambda e, hv=hv, c=c: e.activation(out=bv[:, hv, :], in_=ptb[0:64, 256 + hv * 128:256 + (hv + 1) * 128], func=AF.Copy,
                                                          scale=beta[:, c, hv:hv + 1]), reads=[ps[7], beta], writes=[bv])
            yield
            pG3 = pG.rearrange("p (h i) -> p h i", h=4)
            for h in range(4):
                P.op("dve", lambda e, h=h, c=c: e.tensor_scalar(out=d1[:, h, :], in0=pG3[:, h, :], scalar1=gcum[:, c, h:h + 1], scalar2=0.0,
                                                           op0=ALU.subtract, op1=ALU.min), reads=[ps[3], gcum], writes=[d1])
                P.op("dve", lambda e, h=h, c=c: e.tensor_scalar(out=d2[:, h, :], in0=pG3[:, h, :], scalar1=gcum[:, c, h:h + 1], scalar2=0.0,
                                                           op0=ALU.subtract, op1=ALU.max), reads=[ps[3], gcum], writes=[d2])
            yield
            P.op("act", lambda e: e.activation(out=decT[:], in_=d1[:], func=AF.Exp), reads=[d1], writes=[decT])
            P.op("act", lambda e: e.activation(out=dec[:], in_=d2[:], func=AF.Exp, scale=-1.0), reads=[d2], writes=[dec])
            yield
            P.op("pool", lambda e: e.tensor_tensor(out=X2(decT), in0=X2(decT), in1=UM4[:], op=ALU.mult), reads=[decT, UM4], writes=[decT])
            P.op("pool", lambda e: e.tensor_tensor(out=X2(dec), in0=X2(dec), in1=LS4[:], op=ALU.mult), reads=[dec, LS4], writes=[dec])
            yield
            for hv in range(4):
                hq = hv // 2
                P.op("dve", lambda e, hv=hv, hq=hq, c=c: e.scalar_tensor_tensor(out=Y0[:, hv, :], in0=pKQ[:, hq * 64:(hq + 1) * 64], scalar=nbeta[:, c, hv:hv + 1],
                                                                           in1=dec[:, hv, :], op0=ALU.mult, op1=ALU.mult),
                     reads=[ps[3], nbeta, dec], writes=[Y0])
            for hv in range(4):
                hq = hv // 2
                P.op("dve", lambda e, hv=hv, hq=hq: e.tensor_tensor(out=attnT[:, hv, :], in0=pKQ[:, 128 + hq * 64:128 + (hq + 1) * 64], in1=decT[:, hv, :], op=ALU.mult),
                     reads=[ps[3], decT], writes=[attnT])
            yield
            for hv in range(4):
                P.op("pe", lambda e, hv=hv: e.transpose(pX0[:, hv * 64:(hv + 1) * 64], Y0[:, hv, :], I64), reads=[Y0, I4], writes=[ps[2]])
            yield
            Xc, Yc, Rc = XX[0], Y0, RR[0]
            P.op("act", lambda e, Xc=Xc: e.activation(out=X2(Xc), in_=pX0, func=AF.Copy), reads=[ps[2]], writes=[Xc])
            P.op("dve", lambda e, Rc=Rc: e.tensor_tensor(out=X2(Rc), in0=pX0, in1=I4[:], op=ALU.add), reads=[ps[2], I4], writes=[Rc])
            yield
            pendR = None
            for lvl in range(5):
                Yn = YY[lvl % 2]
                Xn = XX[(lvl + 1) % 2]
                Rn = RR[(lvl + 1) % 2]
                for hv in range(4):
                    P.op("pe", lambda e, hv=hv, Xc=Xc, Yc=Yc: e.matmul(pY[:, hv * 64:(hv + 1) * 64], lhsT=Xc[:, hv, :], rhs=Yc[:, hv, :], start=True, stop=True),
                         reads=[Xc, Yc], writes=[ps[4]])
                if lvl < 4:
                    for hv in range(4):
                        P.op("pe", lambda e, hv=hv, Xc=Xc, Yc=Yc: e.matmul(pX[:, hv * 64:(hv + 1) * 64], lhsT=Yc[:, hv, :], rhs=Xc[:, hv, :], start=True, stop=True),
                             reads=[Xc, Yc], writes=[ps[4]])
                yield
                P.op("act", lambda e, Yn=Yn: e.activation(out=X2(Yn), in_=pY, func=AF.Copy), reads=[ps[4]], writes=[Yn])
                if lvl < 4:
                    P.op("dve", lambda e, Xn=Xn: e.tensor_copy(out=X2(Xn), in_=pX), reads=[ps[4]], writes=[Xn])
                yield
                for hv in range(4):
                    P.op("pe", lambda e, hv=hv, Rc=Rc, Yn=Yn: e.matmul(pR[:, hv * 64:(hv + 1) * 64], lhsT=Yn[:, hv, :], rhs=Rc[:, hv, :], start=True, stop=True),
                         reads=[Yn, Rc], writes=[ps[5]])
                if lvl < 4:
                    P.op("dve", lambda e, Rn=Rn, Rc=Rc: e.tensor_tensor(out=X2(Rn), in0=pR, in1=X2(Rc), op=ALU.add), reads=[ps[5], Rc], writes=[Rn])
                else:
                    yield
                    P.op("dve", lambda e, Rc=Rc: e.tensor_tensor(out=X2(TT), in0=pR, in1=X2(Rc), op=ALU.add), reads=[ps[5], Rc], writes=[TT])
                Xc, Yc, Rc = Xn, Yn, Rn
            yield
            for hv in range(4):
                P.op("pe", lambda e, hv=hv: e.matmul(pwT[:, hv * 64:(hv + 1) * 64], lhsT=kbw[:, hv, :], rhs=TT[:, hv, :], start=True, stop=True),
                     reads=[kbw, TT], writes=[ps[5]])
            yield
            P.op("act", lambda e: e.activation(out=nwT[:].rearrange("p h i -> p (h i)"), in_=pwT, func=AF.Copy, scale=-1.0), reads=[ps[5]], writes=[nwT])
            yield

        def state(c, hb):
            TT, attnT, kdec, bv, nwT = hb
            cs = slice(c * 64, (c + 1) * 64)
            for hv in range(4):
                P.op("pe", lambda e, hv=hv: e.matmul(pVN[:, hv * 128:(hv + 1) * 128], lhsT=TT[:, hv, :], rhs=bv[:, hv, :], start=True, stop=False),
                     reads=[TT, bv], writes=[ps[6]])
                P.op("pe", lambda e, hv=hv: e.matmul(pVN[:, hv * 128:(hv + 1) * 128], lhsT=nwT[:, hv, :], rhs=Sb[:, hv, :], start=False, stop=True),
                     reads=[nwT, Sb], writes=[ps[6]])
            for hv in range(4):
                hq = hv // 2
                P.op("pe", lambda e, hv=hv, hq=hq, cs=cs: e.matmul(pO[:, hv * 128:(hv + 1) * 128], lhsT=qkT[hq][:, cs], rhs=Sb[:, hv, :], start=True, stop=True),
                     reads=[qkT[hq], Sb], writes=[ps[0]])
            yield
            P.op("dve", lambda e: e.tensor_copy(out=vn[:], in_=pVN), reads=[ps[6]], writes=[vn])
            for hv in range(4):
                P.op("act", lambda e, hv=hv, c=c: e.activation(out=osb[:, hv, :], in_=pO[:, hv * 128:(hv + 1) * 128], func=AF.Copy, scale=eg[:, c, hv:hv + 1]),
                     reads=[ps[0], eg], writes=[osb])
            yield
            for hv in range(4):
                P.op("pe", lambda e, hv=hv: e.matmul(pSU[:, hv * 128:(hv + 1) * 128], lhsT=kdec[:, hv, :], rhs=vn[:, hv * 128:(hv + 1) * 128], start=True, stop=True),
                     reads=[kdec, vn], writes=[pSU])
            for hv in range(4):
                P.op("pe", lambda e, hv=hv: e.matmul(pO[:, hv * 128:(hv + 1) * 128], lhsT=attnT[:, hv, :], rhs=vn[:, hv * 128:(hv + 1) * 128],
                                                start=True, stop=True), reads=[attnT, vn], writes=[ps[0]])
            yield
            for hv in range(4):
                P.op("dve", lambda e, hv=hv, c=c: e.scalar_tensor_tensor(out=Sf[:, hv, :], in0=Sf[:, hv, :], scalar=egl[:, c * 4 + hv:c * 4 + hv + 1],
                                                                    in1=pSU[:, hv * 128:(hv + 1) * 128], op0=ALU.mult, op1=ALU.add),
                     reads=[Sf, egl, pSU], writes=[Sf])
            yield
            P.op("act", lambda e: e.activation(out=Sb[:], in_=Sf[:], func=AF.Copy), reads=[Sf], writes=[Sb])
            P.op("dve", lambda e, c=c: e.tensor_tensor(out=Oall[:, c * 4:(c + 1) * 4, :], in0=osb[:], in1=pO.rearrange("p (h d) -> p h d", h=4), op=ALU.add),
                 reads=[ps[0], osb], writes=[Oall])
            yield

        def drive(gens):
            gens = [g for g in gens if g is not None]
            while gens:
                for g in list(gens):
                    try:
                        next(g)
                    except StopIteration:
                        gens.remove(g)
        drive([local(0, HB[0])])
        for c in range(8):
            drive([state(c, HB[c % 2]), local(c + 1, HB[(c + 1) % 2]) if c < 7 else None])

        for c in range(8):
            P.op("pool", lambda e, c=c: e.tensor_tensor(out=Osq[:], in0=Oall[:, c * 4:(c + 1) * 4, :], in1=Oall[:, c * 4:(c + 1) * 4, :], op=ALU.mult),
                 reads=[Oall], writes=[Osq])
            P.op("dve", lambda e, c=c: e.tensor_reduce(out=ss[:, c * 4:(c + 1) * 4], in_=Osq[:], axis=AX.X, op=ALU.add), reads=[Osq], writes=[ss])
        P.op("act", lambda e: e.activation(out=ss[:], in_=ss[:], func=AF.Ln, scale=1.0 / 128.0, bias=C.epsb[0:64, :]), reads=[ss, C.epsb], writes=[ss])
        P.op("act", lambda e: e.activation(out=ss[:], in_=ss[:], func=AF.Exp, scale=-0.5), reads=[ss], writes=[ss])
        for c in range(8):
            for hv in range(4):
                eng = "dve"
                P.op(eng, lambda e, c=c, hv=hv: e.scalar_tensor_tensor(out=ot[:, c, hv * 128:(hv + 1) * 128], in0=Oall[:, c * 4 + hv, :],
                                                                      scalar=ss[:, c * 4 + hv:c * 4 + hv + 1], in1=gz[:, c, hv * 128:(hv + 1) * 128],
                                                                      op0=ALU.mult, op1=ALU.mult),
                     reads=[Oall, ss, (gz, c)], writes=[(ot, c)])
        P.dma("sp", ov[:, ti * 8:(ti + 1) * 8, :], ot[:], reads=[(ot, c) for c in range(8)], writes=[("out", ti)])
        out_keys.append(("out", ti))
    P.wait_all("sp", out_keys + ["out_adax"])
    P.emit()
    return nc


def stage1_inputs(inp, i, consts, NT=16):
    T = NT * 512
    w = inp["gdn_w_in"][0]
    q = w[:, 256 * i:256 * i + 256]
    k = w[:, 2048 + 256 * i:2048 + 256 * i + 256]
    v = w[:, 4096 + 512 * i:4096 + 512 * i + 512]
    z = w[:, 8192 + 512 * i:8192 + 512 * i + 512]
    wba = np.concatenate([w[:, 12288 + 4 * i:12288 + 4 * i + 4], w[:, 12320 + 4 * i:12320 + 4 * i + 4]], axis=1)
    cw = inp["gdn_conv"][0]
    chans = np.concatenate([np.arange(256 * i, 256 * i + 256), 2048 + np.arange(256 * i, 256 * i + 256), 4096 + np.arange(512 * i, 512 * i + 512)])
    convw = np.ascontiguousarray(cw[:, chans].reshape(4, 8, 128).transpose(2, 1, 0))
    rep = lambda a: np.ascontiguousarray(np.broadcast_to(a[None, None, :], (64, 8, 4)), dtype=np.float32)
    m = dict(xT=np.ascontiguousarray(inp["x"][0][:T].T), c=col_layout(inp["c"][0]),
             adaW=consts["adaW0"], adaB=col_layout(inp["ada_b"][0][0:4096]),
             nmix=col_layout(inp["norm_mix"][0]), wqkvz=np.ascontiguousarray(np.concatenate([q, k, v, z], axis=1)),
             wba=np.ascontiguousarray(wba), convw=convw, alog=rep(inp["gdn_a_log"][0][4 * i:4 * i + 4]), dtb=rep(inp["gdn_dt_bias"][0][4 * i:4 * i + 4]),
             gnorm=np.ascontiguousarray(np.broadcast_to(np.tile(inp["gdn_norm"][0], 4)[None, :], (64, 512)), dtype=np.float32))
    m.update({k: v for k, v in consts.items() if k not in ("adaW0", "Wx", "bx")})
    m["Wx"] = np.ascontiguousarray(consts["Wx"][:, 28 * i:28 * (i + 1), :])
    m["bx"] = np.ascontiguousarray(consts["bx"][:, 28 * i:28 * (i + 1)])
    return m


def _run(nc, maps):
    res = run_bass_kernel_spmd(nc, maps, core_ids=list(range(NCORES)))
    return res.results


def kernel(**inp):
    import ml_dtypes
    inp = {k: np.asarray(v) for k, v in inp.items()}
    x = inp["x"][0]
    cc = col_layout(inp["c"][0])
    consts = stage1_host_consts()
    consts["adaW0"] = mv_layout(inp["ada_w"][0][:, 0:4096])
    consts["Wx"] = mv_layout(np.concatenate([inp["ada_w"][0][:, 4096:], inp["kv_ada_w"], inp["ada_w"][1], inp["out_ada_w"]], axis=1))
    consts["bx"] = col_layout(np.concatenate([inp["ada_b"][0][4096:], inp["kv_ada_b"], inp["ada_b"][1], inp["out_ada_b"]]))
    r1 = _run(build_stage1(), [stage1_inputs(inp, i, consts) for i in range(NCORES)])
    o_full = np.concatenate([r1[i]["o"] for i in range(NCORES)], axis=1)
    adax = np.ascontiguousarray(np.concatenate([r1[i]["adax"] for i in range(NCORES)], axis=1))
    ada0x, kvax, ada1m, ada1x, outax = (np.ascontiguousarray(adax[:, a:b]) for a, b in ((0, 64), (64, 96), (96, 128), (128, 192), (192, 224)))
    maps = []
    fb = np.ascontiguousarray(np.broadcast_to(inp["forget_b"][None, :], (128, 16)), dtype=np.float32)
    for i in range(NCORES):
        ts = slice(i * TOK, (i + 1) * TOK)
        maps.append(dict(xT=np.ascontiguousarray(x[ts].T), oT=np.ascontiguousarray(o_full[ts].T),
                         ada=ada0x, kva=kvax, nffn=col_layout(inp["norm_ffn"][0]),
                         nkv=col_layout(inp["kv_norm"]), wout=inp["gdn_w_out"][0], win=inp["ffn_w_in"][0], wo2=inp["ffn_w_out"][0],
                         kvw=inp["kv_w"], knorm=col_layout(inp["k_norm"]), fb=fb))
    r2 = _run(build_stage2(), maps)
    x1 = np.concatenate([r2[i]["x1T"].T for i in range(NCORES)], axis=0)
    KT = np.concatenate([r2[i]["kT"] for i in range(NCORES)], axis=1).reshape(2, 256, SEQ)
    V = np.concatenate([r2[i]["v"] for i in range(NCORES)], axis=0)
    lf = np.concatenate([r2[i]["lf"] for i in range(NCORES)], axis=0)
    U = (np.arange(128)[:, None] <= np.arange(128)[None, :]).astype(np.float32)
    toks = [np.concatenate([np.arange(b * 128, (b + 1) * 128) for b in core_blocks(i)]) for i in range(NCORES)]
    maps = []
    for i in range(NCORES):
        masks, selx = stage3a_host_consts(i)
        maps.append(dict(xT=np.ascontiguousarray(x1[toks[i]].T), ada=ada1m,
                         nmix=col_layout(inp["norm_mix"][1]), fwin=inp["fox_w_in"][0],
                         qnorm=col_layout(inp["q_norm"][0]), KT=np.ascontiguousarray(KT), V=V, lf=lf, U=U, masks=masks, selx=selx))
    r3 = _run(build_stage3a(), maps)
    maps = []
    for i in range(NCORES):
        maps.append(dict(xT=np.ascontiguousarray(x1[toks[i]].T), oT=r3[i]["ogT"],
                         ada=ada1x, kva=outax, nffn=col_layout(inp["norm_ffn"][1]),
                         nkv=col_layout(inp["out_norm"]), wout=inp["fox_w_out"][0], win=inp["ffn_w_in"][1], wo2=inp["ffn_w_out"][1]))
    r4 = _run(build_stage2(final=True), maps)
    out = np.zeros((1, SEQ, D), np.float32)
    for i in range(NCORES):
        out[0, toks[i]] = r4[i]["x1T"].T
    return out
```

```python
import contextlib
import numpy as np
import concourse.bass as bass
import concourse.mybir as mybir
from concourse.bass_utils import run_bass_kernel_spmd

F32 = mybir.dt.float32
BF16 = mybir.dt.bfloat16
AF = mybir.ActivationFunctionType
ALU = mybir.AluOpType
AX = mybir.AxisListType

NCORES = 8
SEM_LIMIT = 30000


class Prog:
    ENGS = ("pe", "act", "dve", "pool", "sp")

    def __init__(self, nc, n_dma_sems=40, self_sync=True):
        self.nc = nc
        self.stack = contextlib.ExitStack()
        self.streams = {e: [] for e in self.ENGS}
        self.cnt = {e: 0 for e in self.ENGS}
        self.esem = {e: nc.alloc_semaphore("s_" + e + "0") for e in self.ENGS}
        self.egen = {e: 0 for e in self.ENGS}
        self.seen = {e: {} for e in self.ENGS}
        self.lastw = {}
        self.readers = {}
        self.self_sync = self_sync
        self.dsem = [nc.alloc_semaphore("s_dma%d" % i) for i in range(n_dma_sems)]
        self.dcnt = [0] * n_dma_sems
        self.drr = 0
        self.sems = {}
        for e in self.ENGS:
            self.sems[id(self.esem[e])] = self.esem[e]
        for s in self.dsem:
            self.sems[id(s)] = s
        self.n_ins = 0
        self.uid = 0

    def sb(self, shape, dtype, name=None):
        self.uid += 1
        return self.stack.enter_context(self.nc.sbuf_tensor(name or "sb%d" % self.uid, list(shape), dtype))

    def ps(self, shape, dtype, name=None):
        self.uid += 1
        return self.stack.enter_context(self.nc.psum_tensor(name or "ps%d" % self.uid, list(shape), dtype))

    @staticmethod
    def _key(k):
        if isinstance(k, tuple):
            return tuple(Prog._key(x) for x in k)
        if isinstance(k, (str, int)):
            return k
        return k.name

    def _deps(self, reads, writes):
        need = {}
        raw = {}
        reads = [self._key(k) for k in reads]
        writes = [self._key(k) for k in writes]

        def add(d, sk, v):
            if d.get(sk, 0) < v:
                d[sk] = v
        for k in reads:
            lw = self.lastw.get(k)
            if lw is not None:
                add(need, *lw)
                add(raw, *lw)
        for k in writes:
            lw = self.lastw.get(k)
            if lw is not None:
                add(need, *lw)
            for sk, v in self.readers.get(k, {}).items():
                add(need, sk, v)
        return need, raw

    def _emit_waits(self, eng, deps):
        need, raw = deps
        own = id(self.esem[eng])
        for sk, v in need.items():
            if sk == own:
                if eng == "pe" or not self.self_sync:
                    continue
            if self.seen[eng].get(sk, 0) >= v:
                continue
            self.seen[eng][sk] = v
            sem = self.sems[sk]
            self.streams[eng].append(lambda e, sem=sem, v=v: e.wait_ge(sem, v))
            self.n_ins += 1

    def _record(self, reads, writes, sk, v):
        reads = [self._key(k) for k in reads]
        writes = [self._key(k) for k in writes]
        for k in writes:
            self.lastw[k] = (sk, v)
            self.readers[k] = {}
        for k in reads:
            r = self.readers.setdefault(k, {})
            if r.get(sk, 0) < v:
                r[sk] = v

    def _excl(self, eng, reads, writes):
        extra = {}
        for k in reads:
            k = self._key(k)
            if isinstance(k, str) and k.startswith("psb"):
                for sk, v in self.readers.get(k, {}).items():
                    if sk != id(self.esem[eng]) and extra.get(sk, 0) < v:
                        extra[sk] = v
        return extra

    def op(self, eng, fn, reads=(), writes=()):
        extra = self._excl(eng, reads, writes)
        need, raw = self._deps(reads, writes)
        for sk, v in extra.items():
            if need.get(sk, 0) < v:
                need[sk] = v
        self._emit_waits(eng, (need, raw))
        if self.cnt[eng] >= SEM_LIMIT:
            self.egen[eng] += 1
            s = self.nc.alloc_semaphore("s_%s%d" % (eng, self.egen[eng]))
            self.esem[eng] = s
            self.sems[id(s)] = s
            self.cnt[eng] = 0
        self.cnt[eng] += 1
        sem = self.esem[eng]
        v = self.cnt[eng]
        self.streams[eng].append(lambda e, fn=fn, sem=sem: fn(e).then_inc(sem, 1))
        self.n_ins += 1
        self._record(reads, writes, id(sem), v)

    def dma(self, eng, out, in_, reads=(), writes=(), **kw):
        half = len(self.dsem) // 2
        if eng == "pool":
            self.drr_sw = (getattr(self, "drr_sw", -1) + 1) % half
            i = half + self.drr_sw
        else:
            self.drr = (self.drr + 1) % half
            i = self.drr
        sem = self.dsem[i]
        need, raw = self._deps(reads, writes)
        if self.dcnt[i] > 0:
            sk = id(sem)
            if need.get(sk, 0) < self.dcnt[i]:
                need[sk] = self.dcnt[i]
        self._emit_waits(eng, (need, raw))
        self.dcnt[i] += 16
        v = self.dcnt[i]
        self.streams[eng].append(
            lambda e, out=out, in_=in_, sem=sem, kw=kw: e.dma_start(out=out, in_=in_, **kw).then_inc(sem, 16))
        self.n_ins += 1
        self._record(reads, writes, id(sem), v)

    def wait_all(self, eng, keys):
        need, _ = self._deps((), keys)
        for sk, v in need.items():
            if self.seen[eng].get(sk, 0) >= v:
                continue
            self.seen[eng][sk] = v
            sem = self.sems[sk]
            self.streams[eng].append(lambda e, sem=sem, v=v: e.wait_ge(sem, v))

    def end_barrier(self, eng="sp"):
        for e2 in self.ENGS:
            if e2 == eng or self.cnt[e2] == 0:
                continue
            sem, v = self.esem[e2], self.cnt[e2]
            self.streams[eng].append(lambda e, sem=sem, v=v: e.wait_ge(sem, v))
        for i, sem in enumerate(self.dsem):
            if self.dcnt[i] > 0:
                v = self.dcnt[i]
                self.streams[eng].append(lambda e, sem=sem, v=v: e.wait_ge(sem, v))

    def emit(self):
        self.end_barrier("sp")
        with self.nc.Block() as block:
            @block.tensor
            def _(e):
                for f in self.streams["pe"]:
                    f(e)

            @block.scalar
            def _(e):
                for f in self.streams["act"]:
                    f(e)

            @block.vector
            def _(e):
                for f in self.streams["dve"]:
                    f(e)

            @block.gpsimd
            def _(e):
                for f in self.streams["pool"]:
                    f(e)

            @block.sync
            def _(e):
                for f in self.streams["sp"]:
                    f(e)
        self.stack.close()


D = 2048
KC = D // 128
EPS = 1e-6
FFN_H = 5632


def mv_layout(w):
    w = np.asarray(w, dtype=np.float32)
    noc = w.shape[1] // 128
    return np.ascontiguousarray(w.reshape(KC, 128, noc, 128).transpose(1, 2, 0, 3).reshape(128, noc, KC * 128))


def col_layout(v):
    v = np.ascontiguousarray(v, dtype=np.float32)
    return np.ascontiguousarray(v.reshape(-1, 128).T)


class Ctx:
    def __init__(self, P, mvblk=None):
        self.P = P
        self.ps = [P.ps([128, 512], F32, name="psb%d" % i) for i in range(8)]
        self.ones = P.sb([128, 128], F32, name="ones_f")
        P.op("pool", lambda e: e.memset(self.ones[:], 1.0), writes=[self.ones])
        self.sq = [P.sb([128, 512], F32, name="sq%d" % i) for i in range(2)]
        self.sqi = 0
        self.rstd = P.sb([128, 512], F32, name="rstd")
        self.tmp = [P.sb([128, 512], F32, name="tmpf%d" % i) for i in range(2)]
        self.tmpi = 0
        if mvblk is None:
            mv = [P.sb([128, KC, 128], F32, name="mvblk%d" % i) for i in range(2)]
            mvblk = [(t[:], t) for t in mv]
        self.mvblk = mvblk
        self.mvi = 0
        self.epsb = P.sb([128, 1], F32, name="epsb")
        P.op("pool", lambda e: e.memset(self.epsb[:], EPS), writes=[self.epsb])
        self.oneb = P.sb([128, 1], F32, name="oneb")
        P.op("pool", lambda e: e.memset(self.oneb[:], 1.0), writes=[self.oneb])

    def next_sq(self):
        self.sqi = (self.sqi + 1) % len(self.sq)
        return self.sq[self.sqi]

    def next_tmp(self):
        self.tmpi = (self.tmpi + 1) % len(self.tmp)
        return self.tmp[self.tmpi]


def load_cond(P, C, c_dram):
    craw = P.sb([128, KC], F32, name="craw")
    cond = P.sb([128, KC], F32, name="cond")
    P.dma("sp", craw[:], c_dram, writes=[craw])
    P.op("act", lambda e: e.activation(out=cond[:], in_=craw[:], func=AF.Silu), reads=[craw], writes=[cond])
    return cond


def matvec(P, C, w_dram, ncols, cond, bias_tile, out_tile, ps):
    noc = ncols // 128
    for oc in range(noc):
        C.mvi ^= 1
        blk, bkey = C.mvblk[C.mvi]
        P.dma("sp", blk, w_dram[:, oc, :].rearrange("p (kc n) -> p kc n", kc=KC), writes=[bkey])
        for kc in range(KC):
            P.op("pe", lambda e, blk=blk, kc=kc, oc=oc: e.matmul(
                ps[:, oc:oc + 1], lhsT=blk[:, kc, :], rhs=cond[:, kc:kc + 1], start=(kc == 0), stop=(kc == KC - 1)),
                reads=[bkey, cond], writes=[ps])
    P.op("dve", lambda e: e.tensor_tensor(out=out_tile[:, 0:noc], in0=ps[:, 0:noc], in1=bias_tile[:, 0:noc], op=ALU.add),
         reads=[ps, bias_tile], writes=[out_tile])


def mod_coeffs(P, normw, scale_ap, name, adakey):
    a = P.sb([128, KC], F32, name=name)
    P.op("dve", lambda e: e.scalar_tensor_tensor(out=a[:], in0=scale_ap, scalar=1.0, in1=normw[:], op0=ALU.add, op1=ALU.mult),
         reads=[normw, adakey], writes=[a])
    return a


def rms_rstd(P, C, ps, n_feat, T, rkeys):
    P.op("act", lambda e: e.activation(out=C.rstd[:, 0:T], in_=ps[:, 0:T], func=AF.Ln, scale=1.0 / n_feat, bias=C.epsb[:]),
         reads=[ps, C.epsb] + list(rkeys), writes=[C.rstd])
    P.op("act", lambda e: e.activation(out=C.rstd[:, 0:T], in_=C.rstd[:, 0:T], func=AF.Exp, scale=-0.5),
         reads=[C.rstd], writes=[C.rstd])


def rmsnorm_mod(P, C, xT, xkey, hT, hkey, acol, bcol, ntt, T=512, inplace=False, tts=None, xoff=None):
    ps = C.ps[7]
    for tt in (tts if tts is not None else range(ntt)):
        ts = slice(tt * T, (tt + 1) * T)
        hs = ts
        if xoff is not None:
            ts = slice(xoff, xoff + T)
        for kc in range(KC):
            sq = C.next_sq()
            P.op("act", lambda e, sq=sq, kc=kc, ts=ts: e.activation(out=sq[:, 0:T], in_=xT[:, kc, ts], func=AF.Square),
                 reads=[(xkey, kc, tt)], writes=[sq])
            P.op("pe", lambda e, sq=sq, kc=kc: e.matmul(ps[:, 0:T], lhsT=C.ones[:], rhs=sq[:, 0:T], start=(kc == 0), stop=(kc == KC - 1)),
                 reads=[sq, C.ones], writes=[ps])
        rms_rstd(P, C, ps, D, T, [])
        for kc in range(KC):
            tmp = C.next_tmp()
            P.op("dve", lambda e, tmp=tmp, kc=kc, ts=ts: e.scalar_tensor_tensor(
                out=tmp[:, 0:T], in0=xT[:, kc, ts], scalar=acol[:, kc:kc + 1], in1=C.rstd[:, 0:T], op0=ALU.mult, op1=ALU.mult),
                reads=[(xkey, kc, tt), acol, C.rstd], writes=[tmp])
            P.op("act", lambda e, tmp=tmp, kc=kc, hs=hs: e.activation(
                out=hT[:, kc, hs], in_=tmp[:, 0:T], func=AF.Identity, bias=bcol[:, kc:kc + 1], scale=1.0),
                reads=[tmp, bcol], writes=[(xkey, kc, tt) if inplace else (hkey, tt)])


def ffn(P, C, xT, xkey, hT, hkey, gcol, gkey, w_in_dram, w_out_dram, wA, wB, hid, ntt, T=512):
    nblk = FFN_H // 256
    wiv = w_in_dram.rearrange("(kc p) n -> p kc n", p=128)
    wov = w_out_dram.rearrange("(c p) n -> p c n", p=128)
    for j in range(nblk):
        wa = wA[j % len(wA)]
        wb = wB[j % len(wB)]
        hd = hid[j % len(hid)]
        wa3 = wa[:].rearrange("p (kc n) -> p kc n", kc=KC)
        wb3 = wb[:].rearrange("p (c n) -> p c n", c=2)
        P.dma("pool", wa3[:, :, 0:256], wiv[:, :, j * 256:(j + 1) * 256], writes=[(wa, 0)])
        P.dma("pool", wa3[:, :, 256:512], wiv[:, :, FFN_H + j * 256:FFN_H + (j + 1) * 256], writes=[(wa, 1)])
        P.dma("pool", wb3, wov[:, 2 * j:2 * j + 2, :], writes=[wb])
        for tt in range(ntt):
            ts = slice(tt * T, (tt + 1) * T)
            for c2 in range(2):
                pg = C.ps[(2 * tt + c2) % 2]
                pu = C.ps[2 + (2 * tt + c2) % 2]
                for kc in range(KC):
                    P.op("pe", lambda e, pg=pg, kc=kc, c2=c2, ts=ts, wa3=wa3: e.matmul(
                        pg[:, 0:T], lhsT=wa3[:, kc, c2 * 128:(c2 + 1) * 128], rhs=hT[:, kc, ts], start=(kc == 0), stop=(kc == KC - 1)),
                        reads=[(wa, 0), (hkey, tt)], writes=[pg])
                for kc in range(KC):
                    P.op("pe", lambda e, pu=pu, kc=kc, c2=c2, ts=ts, wa3=wa3: e.matmul(
                        pu[:, 0:T], lhsT=wa3[:, kc, 256 + c2 * 128:256 + (c2 + 1) * 128], rhs=hT[:, kc, ts], start=(kc == 0), stop=(kc == KC - 1)),
                        reads=[(wa, 1), (hkey, tt)], writes=[pu])
                tmp = C.next_tmp()
                P.op("act", lambda e, tmp=tmp, pg=pg: e.activation(out=tmp[:, 0:T], in_=pg[:, 0:T], func=AF.Silu),
                     reads=[pg], writes=[tmp])
                P.op("dve", lambda e, tmp=tmp, pu=pu, c2=c2, ts=ts, hd=hd: e.tensor_tensor(
                    out=hd[:, c2, ts], in0=pu[:, 0:T], in1=tmp[:, 0:T], op=ALU.mult),
                    reads=[pu, tmp], writes=[(hd, tt)])
            for oc in range(KC):
                po = C.ps[4 + oc % 3]
                for c2 in range(2):
                    P.op("pe", lambda e, po=po, c2=c2, oc=oc, ts=ts, wb3=wb3, hd=hd: e.matmul(
                        po[:, 0:T], lhsT=wb3[:, c2, oc * 128:(oc + 1) * 128], rhs=hd[:, c2, ts], start=(c2 == 0), stop=(c2 == 1)),
                        reads=[wb, (hd, tt)], writes=[po])
                P.op("dve", lambda e, po=po, oc=oc, ts=ts: e.scalar_tensor_tensor(
                    out=xT[:, oc, ts], in0=po[:, 0:T], scalar=gcol[:, oc:oc + 1], in1=xT[:, oc, ts], op0=ALU.mult, op1=ALU.add),
                    reads=[po, (xkey, oc, tt), gkey], writes=[(xkey, oc, tt)])


TOK = 1024


def build_stage2(stop=99, final=False):
    nc = bass.Bass("TRN2", target_bir_lowering=False)
    dt = nc.dram_tensor
    xT_d = dt("xT", [D, TOK], F32, kind="ExternalInput").ap()
    oT_d = dt("oT", [4096, TOK], BF16, kind="ExternalInput").ap()
    ada_d = dt("ada", [128, 64], F32, kind="ExternalInput").ap()
    kva_d = dt("kva", [128, 32], F32, kind="ExternalInput").ap()
    nffn_d = dt("nffn", [128, KC], F32, kind="ExternalInput").ap()
    nkv_d = dt("nkv", [128, KC], F32, kind="ExternalInput").ap()
    wout_d = dt("wout", [4096, D], F32, kind="ExternalInput").ap()
    win_d = dt("win", [D, 2 * FFN_H], F32, kind="ExternalInput").ap()
    wo2_d = dt("wo2", [FFN_H, D], F32, kind="ExternalInput").ap()
    if not final:
        kvw_d = dt("kvw", [D, 1040], F32, kind="ExternalInput").ap()
        knorm_d = dt("knorm", [128, 2], F32, kind="ExternalInput").ap()
        fb_d = dt("fb", [128, 16], F32, kind="ExternalInput").ap()
    x1T_d = dt("x1T", [D, TOK], F32, kind="ExternalOutput").ap()
    if not final:
        kT_d = dt("kT", [512, TOK], BF16, kind="ExternalOutput").ap()
        v_d = dt("v", [TOK, 512], BF16, kind="ExternalOutput").ap()
        lf_d = dt("lf", [TOK, 16], F32, kind="ExternalOutput").ap()

    P = Prog(nc)
    C = Ctx(P, mvblk=[])
    xT = P.sb([128, KC, TOK], F32, name="xT_sb")
    hbuf = P.sb([128, KC * TOK], BF16, name="hbuf")
    hT = hbuf[:].rearrange("p (c t) -> p c t", c=KC)
    wA = [P.sb([128, 8192], BF16, name="wA%d" % i) for i in range(3)]
    wB = [P.sb([128, 4096], BF16, name="wB%d" % i) for i in range(2)]
    hid = [P.sb([128, 2, TOK], BF16, name="hid%d" % i) for i in range(2)]
    small = {}
    smalls = [("nffn", nffn_d, [128, KC]), ("nkv", nkv_d, [128, KC])]
    if not final:
        smalls += [("knorm", knorm_d, [128, 2]), ("fb", fb_d, [128, 16])]
    for nm, d_, shp in smalls:
        t = P.sb(shp, F32, name="sm_" + nm)
        P.dma("sp", t[:], d_, writes=[t])
        small[nm] = t
    xkeys = [("x", kc, tt) for kc in range(KC) for tt in range(2)]
    P.dma("sp", xT[:], xT_d.rearrange("(c p) t -> p c t", p=128), writes=xkeys)
    ada = P.sb([128, 64], F32, name="ada_sb")
    P.dma("sp", ada[:], ada_d, writes=[ada])
    kva = P.sb([128, 32], F32, name="kva_sb")
    P.dma("sp", kva[:], kva_d, writes=[kva])

    def finish():
        P.dma("sp", x1T_d.rearrange("(c p) t -> p c t", p=128), xT[:], reads=xkeys + [ada, kva], writes=["out_x1"])
        P.wait_all("sp", ["out_x1"])
        P.emit()
        return nc
    if stop == 1:
        return finish()
    ov = oT_d.rearrange("(c p) t -> p c t", p=128)
    wov = wout_d.rearrange("(c p) n -> p c n", p=128)
    o3 = hbuf[:].rearrange("p (c t) -> p c t", c=32)
    for tt in range(2):
        ts = slice(tt * 512, (tt + 1) * 512)
        P.dma("sp", o3, ov[:, :, ts], writes=[("h", 0), ("h", 1)])
        for ob in range(8):
            w = wA[ob % 2]
            w3 = w[:].rearrange("p (c n) -> p c n", c=32)
            P.dma("pool", w3, wov[:, :, ob * 256:(ob + 1) * 256], writes=[(w, 0), (w, 1)])
            for o2 in range(2):
                oc = ob * 2 + o2
                po = C.ps[4 + oc % 3]
                for kc in range(32):
                    P.op("pe", lambda e, po=po, kc=kc, o2=o2, w3=w3: e.matmul(
                        po[:], lhsT=w3[:, kc, o2 * 128:(o2 + 1) * 128], rhs=o3[:, kc, :], start=(kc == 0), stop=(kc == 31)),
                        reads=[(w, 0), (w, 1), ("h", 0), ("h", 1)], writes=[po])
                P.op("dve", lambda e, po=po, oc=oc, ts=ts: e.scalar_tensor_tensor(
                    out=xT[:, oc, ts], in0=po[:], scalar=ada[:, oc:oc + 1], in1=xT[:, oc, ts], op0=ALU.mult, op1=ALU.add),
                    reads=[po, ("x", oc, tt), ada], writes=[("x", oc, tt)])

    if stop == 2:
        return finish()
    a_f = mod_coeffs(P, small["nffn"], ada[:, 32:48], "a_f", ada)
    rmsnorm_mod(P, C, xT, "x", hT, "h", a_f, ada[:, 16:32], 2)
    if stop == 3:
        return finish()
    ffn(P, C, xT, "x", hT, "h", ada[:, 48:64], ada, win_d, wo2_d, wA, wB, hid, 2)
    if stop == 4:
        return finish()
    a_kv = mod_coeffs(P, small["nkv"], kva[:, 16:32], "a_kv", kva)
    if final:
        rmsnorm_mod(P, C, xT, "x", xT, "x", a_kv, kva[:, 0:16], 2, inplace=True)
        return finish()
    P.dma("sp", x1T_d.rearrange("(c p) t -> p c t", p=128), xT[:], reads=xkeys, writes=["out_x1"])

    rmsnorm_mod(P, C, xT, "x", hT, "h", a_kv, kva[:, 0:16], 2)
    kvv = kvw_d.rearrange("(kc p) n -> p kc n", p=128)
    wk3 = wA[0][:].rearrange("p (kc n) -> p kc n", kc=KC)
    wv3 = wA[1][:].rearrange("p (kc n) -> p kc n", kc=KC)
    wf = P.sb([128, KC, 16], BF16, name="wf")
    P.dma("pool", wk3, kvv[:, :, 0:512], writes=[(wA[0], 0), (wA[0], 1)])
    P.dma("pool", wv3, kvv[:, :, 512:1024], writes=[(wA[1], 0), (wA[1], 1)])
    P.dma("pool", wf[:], kvv[:, :, 1024:1040], writes=[wf])
    kraw = P.sb([128, 2, 512], F32, name="kraw")
    kout = P.sb([128, 4, TOK], BF16, name="kout")
    for tt in range(2):
        ts = slice(tt * 512, (tt + 1) * 512)
        for kh in range(2):
            for cc in range(2):
                c = kh * 2 + cc
                pk = C.ps[cc]
                for kc in range(KC):
                    P.op("pe", lambda e, pk=pk, kc=kc, c=c, ts=ts: e.matmul(
                        pk[:], lhsT=wk3[:, kc, c * 128:(c + 1) * 128], rhs=hT[:, kc, ts], start=(kc == 0), stop=(kc == KC - 1)),
                        reads=[(wA[0], 0), (wA[0], 1), ("h", tt)], writes=[pk])
                P.op("dve", lambda e, pk=pk, cc=cc: e.tensor_copy(out=kraw[:, cc, :], in_=pk[:]), reads=[pk], writes=[(kraw, cc)])
                sq = C.next_sq()
                P.op("act", lambda e, sq=sq, pk=pk: e.activation(out=sq[:], in_=pk[:], func=AF.Square), reads=[pk], writes=[sq])
                P.op("pe", lambda e, sq=sq, cc=cc: e.matmul(C.ps[7][:], lhsT=C.ones[:], rhs=sq[:], start=(cc == 0), stop=(cc == 1)),
                     reads=[sq, C.ones], writes=[C.ps[7]])
            rms_rstd(P, C, C.ps[7], 256, 512, [])
            for cc in range(2):
                c = kh * 2 + cc
                P.op("dve", lambda e, cc=cc, c=c, ts=ts: e.scalar_tensor_tensor(
                    out=kout[:, c, ts], in0=kraw[:, cc, :], scalar=small["knorm"][:, cc:cc + 1], in1=C.rstd[:], op0=ALU.mult, op1=ALU.mult),
                    reads=[(kraw, cc), small["knorm"], C.rstd], writes=[(kout, c)])
    P.dma("sp", kT_d.rearrange("(c p) t -> p c t", p=128), kout[:], reads=[(kout, c) for c in range(4)], writes=["out_k"])
    vout = P.sb([128, 8, 512], BF16, name="vout")
    lfo = P.sb([128, 8, 16], F32, name="lfo")
    lft = P.sb([128, 8, 16], F32, name="lft")
    for tb in range(8):
        tbs = slice(tb * 128, (tb + 1) * 128)
        pv = C.ps[tb % 2]
        for kc in range(KC):
            P.op("pe", lambda e, pv=pv, kc=kc, tbs=tbs: e.matmul(
                pv[:], lhsT=hT[:, kc, tbs], rhs=wv3[:, kc, :], start=(kc == 0), stop=(kc == KC - 1)),
                reads=[(wA[1], 0), (wA[1], 1), ("h", tb // 4)], writes=[pv])
        P.op("act", lambda e, pv=pv, tb=tb: e.activation(out=vout[:, tb, :], in_=pv[:], func=AF.Copy), reads=[pv], writes=[(vout, tb)])
        pf = C.ps[2 + tb % 2]
        for kc in range(KC):
            P.op("pe", lambda e, pf=pf, kc=kc, tbs=tbs: e.matmul(
                pf[:, 0:16], lhsT=hT[:, kc, tbs], rhs=wf[:, kc, :], start=(kc == 0), stop=(kc == KC - 1)),
                reads=[wf, ("h", tb // 4)], writes=[pf])
        P.op("dve", lambda e, pf=pf, tb=tb: e.tensor_tensor(out=lft[:, tb, :], in0=pf[:, 0:16], in1=small["fb"][:], op=ALU.add),
             reads=[pf, small["fb"]], writes=[(lft, tb)])
    P.op("act", lambda e: e.activation(out=lft[:], in_=lft[:], func=AF.Exp, scale=-1.0), reads=[(lft, tb) for tb in range(8)], writes=[lft])
    P.op("act", lambda e: e.activation(out=lft[:], in_=lft[:], func=AF.Ln, bias=C.oneb[:], scale=1.0), reads=[lft, C.oneb], writes=[lft])
    P.op("dve", lambda e: e.tensor_scalar(out=lfo[:], in0=lft[:], scalar1=-1.0, scalar2=None, op0=ALU.mult), reads=[lft], writes=[lfo])
    P.dma("sp", v_d.rearrange("(b p) n -> p b n", p=128), vout[:], reads=[(vout, tb) for tb in range(8)], writes=["out_v"])
    P.dma("sp", lf_d.rearrange("(b p) n -> p b n", p=128), lfo[:], reads=[lfo], writes=["out_lf"])
    P.wait_all("sp", ["out_x1", "out_k", "out_v", "out_lf"])
    P.emit()
    return nc


def core_blocks(i):
    return [i, 15 - i, 16 + i, 31 - i, 32 + i, 47 - i, 48 + i, 63 - i]


SLOT_EXT = [8, 16, 24, 32, 40, 48, 56, 64]


def stage3a_host_consts(i):
    import ml_dtypes
    blks = core_blocks(i)
    tri = (np.arange(128)[:, None] <= np.arange(128)[None, :]).astype(np.float32)
    masks = np.zeros((128, 8, 8, 128), np.float32)
    selx = np.zeros((8, 128, 64, 16), np.float32)
    for s_, gb in enumerate(blks):
        for m in range(8):
            jb = 8 * s_ + m
            if jb < gb:
                masks[:, s_, m, :] = 1.0
            elif jb == gb:
                masks[:, s_, m, :] = tri
        selx[s_, :, gb, :] = 1.0
    return masks.astype(ml_dtypes.bfloat16), selx


def build_stage3a():
    nc = bass.Bass("TRN2", target_bir_lowering=False)
    dt = nc.dram_tensor
    xT_d = dt("xT", [D, TOK], F32, kind="ExternalInput").ap()
    ada_d = dt("ada", [128, 32], F32, kind="ExternalInput").ap()
    nmix_d = dt("nmix", [128, KC], F32, kind="ExternalInput").ap()
    win_d = dt("fwin", [D, 8192], F32, kind="ExternalInput").ap()
    qn_d = dt("qnorm", [128, 2], F32, kind="ExternalInput").ap()
    K_d = dt("KT", [2, 256, 8192], BF16, kind="ExternalInput").ap()
    V_d = dt("V", [8192, 512], BF16, kind="ExternalInput").ap()
    lf_d = dt("lf", [8192, 16], F32, kind="ExternalInput").ap()
    U_d = dt("U", [128, 128], F32, kind="ExternalInput").ap()
    mask_d = dt("masks", [128, 8, 8, 128], BF16, kind="ExternalInput").ap()
    selx_d = dt("selx", [8, 128, 64, 16], F32, kind="ExternalInput").ap()
    og_d = dt("ogT", [4096, TOK], BF16, kind="ExternalOutput").ap()
    Qs_d = dt("Qs", [4096, TOK], BF16, kind="Internal").ap()
    Gs_d = dt("Gs", [4096, TOK], BF16, kind="Internal").ap()

    P = Prog(nc)
    C = Ctx(P, mvblk=[])
    bufK = P.sb([128, 8192], F32, name="bufK")
    bufV = P.sb([128, 16384], BF16, name="bufV")
    xt3 = bufK[:].rearrange("p (c t) -> p c t", c=KC)
    K3 = bufK[:].bitcast(BF16).rearrange("p (c t) -> p c t", c=2)
    hT = bufV[:].rearrange("p (c t) -> p c t", c=KC)
    V3 = bufV[:].rearrange("p (b d) -> p b d", b=64)
    small = {}
    for nm, d_, shp in (("nmix", nmix_d, [128, KC]), ("qnorm", qn_d, [128, 2]), ("U", U_d, [128, 128])):
        t = P.sb(shp, F32, name="sm_" + nm)
        P.dma("sp", t[:], d_, writes=[t])
        small[nm] = t
    ada = P.sb([128, 32], F32, name="ada_sb")
    P.dma("sp", ada[:], ada_d, writes=[ada])
    a_m = mod_coeffs(P, small["nmix"], ada[:, 16:32], "a_m", ada)
    xv = xT_d.rearrange("(c p) t -> p c t", p=128)
    for tt in range(2):
        P.dma("sp", xt3, xv[:, :, tt * 512:(tt + 1) * 512], writes=[("xt", kc, t2) for kc in range(KC) for t2 in range(2)] + ["bufK"])
        rmsnorm_mod(P, C, xt3, "xt", hT, "h", a_m, ada[:, 0:16], 2, tts=[tt], xoff=0)

    wq = [P.sb([128, KC, 256], BF16, name="wq%d" % i) for i in range(4)]
    qraw = P.sb([128, 2, TOK], F32, name="qraw")
    qo = [P.sb([128, 2, TOK], BF16, name="qo%d" % i) for i in range(2)]
    wiv = win_d.rearrange("(kc p) n -> p kc n", p=128)
    Qsv = Qs_d.rearrange("(h c p) t -> h p c t", p=128, c=2)
    Gsv = Gs_d.rearrange("(h c p) t -> h p c t", p=128, c=2)
    for h in range(16):
        for isg in range(2):
            w = wq[(2 * h + isg) % 4]
            P.dma("pool", w[:], wiv[:, :, isg * 4096 + h * 256: isg * 4096 + (h + 1) * 256], writes=[w])
            out = qo[isg]
            for tt in range(2):
                ts = slice(tt * 512, (tt + 1) * 512)
                for cc in range(2):
                    pq = C.ps[(2 * tt + cc) % 4]
                    for kc in range(KC):
                        P.op("pe", lambda e, pq=pq, kc=kc, cc=cc, ts=ts, w=w: e.matmul(
                            pq[:], lhsT=w[:, kc, cc * 128:(cc + 1) * 128], rhs=hT[:, kc, ts], start=(kc == 0), stop=(kc == KC - 1)),
                            reads=[w, ("h", tt)], writes=[pq])
                    if isg:
                        P.op("act", lambda e, pq=pq, cc=cc, ts=ts, out=out: e.activation(out=out[:, cc, ts], in_=pq[:], func=AF.Sigmoid),
                             reads=[pq], writes=[(out, cc, tt)])
                    else:
                        P.op("dve", lambda e, pq=pq, cc=cc, ts=ts: e.tensor_copy(out=qraw[:, cc, ts], in_=pq[:]), reads=[pq], writes=[(qraw, cc, tt)])
                        sq = C.next_sq()
                        P.op("act", lambda e, sq=sq, pq=pq: e.activation(out=sq[:], in_=pq[:], func=AF.Square), reads=[pq], writes=[sq])
                        P.op("pe", lambda e, sq=sq, cc=cc: e.matmul(C.ps[7][:], lhsT=C.ones[:], rhs=sq[:], start=(cc == 0), stop=(cc == 1)),
                             reads=[sq, C.ones], writes=[C.ps[7]])
                if not isg:
                    rms_rstd(P, C, C.ps[7], 256, 512, [])
                    for cc in range(2):
                        P.op("dve", lambda e, cc=cc, ts=ts, out=out: e.scalar_tensor_tensor(
                            out=out[:, cc, ts], in0=qraw[:, cc, ts], scalar=small["qnorm"][:, cc:cc + 1], in1=C.rstd[:], op0=ALU.mult, op1=ALU.mult),
                            reads=[(qraw, cc, tt), small["qnorm"], C.rstd], writes=[(out, cc, tt)])
            P.dma("sp", (Gsv if isg else Qsv)[h], out[:], reads=[(out, cc, tt) for cc in range(2) for tt in range(2)],
                  writes=[("QG", isg, h)])

    lft = P.sb([128, 64, 16], F32, name="lft")
    csl = P.sb([128, 64, 16], F32, name="csl")
    tot = P.sb([128, 64, 16], F32, name="tot")
    incl = P.sb([128, 64, 16], F32, name="incl")
    P.dma("sp", lft[:], lf_d.rearrange("(b p) h -> p b h", p=128), writes=[lft])
    lf2 = lft[:].rearrange("p b h -> p (b h)")
    for half in range(2):
        hs = slice(half * 512, (half + 1) * 512)
        pa = C.ps[half]
        pb = C.ps[2 + half]
        P.op("pe", lambda e, pa=pa, hs=hs: e.matmul(pa[:], lhsT=small["U"][:], rhs=lf2[:, hs], start=True, stop=True),
             reads=[small["U"], lft], writes=[pa])
        P.op("pe", lambda e, pb=pb, hs=hs: e.matmul(pb[:], lhsT=C.ones[:], rhs=lf2[:, hs], start=True, stop=True),
             reads=[C.ones, lft], writes=[pb])
        P.op("dve", lambda e, pa=pa, hs=hs: e.tensor_copy(out=csl[:].rearrange("p b h -> p (b h)")[:, hs], in_=pa[:]), reads=[pa], writes=[csl])
        P.op("dve", lambda e, pb=pb, hs=hs: e.tensor_copy(out=tot[:].rearrange("p b h -> p (b h)")[:, hs], in_=pb[:]), reads=[pb], writes=[tot])
    P.op("dve", lambda e: e.tensor_copy(out=incl[:, 0, :], in_=tot[:, 0, :]), reads=[tot], writes=[incl])
    for b in range(1, 64):
        P.op("dve", lambda e, b=b: e.tensor_tensor(out=incl[:, b, :], in0=incl[:, b - 1, :], in1=tot[:, b, :], op=ALU.add),
             reads=[incl, tot], writes=[incl])
    P.op("dve", lambda e: e.tensor_tensor(out=csl[:], in0=csl[:], in1=incl[:], op=ALU.add), reads=[csl, incl], writes=[csl])
    P.op("dve", lambda e: e.tensor_tensor(out=lft[:], in0=tot[:], in1=csl[:], op=ALU.subtract), reads=[csl, tot], writes=[lft])
    Fneg = lft
    fref = P.sb([128, 8, 16], F32, name="fref")
    for s_ in range(8):
        P.dma("sp", tot[:], selx_d[s_], writes=[tot])
        P.op("dve", lambda e: e.tensor_tensor(out=tot[:], in0=tot[:], in1=incl[:], op=ALU.mult), reads=[tot, incl], writes=[tot])
        P.op("dve", lambda e, s_=s_: e.tensor_reduce(out=fref[:, s_, :], in_=tot[:].rearrange("p b h -> p h b"), axis=AX.X, op=ALU.add),
             reads=[tot], writes=[fref])

    masks = P.sb([128, 8, 8, 128], BF16, name="masks_sb")
    P.dma("sp", masks[:], mask_d, writes=[masks])
    onesb = P.sb([128, 128], BF16, name="onesb")
    P.op("pool", lambda e: e.memset(onesb[:], 1.0), writes=[onesb])
    Qt = P.sb([128, 4, 2, TOK], BF16, name="Qt")
    Gt = P.sb([128, 4, 2, TOK], BF16, name="Gt")
    bias = P.sb([128, 64, 4], F32, name="bias")
    pT = [P.sb([128, 4, 128], BF16, name="pT%d" % i) for i in range(3)]
    rec = P.sb([128, 512], F32, name="rec")
    pti = 0
    setsel = 0
    Kv = K_d.rearrange("k (c p) t -> k p c t", p=128)
    Vv = V_d.rearrange("(b p) n -> p b n", p=128)
    ogv = og_d.rearrange("(h c p) t -> p h c t", p=128, c=2)
    for kvh in range(2):
        P.dma("sp", K3, Kv[kvh], writes=["bufK"] + [("xt", kc, tt) for kc in range(KC) for tt in range(2)])
        P.dma("sp", V3, Vv[:, :, kvh * 256:(kvh + 1) * 256], writes=[("h", 0), ("h", 1)])
        for hg in range(2):
            h0 = kvh * 8 + hg * 4
            for j in range(4):
                P.dma("sp", Qt[:, j], Qsv[h0 + j], reads=[("QG", 0, h0 + j)], writes=[Qt])
                P.dma("sp", Gt[:, j], Gsv[h0 + j], reads=[("QG", 1, h0 + j)], writes=[Gt])
            for s_ in range(8):
                qs = slice(s_ * 128, (s_ + 1) * 128)
                ext = SLOT_EXT[s_]
                for j in range(4):
                    P.op("dve", lambda e, j=j, s_=s_, ext=ext, h0=h0: e.tensor_scalar(
                        out=bias[:, 0:ext, j], in0=Fneg[:, 0:ext, h0 + j], scalar1=fref[:, s_, h0 + j:h0 + j + 1], scalar2=0.0,
                        op0=ALU.add, op1=ALU.min), reads=[Fneg, fref], writes=[bias])
                setsel ^= 1
                acc = [C.ps[2 + 3 * setsel + k] for k in range(3)]
                def emit_S(jb):
                    ks = slice(jb * 128, (jb + 1) * 128)
                    pS = C.ps[jb % 2]
                    for cc in range(2):
                        P.op("pe", lambda e, pS=pS, cc=cc, ks=ks, qs=qs: e.matmul(
                            pS[:].rearrange("p (j t) -> p j t", j=4), lhsT=K3[:, cc, ks], rhs=Qt[:, :, cc, qs], start=(cc == 0), stop=(cc == 1)),
                            reads=["bufK", Qt], writes=[pS])
                emit_S(0)
                for jb in range(ext):
                    pS = C.ps[jb % 2]
                    if jb + 1 < ext:
                        emit_S(jb + 1)
                    pti = (pti + 1) % 3
                    pt = pT[pti]
                    for j in range(4):
                        P.op("act", lambda e, pS=pS, pt=pt, j=j, jb=jb: e.activation(
                            out=pt[:, j, :], in_=pS[:, j * 128:(j + 1) * 128], func=AF.Exp, scale=1.0 / 16.0, bias=bias[:, jb, j:j + 1]),
                            reads=[pS, bias], writes=[(pt, j)])
                    m = jb - 8 * s_
                    if m >= 0:
                        for j in range(4):
                            P.op("pool", lambda e, pt=pt, j=j, s_=s_, m=m: e.tensor_tensor(
                                out=pt[:, j, :], in0=pt[:, j, :], in1=masks[:, s_, m, :], op=ALU.mult),
                                reads=[(pt, j), masks], writes=[(pt, j)])
                    pt2 = pt[:].rearrange("p j t -> p (j t)")
                    for k in range(3):
                        lhs = onesb[:] if k == 2 else V3[:, jb, k * 128:(k + 1) * 128]
                        P.op("pe", lambda e, k=k, lhs=lhs, pt2=pt2, jb=jb, ext=ext, acc=acc: e.matmul(
                            acc[k][:], lhsT=lhs, rhs=pt2, start=(jb == 0), stop=(jb == ext - 1)),
                            reads=[(pt, 0), (pt, 1), (pt, 2), (pt, 3), ("h", 0), ("h", 1), onesb], writes=[acc[k]])
                P.op("dve", lambda e, acc=acc: e.reciprocal(out=rec[:], in_=acc[2][:]), reads=[acc[2]], writes=[rec])
                for k in range(2):
                    tmp = C.next_tmp()
                    P.op("dve", lambda e, tmp=tmp, k=k, acc=acc: e.tensor_tensor(out=tmp[:], in0=acc[k][:], in1=rec[:], op=ALU.mult),
                         reads=[acc[k], rec], writes=[tmp])
                    P.op("pool", lambda e, tmp=tmp, k=k, qs=qs: e.tensor_tensor(
                        out=Gt[:, :, k, qs], in0=tmp[:].rearrange("p (j t) -> p j t", j=4), in1=Gt[:, :, k, qs], op=ALU.mult),
                        reads=[tmp, Gt], writes=[Gt])
            P.dma("sp", ogv[:, h0:h0 + 4], Gt[:], reads=[Gt], writes=[("og", h0)])
    P.wait_all("sp", [("og", h0) for h0 in (0, 4, 8, 12)])
    P.emit()
    return nc


SEQ = 8192


def stage1_host_consts():
    import ml_dtypes
    I64 = np.eye(64, dtype=np.float32)
    U64 = (np.arange(64)[:, None] <= np.arange(64)[None, :]).astype(np.float32)
    LS = (np.arange(64)[None, :] < np.arange(64)[:, None]).astype(np.float32)
    return dict(I4=np.ascontiguousarray(np.tile(I64, (1, 4))), U64=U64,
                UM4=np.ascontiguousarray(np.tile(U64, (1, 4))), LS4=np.ascontiguousarray(np.tile(LS, (1, 4))),
                identb=np.eye(128, dtype=np.float32).astype(ml_dtypes.bfloat16))


def build_stage1(NT=16, stop=99):
    nc = bass.Bass("TRN2", target_bir_lowering=False)
    dt = nc.dram_tensor
    T = NT * 512
    xT_d = dt("xT", [D, T], F32, kind="ExternalInput").ap()
    c_d = dt("c", [128, KC], F32, kind="ExternalInput").ap()
    adaW_d = dt("adaW", [128, 32, D], F32, kind="ExternalInput").ap()
    adaB_d = dt("adaB", [128, 32], F32, kind="ExternalInput").ap()
    nmix_d = dt("nmix", [128, KC], F32, kind="ExternalInput").ap()
    w_d = dt("wqkvz", [D, 1536], F32, kind="ExternalInput").ap()
    wba_d = dt("wba", [D, 8], F32, kind="ExternalInput").ap()
    convw_d = dt("convw", [128, 8, 4], F32, kind="ExternalInput").ap()
    alog_d = dt("alog", [64, 8, 4], F32, kind="ExternalInput").ap()
    dtb_d = dt("dtb", [64, 8, 4], F32, kind="ExternalInput").ap()
    gn_d = dt("gnorm", [64, 512], F32, kind="ExternalInput").ap()
    I4_d = dt("I4", [64, 256], F32, kind="ExternalInput").ap()
    U64_d = dt("U64", [64, 64], F32, kind="ExternalInput").ap()
    UM4_d = dt("UM4", [64, 256], F32, kind="ExternalInput").ap()
    LS4_d = dt("LS4", [64, 256], F32, kind="ExternalInput").ap()
    idb_d = dt("identb", [128, 128], BF16, kind="ExternalInput").ap()
    Wx_d = dt("Wx", [128, 28, D], F32, kind="ExternalInput").ap()
    bx_d = dt("bx", [128, 28], F32, kind="ExternalInput").ap()
    adax_d = dt("adax", [128, 28], F32, kind="ExternalOutput").ap()
    o_d = dt("o", [T, 512], BF16, kind="ExternalOutput").ap()

    P = Prog(nc)
    xbuf = P.sb([128, KC * 512], F32, name="xbuf")
    xt3 = xbuf[:].rearrange("p (c t) -> p c t", c=KC)
    mv = [(xbuf[:, i * 2048:(i + 1) * 2048].rearrange("p (c n) -> p c n", c=KC), ("mvb", i)) for i in range(2)]
    C = Ctx(P, mvblk=mv)
    ps = C.ps
    ptb = ps[7][:].bitcast(BF16)

    def ld(name, d_, shp, dtype=F32, eng="sp"):
        t = P.sb(shp, dtype, name="c_" + name)
        P.dma(eng, t[:], d_, writes=[t])
        return t
    adaB = ld("adaB", adaB_d, [128, 32])
    nmix = ld("nmix", nmix_d, [128, KC])
    convw = ld("convw", convw_d, [128, 8, 4])
    alog = ld("alog", alog_d, [64, 8, 4])
    dtb = ld("dtb", dtb_d, [64, 8, 4])
    gn = ld("gn", gn_d, [64, 512])
    I4 = ld("I4", I4_d, [64, 256])
    U64 = ld("U64", U64_d, [64, 64])
    UM4 = ld("UM4", UM4_d, [64, 256])
    LS4 = ld("LS4", LS4_d, [64, 256])
    identb = ld("identb", idb_d, [128, 128], BF16)
    I64 = I4[:, 0:64]
    wqk = P.sb([128, KC, 1536], BF16, name="wqkvz_sb")
    wba = P.sb([128, KC, 8], BF16, name="wba_sb")
    wv_ = w_d.rearrange("(kc p) n -> p kc n", p=128)
    for j in range(3):
        P.dma("pool", wqk[:, :, j * 512:(j + 1) * 512], wv_[:, :, j * 512:(j + 1) * 512], writes=[(wqk, j)])
    P.dma("pool", wba[:], wba_d.rearrange("(kc p) n -> p kc n", p=128), writes=[wba])
    wkeys = [(wqk, j) for j in range(3)]

    cond = load_cond(P, C, c_d)
    ada = P.sb([128, 32], F32, name="ada")
    matvec(P, C, adaW_d, 2 * D, cond, adaB, ada, ps[6])
    a_m = mod_coeffs(P, nmix, ada[:, 16:32], "a_m", ada)
    bx = ld("bx", bx_d, [128, 28])
    adax = P.sb([128, 28], F32, name="adax_sb")
    matvec(P, C, Wx_d, 28 * 128, cond, bx, adax, ps[6])
    P.dma("sp", adax_d, adax[:], reads=[adax], writes=["out_adax"])
    eA = P.sb([64, 8, 4], F32, name="eA")
    P.op("act", lambda e: e.activation(out=eA[:], in_=alog[:], func=AF.Exp), reads=[alog], writes=[eA])

    hT = P.sb([128, KC, 512], BF16, name="hT")
    pre = P.sb([128, 8, 515], F32, name="pre")
    P.op("pool", lambda e: e.memset(pre[:], 0.0), writes=[(pre, n) for n in range(8)])
    qkT = [P.sb([128, 512], BF16, name="qkT%d" % n) for n in range(4)]
    vT = [P.sb([128, 512], BF16, name="vT%d" % n) for n in range(4)]
    actf = P.sb([128, 512], F32, name="actf") if False else None
    gz = P.sb([64, 8, 512], F32, name="gz")
    Sf = P.sb([128, 4, 128], F32, name="Sf")
    Sb = P.sb([128, 4, 128], BF16, name="Sb")
    P.op("pool", lambda e: e.memset(Sf[:], 0.0), writes=[Sf])
    P.op("pool", lambda e: e.memset(Sb[:], 0.0), writes=[Sb])
    sm = lambda name, dtype=F32: P.sb([64, 8, 4], dtype, name=name)
    beta, nbeta, xg, graw, gcum, eg, kds, bw = [sm(n) for n in ("beta", "nbeta", "xg", "graw", "gcum", "eg", "kds", "bw")]
    egl = P.sb([128, 32], F32, name="egl")
    t64 = lambda name, dtype=F32: P.sb([64, 4, 64], dtype, name=name)
    Dg, d1, d2, decT, dec, Y0 = [t64(n) for n in ("Dg", "d1", "d2", "decT", "dec", "Y0")]
    XX = [t64("XX%d" % i) for i in range(2)]
    YY = [t64("YY%d" % i) for i in range(2)]
    RR = [t64("RR%d" % i) for i in range(2)]
    HB = [(t64("TT%d" % i, BF16), t64("attnT%d" % i, BF16), P.sb([64, 4, 128], BF16, name="kdec%d" % i),
           P.sb([64, 4, 128], BF16, name="bv%d" % i), P.sb([128, 4, 64], BF16, name="nwT%d" % i)) for i in range(2)]
    kbw = P.sb([64, 4, 128], BF16, name="kbw")
    osb = P.sb([64, 4, 128], F32, name="osb")
    vn = P.sb([64, 512], BF16, name="vn")
    Oall = P.sb([64, 32, 128], F32, name="Oall")
    Osq = P.sb([64, 4, 128], F32, name="Osq")
    ss = P.sb([64, 32], F32, name="ss")
    ot = P.sb([64, 8, 512], BF16, name="ot")
    xv = xT_d.rearrange("(c p) t -> p c t", p=128)
    ov = o_d.rearrange("(c p) n -> p c n", p=64)
    pba = ps[2][0:64, 0:64]
    pgc = ps[2][0:64, 64:96]
    pgl = ps[2][:, 96:128]
    pX0 = ps[2][0:64, 128:384]
    pKQ = ps[3][0:64, 0:256]
    pG = ps[3][0:64, 256:512]
    pX = ps[4][0:64, 0:256]
    pY = ps[4][0:64, 256:512]
    pR = ps[5][0:64, 0:256]
    pwT = ps[5][:, 256:512]
    pVN = ps[6][0:64, :]
    pO = ps[0][0:64, :]
    pSU = ps[1]
    out_keys = []

    def finish_now(extra):
        P.dma("sp", ov[:, 0:8, :], ot[:], reads=list(extra) + [(ot, c) for c in range(8)], writes=[("out", 0)])
        P.wait_all("sp", [("out", 0)])
        P.emit()
        return nc
    if stop == 1:
        return finish_now([ada, a_m, eA, wba, Sf, Sb] + wkeys)

    for ti in range(NT):
        tsl = slice(ti * 512, (ti + 1) * 512)
        if ti == 0:
            P.dma("sp", xt3, xv[:, :, tsl], writes=[("xt", kc, 0) for kc in range(KC)] + [("mvb", 0), ("mvb", 1)])
        rmsnorm_mod(P, C, xt3, "xt", hT, "h", a_m, ada[:, 0:16], 1, tts=[0])
        if ti + 1 < NT:
            P.dma("sp", xt3, xv[:, :, (ti + 1) * 512:(ti + 2) * 512], writes=[("xt", kc, 0) for kc in range(KC)] + [("mvb", 0), ("mvb", 1)])
        if stop == 2:
            return finish_now([("h", 0)])
        for c in range(8):
            for kc in range(KC):
                P.op("pe", lambda e, c=c, kc=kc: e.matmul(pba[:, c * 8:(c + 1) * 8], lhsT=hT[:, kc, c * 64:(c + 1) * 64], rhs=wba[:, kc, :],
                                                      start=(kc == 0), stop=(kc == KC - 1)), reads=[("h", 0), wba], writes=[ps[2]])
        pba3 = pba.rearrange("p (c n) -> p c n", n=8)
        P.op("act", lambda e: e.activation(out=beta[:], in_=pba3[:, :, 0:4], func=AF.Sigmoid), reads=[ps[2]], writes=[beta])
        P.op("dve", lambda e: e.tensor_tensor(out=xg[:], in0=pba3[:, :, 4:8], in1=dtb[:], op=ALU.add), reads=[ps[2], dtb], writes=[xg])
        if stop == 30:
            return finish_now([beta, xg])
        P.op("dve", lambda e: e.tensor_scalar(out=nbeta[:], in0=beta[:], scalar1=-1.0, scalar2=None, op0=ALU.mult), reads=[beta], writes=[nbeta])
        P.op("act", lambda e: e.activation(out=xg[:], in_=xg[:], func=AF.Exp), reads=[xg], writes=[xg])
        P.op("act", lambda e: e.activation(out=xg[:], in_=xg[:], func=AF.Ln, bias=C.oneb[0:64, :], scale=1.0), reads=[xg, C.oneb], writes=[xg])
        P.op("dve", lambda e: e.scalar_tensor_tensor(out=graw[:], in0=xg[:], scalar=-1.0, in1=eA[:], op0=ALU.mult, op1=ALU.mult),
             reads=[xg, eA], writes=[graw])
        if stop == 31:
            return finish_now([beta, nbeta, graw])
        g2 = graw[:].rearrange("p c h -> p (c h)")
        P.op("pe", lambda e: e.matmul(pgc, lhsT=U64[:], rhs=g2, start=True, stop=True), reads=[U64, graw], writes=[ps[2]])
        P.op("pe", lambda e: e.matmul(pgl, lhsT=C.ones[0:64, :], rhs=g2, start=True, stop=True), reads=[C.ones, graw], writes=[ps[2]])
        gc2 = gcum[:].rearrange("p c h -> p (c h)")
        P.op("dve", lambda e: e.tensor_copy(out=gc2, in_=pgc), reads=[ps[2]], writes=[gcum])
        P.op("act", lambda e: e.activation(out=eg[:].rearrange("p c h -> p (c h)"), in_=pgc, func=AF.Exp), reads=[ps[2]], writes=[eg])
        P.op("act", lambda e: e.activation(out=egl[:], in_=pgl, func=AF.Exp), reads=[ps[2]], writes=[egl])
        if stop == 32:
            return finish_now([beta, nbeta, graw, gcum, eg, egl])
        kd2 = kds[:].rearrange("p c h -> p (c h)")
        P.op("dve", lambda e: e.tensor_tensor(out=kd2, in0=pgl[0:64, :], in1=gc2, op=ALU.subtract), reads=[ps[2], gcum], writes=[kds])
        if stop == 33:
            return finish_now([beta, nbeta, graw, gcum, eg, egl, kds])
        P.op("act", lambda e: e.activation(out=kd2, in_=kd2, func=AF.Exp), reads=[kds], writes=[kds])
        if stop == 34:
            return finish_now([beta, nbeta, graw, gcum, eg, egl, kds])
        P.op("dve", lambda e: e.tensor_tensor(out=bw[:], in0=beta[:], in1=eg[:], op=ALU.mult), reads=[beta, eg], writes=[bw])
        if stop == 3:
            return finish_now([beta, nbeta, gcum, eg, egl, kds, bw])
        for n in range(8):
            pp = ps[n % 2]
            for kc in range(KC):
                P.op("pe", lambda e, pp=pp, kc=kc, n=n: e.matmul(pp[:], lhsT=wqk[:, kc, n * 128:(n + 1) * 128], rhs=hT[:, kc, :],
                                                            start=(kc == 0), stop=(kc == KC - 1)), reads=[("h", 0)] + wkeys, writes=[pp])
            P.op("pool", lambda e, n=n: e.tensor_copy(out=pre[:, n, 0:3], in_=pre[:, n, 512:515]), reads=[(pre, n)], writes=[(pre, n)])
            P.op("act", lambda e, pp=pp, n=n: e.activation(out=pre[:, n, 3:515], in_=pp[:], func=AF.Copy), reads=[pp], writes=[(pre, n)])
            acc = C.next_tmp()
            P.op("dve", lambda e, acc=acc, n=n: e.tensor_scalar(out=acc[:], in0=pre[:, n, 0:512], scalar1=convw[:, n, 0:1], scalar2=None, op0=ALU.mult),
                 reads=[(pre, n), convw], writes=[acc])
            for i in range(1, 4):
                P.op("dve", lambda e, acc=acc, n=n, i=i: e.scalar_tensor_tensor(out=acc[:], in0=pre[:, n, i:i + 512], scalar=convw[:, n, i:i + 1],
                                                                             in1=acc[:], op0=ALU.mult, op1=ALU.add),
                     reads=[(pre, n), convw, acc], writes=[acc])
            if n >= 4:
                P.op("act", lambda e, acc=acc, n=n: e.activation(out=vT[n - 4][:], in_=acc[:], func=AF.Silu), reads=[acc], writes=[vT[n - 4]])
            else:
                actf = C.next_tmp()
                P.op("act", lambda e, acc=acc, actf=actf: e.activation(out=actf[:], in_=acc[:], func=AF.Silu), reads=[acc], writes=[actf])
                sq = C.next_sq()
                P.op("act", lambda e, sq=sq, actf=actf: e.activation(out=sq[:], in_=actf[:], func=AF.Square), reads=[actf], writes=[sq])
                P.op("pe", lambda e, sq=sq: e.matmul(ps[6][:], lhsT=C.ones[:], rhs=sq[:], start=True, stop=True), reads=[sq, C.ones], writes=[ps[6]])
                P.op("act", lambda e: e.activation(out=C.rstd[:], in_=ps[6][:], func=AF.Ln, scale=1.0, bias=C.epsb[:]),
                     reads=[ps[6], C.epsb], writes=[C.rstd])
                P.op("act", lambda e: e.activation(out=C.rstd[:], in_=C.rstd[:], func=AF.Exp, scale=-0.5), reads=[C.rstd], writes=[C.rstd])
                sc = (128.0 ** -0.5) if n < 2 else 1.0
                P.op("dve", lambda e, n=n, sc=sc, actf=actf: e.scalar_tensor_tensor(out=qkT[n][:], in0=actf[:], scalar=sc, in1=C.rstd[:], op0=ALU.mult, op1=ALU.mult),
                     reads=[actf, C.rstd], writes=[qkT[n]])
        if stop == 4:
            return finish_now(qkT + vT)
        for c in range(8):
            pz = ps[c % 2]
            for kc in range(KC):
                P.op("pe", lambda e, pz=pz, kc=kc, c=c: e.matmul(pz[0:64, :], lhsT=hT[:, kc, c * 64:(c + 1) * 64], rhs=wqk[:, kc, 1024:1536],
                                                            start=(kc == 0), stop=(kc == KC - 1)), reads=[("h", 0)] + wkeys, writes=[pz])
            zt = C.next_tmp()
            P.op("act", lambda e, pz=pz, zt=zt: e.activation(out=zt[0:64, :], in_=pz[0:64, :], func=AF.Silu), reads=[pz], writes=[zt])
            P.op("pool", lambda e, zt=zt, c=c: e.tensor_tensor(out=gz[:, c, :], in0=zt[0:64, :], in1=gn[:], op=ALU.mult), reads=[zt, gn], writes=[(gz, c)])

        if stop == 5:
            return finish_now([(gz, c) for c in range(8)])
        X2 = lambda t: t[:].rearrange("p h i -> p (h i)")

        def local(c, hb):
            TT, attnT, kdec, bv, nwT = hb
            cs = slice(c * 64, (c + 1) * 64)
            for h in range(4):
                P.op("dve", lambda e, h=h, c=c: e.tensor_scalar(out=Dg[:, h, :], in0=I64, scalar1=gcum[:, c, h:h + 1], scalar2=None, op0=ALU.mult),
                     reads=[I4, gcum], writes=[Dg])
            for hq in range(2):
                P.op("pe", lambda e, hq=hq, cs=cs: e.transpose(ptb[0:64, hq * 128:(hq + 1) * 128], qkT[2 + hq][:, cs], identb[:]),
                     reads=[qkT[2 + hq], identb], writes=[ps[7]])
            for hv in range(4):
                P.op("pe", lambda e, hv=hv, cs=cs: e.transpose(ptb[0:64, 256 + hv * 128:256 + (hv + 1) * 128], vT[hv][:, cs], identb[:]),
                     reads=[vT[hv], identb], writes=[ps[7]])
            for hq in range(2):
                P.op("pe", lambda e, hq=hq, cs=cs: e.matmul(pKQ[:, hq * 64:(hq + 1) * 64], lhsT=qkT[2 + hq][:, cs], rhs=qkT[2 + hq][:, cs], start=True, stop=True),
                     reads=[qkT[2 + hq]], writes=[ps[3]])
                P.op("pe", lambda e, hq=hq, cs=cs: e.matmul(pKQ[:, 128 + hq * 64:128 + (hq + 1) * 64], lhsT=qkT[2 + hq][:, cs], rhs=qkT[hq][:, cs], start=True, stop=True),
                     reads=[qkT[2 + hq], qkT[hq]], writes=[ps[3]])
            yield
            P.op("pe", lambda e: e.matmul(pG, lhsT=C.ones[0:64, 0:64], rhs=Dg[:].rearrange("p h i -> p (h i)"), start=True, stop=True),
                 reads=[C.ones, Dg], writes=[ps[3]])
            for hv in range(4):
                hq = hv // 2
                P.op("act", lambda e, hv=hv, hq=hq, c=c: e.activation(out=kdec[:, hv, :], in_=ptb[0:64, hq * 128:(hq + 1) * 128], func=AF.Copy,
                                                                 scale=kds[:, c, hv:hv + 1]), reads=[ps[7], kds], writes=[kdec])
                P.op("act", lambda e, hv=hv, hq=hq, c=c: e.activation(out=kbw[:, hv, :], in_=ptb[0:64, hq * 128:(hq + 1) * 128], func=AF.Copy,
                                                                 scale=bw[:, c, hv:hv + 1]), reads=[ps[7], bw], writes=[kbw])
                P.op("act", lambda e, hv=hv, c=c: e.activation(out=bv[:, hv, :], in_=ptb[0:64, 256 + hv * 128:256 + (hv + 1) * 128], func=AF.Copy,
                                                          scale=beta[:, c, hv:hv + 1]), reads=[ps[7], beta], writes=[bv])
            yield
            pG3 = pG.rearrange("p (h i) -> p h i", h=4)
            for h in range(4):
                P.op("dve", lambda e, h=h, c=c: e.tensor_scalar(out=d1[:, h, :], in0=pG3[:, h, :], scalar1=gcum[:, c, h:h + 1], scalar2=0.0,
                                                           op0=ALU.subtract, op1=ALU.min), reads=[ps[3], gcum], writes=[d1])
                P.op("dve", lambda e, h=h, c=c: e.tensor_scalar(out=d2[:, h, :], in0=pG3[:, h, :], scalar1=gcum[:, c, h:h + 1], scalar2=0.0,
                                                           op0=ALU.subtract, op1=ALU.max), reads=[ps[3], gcum], writes=[d2])
            yield
            P.op("act", lambda e: e.activation(out=decT[:], in_=d1[:], func=AF.Exp), reads=[d1], writes=[decT])
            P.op("act", lambda e: e.activation(out=dec[:], in_=d2[:], func=AF.Exp, scale=-1.0), reads=[d2], writes=[dec])
            yield
            P.op("pool", lambda e: e.tensor_tensor(out=X2(decT), in0=X2(decT), in1=UM4[:], op=ALU.mult), reads=[decT, UM4], writes=[decT])
            P.op("pool", lambda e: e.tensor_tensor(out=X2(dec), in0=X2(dec), in1=LS4[:], op=ALU.mult), reads=[dec, LS4], writes=[dec])
            yield
            for hv in range(4):
                hq = hv // 2
                P.op("dve", lambda e, hv=hv, hq=hq, c=c: e.scalar_tensor_tensor(out=Y0[:, hv, :], in0=pKQ[:, hq * 64:(hq + 1) * 64], scalar=nbeta[:, c, hv:hv + 1],
                                                                           in1=dec[:, hv, :], op0=ALU.mult, op1=ALU.mult),
                     reads=[ps[3], nbeta, dec], writes=[Y0])
            for hv in range(4):
                hq = hv // 2
                P.op("dve", lambda e, hv=hv, hq=hq: e.tensor_tensor(out=attnT[:, hv, :], in0=pKQ[:, 128 + hq * 64:128 + (hq + 1) * 64], in1=decT[:, hv, :], op=ALU.mult),
                     reads=[ps[3], decT], writes=[attnT])
            yield
            for hv in range(4):
                P.op("pe", lambda e, hv=hv: e.transpose(pX0[:, hv * 64:(hv + 1) * 64], Y0[:, hv, :], I64), reads=[Y0, I4], writes=[ps[2]])
            yield
            Xc, Yc, Rc = XX[0], Y0, RR[0]
            P.op("act", lambda e, Xc=Xc: e.activation(out=X2(Xc), in_=pX0, func=AF.Copy), reads=[ps[2]], writes=[Xc])
            P.op("dve", lambda e, Rc=Rc: e.tensor_tensor(out=X2(Rc), in0=pX0, in1=I4[:], op=ALU.add), reads=[ps[2], I4], writes=[Rc])
            yield
            pendR = None
            for lvl in range(5):
                Yn = YY[lvl % 2]
                Xn = XX[(lvl + 1) % 2]
                Rn = RR[(lvl + 1) % 2]
                for hv in range(4):
                    P.op("pe", lambda e, hv=hv, Xc=Xc, Yc=Yc: e.matmul(pY[:, hv * 64:(hv + 1) * 64], lhsT=Xc[:, hv, :], rhs=Yc[:, hv, :], start=True, stop=True),
                         reads=[Xc, Yc], writes=[ps[4]])
                if lvl < 4:
                    for hv in range(4):
                        P.op("pe", lambda e, hv=hv, Xc=Xc, Yc=Yc: e.matmul(pX[:, hv * 64:(hv + 1) * 64], lhsT=Yc[:, hv, :], rhs=Xc[:, hv, :], start=True, stop=True),
                             reads=[Xc, Yc], writes=[ps[4]])
                yield
                P.op("act", lambda e, Yn=Yn: e.activation(out=X2(Yn), in_=pY, func=AF.Copy), reads=[ps[4]], writes=[Yn])
                if lvl < 4:
                    P.op("dve", lambda e, Xn=Xn: e.tensor_copy(out=X2(Xn), in_=pX), reads=[ps[4]], writes=[Xn])
                yield
                for hv in range(4):
                    P.op("pe", lambda e, hv=hv, Rc=Rc, Yn=Yn: e.matmul(pR[:, hv * 64:(hv + 1) * 64], lhsT=Yn[:, hv, :], rhs=Rc[:, hv, :], start=True, stop=True),
                         reads=[Yn, Rc], writes=[ps[5]])
                if lvl < 4:
                    P.op("dve", lambda e, Rn=Rn, Rc=Rc: e.tensor_tensor(out=X2(Rn), in0=pR, in1=X2(Rc), op=ALU.add), reads=[ps[5], Rc], writes=[Rn])
                else:
                    yield
                    P.op("dve", lambda e, Rc=Rc: e.tensor_tensor(out=X2(TT), in0=pR, in1=X2(Rc), op=ALU.add), reads=[ps[5], Rc], writes=[TT])
                Xc, Yc, Rc = Xn, Yn, Rn
            yield
            for hv in range(4):
                P.op("pe", lambda e, hv=hv: e.matmul(pwT[:, hv * 64:(hv + 1) * 64], lhsT=kbw[:, hv, :], rhs=TT[:, hv, :], start=True, stop=True),
                     reads=[kbw, TT], writes=[ps[5]])
            yield
            P.op("act", lambda e: e.activation(out=nwT[:].rearrange("p h i -> p (h i)"), in_=pwT, func=AF.Copy, scale=-1.0), reads=[ps[5]], writes=[nwT])
            yield

        def state(c, hb):
            TT, attnT, kdec, bv, nwT = hb
            cs = slice(c * 64, (c + 1) * 64)
            for hv in range(4):
                P.op("pe", lambda e, hv=hv: e.matmul(pVN[:, hv * 128:(hv + 1) * 128], lhsT=TT[:, hv, :], rhs=bv[:, hv, :], start=True, stop=False),
                     reads=[TT, bv], writes=[ps[6]])
                P.op("pe", lambda e, hv=hv: e.matmul(pVN[:, hv * 128:(hv + 1) * 128], lhsT=nwT[:, hv, :], rhs=Sb[:, hv, :], start=False, stop=True),
                     reads=[nwT, Sb], writes=[ps[6]])
            for hv in range(4):
                hq = hv // 2
                P.op("pe", lambda e, hv=hv, hq=hq, cs=cs: e.matmul(pO[:, hv * 128:(hv + 1) * 128], lhsT=qkT[hq][:, cs], rhs=Sb[:, hv, :], start=True, stop=True),
                     reads=[qkT[hq], Sb], writes=[ps[0]])
            yield
            P.op("dve", lambda e: e.tensor_copy(out=vn[:], in_=pVN), reads=[ps[6]], writes=[vn])
            for hv in range(4):
                P.op("act", lambda e, hv=hv, c=c: e.activation(out=osb[:, hv, :], in_=pO[:, hv * 128:(hv + 1) * 128], func=AF.Copy, scale=eg[:, c, hv:hv + 1]),
                     reads=[ps[0], eg], writes=[osb])
            yield
            for hv in range(4):
                P.op("pe", lambda e, hv=hv: e.matmul(pSU[:, hv * 128:(hv + 1) * 128], lhsT=kdec[:, hv, :], rhs=vn[:, hv * 128:(hv + 1) * 128], start=True, stop=True),
                     reads=[kdec, vn], writes=[pSU])
            for hv in range(4):
                P.op("pe", lambda e, hv=hv: e.matmul(pO[:, hv * 128:(hv + 1) * 128], lhsT=attnT[:, hv, :], rhs=vn[:, hv * 128:(hv + 1) * 128],
                                                start=True, stop=True), reads=[attnT, vn], writes=[ps[0]])
            yield
            for hv in range(4):
                P.op("dve", lambda e, hv=hv, c=c: e.scalar_tensor_tensor(out=Sf[:, hv, :], in0=Sf[:, hv, :], scalar=egl[:, c * 4 + hv:c * 4 + hv + 1],
                                                                    in1=pSU[:, hv * 128:(hv + 1) * 128], op0=ALU.mult, op1=ALU.add),
                     reads=[Sf, egl, pSU], writes=[Sf])
            yield
            P.op("act", lambda e: e.activation(out=Sb[:], in_=Sf[:], func=AF.Copy), reads=[Sf], writes=[Sb])
            P.op("dve", lambda e, c=c: e.tensor_tensor(out=Oall[:, c * 4:(c + 1) * 4, :], in0=osb[:], in1=pO.rearrange("p (h d) -> p h d", h=4), op=ALU.add),
                 reads=[ps[0], osb], writes=[Oall])
            yield

        def drive(gens):
            gens = [g for g in gens if g is not None]
            while gens:
                for g in list(gens):
                    try:
                        next(g)
                    except StopIteration:
                        gens.remove(g)
        drive([local(0, HB[0])])
        for c in range(8):
            drive([state(c, HB[c % 2]), local(c + 1, HB[(c + 1) % 2]) if c < 7 else None])

        for c in range(8):
            P.op("pool", lambda e, c=c: e.tensor_tensor(out=Osq[:], in0=Oall[:, c * 4:(c + 1) * 4, :], in1=Oall[:, c * 4:(c + 1) * 4, :], op=ALU.mult),
                 reads=[Oall], writes=[Osq])
            P.op("dve", lambda e, c=c: e.tensor_reduce(out=ss[:, c * 4:(c + 1) * 4], in_=Osq[:], axis=AX.X, op=ALU.add), reads=[Osq], writes=[ss])
        P.op("act", lambda e: e.activation(out=ss[:], in_=ss[:], func=AF.Ln, scale=1.0 / 128.0, bias=C.epsb[0:64, :]), reads=[ss, C.epsb], writes=[ss])
        P.op("act", lambda e: e.activation(out=ss[:], in_=ss[:], func=AF.Exp, scale=-0.5), reads=[ss], writes=[ss])
        for c in range(8):
            for hv in range(4):
                eng = "dve"
                P.op(eng, lambda e, c=c, hv=hv: e.scalar_tensor_tensor(out=ot[:, c, hv * 128:(hv + 1) * 128], in0=Oall[:, c * 4 + hv, :],
                                                                      scalar=ss[:, c * 4 + hv:c * 4 + hv + 1], in1=gz[:, c, hv * 128:(hv + 1) * 128],
                                                                      op0=ALU.mult, op1=ALU.mult),
                     reads=[Oall, ss, (gz, c)], writes=[(ot, c)])
        P.dma("sp", ov[:, ti * 8:(ti + 1) * 8, :], ot[:], reads=[(ot, c) for c in range(8)], writes=[("out", ti)])
        out_keys.append(("out", ti))
    P.wait_all("sp", out_keys + ["out_adax"])
    P.emit()
    return nc


def stage1_inputs(inp, i, consts, NT=16):
    T = NT * 512
    w = inp["gdn_w_in"][0]
    q = w[:, 256 * i:256 * i + 256]
    k = w[:, 2048 + 256 * i:2048 + 256 * i + 256]
    v = w[:, 4096 + 512 * i:4096 + 512 * i + 512]
    z = w[:, 8192 + 512 * i:8192 + 512 * i + 512]
    wba = np.concatenate([w[:, 12288 + 4 * i:12288 + 4 * i + 4], w[:, 12320 + 4 * i:12320 + 4 * i + 4]], axis=1)
    cw = inp["gdn_conv"][0]
    chans = np.concatenate([np.arange(256 * i, 256 * i + 256), 2048 + np.arange(256 * i, 256 * i + 256), 4096 + np.arange(512 * i, 512 * i + 512)])
    convw = np.ascontiguousarray(cw[:, chans].reshape(4, 8, 128).transpose(2, 1, 0))
    rep = lambda a: np.ascontiguousarray(np.broadcast_to(a[None, None, :], (64, 8, 4)), dtype=np.float32)
    m = dict(xT=np.ascontiguousarray(inp["x"][0][:T].T), c=col_layout(inp["c"][0]),
             adaW=consts["adaW0"], adaB=col_layout(inp["ada_b"][0][0:4096]),
             nmix=col_layout(inp["norm_mix"][0]), wqkvz=np.ascontiguousarray(np.concatenate([q, k, v, z], axis=1)),
             wba=np.ascontiguousarray(wba), convw=convw, alog=rep(inp["gdn_a_log"][0][4 * i:4 * i + 4]), dtb=rep(inp["gdn_dt_bias"][0][4 * i:4 * i + 4]),
             gnorm=np.ascontiguousarray(np.broadcast_to(np.tile(inp["gdn_norm"][0], 4)[None, :], (64, 512)), dtype=np.float32))
    m.update({k: v for k, v in consts.items() if k not in ("adaW0", "Wx", "bx")})
    m["Wx"] = np.ascontiguousarray(consts["Wx"][:, 28 * i:28 * (i + 1), :])
    m["bx"] = np.ascontiguousarray(consts["bx"][:, 28 * i:28 * (i + 1)])
    return m


def _run(nc, maps):
    res = run_bass_kernel_spmd(nc, maps, core_ids=list(range(NCORES)))
    return res.results


def kernel(**inp):
    import ml_dtypes
    inp = {k: np.asarray(v) for k, v in inp.items()}
    x = inp["x"][0]
    cc = col_layout(inp["c"][0])
    consts = stage1_host_consts()
    consts["adaW0"] = mv_layout(inp["ada_w"][0][:, 0:4096])
    consts["Wx"] = mv_layout(np.concatenate([inp["ada_w"][0][:, 4096:], inp["kv_ada_w"], inp["ada_w"][1], inp["out_ada_w"]], axis=1))
    consts["bx"] = col_layout(np.concatenate([inp["ada_b"][0][4096:], inp["kv_ada_b"], inp["ada_b"][1], inp["out_ada_b"]]))
    r1 = _run(build_stage1(), [stage1_inputs(inp, i, consts) for i in range(NCORES)])
    o_full = np.concatenate([r1[i]["o"] for i in range(NCORES)], axis=1)
    adax = np.ascontiguousarray(np.concatenate([r1[i]["adax"] for i in range(NCORES)], axis=1))
    ada0x, kvax, ada1m, ada1x, outax = (np.ascontiguousarray(adax[:, a:b]) for a, b in ((0, 64), (64, 96), (96, 128), (128, 192), (192, 224)))
    maps = []
    fb = np.ascontiguousarray(np.broadcast_to(inp["forget_b"][None, :], (128, 16)), dtype=np.float32)
    for i in range(NCORES):
        ts = slice(i * TOK, (i + 1) * TOK)
        maps.append(dict(xT=np.ascontiguousarray(x[ts].T), oT=np.ascontiguousarray(o_full[ts].T),
                         ada=ada0x, kva=kvax, nffn=col_layout(inp["norm_ffn"][0]),
                         nkv=col_layout(inp["kv_norm"]), wout=inp["gdn_w_out"][0], win=inp["ffn_w_in"][0], wo2=inp["ffn_w_out"][0],
                         kvw=inp["kv_w"], knorm=col_layout(inp["k_norm"]), fb=fb))
    r2 = _run(build_stage2(), maps)
    x1 = np.concatenate([r2[i]["x1T"].T for i in range(NCORES)], axis=0)
    KT = np.concatenate([r2[i]["kT"] for i in range(NCORES)], axis=1).reshape(2, 256, SEQ)
    V = np.concatenate([r2[i]["v"] for i in range(NCORES)], axis=0)
    lf = np.concatenate([r2[i]["lf"] for i in range(NCORES)], axis=0)
    U = (np.arange(128)[:, None] <= np.arange(128)[None, :]).astype(np.float32)
    toks = [np.concatenate([np.arange(b * 128, (b + 1) * 128) for b in core_blocks(i)]) for i in range(NCORES)]
    maps = []
    for i in range(NCORES):
        masks, selx = stage3a_host_consts(i)
        maps.append(dict(xT=np.ascontiguousarray(x1[toks[i]].T), ada=ada1m,
                         nmix=col_layout(inp["norm_mix"][1]), fwin=inp["fox_w_in"][0],
                         qnorm=col_layout(inp["q_norm"][0]), KT=np.ascontiguousarray(KT), V=V, lf=lf, U=U, masks=masks, selx=selx))
    r3 = _run(build_stage3a(), maps)
    maps = []
    for i in range(NCORES):
        maps.append(dict(xT=np.ascontiguousarray(x1[toks[i]].T), oT=r3[i]["ogT"],
                         ada=ada1x, kva=outax, nffn=col_layout(inp["norm_ffn"][1]),
                         nkv=col_layout(inp["out_norm"]), wout=inp["fox_w_out"][0], win=inp["ffn_w_in"][1], wo2=inp["ffn_w_out"][1]))
    r4 = _run(build_stage2(final=True), maps)
    out = np.zeros((1, SEQ, D), np.float32)
    for i in range(NCORES):
        out[0, toks[i]] = r4[i]["x1T"].T
    return out
```
